# Optimizing a Trainium2 kernel written in Bass

```python
import math
import jax, jax.numpy as jnp
from jax import lax
import numpy as np

D_MODEL = 1024
BATCH = 2
SEQ = 8192
DEPTH = 4

GRID_W = 64
CTX_LEN = 256
N_MIXERS = 4
Q_BLOCK = 128
ROPE_BASE = 10000.0
EPS = 1e-6
F32 = jnp.float32

MLA_HEADS = 16
MLA_Q_RANK = 384
MLA_KV_RANK = 256
MLA_NOPE = 64
MLA_ROPE = 32
MLA_V = 64
MLA_WIDTH = MLA_HEADS * MLA_V

GLA_HEADS = 4
GLA_DK = D_MODEL // 2
GLA_DV = D_MODEL
GLA_GATE_RANK = 16
GLA_GATE_NORM = 16.0
GLA_CHUNK = 64

GQA_HEADS = 16
GQA_KV_HEADS = 4
GQA_HEAD_DIM = 64
GQA_WIDTH = GQA_HEADS * GQA_HEAD_DIM

SSD_INNER = 2 * D_MODEL
SSD_HEAD_DIM = 64
SSD_HEADS = SSD_INNER // SSD_HEAD_DIM
SSD_GROUPS = 4
SSD_STATE = 128
SSD_CONV = 5
SSD_CHUNK = 64
SSD_XBC = SSD_INNER + 2 * SSD_GROUPS * SSD_STATE

kernel_name = "hybrid_prefix_dit_mla_gla_gqa_ssd"


def _n_layers_of(kind):
    return len(range(kind, DEPTH, N_MIXERS))


def split_last(t, sizes):
    idx = [int(v) for v in np.cumsum(sizes)[:-1]]
    return jnp.split(t, idx, axis=-1)


def rms_norm(x, g, eps=EPS):
    xf = x.astype(F32)
    y = xf * lax.rsqrt(jnp.mean(xf * xf, axis=-1, keepdims=True) + eps)
    return (y * g.astype(F32)).astype(x.dtype)


def grid_positions(n):
    rows = n // GRID_W
    row = jnp.repeat(jnp.arange(rows, dtype=jnp.int32), GRID_W)
    col = jnp.tile(jnp.arange(GRID_W, dtype=jnp.int32), rows)
    return row, col


def _rope_1d(x, pos):
    half = x.shape[-1] // 2
    inv = ROPE_BASE ** (-jnp.arange(half, dtype=F32) / half)
    ang = pos.astype(F32)[:, None] * inv[None, :]
    cos = jnp.cos(ang)[:, None, :]
    sin = jnp.sin(ang)[:, None, :]
    xf = x.astype(F32)
    x1, x2 = xf[..., :half], xf[..., half:]
    return jnp.concatenate([x1 * cos - x2 * sin, x2 * cos + x1 * sin], axis=-1).astype(x.dtype)


def axial_rope(x, row, col):
    d = x.shape[-1] // 2
    return jnp.concatenate([_rope_1d(x[..., :d], row), _rope_1d(x[..., d:], col)], axis=-1)


def attention_blocks(q, k, v):
    b, lq, h, dk = q.shape
    hkv = k.shape[2]
    rep = h // hkv
    dv = v.shape[-1]
    scale = dk ** -0.5
    blk = min(Q_BLOCK, lq)
    nb = lq // blk
    qb = jnp.moveaxis(q.reshape(b, nb, blk, hkv, rep, dk), 1, 0)

    def one_block(qi):
        s = jnp.einsum('bqhrd,bkhd->bhrqk', qi, k, preferred_element_type=F32) * scale
        p = jax.nn.softmax(s, axis=-1).astype(v.dtype)
        return jnp.einsum('bhrqk,bkhe->bqhre', p, v)

    o = lax.map(one_block, qb)
    return jnp.moveaxis(o, 0, 1).reshape(b, lq, h, dv)


def modulation(cond, w, bias):
    m = jax.nn.silu(cond) @ w + bias
    return jnp.split(m, 3, axis=-1)


def mla_mixer(hc, hx, row, col, w_in, q_norm_g, w_uq, kv_norm_g, w_ukv, w_out):
    def project(h, rotate):
        b, n, _ = h.shape
        cq, ckv, kr, gate = split_last(h @ w_in, (MLA_Q_RANK, MLA_KV_RANK, MLA_ROPE, MLA_WIDTH))
        q = (rms_norm(cq, q_norm_g) @ w_uq).reshape(b, n, MLA_HEADS, MLA_NOPE + MLA_ROPE)
        kv = (rms_norm(ckv, kv_norm_g) @ w_ukv).reshape(b, n, MLA_HEADS, MLA_NOPE + MLA_V)
        q_nope, q_rope = q[..., :MLA_NOPE], q[..., MLA_NOPE:]
        k_nope, v = kv[..., :MLA_NOPE], kv[..., MLA_NOPE:]
        kr = kr[:, :, None, :]
        if rotate:
            q_rope = axial_rope(q_rope, row, col)
            kr = axial_rope(kr, row, col)
        k = jnp.concatenate([k_nope, jnp.broadcast_to(kr, (b, n, MLA_HEADS, MLA_ROPE))], axis=-1)
        q = jnp.concatenate([q_nope, q_rope], axis=-1)
        return q, k, v, gate

    qc, kc, vc, gc = project(hc, False)
    qx, kx, vx, gx = project(hx, True)
    oc = attention_blocks(qc, kc, vc)
    ox = attention_blocks(qx, jnp.concatenate([kc, kx], axis=1), jnp.concatenate([vc, vx], axis=1))

    def out(o, gate):
        b, n = gate.shape[:2]
        return (o.reshape(b, n, MLA_WIDTH) * jax.nn.silu(gate)) @ w_out

    return out(oc, gc), out(ox, gx)


def gla_chunk_scan(q, k, v, g, s0):
    b, l, h, dk = q.shape
    dv = v.shape[-1]
    c = GLA_CHUNK
    nc = l // c
    causal = jnp.tril(jnp.ones((c, c), dtype=bool))

    def to_chunks(t):
        return t.astype(F32).reshape(b, nc, c, h, t.shape[-1]).transpose(1, 0, 3, 2, 4)

    def step(s, inp):
        qc, kc, vc, gc = inp
        bc = jnp.cumsum(gc, axis=2)
        o_inter = jnp.einsum('bhcd,bhde->bhce', qc * jnp.exp(bc), s)
        diff = bc[:, :, :, None, :] - bc[:, :, None, :, :]
        decay = jnp.exp(jnp.where(causal[:, :, None], diff, -jnp.inf))
        a = jnp.einsum('bhid,bhijd->bhij', qc, kc[:, :, None, :, :] * decay)
        o = o_inter + jnp.einsum('bhij,bhje->bhie', a, vc)
        blast = bc[:, :, -1:, :]
        s_new = jnp.exp(blast[:, :, 0, :])[..., None] * s + jnp.einsum('bhcd,bhce->bhde', kc * jnp.exp(blast - bc), vc)
        return s_new, o

    s, o = lax.scan(step, s0, (to_chunks(q), to_chunks(k), to_chunks(v), to_chunks(g)))
    o = o.transpose(1, 0, 3, 2, 4).reshape(b, l, h, dv)
    return o, s


def gla_bidirectional(q, k, v, gf, gb, s0f, s0b):
    flip = lambda t: t[:, ::-1]
    of, sf = gla_chunk_scan(q, k, v, gf, s0f)
    ob, sb = gla_chunk_scan(flip(q), flip(k), flip(v), flip(gb), s0b)
    return of + flip(ob), sf, sb


def gla_mixer(hc, hx, w_in, w_gf, b_gf, w_gb, b_gb, o_norm_g, w_out):
    dk = GLA_DK // GLA_HEADS
    dv = GLA_DV // GLA_HEADS
    sizes = (GLA_DK, GLA_DK, GLA_DV, GLA_DV, GLA_GATE_RANK, GLA_GATE_RANK)

    def project(h):
        b, n, _ = h.shape
        q, k, v, gate, rf, rb = split_last(h @ w_in, sizes)
        heads = lambda t, d: t.reshape(b, n, GLA_HEADS, d)
        gf = jax.nn.log_sigmoid((rf @ w_gf + b_gf).astype(F32)) / GLA_GATE_NORM
        gb = jax.nn.log_sigmoid((rb @ w_gb + b_gb).astype(F32)) / GLA_GATE_NORM
        return heads(q, dk) * (dk ** -0.5), heads(k, dk), heads(v, dv), heads(gf, dk), heads(gb, dk), gate

    qc, kc, vc, gfc, gbc, gatec = project(hc)
    qx, kx, vx, gfx, gbx, gatex = project(hx)
    s0 = jnp.zeros((hc.shape[0], GLA_HEADS, dk, dv), F32)
    oc, sf, sb = gla_bidirectional(qc, kc, vc, gfc, gbc, s0, s0)
    ox, _, _ = gla_bidirectional(qx, kx, vx, gfx, gbx, sf, sb)

    def out(o, gate):
        b, n = gate.shape[:2]
        o = rms_norm(o.astype(gate.dtype), o_norm_g).reshape(b, n, GLA_DV)
        return (o * jax.nn.silu(gate)) @ w_out

    return out(oc, gatec), out(ox, gatex)


def gqa_mixer(hc, hx, row, col, w_in, q_norm_g, k_norm_g, w_out):
    kvw = GQA_KV_HEADS * GQA_HEAD_DIM

    def project(h, rotate):
        b, n, _ = h.shape
        q, k, v, gate = split_last(h @ w_in, (GQA_WIDTH, kvw, kvw, GQA_WIDTH))
        q = rms_norm(q.reshape(b, n, GQA_HEADS, GQA_HEAD_DIM), q_norm_g)
        k = rms_norm(k.reshape(b, n, GQA_KV_HEADS, GQA_HEAD_DIM), k_norm_g)
        v = v.reshape(b, n, GQA_KV_HEADS, GQA_HEAD_DIM)
        if rotate:
            q = axial_rope(q, row, col)
            k = axial_rope(k, row, col)
        return q, k, v, gate

    qc, kc, vc, gc = project(hc, False)
    qx, kx, vx, gx = project(hx, True)
    oc = attention_blocks(qc, kc, vc)
    ox = attention_blocks(qx, jnp.concatenate([kc, kx], axis=1), jnp.concatenate([vc, vx], axis=1))

    def out(o, gate):
        b, n = gate.shape[:2]
        return (o.reshape(b, n, GQA_WIDTH) * jax.nn.silu(gate)) @ w_out

    return out(oc, gc), out(ox, gx)


def dwconv_centred(u, w, bias):
    kw = w.shape[0]
    pad = kw // 2
    out = lax.conv_general_dilated(u, w[:, None, :], window_strides=(1,), padding=[(pad, kw - 1 - pad)],
                                   dimension_numbers=('NWC', 'WIO', 'NWC'), feature_group_count=u.shape[-1])
    return out + bias


def ssd_chunk_scan(x, dt, a, bm, cm, h0):
    b, l, h, p = x.shape
    g, n = bm.shape[2], bm.shape[3]
    rep = h // g
    c = SSD_CHUNK
    nc = l // c
    causal = jnp.tril(jnp.ones((c, c), dtype=bool))

    def to_chunks(t):
        return jnp.swapaxes(t.astype(F32).reshape(b, nc, c, *t.shape[2:]), 0, 1)

    def step(hs, inp):
        xc, dtc, bc, cc = inp
        la = jnp.cumsum(dtc * a, axis=1)
        xg = (xc * dtc[..., None]).reshape(b, c, g, rep, p)
        hg = hs.reshape(b, g, rep, p, n)
        y_inter = jnp.einsum('bcgn,bgrpn->bcgrp', cc, hg) * jnp.exp(la).reshape(b, c, g, rep)[..., None]
        diff = la[:, :, None, :] - la[:, None, :, :]
        lmat = jnp.exp(jnp.where(causal[None, :, :, None], diff, -jnp.inf)).reshape(b, c, c, g, rep)
        cb = jnp.einsum('bign,bjgn->bijg', cc, bc)
        y_intra = jnp.einsum('bijgr,bjgrp->bigrp', cb[..., None] * lmat, xg)
        dlast = jnp.exp(la[:, -1:, :] - la).reshape(b, c, g, rep)
        h_new = jnp.exp(la[:, -1, :]).reshape(b, g, rep)[..., None, None] * hg + \
            jnp.einsum('bcgrp,bcgn->bgrpn', dlast[..., None] * xg, bc)
        return h_new.reshape(b, h, p, n), (y_inter + y_intra).reshape(b, c, h, p)

    hs, y = lax.scan(step, h0, (to_chunks(x), to_chunks(dt), to_chunks(bm), to_chunks(cm)))
    return jnp.swapaxes(y, 0, 1).reshape(b, l, h, p), hs


def ssd_bidirectional(x, dtf, dtb, af, ab, bm, cm, h0f, h0b):
    flip = lambda t: t[:, ::-1]
    yf, hf = ssd_chunk_scan(x, dtf, af, bm, cm, h0f)
    yb, hb = ssd_chunk_scan(flip(x), flip(dtb), ab, flip(bm), flip(cm), h0b)
    return yf + flip(yb), hf, hb


def ssd_mixer(hc, hx, w_in, conv_w, conv_b, dt_bias_f, dt_bias_b, a_log_f, a_log_b, d_skip, gnorm_g, w_out):
    af = -jnp.exp(a_log_f.astype(F32))
    ab = -jnp.exp(a_log_b.astype(F32))
    gw = SSD_GROUPS * SSD_STATE

    def project(h):
        b, n, _ = h.shape
        z, xbc, dtf, dtb = split_last(h @ w_in, (SSD_INNER, SSD_XBC, SSD_HEADS, SSD_HEADS))
        xbc = jax.nn.silu(dwconv_centred(xbc, conv_w, conv_b))
        xs, bm, cm = split_last(xbc, (SSD_INNER, gw, gw))
        xs = xs.reshape(b, n, SSD_HEADS, SSD_HEAD_DIM)
        bm = bm.reshape(b, n, SSD_GROUPS, SSD_STATE)
        cm = cm.reshape(b, n, SSD_GROUPS, SSD_STATE)
        dtf = jax.nn.softplus(dtf.astype(F32) + dt_bias_f.astype(F32))
        dtb = jax.nn.softplus(dtb.astype(F32) + dt_bias_b.astype(F32))
        return z, xs, bm, cm, dtf, dtb

    zc, xsc, bmc, cmc, dtfc, dtbc = project(hc)
    zx, xsx, bmx, cmx, dtfx, dtbx = project(hx)
    h0 = jnp.zeros((hc.shape[0], SSD_HEADS, SSD_HEAD_DIM, SSD_STATE), F32)
    yc, hf, hb = ssd_bidirectional(xsc, dtfc, dtbc, af, ab, bmc, cmc, h0, h0)
    yx, _, _ = ssd_bidirectional(xsx, dtfx, dtbx, af, ab, bmx, cmx, hf, hb)

    def out(y, xs, z):
        b, n = z.shape[:2]
        y = y.astype(z.dtype) + xs * d_skip[:, None]
        y = y.reshape(b, n, SSD_INNER) * jax.nn.silu(z)
        y = rms_norm(y.reshape(b, n, SSD_GROUPS, SSD_INNER // SSD_GROUPS),
                     gnorm_g.reshape(SSD_GROUPS, SSD_INNER // SSD_GROUPS)).reshape(b, n, SSD_INNER)
        return y @ w_out

    return out(yc, xsc, zc), out(yx, xsx, zx)


def setup_inputs(seed: int = 0) -> dict:
    key = jax.random.key(seed)
    ks = iter(jax.random.split(key, 64))
    nA, nB, nC, nD = (_n_layers_of(t) for t in range(N_MIXERS))

    def nrm(shape, fan_in, scale=1.0):
        return jax.random.normal(next(ks), shape, F32) * (scale * fan_in ** -0.5)

    def gain(shape):
        return 1.0 + 0.05 * jax.random.normal(next(ks), shape, F32)

    def small(shape, s=0.02):
        return s * jax.random.normal(next(ks), shape, F32)

    def dt_bias(shape):
        u = jax.random.uniform(next(ks), shape, F32)
        dt = jnp.exp(u * (math.log(0.1) - math.log(0.001)) + math.log(0.001))
        dt = jnp.maximum(dt, 1e-4)
        return dt + jnp.log(-jnp.expm1(-dt))

    def a_log(shape):
        return jnp.log(jax.random.uniform(next(ks), shape, F32, 1.0, 16.0))

    D = D_MODEL
    mla_in = MLA_Q_RANK + MLA_KV_RANK + MLA_ROPE + MLA_WIDTH
    gla_in = 2 * GLA_DK + 2 * GLA_DV + 2 * GLA_GATE_RANK
    gqa_in = 2 * GQA_WIDTH + 2 * GQA_KV_HEADS * GQA_HEAD_DIM
    ssd_in = SSD_INNER + SSD_XBC + 2 * SSD_HEADS
    return {
        "x": jax.random.normal(next(ks), (BATCH, SEQ, D), F32),
        "c": jax.random.normal(next(ks), (BATCH, D), F32),
        "ctx": jax.random.normal(next(ks), (BATCH, CTX_LEN, D), F32),
        "c_ctx": jax.random.normal(next(ks), (D,), F32),
        "ada_w": nrm((DEPTH, D, 3 * D), D, 0.5),
        "ada_b": small((DEPTH, 3 * D)),
        "norm_g": gain((DEPTH, D)),
        "final_g": gain((D,)),
        "mla_w_in": nrm((nA, D, mla_in), D),
        "mla_q_norm": gain((nA, MLA_Q_RANK)),
        "mla_w_uq": nrm((nA, MLA_Q_RANK, MLA_HEADS * (MLA_NOPE + MLA_ROPE)), MLA_Q_RANK),
        "mla_kv_norm": gain((nA, MLA_KV_RANK)),
        "mla_w_ukv": nrm((nA, MLA_KV_RANK, MLA_HEADS * (MLA_NOPE + MLA_V)), MLA_KV_RANK),
        "mla_w_out": nrm((nA, MLA_WIDTH, D), MLA_WIDTH),
        "gla_w_in": nrm((nB, D, gla_in), D),
        "gla_w_gf": nrm((nB, GLA_GATE_RANK, GLA_DK), GLA_GATE_RANK),
        "gla_b_gf": small((nB, GLA_DK), 0.1),
        "gla_w_gb": nrm((nB, GLA_GATE_RANK, GLA_DK), GLA_GATE_RANK),
        "gla_b_gb": small((nB, GLA_DK), 0.1),
        "gla_o_norm": gain((nB, GLA_DV // GLA_HEADS)),
        "gla_w_out": nrm((nB, GLA_DV, D), GLA_DV),
        "gqa_w_in": nrm((nC, D, gqa_in), D),
        "gqa_q_norm": gain((nC, GQA_HEAD_DIM)),
        "gqa_k_norm": gain((nC, GQA_HEAD_DIM)),
        "gqa_w_out": nrm((nC, GQA_WIDTH, D), GQA_WIDTH),
        "ssd_w_in": nrm((nD, D, ssd_in), D),
        "ssd_conv_w": nrm((nD, SSD_CONV, SSD_XBC), SSD_CONV),
        "ssd_conv_b": small((nD, SSD_XBC)),
        "ssd_dt_bias_f": dt_bias((nD, SSD_HEADS)),
        "ssd_dt_bias_b": dt_bias((nD, SSD_HEADS)),
        "ssd_a_log_f": a_log((nD, SSD_HEADS)),
        "ssd_a_log_b": a_log((nD, SSD_HEADS)),
        "ssd_d": gain((nD, SSD_HEADS)),
        "ssd_norm": gain((nD, SSD_INNER)),
        "ssd_w_out": nrm((nD, SSD_INNER, D), SSD_INNER),
    }


def reference(x, c, ctx, c_ctx, ada_w, ada_b, norm_g, final_g,
              mla_w_in, mla_q_norm, mla_w_uq, mla_kv_norm, mla_w_ukv, mla_w_out,
              gla_w_in, gla_w_gf, gla_b_gf, gla_w_gb, gla_b_gb, gla_o_norm, gla_w_out,
              gqa_w_in, gqa_q_norm, gqa_k_norm, gqa_w_out,
              ssd_w_in, ssd_conv_w, ssd_conv_b, ssd_dt_bias_f, ssd_dt_bias_b,
              ssd_a_log_f, ssd_a_log_b, ssd_d, ssd_norm, ssd_w_out):
    row, col = grid_positions(x.shape[1])
    for i in range(DEPTH):
        kind, j = i % N_MIXERS, i // N_MIXERS
        sx, scx, gx = (m[:, None, :] for m in modulation(c, ada_w[i], ada_b[i]))
        sc, scc, gc = modulation(c_ctx, ada_w[i], ada_b[i])
        hx = rms_norm(x, norm_g[i]) * (1 + scx) + sx
        hc = rms_norm(ctx, norm_g[i]) * (1 + scc) + sc
        if kind == 0:
            yc, yx = mla_mixer(hc, hx, row, col, mla_w_in[j], mla_q_norm[j], mla_w_uq[j],
                               mla_kv_norm[j], mla_w_ukv[j], mla_w_out[j])
        elif kind == 1:
            yc, yx = gla_mixer(hc, hx, gla_w_in[j], gla_w_gf[j], gla_b_gf[j], gla_w_gb[j], gla_b_gb[j],
                               gla_o_norm[j], gla_w_out[j])
        elif kind == 2:
            yc, yx = gqa_mixer(hc, hx, row, col, gqa_w_in[j], gqa_q_norm[j], gqa_k_norm[j], gqa_w_out[j])
        else:
            yc, yx = ssd_mixer(hc, hx, ssd_w_in[j], ssd_conv_w[j], ssd_conv_b[j], ssd_dt_bias_f[j],
                               ssd_dt_bias_b[j], ssd_a_log_f[j], ssd_a_log_b[j], ssd_d[j], ssd_norm[j],
                               ssd_w_out[j])
        x = x + gx * yx
        if i < DEPTH - 1:
            ctx = ctx + gc * yc
    return rms_norm(x, final_g)
```

```python
import math
from contextlib import ExitStack

import numpy as np
import ml_dtypes
import concourse.bass as bass
import concourse.mybir as mybir
from concourse.bass_utils import run_bass_kernel_spmd

F32 = mybir.dt.float32
BF16 = mybir.dt.bfloat16
AF = mybir.ActivationFunctionType
ALU = mybir.AluOpType
AX = mybir.AxisListType

D = 1024
SEQ = 8192
CTX = 256
T = SEQ + CTX
NT = T // 128
EPS = 1e-6
EPOCH = 30000

BLOCKS = [(0, 256, 0)] + [(256 + 512 * i, 512, 1) for i in range(16)]
NB = len(BLOCKS)


class Buf:
    __slots__ = ("w", "r")

    def __init__(self):
        self.w = None
        self.r = {}


class TT:
    def __init__(self, t):
        self.t = t
        self.b = Buf()

    def __getitem__(self, idx):
        return self.t[idx]


class Ring:
    def __init__(self, items):
        self.items = items
        self.i = 0

    def next(self):
        it = self.items[self.i % len(self.items)]
        self.i += 1
        return it


class KB:
    def __init__(self, nc, es):
        self.nc = nc
        self.es = es
        self.eng = {"pe": nc.tensor, "act": nc.scalar, "dve": nc.vector, "pool": nc.gpsimd, "sp": nc.sync}
        self.sems = {e: [] for e in self.eng}
        self.cnt = {e: 0 for e in self.eng}
        self.seen = {e: {} for e in self.eng}
        self.last = {e: None for e in self.eng}
        self.slots = {}
        self.slot_i = {}
        for q in ("sp", "pool", "act"):
            self.slots[q] = [[es.enter_context(nc.semaphore(f"d_{q}_{i}")), 0, f"d_{q}_{i}"] for i in range(12)]
            self.slot_i[q] = 0
        self.nsb = 0

    def sb(self, es, shape, dt, name=None):
        self.nsb += 1
        return TT(es.enter_context(self.nc.sbuf_tensor(name or f"sb{self.nsb}", list(shape), dt)))

    def dram(self, shape, dt, name):
        h = self.nc.dram_tensor(name, list(shape), dt, kind="Internal")
        return TT(h.ap())

    def _wait(self, e, deps):
        seen = self.seen[e]
        for ev in deps:
            key, sem, val, src = ev
            if src == "pe" and e == "pe":
                continue
            if seen.get(key, 0) >= val:
                continue
            self.eng[e].wait_ge(sem, val)
            seen[key] = val

    def _deps(self, reads, writes):
        deps = []
        for t in reads:
            if t.b.w is not None:
                deps.append(t.b.w)
        for t in writes:
            if t.b.w is not None:
                deps.append(t.b.w)
            deps.extend(t.b.r.values())
        return deps

    def _mark(self, ev, reads, writes):
        for t in reads:
            t.b.r[ev[0]] = ev
        for t in writes:
            t.b.w = ev
            t.b.r = {}

    def op(self, e, fn, r=(), w=()):
        self._wait(e, self._deps(r, w))
        ins = fn(self.eng[e])
        epoch = self.cnt[e] // EPOCH
        while len(self.sems[e]) <= epoch:
            self.sems[e].append(self.es.enter_context(self.nc.semaphore(f"s_{e}_{len(self.sems[e])}")))
        sem = self.sems[e][epoch]
        val = self.cnt[e] % EPOCH + 1
        ins.then_inc(sem, 1)
        self.cnt[e] += 1
        ev = ((e, epoch), sem, val, e)
        self.last[e] = ev
        self._mark(ev, r, w)
        return ev

    def dma(self, q, out, in_, r=(), w=(), **kw):
        deps = self._deps(r, w)
        slots = self.slots[q]
        si = self.slot_i[q] % len(slots)
        self.slot_i[q] += 1
        slot = slots[si]
        if slot[1] > 0:
            deps.append((slot[2], slot[0], 16 * slot[1], "dma"))
        self._wait(q, deps)
        ins = self.eng[q].dma_start(out=out, in_=in_, **kw)
        ins.then_inc(slot[0], 16)
        slot[1] += 1
        ev = (slot[2], slot[0], 16 * slot[1], "dma")
        self._mark(ev, r, w)
        return ev

    def barrier(self):
        evs = [self.last[e] for e in self.eng if self.last[e] is not None]
        for q in self.slots:
            for slot in self.slots[q]:
                if slot[1] > 0:
                    evs.append((slot[2], slot[0], 16 * slot[1], "dma"))
        for e in self.eng:
            seen = self.seen[e]
            for ev in evs:
                key, sem, val, src = ev
                if src == e:
                    continue
                if seen.get(key, 0) >= val:
                    continue
                self.eng[e].wait_ge(sem, val)
                seen[key] = val


class Prog:
    def __init__(self, nc, layers=(0, 1, 2, 3), debug_x=False, stop=None):
        self.nc = nc
        self.stop = stop
        self.layers = layers
        self.debug_x = debug_x

    def din(self, name, shape, dt=F32):
        return TT(self.nc.dram_tensor(name, list(shape), dt, kind="ExternalInput").ap())

    def build(self):
        nc = self.nc
        with ExitStack() as es:
            self.k = k = KB(nc, es)
            self.es = es
            self.xin = self.din("xin", [T, D])
            self.c2 = self.din("c2", [2, D])
            self.ada_w = self.din("ada_w", [4, D, 3 * D])
            self.ada_b = self.din("ada_b", [4, 3 * D])
            self.norm_g = self.din("norm_g", [4, D])
            self.final_g = self.din("final_g", [D])
            self.W = {}
            for nm, shp in WSHAPES.items():
                self.W[nm] = self.din(nm, shp)
            self.identb_d = self.din("ident_bf", [128, 128], BF16)
            self.rope_mla = self.din("rope_mla", [T, 2, 32])
            self.rope_gqa = self.din("rope_gqa", [T, 2, 64])
            self.tri_d = self.din("tri", [4, 128, 128])
            self.negm_d = self.din("negm", [2, 128, 128])
            self.onehot_d = self.din("onehot3", [96, 32, 128], BF16)
            self.negmb_d = self.din("negmb", [2, 128, 128], BF16)
            self.identf_d = self.din("ident_f", [128, 128])
            if self.debug_x:
                self.out = TT(nc.dram_tensor("y", [T, D], F32, kind="ExternalOutput").ap())
            else:
                self.out = TT(nc.dram_tensor("y", [SEQ, D], F32, kind="ExternalOutput").ap())
            self.xres = k.dram([T, D], F32, "xres")
            self.xblk = [TT(self.xres.t) for _ in range(NB)]
            self.modd = k.dram([4, 2, 3 * D], F32, "modd")
            self.identb = k.sb(es, [128, 128], BF16, "identb")
            k.dma("sp", self.identb[:, :], self.identb_d.t[:, :], w=[self.identb])
            self.ones_f = k.sb(es, [128, 128], F32, "ones_f")
            k.op("pool", lambda g: g.memset(self.ones_f[:, :], 1.0), w=[self.ones_f])
            self.psb = [TT(es.enter_context(nc.psum_tensor(f"ps{i}", [128, 512], F32))) for i in range(8)]
            self.psg = Ring(self.psb[0:6])
            self.pso = Ring(self.psb[6:8])
            self.ps8 = Ring(self.psb)
            self.bcm = [k.sb(es, [128, 3 * D], F32, f"bcm{c}") for c in range(2)]
            self.gmod = [k.sb(es, [128, D], F32, f"gmod{c}") for c in range(2)]
            self.ngbc = k.sb(es, [128, D], F32, "ngbc")

            first = True
            for L in self.layers:
                self.modulation(L)
                if self.stop == "mod":
                    break
                xsrc = self.xin if first else None
                if L == 0:
                    self.layer_attn(L, "mla", xsrc)
                elif L == 2:
                    self.layer_attn(L, "gqa", xsrc)
                elif L == 1:
                    self.layer_gla(L, xsrc)
                elif L == 3:
                    self.layer_ssd(L, xsrc)
                first = False
                k.barrier()
            self.final(self.xin if first else None)
            k.barrier()
        return nc

    def xsrc_ap(self, xsrc, t0, n):
        if xsrc is not None:
            return xsrc.t[t0:t0 + n, :], [xsrc]
        return self.xres.t[t0:t0 + n, :], None

    def modulation(self, L):
        k = self.k
        with ExitStack() as es:
            cT = k.sb(es, [128, 8, 2], F32)
            sT = k.sb(es, [128, 8, 2], F32)
            for kk in range(8):
                k.dma("sp", cT[:, kk, :], self.c2.t[:, kk * 128:(kk + 1) * 128].rearrange("c p -> p c"), w=[cT],
                      allow_slow_non_contiguous=True)
            k.op("act", lambda a: a.activation(out=sT[:, :, :], in_=cT[:, :, :], func=AF.Silu), r=[cT], w=[sT])
            msb = k.sb(es, [2, 3 * D], F32)
            bb = k.sb(es, [2, 3 * D], F32)
            k.dma("sp", bb[:, :], self.ada_b.t[L, :].partition_broadcast(2), w=[bb])
            wr = Ring([k.sb(es, [128, 8, 512], F32) for _ in range(2)])
            for cb in range(6):
                wt = wr.next()
                k.dma("sp", wt[:, :, :],
                      self.ada_w.t[L, :, cb * 512:(cb + 1) * 512].rearrange("(k p) n -> p k n", p=128), w=[wt])
                ps = self.psg.next()
                for kk in range(8):
                    k.op("pe", lambda pe, kk=kk: pe.matmul(ps[0:2, :], lhsT=sT[:, kk, :], rhs=wt[:, kk, :],
                                                          start=(kk == 0), stop=(kk == 7)), r=[sT, wt], w=[ps])
                k.op("dve", lambda v: v.tensor_tensor(out=msb[:, cb * 512:(cb + 1) * 512], in0=ps[0:2, :],
                                                      in1=bb[:, cb * 512:(cb + 1) * 512], op=ALU.add),
                     r=[ps, bb], w=[msb])
            md = TT(self.modd.t)
            k.dma("sp", self.modd.t[L, :, :], msb[:, :], r=[msb], w=[md])
            for c in range(2):
                k.dma("sp", self.bcm[c][:, :], self.modd.t[L, c, :].partition_broadcast(128), r=[md], w=[self.bcm[c]])
            k.dma("sp", self.ngbc[:, :], self.norm_g.t[L, :].partition_broadcast(128), w=[self.ngbc])
            for c in range(2):
                k.op("dve", lambda v, c=c: v.scalar_tensor_tensor(out=self.gmod[c][:, :], in0=self.bcm[c][:, D:2 * D],
                                                                 scalar=1.0, in1=self.ngbc[:, :], op0=ALU.add,
                                                                 op1=ALU.mult),
                     r=[self.bcm[c], self.ngbc], w=[self.gmod[c]])
            k.barrier()

    def load_w(self, es_stage, dst, dview, src_ap, kc, kp, n, stg):
        k = self.k
        CH = 1024
        for kk in range(kc):
            for c0 in range(0, n, CH):
                cn = min(CH, n - c0)
                st = stg.next()
                k.dma("sp", st[0:kp, 0:cn], src_ap[kk * kp:(kk + 1) * kp, c0:c0 + cn], w=[st])
                k.op("pool", lambda g, st=st, kk=kk, c0=c0, cn=cn: g.tensor_copy(out=dview[0:kp, kk, c0:c0 + cn],
                                                                                in_=st[0:kp, 0:cn]),
                     r=[st], w=[dst])

    def norm_block(self, es, bi, xsrc, xr, hT, scr):
        k = self.k
        t0, nt, cond = BLOCKS[bi]
        junk, ss, rs, hf, hb = scr
        for j in range(nt // 128):
            xt = xr.next()
            src, rr = self.xsrc_ap(xsrc, t0 + j * 128, 128)
            k.dma("sp", xt[:, :], src, r=(rr or [self.xblk[bi]]), w=[xt])
            k.op("act", lambda a, xt=xt: a.activation(out=junk[:, :], in_=xt[:, :], func=AF.Square, accum_out=ss[:, :]),
                 r=[xt], w=[junk, ss])
            k.op("act", lambda a: a.activation(out=rs[:, :], in_=ss[:, :], func=AF.Sqrt, scale=1.0 / D, bias=self.epsb[:, :]),
                 r=[ss, self.epsb], w=[rs])
            k.op("dve", lambda v: v.reciprocal(out=rs[:, :], in_=rs[:, :]), r=[rs], w=[rs])
            k.op("dve", lambda v, xt=xt: v.scalar_tensor_tensor(out=hf[:, :], in0=xt[:, :], scalar=rs[:, 0:1],
                                                               in1=self.gmod[cond][:, :], op0=ALU.mult, op1=ALU.mult),
                 r=[xt, rs, self.gmod[cond]], w=[hf])
            k.op("dve", lambda v: v.tensor_tensor(out=hb[:, :], in0=hf[:, :], in1=self.bcm[cond][:, 0:D], op=ALU.add),
                 r=[hf, self.bcm[cond]], w=[hb])
            ps = self.psg.next()
            pv = ps[:, :].bitcast(BF16).rearrange("p (c t) -> p c t", c=8)
            for c in range(8):
                k.op("pe", lambda pe, c=c: pe.transpose(out=pv[:, c, :], in_=hb[:, c * 128:(c + 1) * 128],
                                                        identity=self.identb[:, :]),
                     r=[hb, self.identb], w=[ps])
            k.op("act", lambda a, j=j: a.copy(out=hT[:, :, j * 128:(j + 1) * 128], in_=pv), r=[ps], w=[hT])

    def layer_attn(self, L, kind, xsrc):
        k = self.k
        nc = self.nc
        if kind == "mla":
            H, HK, DQ = 16, 16, 96
            w_in = self.W["mla_w_in"].t[0]
            w_out = self.W["mla_w_out"].t[0]
            GOFF = 672
            scale = 96 ** -0.5
        else:
            H, HK, DQ = 16, 4, 64
            w_in = self.W["gqa_w_in"].t[0]
            w_out = self.W["gqa_w_out"].t[0]
            GOFF = 1536
            scale = 64 ** -0.5
        REP = H // HK
        QT = k.dram([H, DQ, T], BF16, f"QT{L}")
        KT = k.dram([HK, DQ, T], BF16, f"KT{L}")
        VV = k.dram([HK, 128, NT, 65], BF16, f"VV{L}")
        GS = k.dram([8, 128, T], BF16, f"GS{L}")
        OG = k.dram([NB, 64, 16, 512], BF16, f"OG{L}")

        with ExitStack() as es:
            self.epsb = k.sb(es, [128, 1], F32)
            k.op("pool", lambda g: g.memset(self.epsb[:, :], EPS), w=[self.epsb])
            stg = Ring([k.sb(es, [128, 1024], F32) for _ in range(2)])
            NIN = 1696 if kind == "mla" else 2560
            win = k.sb(es, [128, 8, NIN], BF16)
            self.load_w(es, win, win, w_in, 8, 128, NIN, stg)
            if kind == "mla":
                wuq = k.sb(es, [128, 3, 1536], BF16)
                self.load_w(es, wuq, wuq, self.W["mla_w_uq"].t[0], 3, 128, 1536, stg)
                wukv = k.sb(es, [128, 2, 2048], BF16)
                self.load_w(es, wukv, wukv, self.W["mla_w_ukv"].t[0], 2, 128, 2048, stg)
                qnbc = k.sb(es, [128, 384], F32)
                k.dma("sp", qnbc[:, :], self.W["mla_q_norm"].t[0, :].partition_broadcast(128), w=[qnbc])
                kvnbc = k.sb(es, [128, 256], F32)
                k.dma("sp", kvnbc[:, :], self.W["mla_kv_norm"].t[0, :].partition_broadcast(128), w=[kvnbc])
                RD, HF = 32, 8
                rope_d = self.rope_mla
            else:
                qnbc = k.sb(es, [128, 64], F32)
                k.dma("sp", qnbc[:, :], self.W["gqa_q_norm"].t[0, :].partition_broadcast(128), w=[qnbc])
                knbc = k.sb(es, [128, 64], F32)
                k.dma("sp", knbc[:, :], self.W["gqa_k_norm"].t[0, :].partition_broadcast(128), w=[knbc])
                RD, HF = 64, 16
                rope_d = self.rope_gqa
            xr = Ring([k.sb(es, [128, D], F32) for _ in range(2)])
            hTr = Ring([k.sb(es, [128, 8, 512], BF16) for _ in range(2)])
            scr = (k.sb(es, [128, D], BF16), k.sb(es, [128, 1], F32), k.sb(es, [128, 1], F32),
                   k.sb(es, [128, D], F32), k.sb(es, [128, D], BF16))
            qsb = k.sb(es, [128, H * DQ], F32)
            qb = k.sb(es, [128, H, DQ], BF16)
            kb = k.sb(es, [128, HK, DQ], BF16)
            ksb = k.sb(es, [128, HK * DQ if kind == "gqa" else 32], F32)
            vblk = Ring([k.sb(es, [128, 4, HK, 65], BF16) for _ in range(1)])
            for vb_ in vblk.items:
                k.op("pool", lambda g, vb_=vb_: g.memset(vb_[:, :, :, :], 1.0), w=[vb_])
            qTb = Ring([k.sb(es, [DQ, H, 512], BF16) for _ in range(1)])
            kTb = Ring([k.sb(es, [DQ, HK, 512], BF16) for _ in range(1)])
            rtab = Ring([k.sb(es, [128, 2, RD], F32) for _ in range(2)])
            ra = k.sb(es, [128, H, RD], F32)
            rb_ = k.sb(es, [128, H, RD], F32)
            ss2 = k.sb(es, [128, 32], F32)
            rs2 = k.sb(es, [128, 32], F32)
            sq = k.sb(es, [128, H * DQ], F32)
            if kind == "mla":
                cqn = k.sb(es, [128, 640], BF16)
                cT = k.sb(es, [128, 5, 128], BF16)
            gsr = Ring([k.sb(es, [128, 512], BF16) for _ in range(2)])

            def rope(xv, nh, dst, tab):
                cosb = tab[:, 0:1, :].broadcast_to([128, nh, RD])
                k.op("dve", lambda v: v.tensor_tensor(out=ra[:, 0:nh, :], in0=xv, in1=cosb, op=ALU.mult),
                     r=[tab, qsb, ksb], w=[ra])
                x5 = xv.rearrange("p h (g s f) -> p h g s f", g=2, s=2)
                b5 = rb_[:, 0:nh, :].rearrange("p h (g s f) -> p h g s f", g=2, s=2)
                s5 = tab[:, 1, :].rearrange("p (g s f) -> p g s f", g=2, s=2)
                for g in range(2):
                    for s in range(2):
                        sinb = s5[:, g:g + 1, s, :].broadcast_to([128, nh, HF])
                        k.op("dve", lambda v, g=g, s=s, sinb=sinb: v.tensor_tensor(
                            out=b5[:, :, g, s, :], in0=x5[:, :, g, 1 - s, :], in1=sinb, op=ALU.mult),
                            r=[tab, qsb, ksb], w=[rb_])
                k.op("dve", lambda v: v.tensor_tensor(out=dst, in0=ra[:, 0:nh, :], in1=rb_[:, 0:nh, :], op=ALU.add),
                     r=[ra, rb_], w=[qb, kb])

            import os
            for bi, (t0, nt, cond) in enumerate(BLOCKS[:int(os.environ.get('DBG_P1_BLOCKS', NB))]):
                hT = hTr.next()
                self.norm_block(es, bi, xsrc, xr, hT, scr)
                ntile = nt // 128
                vb4 = vblk.next()
                qT = qTb.next()
                kT = kTb.next()
                for j in range(ntile):
                    tt0 = t0 + j * 128
                    kt = tt0 // 128
                    tab = rtab.next()
                    k.dma("sp", tab[:, :, :], rope_d.t[tt0:tt0 + 128, :, :], w=[tab])
                    hTj = lambda kk: hT[:, kk, j * 128:(j + 1) * 128]
                    if kind == "mla":
                        psA = self.psg.next()
                        psB = self.psg.next()
                        for kk in range(8):
                            k.op("pe", lambda pe, kk=kk: pe.matmul(psA[:, 0:384], lhsT=hTj(kk), rhs=win[:, kk, 0:384],
                                                                  start=(kk == 0), stop=(kk == 7)), r=[hT, win], w=[psA])
                        for kk in range(8):
                            k.op("pe", lambda pe, kk=kk: pe.matmul(psB[:, 0:288], lhsT=hTj(kk), rhs=win[:, kk, 384:672],
                                                                  start=(kk == 0), stop=(kk == 7)), r=[hT, win], w=[psB])
                        for (ps_, n_, gb_, o_) in ((psA, 384, qnbc, 0), (psB, 256, kvnbc, 384)):
                            k.op("act", lambda a, ps_=ps_, n_=n_: a.activation(out=sq[:, 0:n_], in_=ps_[:, 0:n_], func=AF.Square,
                                                                              accum_out=ss2[:, 0:1]), r=[ps_], w=[sq, ss2])
                            k.op("act", lambda a, n_=n_: a.activation(out=rs2[:, 0:1], in_=ss2[:, 0:1], func=AF.Sqrt,
                                                                     scale=1.0 / n_, bias=self.epsb[:, :]),
                                 r=[ss2, self.epsb], w=[rs2])
                            k.op("dve", lambda v: v.reciprocal(out=rs2[:, 0:1], in_=rs2[:, 0:1]), r=[rs2], w=[rs2])
                            k.op("dve", lambda v, ps_=ps_, n_=n_, gb_=gb_, o_=o_: v.scalar_tensor_tensor(
                                out=cqn[:, o_:o_ + n_], in0=ps_[:, 0:n_], scalar=rs2[:, 0:1], in1=gb_[:, :],
                                op0=ALU.mult, op1=ALU.mult), r=[ps_, rs2, gb_], w=[cqn])
                        k.op("act", lambda a: a.copy(out=ksb[:, 0:32], in_=psB[:, 256:288]), r=[psB], w=[ksb])
                        pst = self.psg.next()
                        ptv = pst[:, :].bitcast(BF16).rearrange("p (c t) -> p c t", c=8)
                        for c in range(5):
                            k.op("pe", lambda pe, c=c: pe.transpose(out=ptv[:, c, :], in_=cqn[:, c * 128:(c + 1) * 128],
                                                                    identity=self.identb[:, :]), r=[cqn, self.identb], w=[pst])
                        k.op("act", lambda a: a.copy(out=cT[:, :, :], in_=ptv[:, 0:5, :]), r=[pst], w=[cT])
                        for cb in range(3):
                            ps = self.psg.next()
                            for kk in range(3):
                                k.op("pe", lambda pe, kk=kk, cb=cb, ps=ps: pe.matmul(
                                    ps[:, :], lhsT=cT[:, kk, :], rhs=wuq[:, kk, cb * 512:(cb + 1) * 512],
                                    start=(kk == 0), stop=(kk == 2)), r=[cT, wuq], w=[ps])
                            k.op("act", lambda a, cb=cb, ps=ps: a.copy(out=qsb[:, cb * 512:(cb + 1) * 512], in_=ps[:, :]),
                                 r=[ps], w=[qsb])
                        q3 = qsb[:, :].rearrange("p (h d) -> p h d", h=16)
                        k.op("pool", lambda g: g.tensor_copy(out=qb[:, :, 0:64], in_=q3[:, :, 0:64]), r=[qsb], w=[qb])
                        rope(q3[:, :, 64:96], 16, qb[:, :, 64:96], tab)
                        krv = ksb[:, 0:32].rearrange("p (h d) -> p h d", h=1)
                        rope(krv, 1, kb[:, 0:1, 64:96], tab)
                        k.op("pool", lambda g: g.tensor_copy(out=kb[:, 1:16, 64:96],
                                                             in_=kb[:, 0:1, 64:96].broadcast_to([128, 15, 32])),
                             r=[kb], w=[kb])
                        for cb in range(4):
                            ps = self.psg.next()
                            for kk in range(2):
                                k.op("pe", lambda pe, kk=kk, cb=cb, ps=ps: pe.matmul(
                                    ps[:, :], lhsT=cT[:, 3 + kk, :], rhs=wukv[:, kk, cb * 512:(cb + 1) * 512],
                                    start=(kk == 0), stop=(kk == 1)), r=[cT, wukv], w=[ps])
                            p3 = ps[:, :].rearrange("p (h d) -> p h d", h=4)
                            k.op("act", lambda a, cb=cb, p3=p3: a.copy(out=kb[:, cb * 4:(cb + 1) * 4, 0:64], in_=p3[:, :, 0:64]),
                                 r=[ps], w=[kb])
                            k.op("dve", lambda v, cb=cb, p3=p3: v.tensor_copy(out=vb4[:, j, cb * 4:(cb + 1) * 4, 0:64],
                                                                              in_=p3[:, :, 64:128]), r=[ps], w=[vb4])
                    else:
                        import os
                        for cb in range(3 if int(os.environ.get('DBG_STEP', 9)) >= 1 else 0):
                            ps = self.psg.next()
                            for kk in range(8):
                                k.op("pe", lambda pe, kk=kk, cb=cb, ps=ps: pe.matmul(
                                    ps[:, :], lhsT=hTj(kk), rhs=win[:, kk, cb * 512:(cb + 1) * 512],
                                    start=(kk == 0), stop=(kk == 7)), r=[hT, win], w=[ps])
                            SUB = os.environ.get('DBG_SUB', 'abc')
                            if cb < 2:
                                if 'a' in SUB:
                                    k.op("act", lambda a, cb=cb, ps=ps: a.copy(out=qsb[:, cb * 512:(cb + 1) * 512], in_=ps[:, :]),
                                         r=[ps], w=[qsb])
                            elif 'b' in SUB:
                                k.op("act", lambda a, ps=ps: a.copy(out=ksb[:, 0:256], in_=ps[:, 0:256]), r=[ps], w=[ksb])
                                p3 = ps[:, 256:512].rearrange("p (h d) -> p h d", h=4)
                                if 'c' in SUB:
                                    for hh in range(4):
                                        k.op("act", lambda a, hh=hh: a.copy(out=vb4[:, j, hh, 0:64], in_=ps[:, 256 + hh * 64:256 + (hh + 1) * 64]),
                                             r=[ps], w=[vb4])
                        import os
                        DS = int(os.environ.get('DBG_STEP', 9))
                        for (src_, nh, gb_, dstb) in ((qsb, 16, qnbc, qb), (ksb, 4, knbc, kb)) if DS >= 2 else ():
                            s3 = src_[:, 0:nh * 64].rearrange("p (h d) -> p h d", h=nh)
                            sq3 = sq[:, 0:nh * 64].rearrange("p (h d) -> p h d", h=nh)
                            k.op("dve", lambda v, s3=s3, sq3=sq3: v.tensor_tensor(out=sq3, in0=s3, in1=s3, op=ALU.mult),
                                 r=[src_], w=[sq])
                            k.op("dve", lambda v, sq3=sq3, nh=nh: v.tensor_reduce(out=ss2[:, 0:nh], in_=sq3, axis=AX.X, op=ALU.add),
                                 r=[sq], w=[ss2])
                            k.op("act", lambda a, nh=nh: a.activation(out=rs2[:, 0:nh], in_=ss2[:, 0:nh], func=AF.Sqrt,
                                                                     scale=1.0 / 64, bias=self.epsb[:, :]),
                                 r=[ss2, self.epsb], w=[rs2])
                            k.op("dve", lambda v, nh=nh: v.reciprocal(out=rs2[:, 0:nh], in_=rs2[:, 0:nh]), r=[rs2], w=[rs2])
                            k.op("dve", lambda v, s3=s3, nh=nh: v.tensor_tensor(
                                out=s3, in0=s3, in1=rs2[:, 0:nh].unsqueeze(2).broadcast_to([128, nh, 64]), op=ALU.mult),
                                r=[src_, rs2], w=[src_])
                            k.op("dve", lambda v, s3=s3, nh=nh, gb_=gb_: v.tensor_tensor(
                                out=s3, in0=s3, in1=gb_[:, :].unsqueeze(1).broadcast_to([128, nh, 64]), op=ALU.mult),
                                r=[src_, gb_], w=[src_])
                            if DS >= 3:
                                rope(s3, nh, dstb[:, :, :], tab)
                    import os
                    for (srcb, nh, dstT) in ((qb, H, qT), (kb, HK, kT)) if int(os.environ.get('DBG_STEP', 9)) >= 4 else ():
                        for h0 in range(0, nh, 8):
                            hn = min(8, nh - h0)
                            ps = self.psg.next()
                            ptv = ps[:, :].bitcast(BF16).rearrange("p (c t) -> p c t", c=8)
                            for hh in range(hn):
                                k.op("pe", lambda pe, hh=hh, h0=h0, ptv=ptv, srcb=srcb: pe.transpose(
                                    out=ptv[0:DQ, hh, :], in_=srcb[:, h0 + hh, :], identity=self.identb[:, :]),
                                    r=[srcb, self.identb], w=[ps])
                            k.op("act", lambda a, h0=h0, hn=hn, ptv=ptv, dstT=dstT: a.copy(
                                out=dstT[:, h0:h0 + hn, j * 128:(j + 1) * 128], in_=ptv[0:DQ, 0:hn, :]), r=[ps], w=[dstT])
                for hp in range(8):
                    ps = self.psg.next()
                    for kk in range(8):
                        k.op("pe", lambda pe, kk=kk, hp=hp, ps=ps: pe.matmul(
                            ps[:, 0:nt], lhsT=win[:, kk, GOFF + hp * 128:GOFF + (hp + 1) * 128], rhs=hT[:, kk, 0:nt],
                            start=(kk == 0), stop=(kk == 7)), r=[hT, win], w=[ps])
                    gs = gsr.next()
                    k.op("act", lambda a, ps=ps, gs=gs: a.activation(out=gs[:, 0:nt], in_=ps[:, 0:nt], func=AF.Silu),
                         r=[ps], w=[gs])
                    k.dma("pool", GS.t[hp, :, t0:t0 + nt], gs[:, 0:nt], r=[gs], w=[GS])
                for h0 in range(0, H, 4):
                    k.dma("act", QT.t[h0:h0 + 4, :, t0:t0 + nt].rearrange("h d t -> d h t"), qT[:, h0:h0 + 4, 0:nt], r=[qT], w=[QT])
                for h0 in range(0, HK, 4):
                    k.dma("act", KT.t[h0:h0 + 4, :, t0:t0 + nt].rearrange("h d t -> d h t"), kT[:, h0:h0 + 4, 0:nt], r=[kT], w=[KT])
                kt0 = t0 // 128
                for j in range(ntile):
                    for h0 in range(0, HK, 4):
                        k.dma("act", VV.t[h0:h0 + 4, :, kt0 + j, :].rearrange("h p e -> p h e"), vb4[:, j, h0:h0 + 4, :],
                              r=[vb4], w=[VV], allow_slow_non_contiguous=True)
            k.barrier()

        if self.stop == "p1":
            return
        with ExitStack() as es:
            DQP = 128 if DQ == 64 else DQ
            KTs = Ring([k.sb(es, [DQP, T], BF16) for _ in range(2)])
            Vs = Ring([k.sb(es, [128, NT, 65], BF16) for _ in range(2)])
            Qs = Ring([k.sb(es, [DQP, T], BF16) for _ in range(2)])
            if DQP != DQ:
                for b_ in KTs.items + Qs.items:
                    k.op("pool", lambda g, b_=b_: g.memset(b_[DQ:DQP, :], 0.0), w=[b_])
            Gs = Ring([k.sb(es, [64, T], BF16) for _ in range(2)])
            Ps = Ring([k.sb(es, [128, 512], BF16) for _ in range(6)])
            rr = k.sb(es, [65, 512], F32)
            tmp = k.sb(es, [64, 512], F32)
            ogr = Ring([k.sb(es, [64, 512], BF16) for _ in range(2)])
            pss = Ring(self.psb[0:5])
            psm = Ring(self.psb[5:6])
            import os
            for hk in range(int(os.environ.get('DBG_P2_HEADS', HK))):
                Kt = KTs.next()
                Vt = Vs.next()
                k.dma("sp", Kt[0:DQ, :], KT.t[hk], r=[KT], w=[Kt])
                k.dma("sp", Vt[:, :, :], VV.t[hk], r=[VV], w=[Vt])
                for hr in range(REP):
                    h = hk * REP + hr
                    Qt = Qs.next()
                    Gt = Gs.next()
                    k.dma("sp", Qt[0:DQ, :], QT.t[h], r=[QT], w=[Qt])
                    k.dma("sp", Gt[:, :], GS.t[h // 2, (h % 2) * 64:(h % 2) * 64 + 64, :], r=[GS], w=[Gt])
                    for bi, (t0, nt, cond) in enumerate(BLOCKS):
                        nkt = 2 if cond == 0 else NT
                        po = self.pso.next()
                        pend = []

                        def s_mm(kt):
                            ps = pss.next()
                            k.op("pe", lambda pe: pe.matmul(ps[:, 0:nt], lhsT=Kt[:, kt * 128:(kt + 1) * 128], rhs=Qt[:, t0:t0 + nt],
                                                            start=True, stop=True), r=[Kt, Qt], w=[ps])
                            pt = Ps.next()
                            k.op("act", lambda a: a.activation(out=pt[:, 0:nt], in_=ps[:, 0:nt], func=AF.Exp, scale=scale),
                                 r=[ps], w=[pt])
                            return pt

                        def pv_mm(kt, pt):
                            k.op("pe", lambda pe: pe.matmul(po[0:65, 0:nt], lhsT=Vt[:, kt, :], rhs=pt[:, 0:nt],
                                                            start=(kt == 0), stop=(kt == nkt - 1)), r=[Vt, pt], w=[po])

                        SK = 3
                        for kt in range(nkt + SK):
                            if kt < nkt:
                                pend.append((kt, s_mm(kt)))
                            if kt >= SK:
                                a_, b_ = pend.pop(0)
                                pv_mm(a_, b_)
                        k.op("dve", lambda v: v.reciprocal(out=rr[64:65, 0:nt], in_=po[64:65, 0:nt]), r=[po], w=[rr])
                        pm = psm.next()
                        k.op("pe", lambda pe: pe.matmul(pm[0:64, 0:nt], lhsT=self.ones_f[64:65, 0:64], rhs=rr[64:65, 0:nt],
                                                        start=True, stop=True), r=[rr, self.ones_f], w=[pm])
                        k.op("dve", lambda v: v.tensor_tensor(out=tmp[:, 0:nt], in0=pm[0:64, 0:nt], in1=Gt[:, t0:t0 + nt], op=ALU.mult),
                             r=[pm, Gt], w=[tmp])
                        og = ogr.next()
                        k.op("dve", lambda v: v.tensor_tensor(out=og[:, 0:nt], in0=po[0:64, 0:nt], in1=tmp[:, 0:nt], op=ALU.mult),
                             r=[po, tmp], w=[og])
                        k.dma("pool", OG.t[bi, :, h, 0:nt], og[:, 0:nt], r=[og], w=[OG])
            k.barrier()

        if self.stop == "p2":
            return
        with ExitStack() as es:
            stg = Ring([k.sb(es, [128, 1024], F32) for _ in range(2)])
            wo = k.sb(es, [64, 16, D], BF16)
            self.load_w(es, wo, wo, w_out, 16, 64, D, stg)
            ogb = Ring([k.sb(es, [64, 16, 512], BF16) for _ in range(2)])
            xr = Ring([k.sb(es, [128, D], F32) for _ in range(3)])
            tm = k.sb(es, [128, 512], F32)
            self.outproj(es, lambda bi, nt: (OG.t[bi, :, :, 0:nt], OG), ogb, 64, 16, wo, xsrc, xr, tm)
            k.barrier()


    def layer_gla(self, L, xsrc):
        k = self.k
        w_in = self.W["gla_w_in"].t[0]
        w_out = self.W["gla_w_out"].t[0]
        REC = k.dram([NT, 128, 3584], BF16, f"GREC{L}")
        GG = k.dram([NT, 128, 1024], F32, f"GGG{L}")
        GOF = k.dram([NT, 128, 1024], F32, f"GOF{L}")
        recT = [TT(REC.t) for _ in range(NT)]
        ggT = [TT(GG.t) for _ in range(NT)]
        ofT = [TT(GOF.t) for _ in range(NT)]
        ps8 = self.ps8
        with ExitStack() as es:
            self.epsb = k.sb(es, [128, 1], F32)
            k.op("pool", lambda g: g.memset(self.epsb[:, :], EPS), w=[self.epsb])
            onec = k.sb(es, [128, 1], F32)
            k.op("pool", lambda g: g.memset(onec[:, :], 1.0), w=[onec])
            stg = Ring([k.sb(es, [128, 1024], F32) for _ in range(2)])
            win = k.sb(es, [128, 8, 3104], BF16)
            self.load_w(es, win, win, w_in, 8, 128, 3104, stg)
            wg = k.sb(es, [16, 2, 512], F32)
            k.dma("sp", wg[:, 0, :], self.W["gla_w_gf"].t[0], w=[wg])
            k.dma("sp", wg[:, 1, :], self.W["gla_w_gb"].t[0], w=[wg])
            bg = k.sb(es, [128, 2, 512], F32)
            k.dma("sp", bg[:, 0, :], self.W["gla_b_gf"].t[0, :].partition_broadcast(128), w=[bg])
            k.dma("sp", bg[:, 1, :], self.W["gla_b_gb"].t[0, :].partition_broadcast(128), w=[bg])
            xr = Ring([k.sb(es, [128, D], F32) for _ in range(2)])
            hTr = Ring([k.sb(es, [128, 8, 512], BF16) for _ in range(2)])
            scr = (k.sb(es, [128, D], BF16), k.sb(es, [128, 1], F32), k.sb(es, [128, 1], F32),
                   k.sb(es, [128, D], F32), k.sb(es, [128, D], BF16))
            recr = Ring([k.sb(es, [128, 3584], BF16) for _ in range(2)])
            ggr = Ring([k.sb(es, [128, 1024], F32) for _ in range(2)])
            rT = k.sb(es, [16, 2, 128], F32)
            zt = k.sb(es, [128, 512], F32)
            for bi, (t0, nt, cond) in enumerate(BLOCKS):
                hT = hTr.next()
                self.norm_block(es, bi, xsrc, xr, hT, scr)
                for j in range(nt // 128):
                    t = (t0 + j * 128) // 128
                    rec = recr.next()
                    gg = ggr.next()
                    hTj = lambda kk: hT[:, kk, j * 128:(j + 1) * 128]

                    def tokmm(c0, n):
                        ps = ps8.next()
                        for kk in range(8):
                            k.op("pe", lambda pe: pe.matmul(ps[:, 0:n], lhsT=hTj(kk), rhs=win[:, kk, c0:c0 + n],
                                                            start=(kk == 0), stop=(kk == 7)), r=[hT, win], w=[ps])
                        return ps

                    ps = tokmm(512, 512)
                    k.op("act", lambda a: a.copy(out=rec[:, 1024:1536], in_=ps[:, :]), r=[ps], w=[rec])
                    for cb in range(2):
                        ps = tokmm(1024 + cb * 512, 512)
                        k.op("dve", lambda v: v.tensor_copy(out=rec[:, 1536 + cb * 512:2048 + cb * 512], in_=ps[:, :]),
                             r=[ps], w=[rec])
                    for cb in range(2):
                        ps = tokmm(2048 + cb * 512, 512)
                        k.op("act", lambda a: a.activation(out=rec[:, 2560 + cb * 512:3072 + cb * 512], in_=ps[:, :],
                                                           func=AF.Silu), r=[ps], w=[rec])
                    for qk in range(2):
                        ps = ps8.next()
                        for h in range(4):
                            for kk in range(8):
                                k.op("pe", lambda pe: pe.matmul(
                                    ps[:, h * 128:(h + 1) * 128], lhsT=win[:, kk, qk * 512 + h * 128:qk * 512 + (h + 1) * 128],
                                    rhs=hTj(kk), start=(kk == 0), stop=(kk == 7)), r=[hT, win], w=[ps])
                        if qk == 0:
                            k.op("act", lambda a: a.mul(out=rec[:, 0:512], in_=ps[:, :], mul=128 ** -0.5), r=[ps], w=[rec])
                        else:
                            k.op("dve", lambda v: v.tensor_copy(out=rec[:, 512:1024], in_=ps[:, :]), r=[ps], w=[rec])
                    ps = ps8.next()
                    for d in range(2):
                        for kk in range(8):
                            k.op("pe", lambda pe: pe.matmul(
                                ps[0:16, d * 128:(d + 1) * 128], lhsT=win[:, kk, 3072 + 16 * d:3088 + 16 * d],
                                rhs=hTj(kk), start=(kk == 0), stop=(kk == 7)), r=[hT, win], w=[ps])
                    k.op("act", lambda a: a.copy(out=rT[:, :, :], in_=ps[0:16, 0:256].rearrange("p (d t) -> p d t", d=2)),
                         r=[ps], w=[rT])
                    for d in range(2):
                        ps = ps8.next()
                        k.op("pe", lambda pe: pe.matmul(ps[:, :], lhsT=rT[:, d, :], rhs=wg[:, d, :], start=True, stop=True),
                             r=[rT, wg], w=[ps])
                        k.op("dve", lambda v: v.tensor_tensor(out=zt[:, :], in0=ps[:, :], in1=bg[:, d, :], op=ALU.add),
                             r=[ps, bg], w=[zt])
                        k.op("act", lambda a: a.activation(out=zt[:, :], in_=zt[:, :], func=AF.Exp, scale=-1.0), r=[zt], w=[zt])
                        k.op("act", lambda a: a.activation(out=zt[:, :], in_=zt[:, :], func=AF.Ln, bias=onec[:, :]),
                             r=[zt, onec], w=[zt])
                        k.op("dve", lambda v: v.tensor_scalar(out=gg[:, d * 512:(d + 1) * 512], in0=zt[:, :],
                                                              scalar1=-1.0 / 16.0, scalar2=None, op0=ALU.mult),
                             r=[zt], w=[gg])
                    k.dma("pool", REC.t[t], rec[:, :], r=[rec], w=[recT[t]])
                    k.dma("pool", GG.t[t], gg[:, :], r=[gg], w=[ggT[t]])
            k.barrier()

        with ExitStack() as es:
            self.epsb = k.sb(es, [128, 1], F32)
            k.op("pool", lambda g: g.memset(self.epsb[:, :], EPS), w=[self.epsb])
            tri = k.sb(es, [128, 4, 128], F32)
            k.dma("sp", tri[:, :, :], self.tri_d.t.rearrange("m j i -> j m i"), w=[tri])
            S = [k.sb(es, [128, 256], F32) for _ in range(4)]
            Sb = [k.sb(es, [128, 256], BF16) for _ in range(4)]
            recr = Ring([k.sb(es, [128, 3584], BF16) for _ in range(3)])
            ggr = Ring([k.sb(es, [128, 1024], F32) for _ in range(3)])
            E1r = Ring([k.sb(es, [128, 512], F32) for _ in range(2)])
            E2r = Ring([k.sb(es, [128, 512], F32) for _ in range(2)])
            E3r = Ring([k.sb(es, [128, 512], F32) for _ in range(2)])
            qtr = Ring([k.sb(es, [128, 512], BF16) for _ in range(2)])
            ktr = Ring([k.sb(es, [128, 512], BF16) for _ in range(2)])
            khr = Ring([k.sb(es, [128, 512], BF16) for _ in range(2)])
            Amr = Ring([k.sb(es, [128, 512], BF16) for _ in range(2)])
            osb = Ring([k.sb(es, [128, 1024], F32) for _ in range(2)])
            stg = Ring([k.sb(es, [128, 1024], F32) for _ in range(2)])
            wo = k.sb(es, [128, 8, D], BF16)
            self.load_w(es, wo, wo, w_out, 8, 128, D, stg)
            onbc = k.sb(es, [128, 256], F32)
            k.dma("sp", onbc[:, :], self.W["gla_o_norm"].t[0, :].partition_broadcast(128), w=[onbc])
            ofr = Ring([k.sb(es, [128, 1024], F32) for _ in range(2)])
            xr = Ring([k.sb(es, [128, D], F32) for _ in range(2)])
            osum = k.sb(es, [128, 1024], F32)
            ogf = k.sb(es, [128, 1024], F32)
            ogb = k.sb(es, [128, 1024], BF16)
            ogT = k.sb(es, [128, 8, 128], BF16)
            junk = k.sb(es, [128, 256], BF16)
            ss4 = k.sb(es, [128, 4], F32)
            rs4 = k.sb(es, [128, 4], F32)
            tm = k.sb(es, [128, 512], F32)

            def stageA(d, rec, gg):
                cm = tri[:, 0 if d == 0 else 2, :]
                sm = tri[:, 1 if d == 0 else 3, :]
                g = lambda a_, b_: gg[:, d * 512 + a_:d * 512 + b_]
                E1, E2, E3 = E1r.next(), E2r.next(), E3r.next()
                qt, kt_, kh, Am = qtr.next(), ktr.next(), khr.next(), Amr.next()
                psA = ps8.next()
                for h in range(4):
                    k.op("pe", lambda pe: pe.matmul(psA[:, h * 128:(h + 1) * 128], lhsT=g(h * 128, (h + 1) * 128), rhs=cm,
                                                    start=True, stop=True), r=[gg, tri], w=[psA])
                psB = ps8.next()
                k.op("pe", lambda pe: pe.matmul(psB[:, :], lhsT=sm, rhs=g(0, 512), start=True, stop=True), r=[gg, tri], w=[psB])
                k.op("act", lambda a: a.activation(out=E1[:, :], in_=psA[:, :], func=AF.Exp), r=[psA], w=[E1])
                k.op("act", lambda a: a.activation(out=E2[:, :], in_=psA[:, :], func=AF.Exp, scale=-1.0), r=[psA], w=[E2])
                k.op("act", lambda a: a.activation(out=E3[:, :], in_=psB[:, :], func=AF.Exp), r=[psB], w=[E3])
                k.op("dve", lambda v: v.tensor_tensor(out=qt[:, :], in0=rec[:, 0:512], in1=E1[:, :], op=ALU.mult), r=[rec, E1], w=[qt])
                k.op("pool", lambda v: v.tensor_tensor(out=kt_[:, :], in0=rec[:, 512:1024], in1=E2[:, :], op=ALU.mult), r=[rec, E2], w=[kt_])
                k.op("pool", lambda v: v.tensor_tensor(out=kh[:, :], in0=rec[:, 1024:1536], in1=E3[:, :], op=ALU.mult), r=[rec, E3], w=[kh])
                return dict(d=d, rec=rec, E1=E1, qt=qt, kh=kh, Am=Am, kt_=kt_, cm=cm)

            def stageA2(c):
                qt, kt_, Am, cm = c["qt"], c["kt_"], c["Am"], c["cm"]
                psD = ps8.next()
                for h in range(4):
                    hs = slice(h * 128, (h + 1) * 128)
                    k.op("pe", lambda pe: pe.matmul(psD[:, hs], lhsT=kt_[:, hs], rhs=qt[:, hs], start=True, stop=True),
                         r=[kt_, qt], w=[psD])
                k.op("dve", lambda v: v.tensor_tensor(out=Am[:, :].rearrange("p (h i) -> p h i", h=4),
                                                      in0=psD[:, :].rearrange("p (h i) -> p h i", h=4),
                                                      in1=cm.unsqueeze(1).broadcast_to([128, 4, 128]), op=ALU.mult),
                     r=[psD, tri], w=[Am])

            def stageB(c):
                d, rec, E1, qt, kh, Am = (c[n] for n in ("d", "rec", "E1", "qt", "kh", "Am"))
                ecol = 127 if d == 0 else 0
                po = [ps8.next(), ps8.next()]
                for h in range(4):
                    hs = slice(h * 128, (h + 1) * 128)
                    bank = po[h // 2]
                    cs = slice((h % 2) * 256, (h % 2) * 256 + 256)
                    vs = slice(1536 + h * 256, 1536 + (h + 1) * 256)
                    k.op("pe", lambda pe: pe.matmul(bank[:, cs], lhsT=qt[:, hs], rhs=Sb[h][:, :], start=True, stop=False),
                         r=[qt, Sb[h]], w=[bank])
                    k.op("pe", lambda pe: pe.matmul(bank[:, cs], lhsT=Am[:, hs], rhs=rec[:, vs], start=False, stop=True),
                         r=[Am, rec], w=[bank])
                pss = [ps8.next(), ps8.next()]
                for h in range(4):
                    hs = slice(h * 128, (h + 1) * 128)
                    bank = pss[h // 2]
                    cs = slice((h % 2) * 256, (h % 2) * 256 + 256)
                    vs = slice(1536 + h * 256, 1536 + (h + 1) * 256)
                    k.op("pe", lambda pe: pe.matmul(bank[:, cs], lhsT=kh[:, hs], rhs=rec[:, vs], start=True, stop=True),
                         r=[kh, rec], w=[bank])
                    k.op("dve", lambda v: v.scalar_tensor_tensor(out=S[h][:, :], in0=S[h][:, :],
                                                                 scalar=E1[:, h * 128 + ecol:h * 128 + ecol + 1],
                                                                 in1=bank[:, cs], op0=ALU.mult, op1=ALU.add),
                         r=[S[h], E1, bank], w=[S[h]])
                    k.op("act", lambda gp: gp.copy(out=Sb[h][:, :], in_=S[h][:, :]), r=[S[h]], w=[Sb[h]])
                return po

            def reset_state():
                for h in range(4):
                    k.op("pool", lambda gp: gp.memset(S[h][:, :], 0.0), w=[S[h]])
                    k.op("pool", lambda gp: gp.memset(Sb[h][:, :], 0.0), w=[Sb[h]])

            def gl_loads(t):
                rec = recr.next()
                gg = ggr.next()
                k.dma("sp", rec[:, :], REC.t[t], r=[recT[t]], w=[rec])
                k.dma("sp", gg[:, :], GG.t[t], r=[ggT[t]], w=[gg])
                return rec, gg

            reset_state()
            cnext = stageA(0, *gl_loads(0))
            stageA2(cnext)
            for t in range(NT):
                c = cnext
                if t + 1 < NT:
                    cnext = stageA(0, *gl_loads(t + 1))
                po = stageB(c)
                if t + 1 < NT:
                    stageA2(cnext)
                ob = osb.next()
                for c in range(2):
                    k.op("act", lambda a: a.copy(out=ob[:, c * 512:(c + 1) * 512], in_=po[c][:, :]), r=[po[c]], w=[ob])
                k.dma("pool", GOF.t[t], ob[:, :], r=[ob], w=[ofT[t]])
            reset_state()
            order = [1, 0] + list(range(NT - 1, 1, -1))
            cnext = stageA(1, *gl_loads(order[0]))
            stageA2(cnext)
            for oi, t in enumerate(order):
                cond = 0 if t < 2 else 1
                bi = 0 if t < 2 else 1 + (t - 2) // 4
                c = cnext
                rec = c["rec"]
                if oi + 1 < NT:
                    cnext = stageA(1, *gl_loads(order[oi + 1]))
                of = ofr.next()
                xt = xr.next()
                k.dma("sp", of[:, :], GOF.t[t], r=[ofT[t]], w=[of])
                sap, rr = self.xsrc_ap(xsrc, t * 128, 128)
                k.dma("sp", xt[:, :], sap, r=(rr or [self.xblk[bi]]), w=[xt])
                po = stageB(c)
                if oi + 1 < NT:
                    stageA2(cnext)
                for c in range(2):
                    k.op("dve", lambda v: v.tensor_tensor(out=osum[:, c * 512:(c + 1) * 512], in0=po[c][:, :],
                                                          in1=of[:, c * 512:(c + 1) * 512], op=ALU.add),
                         r=[po[c], of], w=[osum])
                for h in range(4):
                    k.op("act", lambda a: a.activation(out=junk[:, :], in_=osum[:, h * 256:(h + 1) * 256], func=AF.Square,
                                                       accum_out=ss4[:, h:h + 1]), r=[osum], w=[junk, ss4])
                k.op("act", lambda a: a.activation(out=rs4[:, :], in_=ss4[:, :], func=AF.Sqrt, scale=1.0 / 256, bias=self.epsb[:, :]),
                     r=[ss4, self.epsb], w=[rs4])
                k.op("dve", lambda v: v.reciprocal(out=rs4[:, :], in_=rs4[:, :]), r=[rs4], w=[rs4])
                for h in range(4):
                    k.op("dve", lambda v: v.scalar_tensor_tensor(out=ogf[:, h * 256:(h + 1) * 256], in0=osum[:, h * 256:(h + 1) * 256],
                                                                 scalar=rs4[:, h:h + 1], in1=onbc[:, :], op0=ALU.mult, op1=ALU.mult),
                         r=[osum, rs4, onbc], w=[ogf])
                k.op("dve", lambda v: v.tensor_tensor(out=ogb[:, :], in0=ogf[:, :], in1=rec[:, 2560:3584], op=ALU.mult),
                     r=[ogf, rec], w=[ogb])
                self.tok_outproj(ogb, 8, ogT, wo, xt, tm, cond)
                k.dma("pool", self.xres.t[t * 128:(t + 1) * 128, :], xt[:, :], r=[xt], w=[self.xblk[bi]])
            k.barrier()


    def norm_rows(self, src, rtrk, n, cond, dst, scr, xt):
        k = self.k
        junk, ss, rs, hf, hb = scr
        k.dma("sp", xt[0:n, :], src, r=rtrk, w=[xt])
        k.op("act", lambda a: a.activation(out=junk[0:n, :], in_=xt[0:n, :], func=AF.Square, accum_out=ss[0:n, :]),
             r=[xt], w=[junk, ss])
        k.op("act", lambda a: a.activation(out=rs[0:n, :], in_=ss[0:n, :], func=AF.Sqrt, scale=1.0 / D, bias=self.epsb[0:n, :]),
             r=[ss, self.epsb], w=[rs])
        k.op("dve", lambda v: v.reciprocal(out=rs[0:n, :], in_=rs[0:n, :]), r=[rs], w=[rs])
        k.op("dve", lambda v: v.scalar_tensor_tensor(out=hf[0:n, :], in0=xt[0:n, :], scalar=rs[0:n, 0:1],
                                                     in1=self.gmod[cond][0:n, :], op0=ALU.mult, op1=ALU.mult),
             r=[xt, rs, self.gmod[cond]], w=[hf])
        k.op("dve", lambda v: v.tensor_tensor(out=hb[0:n, :], in0=hf[0:n, :], in1=self.bcm[cond][0:n, 0:D], op=ALU.add),
             r=[hf, self.bcm[cond]], w=[hb])
        ps = self.ps8.next()
        pv = ps[:, :].bitcast(BF16).rearrange("p (c t) -> p c t", c=8)
        for c in range(8):
            k.op("pe", lambda pe: pe.transpose(out=pv[:, c, 0:n], in_=hb[0:n, c * 128:(c + 1) * 128],
                                               identity=self.identb[0:n, 0:n]), r=[hb, self.identb], w=[ps])
        return ps, pv

    def layer_ssd(self, L, xsrc):
        k = self.k
        ps8 = self.ps8
        w_in = self.W["ssd_w_in"].t[0]
        w_out = self.W["ssd_w_out"].t[0]
        RX = k.dram([NT, 128, 2560], BF16, f"SRX{L}")
        RZ = k.dram([NT, 128, 2048], BF16, f"SRZ{L}")
        RBC = k.dram([NT, 128, 1024], BF16, f"SRBC{L}")
        DD = k.dram([NT, 128, 128], F32, f"SDD{L}")
        YF = k.dram([NT, 128, 2048], F32, f"SYF{L}")
        rxT = [TT(RX.t) for _ in range(NT)]
        rzT = [TT(RZ.t) for _ in range(NT)]
        rbcT = [TT(RBC.t) for _ in range(NB)]
        ddT = [TT(DD.t) for _ in range(NT)]
        yfT = [TT(YF.t) for _ in range(NT)]
        with ExitStack() as es:
            self.epsb = k.sb(es, [128, 1], F32)
            k.op("pool", lambda g: g.memset(self.epsb[:, :], EPS), w=[self.epsb])
            onec = k.sb(es, [128, 1], F32)
            k.op("pool", lambda g: g.memset(onec[:, :], 1.0), w=[onec])
            stg = Ring([k.sb(es, [128, 1024], F32) for _ in range(2)])
            wz = k.sb(es, [128, 8, 2048], BF16)
            self.load_w(es, wz, wz, w_in[:, 0:2048], 8, 128, 2048, stg)
            wx = k.sb(es, [128, 8, 3072], BF16)
            self.load_w(es, wx, wx, w_in[:, 2048:5120], 8, 128, 3072, stg)
            wdt = k.sb(es, [128, 8, 64], BF16)
            self.load_w(es, wdt, wdt, w_in[:, 5120:5184], 8, 128, 64, stg)
            cw = k.sb(es, [128, 24, 5], F32)
            for kk in range(5):
                k.dma("sp", cw[:, :, kk], self.W["ssd_conv_w"].t[0, kk, :].rearrange("(c p) -> p c", p=128), w=[cw],
                      allow_slow_non_contiguous=True)
            cbias = k.sb(es, [128, 24], F32)
            k.dma("sp", cbias[:, :], self.W["ssd_conv_b"].t[0, :].rearrange("(c p) -> p c", p=128), w=[cbias],
                  allow_slow_non_contiguous=True)
            dtb = k.sb(es, [128, 64], F32)
            k.dma("sp", dtb[:, 0:32], self.W["ssd_dt_bias_f"].t[0, :].partition_broadcast(128), w=[dtb])
            k.dma("sp", dtb[:, 32:64], self.W["ssd_dt_bias_b"].t[0, :].partition_broadcast(128), w=[dtb])
            abc = k.sb(es, [128, 64], F32)
            k.dma("sp", abc[:, 0:32], self.W["ssd_a_log_f"].t[0, :].partition_broadcast(128), w=[abc])
            k.dma("sp", abc[:, 32:64], self.W["ssd_a_log_b"].t[0, :].partition_broadcast(128), w=[abc])
            k.op("act", lambda a: a.activation(out=abc[:, :], in_=abc[:, :], func=AF.Exp), r=[abc], w=[abc])
            k.op("dve", lambda v: v.tensor_scalar(out=abc[:, :], in0=abc[:, :], scalar1=-1.0, scalar2=None, op0=ALU.mult),
                 r=[abc], w=[abc])
            xr = Ring([k.sb(es, [128, D], F32) for _ in range(2)])
            hT = k.sb(es, [128, 8, 516], BF16)
            scr = (k.sb(es, [128, D], BF16), k.sb(es, [128, 1], F32), k.sb(es, [128, 1], F32),
                   k.sb(es, [128, D], F32), k.sb(es, [128, D], BF16))
            prer = Ring([k.sb(es, [128, 516], BF16) for _ in range(3)])
            accr = Ring([k.sb(es, [128, 512], F32) for _ in range(2)])
            xc = k.sb(es, [128, 24, 512], BF16)
            rxr = Ring([k.sb(es, [128, 2560], BF16) for _ in range(2)])
            rzr = Ring([k.sb(es, [128, 2048], BF16) for _ in range(2)])
            ddr = Ring([k.sb(es, [128, 128], F32) for _ in range(2)])
            dtt = k.sb(es, [128, 64], F32)
            for bi, (t0, nt, cond) in enumerate(BLOCKS):
                ntile = nt // 128
                for j in range(ntile):
                    src, rr = self.xsrc_ap(xsrc, t0 + j * 128, 128)
                    ps, pv = self.norm_rows(src, rr or [self.xblk[bi]], 128, cond, None, scr, xr.next())
                    k.op("act", lambda a: a.copy(out=hT[:, :, j * 128:(j + 1) * 128], in_=pv), r=[ps], w=[hT])
                for side in range(2):
                    col = 512 + 2 * side
                    has = (bi >= 2) if side == 0 else (1 <= bi < NB - 1)
                    if not has:
                        k.op("pool", lambda g: g.memset(hT[:, :, col:col + 2], 0.0), w=[hT])
                    else:
                        r0 = t0 - 2 if side == 0 else t0 + nt
                        nb_ = bi - 1 if side == 0 else bi + 1
                        src, rr = self.xsrc_ap(xsrc, r0, 2)
                        ps, pv = self.norm_rows(src, rr or [self.xblk[nb_]], 2, cond, None, scr, xr.next())
                        k.op("act", lambda a: a.copy(out=hT[:, :, col:col + 2], in_=pv[:, :, 0:2]), r=[ps], w=[hT])
                for c in range(24):
                    ps = ps8.next()
                    for kk in range(8):
                        k.op("pe", lambda pe: pe.matmul(ps[:, 0:nt], lhsT=wx[:, kk, c * 128:(c + 1) * 128], rhs=hT[:, kk, 0:nt],
                                                        start=(kk == 0), stop=(kk == 7)), r=[wx, hT], w=[ps])
                    ps2 = ps8.next()
                    for kk in range(8):
                        k.op("pe", lambda pe: pe.matmul(ps2[:, 0:4], lhsT=wx[:, kk, c * 128:(c + 1) * 128], rhs=hT[:, kk, 512:516],
                                                        start=(kk == 0), stop=(kk == 7)), r=[wx, hT], w=[ps2])
                    pre = prer.next()
                    k.op("act", lambda a: a.copy(out=pre[:, 2:2 + nt], in_=ps[:, 0:nt]), r=[ps], w=[pre])
                    k.op("act", lambda a: a.copy(out=pre[:, 0:2], in_=ps2[:, 0:2]), r=[ps2], w=[pre])
                    k.op("act", lambda a: a.copy(out=pre[:, 2 + nt:4 + nt], in_=ps2[:, 2:4]), r=[ps2], w=[pre])
                    acc = accr.next()
                    k.op("dve", lambda v: v.tensor_scalar(out=acc[:, 0:nt], in0=pre[:, 0:nt], scalar1=cw[:, c, 0:1], scalar2=None,
                                                          op0=ALU.mult), r=[pre, cw], w=[acc])
                    for kk in range(1, 5):
                        k.op("dve", lambda v: v.scalar_tensor_tensor(out=acc[:, 0:nt], in0=pre[:, kk:kk + nt], scalar=cw[:, c, kk:kk + 1],
                                                                     in1=acc[:, 0:nt], op0=ALU.mult, op1=ALU.add),
                             r=[pre, cw, acc], w=[acc])
                    k.op("act", lambda a: a.activation(out=xc[:, c, 0:nt], in_=acc[:, 0:nt], func=AF.Silu, bias=cbias[:, c:c + 1]),
                         r=[acc, cbias], w=[xc])
                kt0 = t0 // 128
                for j in range(ntile):
                    for s_ in range(2):
                        k.dma("act", RBC.t[kt0 + j, :, s_ * 512:(s_ + 1) * 512].rearrange("p (g t) -> p g t", g=4),
                              xc[:, 16 + 4 * s_:20 + 4 * s_, j * 128:(j + 1) * 128], r=[xc], w=[rbcT[bi]])
                for j in range(ntile):
                    t = kt0 + j
                    rx = rxr.next()
                    for c0 in (0, 8, 16):
                        cn = 8 if c0 < 16 else 4
                        ps = ps8.next()
                        pv = ps[:, :].bitcast(BF16).rearrange("p (c t) -> p c t", c=8)
                        for c in range(cn):
                            k.op("pe", lambda pe: pe.transpose(out=pv[:, c, :], in_=xc[:, c0 + c, j * 128:(j + 1) * 128],
                                                               identity=self.identb[:, :]), r=[xc, self.identb], w=[ps])
                        k.op("act" if c0 != 8 else "dve",
                             (lambda a: a.copy(out=rx[:, c0 * 128:(c0 + cn) * 128].rearrange("p (c t) -> p c t", c=cn), in_=pv[:, 0:cn, :]))
                             if c0 != 8 else
                             (lambda v: v.tensor_copy(out=rx[:, c0 * 128:(c0 + cn) * 128].rearrange("p (c t) -> p c t", c=cn), in_=pv[:, 0:cn, :])),
                             r=[ps], w=[rx])
                    k.dma("pool", RX.t[t], rx[:, :], r=[rx], w=[rxT[t]])
                    rz = rzr.next()
                    for cb in range(4):
                        ps = ps8.next()
                        for kk in range(8):
                            k.op("pe", lambda pe: pe.matmul(ps[:, :], lhsT=hT[:, kk, j * 128:(j + 1) * 128],
                                                            rhs=wz[:, kk, cb * 512:(cb + 1) * 512],
                                                            start=(kk == 0), stop=(kk == 7)), r=[hT, wz], w=[ps])
                        k.op("act", lambda a: a.activation(out=rz[:, cb * 512:(cb + 1) * 512], in_=ps[:, :], func=AF.Silu),
                             r=[ps], w=[rz])
                    k.dma("pool", RZ.t[t], rz[:, :], r=[rz], w=[rzT[t]])
                    dd = ddr.next()
                    ps = ps8.next()
                    for kk in range(8):
                        k.op("pe", lambda pe: pe.matmul(ps[:, 0:64], lhsT=hT[:, kk, j * 128:(j + 1) * 128], rhs=wdt[:, kk, :],
                                                        start=(kk == 0), stop=(kk == 7)), r=[hT, wdt], w=[ps])
                    k.op("dve", lambda v: v.tensor_tensor(out=dtt[:, :], in0=ps[:, 0:64], in1=dtb[:, :], op=ALU.add),
                         r=[ps, dtb], w=[dtt])
                    k.op("act", lambda a: a.activation(out=dtt[:, :], in_=dtt[:, :], func=AF.Exp), r=[dtt], w=[dtt])
                    k.op("act", lambda a: a.activation(out=dd[:, 0:64], in_=dtt[:, :], func=AF.Ln, bias=onec[:, :]),
                         r=[dtt, onec], w=[dd])
                    k.op("dve", lambda v: v.tensor_tensor(out=dd[:, 64:128], in0=dd[:, 0:64], in1=abc[:, :], op=ALU.mult),
                         r=[dd, abc], w=[dd])
                    k.dma("pool", DD.t[t], dd[:, :], r=[dd], w=[ddT[t]])
            k.barrier()

        with ExitStack() as es:
            self.epsb = k.sb(es, [128, 1], F32)
            k.op("pool", lambda g: g.memset(self.epsb[:, :], EPS), w=[self.epsb])
            tri = k.sb(es, [128, 4, 128], F32)
            k.dma("sp", tri[:, :, :], self.tri_d.t.rearrange("m j i -> j m i"), w=[tri])
            negm = k.sb(es, [128, 2, 128], F32)
            k.dma("sp", negm[:, :, :], self.negm_d.t.rearrange("m j i -> j m i"), w=[negm])
            onehot = k.sb(es, [96, 32, 128], BF16)
            k.dma("sp", onehot[:, :, :], self.onehot_d.t, w=[onehot])
            negmb = k.sb(es, [128, 2, 128], BF16)
            k.dma("sp", negmb[:, :, :], self.negmb_d.t.rearrange("m j i -> j m i"), w=[negmb])
            identf = k.sb(es, [128, 128], F32)
            k.dma("sp", identf[:, :], self.identf_d.t, w=[identf])
            Hs = [k.sb(es, [128, 512], F32) for _ in range(4)]
            Hb = [k.sb(es, [128, 512], BF16) for _ in range(4)]
            rxr = Ring([k.sb(es, [128, 2560], BF16) for _ in range(2)])
            rbcr = Ring([k.sb(es, [128, 1024], BF16) for _ in range(2)])
            ddr = Ring([k.sb(es, [128, 128], F32) for _ in range(3)])
            def ring2(shape, dt):
                return Ring([k.sb(es, shape, dt) for _ in range(2)])
            ela_r = ring2([128, 32], F32)
            edl_r = ring2([128, 32], F32)
            etot_r = ring2([128, 32], F32)
            laT_r = ring2([96, 128], BF16)
            larep_r = ring2([128, 96], F32)
            Abf_r = ring2([96, 128], BF16)
            Bbf_r = ring2([96, 128], BF16)
            R1_r = ring2([96, 128], F32)
            nla_r = ring2([128, 32], F32)
            xdt_r = ring2([128, 2048], BF16)
            xdl_r = ring2([128, 2048], BF16)
            CBm_r = ring2([128, 4, 128], BF16)
            Eh_r = Ring([k.sb(es, [128, 4, 128], BF16) for _ in range(3)])
            Mh = Ring([k.sb(es, [128, 4, 128], BF16) for _ in range(3)])
            ysr = Ring([k.sb(es, [128, 2048], F32) for _ in range(2)])
            wo = k.sb(es, [128, 16, D], BF16)
            with ExitStack() as es2:
                stg = Ring([k.sb(es2, [128, 1024], F32) for _ in range(2)])
                self.load_w(es2, wo, wo, w_out, 16, 128, D, stg)
                k.barrier()
            gnbc = k.sb(es, [128, 2048], F32)
            k.dma("sp", gnbc[:, :], self.W["ssd_norm"].t[0, :].partition_broadcast(128), w=[gnbc])
            dsk = k.sb(es, [128, 32], F32)
            k.dma("sp", dsk[:, :], self.W["ssd_d"].t[0, :].partition_broadcast(128), w=[dsk])
            rzr = Ring([k.sb(es, [128, 2048], BF16) for _ in range(1)])
            yfr = Ring([k.sb(es, [128, 2048], F32) for _ in range(1)])
            xr = Ring([k.sb(es, [128, D], F32) for _ in range(2)])
            ynb = k.sb(es, [128, 2048], BF16)
            ogT = k.sb(es, [128, 16, 128], BF16)
            junk = k.sb(es, [128, 512], BF16)
            ss4 = k.sb(es, [128, 4], F32)
            rs4 = k.sb(es, [128, 4], F32)
            tm = k.sb(es, [128, 512], F32)

            def stageA(d, rx, rbc, dd):
                c = dict(d=d, rx=rx, rbc=rbc, dd=dd)
                cm = tri[:, 0 if d == 0 else 2, :]
                sm = tri[:, 1 if d == 0 else 3, :]
                dA = dd[:, 64 + 32 * d:96 + 32 * d]
                dt = dd[:, 32 * d:32 * d + 32]
                larep = larep_r.next()
                la_sb = larep
                ela, edl, etot, laT = ela_r.next(), edl_r.next(), etot_r.next(), laT_r.next()
                Abf, Bbf, R1 = Abf_r.next(), Bbf_r.next(), R1_r.next()
                xdt, xdl, CBm = xdt_r.next(), xdl_r.next(), CBm_r.next()
                nla = nla_r.next()
                c.update(la_sb=la_sb, ela=ela, etot=etot, laT=laT, xdt=xdt, xdl=xdl, CBm=CBm, nla=nla)
                psL = ps8.next()
                k.op("pe", lambda pe: pe.matmul(psL[:, 0:32], lhsT=cm, rhs=dA, start=True, stop=True), r=[tri, dd], w=[psL])
                k.op("pe", lambda pe: pe.matmul(psL[:, 32:64], lhsT=sm, rhs=dA, start=True, stop=True), r=[tri, dd], w=[psL])
                k.op("pe", lambda pe: pe.matmul(psL[:, 64:96], lhsT=self.ones_f[:, :], rhs=dA, start=True, stop=True),
                     r=[self.ones_f, dd], w=[psL])
                for rep in range(3):
                    k.op("act", lambda a: a.copy(out=larep[:, rep * 32:(rep + 1) * 32], in_=psL[:, 0:32]), r=[psL], w=[larep])
                k.op("act", lambda a: a.mul(out=nla[:, :], in_=psL[:, 0:32], mul=-1.0), r=[psL], w=[nla])
                k.op("act", lambda a: a.activation(out=ela[:, :], in_=psL[:, 0:32], func=AF.Exp), r=[psL], w=[ela])
                k.op("act", lambda a: a.activation(out=edl[:, :], in_=psL[:, 32:64], func=AF.Exp), r=[psL], w=[edl])
                k.op("act", lambda a: a.activation(out=etot[:, :], in_=psL[:, 64:96], func=AF.Exp), r=[psL], w=[etot])
                xs3 = rx[:, 0:2048].rearrange("p (h e) -> p h e", h=32)
                k.op("pool", lambda g: g.tensor_tensor(out=xdt[:, :].rearrange("p (h e) -> p h e", h=32), in0=xs3,
                                                       in1=dt.unsqueeze(2).broadcast_to([128, 32, 64]), op=ALU.mult),
                     r=[rx, dd], w=[xdt])
                k.op("pool", lambda g: g.tensor_tensor(out=xdl[:, :].rearrange("p (h e) -> p h e", h=32),
                                                       in0=xdt[:, :].rearrange("p (h e) -> p h e", h=32),
                                                       in1=edl[:, :].unsqueeze(2).broadcast_to([128, 32, 64]), op=ALU.mult),
                     r=[xdt, edl], w=[xdl])
                psT = ps8.next()
                k.op("pe", lambda pe: pe.transpose(out=psT[0:96, 0:128], in_=larep[:, 0:96], identity=identf[:, :]),
                     r=[larep, identf], w=[psT])
                k.op("act", lambda a: a.copy(out=Abf[:, :], in_=psT[0:96, 0:128]), r=[psT], w=[Abf])
                k.op("dve", lambda v: v.tensor_tensor(out=R1[:, :], in0=psT[0:96, 0:128], in1=Abf[:, :], op=ALU.subtract),
                     r=[psT, Abf], w=[R1])
                k.op("act", lambda a: a.copy(out=Bbf[:, :], in_=R1[:, :]), r=[R1], w=[Bbf])
                k.op("dve", lambda v: v.tensor_tensor(out=R1[:, :], in0=R1[:, :], in1=Bbf[:, :], op=ALU.subtract),
                     r=[R1, Bbf], w=[R1])
                k.op("pool", lambda g: g.tensor_copy(out=laT[0:32, :], in_=Abf[0:32, :]), r=[Abf], w=[laT])
                k.op("pool", lambda g: g.tensor_copy(out=laT[32:64, :], in_=Bbf[32:64, :]), r=[Bbf], w=[laT])
                k.op("pool", lambda g: g.tensor_copy(out=laT[64:96, :], in_=R1[64:96, :]), r=[R1], w=[laT])
                psCB = ps8.next()
                for g in range(4):
                    k.op("pe", lambda pe: pe.matmul(psCB[:, g * 128:(g + 1) * 128], lhsT=rbc[:, g * 128:(g + 1) * 128],
                                                    rhs=rbc[:, 512 + g * 128:512 + (g + 1) * 128], start=True, stop=True),
                         r=[rbc], w=[psCB])
                k.op("dve", lambda v: v.tensor_tensor(out=CBm[:, :, :], in0=psCB[:, :].rearrange("p (g i) -> p g i", g=4),
                                                      in1=cm.unsqueeze(1).broadcast_to([128, 4, 128]), op=ALU.mult),
                     r=[psCB, tri], w=[CBm])
                return c

            def stageB(c, ys):
                d, rx, rbc = c["d"], c["rx"], c["rbc"]
                la_sb, ela, etot, laT, xdt, xdl, CBm = (c[n] for n in ("la_sb", "ela", "etot", "laT", "xdt", "xdl", "CBm"))
                ng = negm[:, d, :]
                nla = c["nla"]

                def lb_mm(i):
                    h0 = i * 4
                    psLb = ps8.next()
                    for hh in range(4):
                        k.op("pe", lambda pe: pe.matmul(psLb[:, hh * 128:(hh + 1) * 128], lhsT=onehot[:, h0 + hh, :],
                                                        rhs=laT[:, :], start=True, stop=False), r=[onehot, laT], w=[psLb])
                        k.op("pe", lambda pe: pe.matmul(psLb[:, hh * 128:(hh + 1) * 128], lhsT=self.identb[:, :],
                                                        rhs=negmb[:, d, :], start=False, stop=True), r=[self.identb, negmb], w=[psLb])
                    return psLb

                pend = [lb_mm(0), lb_mm(1)]
                psY = None
                for i in range(8):
                    g, hq = i // 2, i % 2
                    h0 = i * 4
                    psLb = pend.pop(0)
                    if i + 2 < 8:
                        pend.append(lb_mm(i + 2))
                    if hq == 0:
                        psY = ps8.next()
                    Eh = Eh_r.next()
                    for hh in range(4):
                        k.op("act", lambda a: a.activation(out=Eh[:, hh, :], in_=psLb[:, hh * 128:(hh + 1) * 128], func=AF.Exp,
                                                           bias=nla[:, h0 + hh:h0 + hh + 1]), r=[psLb, nla], w=[Eh])
                    mh = Mh.next()
                    k.op("dve", lambda v: v.tensor_tensor(out=mh[:, :, :], in0=Eh[:, :, :],
                                                          in1=CBm[:, g:g + 1, :].broadcast_to([128, 4, 128]), op=ALU.mult),
                         r=[Eh, CBm], w=[mh])
                    for hh in range(4):
                        h = h0 + hh
                        k.op("pe", lambda pe: pe.matmul(psY[:, (h % 8) * 64:(h % 8 + 1) * 64], lhsT=mh[:, hh, :],
                                                        rhs=xdt[:, h * 64:(h + 1) * 64], start=True, stop=True),
                             r=[mh, xdt], w=[psY])
                    if hq == 1:
                        psYi = ps8.next()
                        k.op("pe", lambda pe: pe.matmul(psYi[:, :], lhsT=rbc[:, 512 + g * 128:512 + (g + 1) * 128], rhs=Hb[g][:, :],
                                                        start=True, stop=True), r=[rbc, Hb[g]], w=[psYi])
                        yv = ys[:, g * 512:(g + 1) * 512]
                        k.op("dve", lambda v: v.tensor_tensor(out=yv.rearrange("p (h e) -> p h e", h=8),
                                                              in0=psYi[:, :].rearrange("p (h e) -> p h e", h=8),
                                                              in1=ela[:, g * 8:(g + 1) * 8].unsqueeze(2).broadcast_to([128, 8, 64]),
                                                              op=ALU.mult), r=[psYi, ela], w=[ys])
                        k.op("dve", lambda v: v.tensor_tensor(out=yv, in0=yv, in1=psY[:, :], op=ALU.add), r=[ys, psY], w=[ys])
                        psH = ps8.next()
                        k.op("pe", lambda pe: pe.matmul(psH[:, :], lhsT=rx[:, 2048 + g * 128:2048 + (g + 1) * 128],
                                                        rhs=xdl[:, g * 512:(g + 1) * 512], start=True, stop=True), r=[rx, xdl], w=[psH])
                        k.op("pool", lambda v: v.tensor_tensor(out=Hs[g][:, :].rearrange("p (h e) -> p h e", h=8),
                                                               in0=Hs[g][:, :].rearrange("p (h e) -> p h e", h=8),
                                                               in1=etot[:, g * 8:(g + 1) * 8].unsqueeze(2).broadcast_to([128, 8, 64]),
                                                               op=ALU.mult), r=[Hs[g], etot], w=[Hs[g]])
                        k.op("dve", lambda v: v.tensor_tensor(out=Hs[g][:, :], in0=Hs[g][:, :], in1=psH[:, :], op=ALU.add),
                             r=[Hs[g], psH], w=[Hs[g]])
                        k.op("act", lambda gp: gp.copy(out=Hb[g][:, :], in_=Hs[g][:, :]), r=[Hs[g]], w=[Hb[g]])

            def reset_state():
                for g in range(4):
                    k.op("pool", lambda gp: gp.memset(Hs[g][:, :], 0.0), w=[Hs[g]])
                    k.op("pool", lambda gp: gp.memset(Hb[g][:, :], 0.0), w=[Hb[g]])

            def loads(t):
                bi = 0 if t < 2 else 1 + (t - 2) // 4
                rx = rxr.next()
                rbc = rbcr.next()
                dd = ddr.next()
                k.dma("sp", rx[:, :], RX.t[t], r=[rxT[t]], w=[rx])
                k.dma("sp", rbc[:, :], RBC.t[t], r=[rbcT[bi]], w=[rbc])
                k.dma("sp", dd[:, :], DD.t[t], r=[ddT[t]], w=[dd])
                return rx, rbc, dd

            reset_state()
            cnext = stageA(0, *loads(0))
            for t in range(NT):
                c = cnext
                if t + 1 < NT:
                    cnext = stageA(0, *loads(t + 1))
                ys = ysr.next()
                stageB(c, ys)
                k.dma("pool", YF.t[t], ys[:, :], r=[ys], w=[yfT[t]])
            reset_state()
            order = [1, 0] + list(range(NT - 1, 1, -1))
            cnext = stageA(1, *loads(order[0]))
            for oi, t in enumerate(order):
                cond = 0 if t < 2 else 1
                bi = 0 if t < 2 else 1 + (t - 2) // 4
                c = cnext
                rx = c["rx"]
                if oi + 1 < NT:
                    cnext = stageA(1, *loads(order[oi + 1]))
                rz = rzr.next()
                yf = yfr.next()
                xt = xr.next()
                k.dma("sp", rz[:, :], RZ.t[t], r=[rzT[t]], w=[rz])
                k.dma("sp", yf[:, :], YF.t[t], r=[yfT[t]], w=[yf])
                sap, rr = self.xsrc_ap(xsrc, t * 128, 128)
                k.dma("sp", xt[:, :], sap, r=(rr or [self.xblk[bi]]), w=[xt])
                ys = ysr.next()
                stageB(c, ys)
                k.op("dve", lambda v: v.tensor_tensor(out=ys[:, :], in0=ys[:, :], in1=yf[:, :], op=ALU.add), r=[ys, yf], w=[ys])
                ytmp = yf
                k.op("pool", lambda g: g.tensor_tensor(out=ytmp[:, :].rearrange("p (h e) -> p h e", h=32),
                                                       in0=rx[:, 0:2048].rearrange("p (h e) -> p h e", h=32),
                                                       in1=dsk[:, :].unsqueeze(2).broadcast_to([128, 32, 64]), op=ALU.mult),
                     r=[rx, dsk, yf], w=[ytmp])
                k.op("dve", lambda v: v.tensor_tensor(out=ys[:, :], in0=ys[:, :], in1=ytmp[:, :], op=ALU.add), r=[ys, ytmp], w=[ys])
                k.op("dve", lambda v: v.tensor_tensor(out=ys[:, :], in0=ys[:, :], in1=rz[:, :], op=ALU.mult), r=[ys, rz], w=[ys])
                for g in range(4):
                    k.op("act", lambda a: a.activation(out=junk[:, :], in_=ys[:, g * 512:(g + 1) * 512], func=AF.Square,
                                                       accum_out=ss4[:, g:g + 1]), r=[ys], w=[junk, ss4])
                k.op("act", lambda a: a.activation(out=rs4[:, :], in_=ss4[:, :], func=AF.Sqrt, scale=1.0 / 512, bias=self.epsb[:, :]),
                     r=[ss4, self.epsb], w=[rs4])
                k.op("dve", lambda v: v.reciprocal(out=rs4[:, :], in_=rs4[:, :]), r=[rs4], w=[rs4])
                for g in range(4):
                    k.op("dve", lambda v: v.scalar_tensor_tensor(out=ynb[:, g * 512:(g + 1) * 512], in0=ys[:, g * 512:(g + 1) * 512],
                                                                 scalar=rs4[:, g:g + 1], in1=gnbc[:, g * 512:(g + 1) * 512],
                                                                 op0=ALU.mult, op1=ALU.mult), r=[ys, rs4, gnbc], w=[ynb])
                self.tok_outproj(ynb, 16, ogT, wo, xt, tm, cond)
                k.dma("pool", self.xres.t[t * 128:(t + 1) * 128, :], xt[:, :], r=[xt], w=[self.xblk[bi]])
            k.barrier()

    def tok_outproj(self, ogb, kc, ogT, wo, xt, tm, cond):
        k = self.k
        for c0 in range(0, kc, 8):
            ps = self.ps8.next()
            pv = ps[:, :].bitcast(BF16).rearrange("p (c t) -> p c t", c=8)
            for c in range(8):
                k.op("pe", lambda pe: pe.transpose(out=pv[:, c, :], in_=ogb[:, (c0 + c) * 128:(c0 + c + 1) * 128],
                                                   identity=self.identb[:, :]), r=[ogb, self.identb], w=[ps])
            k.op("act", lambda a: a.copy(out=ogT[:, c0:c0 + 8, :], in_=pv), r=[ps], w=[ogT])
        for cb in range(2):
            ps = self.ps8.next()
            for kk in range(kc):
                k.op("pe", lambda pe: pe.matmul(ps[:, :], lhsT=ogT[:, kk, :], rhs=wo[:, kk, cb * 512:(cb + 1) * 512],
                                                start=(kk == 0), stop=(kk == kc - 1)), r=[ogT, wo], w=[ps])
            k.op("dve", lambda v: v.tensor_tensor(out=tm[:, :], in0=ps[:, :],
                                                  in1=self.bcm[cond][:, 2 * D + cb * 512:2 * D + (cb + 1) * 512], op=ALU.mult),
                 r=[ps, self.bcm[cond]], w=[tm])
            k.op("dve", lambda v: v.tensor_tensor(out=xt[:, cb * 512:(cb + 1) * 512], in0=tm[:, :],
                                                  in1=xt[:, cb * 512:(cb + 1) * 512], op=ALU.add), r=[tm, xt], w=[xt])

    def outproj(self, es, og_src, ogb, kp, kc, wo, xsrc, xr, tm):
        k = self.k
        for bi, (t0, nt, cond) in enumerate(BLOCKS):
            ob = ogb.next()
            src, trk = og_src(bi, nt)
            k.dma("sp", ob[:, :, 0:nt], src, r=[trk], w=[ob])
            for j in range(nt // 128):
                xt = xr.next()
                sap, rr = self.xsrc_ap(xsrc, t0 + j * 128, 128)
                k.dma("sp", xt[:, :], sap, r=(rr or [self.xblk[bi]]), w=[xt])
                import os
                for cb in range(2 if not os.environ.get("DBG_SKIPMM") else 0):
                    ps = self.psg.next()
                    for kk in range(kc):
                        k.op("pe", lambda pe, kk=kk, cb=cb, ps=ps: pe.matmul(
                            ps[:, :], lhsT=ob[0:kp, kk, j * 128:(j + 1) * 128], rhs=wo[0:kp, kk, cb * 512:(cb + 1) * 512],
                            start=(kk == 0), stop=(kk == kc - 1)), r=[ob, wo], w=[ps])
                    k.op("dve", lambda v, cb=cb, ps=ps: v.tensor_tensor(
                        out=tm[:, :], in0=ps[:, :], in1=self.bcm[cond][:, 2 * D + cb * 512:2 * D + (cb + 1) * 512], op=ALU.mult),
                        r=[ps, self.bcm[cond]], w=[tm])
                    k.op("dve", lambda v, cb=cb, xt=xt: v.tensor_tensor(
                        out=xt[:, cb * 512:(cb + 1) * 512], in0=tm[:, :], in1=xt[:, cb * 512:(cb + 1) * 512], op=ALU.add),
                        r=[tm, xt], w=[xt])
                k.dma("pool", self.xres.t[t0 + j * 128:t0 + (j + 1) * 128, :], xt[:, :], r=[xt], w=[self.xblk[bi]])

    def final(self, xsrc):
        k = self.k
        with ExitStack() as es:
            xr = Ring([k.sb(es, [128, D], F32) for _ in range(3)])
            junk = k.sb(es, [128, D], BF16)
            ss = k.sb(es, [128, 1], F32)
            rs = k.sb(es, [128, 1], F32)
            epsb = k.sb(es, [128, 1], F32)
            k.op("pool", lambda g: g.memset(epsb[:, :], EPS), w=[epsb])
            fg = k.sb(es, [128, D], F32)
            k.dma("sp", fg[:, :], self.final_g.t.partition_broadcast(128), w=[fg])
            for bi, (t0, nt, cond) in enumerate(BLOCKS):
                if cond == 0 and not self.debug_x:
                    continue
                for j in range(nt // 128):
                    xt = xr.next()
                    tt0 = t0 + j * 128
                    sap, rr = self.xsrc_ap(xsrc, tt0, 128)
                    k.dma("sp", xt[:, :], sap, r=(rr or [self.xblk[bi]]), w=[xt])
                    if self.debug_x:
                        k.dma("pool", self.out.t[tt0:tt0 + 128, :], xt[:, :], r=[xt], w=[self.out])
                        continue
                    k.op("act", lambda a, xt=xt: a.activation(out=junk[:, :], in_=xt[:, :], func=AF.Square, accum_out=ss[:, :]),
                         r=[xt], w=[junk, ss])
                    k.op("act", lambda a: a.activation(out=rs[:, :], in_=ss[:, :], func=AF.Sqrt, scale=1.0 / D, bias=epsb[:, :]),
                         r=[ss, epsb], w=[rs])
                    k.op("dve", lambda v: v.reciprocal(out=rs[:, :], in_=rs[:, :]), r=[rs], w=[rs])
                    k.op("dve", lambda v, xt=xt: v.scalar_tensor_tensor(out=xt[:, :], in0=xt[:, :], scalar=rs[:, 0:1],
                                                                       in1=fg[:, :], op0=ALU.mult, op1=ALU.mult),
                         r=[xt, rs, fg], w=[xt])
                    k.dma("pool", self.out.t[tt0 - CTX:tt0 - CTX + 128, :], xt[:, :], r=[xt], w=[self.out])


WSHAPES = {
    "mla_w_in": [1, 1024, 1696], "mla_q_norm": [1, 384], "mla_w_uq": [1, 384, 1536], "mla_kv_norm": [1, 256],
    "mla_w_ukv": [1, 256, 2048], "mla_w_out": [1, 1024, 1024],
    "gla_w_in": [1, 1024, 3104], "gla_w_gf": [1, 16, 512], "gla_b_gf": [1, 512], "gla_w_gb": [1, 16, 512],
    "gla_b_gb": [1, 512], "gla_o_norm": [1, 256], "gla_w_out": [1, 1024, 1024],
    "gqa_w_in": [1, 1024, 2560], "gqa_q_norm": [1, 64], "gqa_k_norm": [1, 64], "gqa_w_out": [1, 1024, 1024],
    "ssd_w_in": [1, 1024, 5184], "ssd_conv_w": [1, 5, 3072], "ssd_conv_b": [1, 3072], "ssd_dt_bias_f": [1, 32],
    "ssd_dt_bias_b": [1, 32], "ssd_a_log_f": [1, 32], "ssd_a_log_b": [1, 32], "ssd_d": [1, 32], "ssd_norm": [1, 2048],
    "ssd_w_out": [1, 2048, 1024],
}


def tri_consts():
    j = np.arange(128)[:, None]
    i = np.arange(128)[None, :]
    return np.stack([(j <= i), (j > i), (j >= i), (j < i)]).astype(np.float32)


def rope_tables(rd):
    hf = rd // 4
    inv = 10000.0 ** (-np.arange(hf, dtype=np.float64) / hf)
    p = np.arange(SEQ)
    row = (p // 64).astype(np.float64)[:, None] * inv[None, :]
    col = (p % 64).astype(np.float64)[:, None] * inv[None, :]
    cos = np.concatenate([np.cos(row), np.cos(row), np.cos(col), np.cos(col)], axis=1)
    sin = np.concatenate([-np.sin(row), np.sin(row), -np.sin(col), np.sin(col)], axis=1)
    tab = np.zeros((T, 2, rd), np.float32)
    tab[:CTX, 0, :] = 1.0
    tab[CTX:, 0, :] = cos
    tab[CTX:, 1, :] = sin
    return tab


def run(inputs, layers=(0, 1, 2, 3), debug_x=False, cores=(0, 1), stop=None):
    nc = bass.Bass("TRN2", target_bir_lowering=False)
    Prog(nc, layers=layers, debug_x=debug_x, stop=stop).build()
    f = lambda a: np.ascontiguousarray(np.asarray(a, dtype=np.float32))
    common = {nm: f(inputs[nm]) for nm in WSHAPES}
    for nm in ("ada_w", "ada_b", "norm_g", "final_g"):
        common[nm] = f(inputs[nm])
    common["ident_bf"] = np.eye(128, dtype=np.float32).astype(ml_dtypes.bfloat16)
    common["rope_mla"] = rope_tables(32)
    common["rope_gqa"] = rope_tables(64)
    common["tri"] = tri_consts()
    common["negm"] = ((1.0 - tri_consts()[[0, 2]]) * -1e30).astype(np.float32)
    oh = np.zeros((3, 32, 32, 128), np.float32)
    oh[:, np.arange(32), np.arange(32), :] = 1.0
    common["onehot3"] = oh.reshape(96, 32, 128).astype(ml_dtypes.bfloat16)
    common["negmb"] = common["negm"].astype(ml_dtypes.bfloat16)
    common["ident_f"] = np.eye(128, dtype=np.float32)
    in_maps = []
    for b in cores:
        m = dict(common)
        m["xin"] = np.ascontiguousarray(np.concatenate([f(inputs["ctx"])[b], f(inputs["x"])[b]], axis=0))
        m["c2"] = np.ascontiguousarray(np.stack([f(inputs["c_ctx"]), f(inputs["c"])[b]], axis=0))
        in_maps.append(m)
    res = run_bass_kernel_spmd(nc, in_maps, core_ids=list(range(len(cores))))
    return [r["y"] for r in res.results]


FUSED = True


def kernel(**inputs):
    if FUSED:
        outs = run(inputs)
        return np.stack(outs, axis=0).astype(np.float32)
    cur = dict(inputs)
    for L in (0, 1, 2):
        outs = run(cur, layers=(L,), debug_x=True)
        st = np.stack(outs, axis=0)
        cur["ctx"] = np.ascontiguousarray(st[:, :CTX])
        cur["x"] = np.ascontiguousarray(st[:, CTX:])
    outs = run(cur, layers=(3,), debug_x=False)
    return np.stack(outs, axis=0).astype(np.float32)
```

```python
import math
from contextlib import ExitStack

import numpy as np
import ml_dtypes
import concourse.bass as bass
import concourse.mybir as mybir
from concourse.bass_utils import run_bass_kernel_spmd

F32 = mybir.dt.float32
BF16 = mybir.dt.bfloat16
AF = mybir.ActivationFunctionType
ALU = mybir.AluOpType
AX = mybir.AxisListType

D = 1024
SEQ = 8192
CTX = 256
T = SEQ + CTX
NT = T // 128
EPS = 1e-6
EPOCH = 30000

BLOCKS = [(0, 256, 0)] + [(256 + 512 * i, 512, 1) for i in range(16)]
NB = len(BLOCKS)


class Buf:
    __slots__ = ("w", "r")

    def __init__(self):
        self.w = None
        self.r = {}


class TT:
    def __init__(self, t):
        self.t = t
        self.b = Buf()

    def __getitem__(self, idx):
        return self.t[idx]


class Ring:
    def __init__(self, items):
        self.items = items
        self.i = 0

    def next(self):
        it = self.items[self.i % len(self.items)]
        self.i += 1
        return it


class KB:
    def __init__(self, nc, es):
        self.nc = nc
        self.es = es
        self.eng = {"pe": nc.tensor, "act": nc.scalar, "dve": nc.vector, "pool": nc.gpsimd, "sp": nc.sync}
        self.sems = {e: [] for e in self.eng}
        self.cnt = {e: 0 for e in self.eng}
        self.seen = {e: {} for e in self.eng}
        self.last = {e: None for e in self.eng}
        self.slots = {}
        self.slot_i = {}
        for q in ("sp", "pool", "act"):
            self.slots[q] = [[es.enter_context(nc.semaphore(f"d_{q}_{i}")), 0, f"d_{q}_{i}"] for i in range(12)]
            self.slot_i[q] = 0
        self.nsb = 0

    def sb(self, es, shape, dt, name=None):
        self.nsb += 1
        return TT(es.enter_context(self.nc.sbuf_tensor(name or f"sb{self.nsb}", list(shape), dt)))

    def dram(self, shape, dt, name):
        h = self.nc.dram_tensor(name, list(shape), dt, kind="Internal")
        return TT(h.ap())

    def _wait(self, e, deps):
        seen = self.seen[e]
        for ev in deps:
            key, sem, val, src = ev
            if src == "pe" and e == "pe":
                continue
            if seen.get(key, 0) >= val:
                continue
            self.eng[e].wait_ge(sem, val)
            seen[key] = val

    def _deps(self, reads, writes):
        deps = []
        for t in reads:
            if t.b.w is not None:
                deps.append(t.b.w)
        for t in writes:
            if t.b.w is not None:
                deps.append(t.b.w)
            deps.extend(t.b.r.values())
        return deps

    def _mark(self, ev, reads, writes):
        for t in reads:
            t.b.r[ev[0]] = ev
        for t in writes:
            t.b.w = ev
            t.b.r = {}

    def op(self, e, fn, r=(), w=()):
        self._wait(e, self._deps(r, w))
        ins = fn(self.eng[e])
        epoch = self.cnt[e] // EPOCH
        while len(self.sems[e]) <= epoch:
            self.sems[e].append(self.es.enter_context(self.nc.semaphore(f"s_{e}_{len(self.sems[e])}")))
        sem = self.sems[e][epoch]
        val = self.cnt[e] % EPOCH + 1
        ins.then_inc(sem, 1)
        self.cnt[e] += 1
        ev = ((e, epoch), sem, val, e)
        self.last[e] = ev
        self._mark(ev, r, w)
        return ev

    def dma(self, q, out, in_, r=(), w=(), **kw):
        deps = self._deps(r, w)
        slots = self.slots[q]
        si = self.slot_i[q] % len(slots)
        self.slot_i[q] += 1
        slot = slots[si]
        if slot[1] > 0:
            deps.append((slot[2], slot[0], 16 * slot[1], "dma"))
        self._wait(q, deps)
        ins = self.eng[q].dma_start(out=out, in_=in_, **kw)
        ins.then_inc(slot[0], 16)
        slot[1] += 1
        ev = (slot[2], slot[0], 16 * slot[1], "dma")
        self._mark(ev, r, w)
        return ev

    def barrier(self):
        evs = [self.last[e] for e in self.eng if self.last[e] is not None]
        for q in self.slots:
            for slot in self.slots[q]:
                if slot[1] > 0:
                    evs.append((slot[2], slot[0], 16 * slot[1], "dma"))
        for e in self.eng:
            seen = self.seen[e]
            for ev in evs:
                key, sem, val, src = ev
                if src == e:
                    continue
                if seen.get(key, 0) >= val:
                    continue
                self.eng[e].wait_ge(sem, val)
                seen[key] = val


class Prog:
    def __init__(self, nc, layers=(0, 1, 2, 3), debug_x=False, stop=None):
        self.nc = nc
        self.stop = stop
        self.layers = layers
        self.debug_x = debug_x

    def din(self, name, shape, dt=F32):
        return TT(self.nc.dram_tensor(name, list(shape), dt, kind="ExternalInput").ap())

    def build(self):
        nc = self.nc
        with ExitStack() as es:
            self.k = k = KB(nc, es)
            self.es = es
            self.xin = self.din("xin", [T, D])
            self.c2 = self.din("c2", [2, D])
            self.ada_w = self.din("ada_w", [4, D, 3 * D])
            self.ada_b = self.din("ada_b", [4, 3 * D])
            self.norm_g = self.din("norm_g", [4, D])
            self.final_g = self.din("final_g", [D])
            self.W = {}
            for nm, shp in WSHAPES.items():
                self.W[nm] = self.din(nm, shp)
            self.identb_d = self.din("ident_bf", [128, 128], BF16)
            self.rope_mla = self.din("rope_mla", [T, 2, 32])
            self.rope_gqa = self.din("rope_gqa", [T, 2, 64])
            self.tri_d = self.din("tri", [4, 128, 128])
            self.negm_d = self.din("negm", [2, 128, 128])
            self.onehot_d = self.din("onehot3", [96, 32, 128], BF16)
            self.negmb_d = self.din("negmb", [2, 128, 128], BF16)
            self.identf_d = self.din("ident_f", [128, 128])
            if self.debug_x:
                self.out = TT(nc.dram_tensor("y", [T, D], F32, kind="ExternalOutput").ap())
            else:
                self.out = TT(nc.dram_tensor("y", [SEQ, D], F32, kind="ExternalOutput").ap())
            self.xres = k.dram([T, D], F32, "xres")
            self.xblk = [TT(self.xres.t) for _ in range(NB)]
            self.modd = k.dram([4, 2, 3 * D], F32, "modd")
            self.identb = k.sb(es, [128, 128], BF16, "identb")
            k.dma("sp", self.identb[:, :], self.identb_d.t[:, :], w=[self.identb])
            self.ones_f = k.sb(es, [128, 128], F32, "ones_f")
            k.op("pool", lambda g: g.memset(self.ones_f[:, :], 1.0), w=[self.ones_f])
            self.psb = [TT(es.enter_context(nc.psum_tensor(f"ps{i}", [128, 512], F32))) for i in range(8)]
            self.psg = Ring(self.psb[0:6])
            self.pso = Ring(self.psb[6:8])
            self.ps8 = Ring(self.psb)
            self.bcm = [k.sb(es, [128, 3 * D], F32, f"bcm{c}") for c in range(2)]
            self.gmod = [k.sb(es, [128, D], F32, f"gmod{c}") for c in range(2)]
            self.ngbc = k.sb(es, [128, D], F32, "ngbc")

            first = True
            for L in self.layers:
                self.modulation(L)
                if self.stop == "mod":
                    break
                xsrc = self.xin if first else None
                if L == 0:
                    self.layer_attn(L, "mla", xsrc)
                elif L == 2:
                    self.layer_attn(L, "gqa", xsrc)
                elif L == 1:
                    self.layer_gla(L, xsrc)
                elif L == 3:
                    self.layer_ssd(L, xsrc)
                first = False
                k.barrier()
            self.final(self.xin if first else None)
            k.barrier()
        return nc

    def xsrc_ap(self, xsrc, t0, n):
        if xsrc is not None:
            return xsrc.t[t0:t0 + n, :], [xsrc]
        return self.xres.t[t0:t0 + n, :], None

    def modulation(self, L):
        k = self.k
        with ExitStack() as es:
            cT = k.sb(es, [128, 8, 2], F32)
            sT = k.sb(es, [128, 8, 2], F32)
            for kk in range(8):
                k.dma("sp", cT[:, kk, :], self.c2.t[:, kk * 128:(kk + 1) * 128].rearrange("c p -> p c"), w=[cT],
                      allow_slow_non_contiguous=True)
            k.op("act", lambda a: a.activation(out=sT[:, :, :], in_=cT[:, :, :], func=AF.Silu), r=[cT], w=[sT])
            msb = k.sb(es, [2, 3 * D], F32)
            bb = k.sb(es, [2, 3 * D], F32)
            k.dma("sp", bb[:, :], self.ada_b.t[L, :].partition_broadcast(2), w=[bb])
            wr = Ring([k.sb(es, [128, 8, 512], F32) for _ in range(2)])
            for cb in range(6):
                wt = wr.next()
                k.dma("sp", wt[:, :, :],
                      self.ada_w.t[L, :, cb * 512:(cb + 1) * 512].rearrange("(k p) n -> p k n", p=128), w=[wt])
                ps = self.psg.next()
                for kk in range(8):
                    k.op("pe", lambda pe, kk=kk: pe.matmul(ps[0:2, :], lhsT=sT[:, kk, :], rhs=wt[:, kk, :],
                                                          start=(kk == 0), stop=(kk == 7)), r=[sT, wt], w=[ps])
                k.op("dve", lambda v: v.tensor_tensor(out=msb[:, cb * 512:(cb + 1) * 512], in0=ps[0:2, :],
                                                      in1=bb[:, cb * 512:(cb + 1) * 512], op=ALU.add),
                     r=[ps, bb], w=[msb])
            md = TT(self.modd.t)
            k.dma("sp", self.modd.t[L, :, :], msb[:, :], r=[msb], w=[md])
            for c in range(2):
                k.dma("sp", self.bcm[c][:, :], self.modd.t[L, c, :].partition_broadcast(128), r=[md], w=[self.bcm[c]])
            k.dma("sp", self.ngbc[:, :], self.norm_g.t[L, :].partition_broadcast(128), w=[self.ngbc])
            for c in range(2):
                k.op("dve", lambda v, c=c: v.scalar_tensor_tensor(out=self.gmod[c][:, :], in0=self.bcm[c][:, D:2 * D],
                                                                 scalar=1.0, in1=self.ngbc[:, :], op0=ALU.add,
                                                                 op1=ALU.mult),
                     r=[self.bcm[c], self.ngbc], w=[self.gmod[c]])
            k.barrier()

    def load_w(self, es_stage, dst, dview, src_ap, kc, kp, n, stg):
        k = self.k
        CH = 1024
        for kk in range(kc):
            for c0 in range(0, n, CH):
                cn = min(CH, n - c0)
                st = stg.next()
                k.dma("sp", st[0:kp, 0:cn], src_ap[kk * kp:(kk + 1) * kp, c0:c0 + cn], w=[st])
                k.op("pool", lambda g, st=st, kk=kk, c0=c0, cn=cn: g.tensor_copy(out=dview[0:kp, kk, c0:c0 + cn],
                                                                                in_=st[0:kp, 0:cn]),
                     r=[st], w=[dst])

    def norm_block(self, es, bi, xsrc, xr, hT, scr):
        k = self.k
        t0, nt, cond = BLOCKS[bi]
        junk, ss, rs, hf, hb = scr
        for j in range(nt // 128):
            xt = xr.next()
            src, rr = self.xsrc_ap(xsrc, t0 + j * 128, 128)
            k.dma("sp", xt[:, :], src, r=(rr or [self.xblk[bi]]), w=[xt])
            k.op("act", lambda a, xt=xt: a.activation(out=junk[:, :], in_=xt[:, :], func=AF.Square, accum_out=ss[:, :]),
                 r=[xt], w=[junk, ss])
            k.op("act", lambda a: a.activation(out=rs[:, :], in_=ss[:, :], func=AF.Sqrt, scale=1.0 / D, bias=self.epsb[:, :]),
                 r=[ss, self.epsb], w=[rs])
            k.op("dve", lambda v: v.reciprocal(out=rs[:, :], in_=rs[:, :]), r=[rs], w=[rs])
            k.op("dve", lambda v, xt=xt: v.scalar_tensor_tensor(out=hf[:, :], in0=xt[:, :], scalar=rs[:, 0:1],
                                                               in1=self.gmod[cond][:, :], op0=ALU.mult, op1=ALU.mult),
                 r=[xt, rs, self.gmod[cond]], w=[hf])
            k.op("dve", lambda v: v.tensor_tensor(out=hb[:, :], in0=hf[:, :], in1=self.bcm[cond][:, 0:D], op=ALU.add),
                 r=[hf, self.bcm[cond]], w=[hb])
            ps = self.psg.next()
            pv = ps[:, :].bitcast(BF16).rearrange("p (c t) -> p c t", c=8)
            for c in range(8):
                k.op("pe", lambda pe, c=c: pe.transpose(out=pv[:, c, :], in_=hb[:, c * 128:(c + 1) * 128],
                                                        identity=self.identb[:, :]),
                     r=[hb, self.identb], w=[ps])
            k.op("act", lambda a, j=j: a.copy(out=hT[:, :, j * 128:(j + 1) * 128], in_=pv), r=[ps], w=[hT])

    def layer_attn(self, L, kind, xsrc):
        k = self.k
        nc = self.nc
        if kind == "mla":
            H, HK, DQ = 16, 16, 96
            w_in = self.W["mla_w_in"].t[0]
            w_out = self.W["mla_w_out"].t[0]
            GOFF = 672
            scale = 96 ** -0.5
        else:
            H, HK, DQ = 16, 4, 64
            w_in = self.W["gqa_w_in"].t[0]
            w_out = self.W["gqa_w_out"].t[0]
            GOFF = 1536
            scale = 64 ** -0.5
        REP = H // HK
        QT = k.dram([H, DQ, T], BF16, f"QT{L}")
        KT = k.dram([HK, DQ, T], BF16, f"KT{L}")
        VV = k.dram([HK, 128, NT, 65], BF16, f"VV{L}")
        GS = k.dram([8, 128, T], BF16, f"GS{L}")
        OG = k.dram([NB, 64, 16, 512], BF16, f"OG{L}")

        with ExitStack() as es:
            self.epsb = k.sb(es, [128, 1], F32)
            k.op("pool", lambda g: g.memset(self.epsb[:, :], EPS), w=[self.epsb])
            stg = Ring([k.sb(es, [128, 1024], F32) for _ in range(2)])
            NIN = 1696 if kind == "mla" else 2560
            win = k.sb(es, [128, 8, NIN], BF16)
            self.load_w(es, win, win, w_in, 8, 128, NIN, stg)
            if kind == "mla":
                wuq = k.sb(es, [128, 3, 1536], BF16)
                self.load_w(es, wuq, wuq, self.W["mla_w_uq"].t[0], 3, 128, 1536, stg)
                wukv = k.sb(es, [128, 2, 2048], BF16)
                self.load_w(es, wukv, wukv, self.W["mla_w_ukv"].t[0], 2, 128, 2048, stg)
                qnbc = k.sb(es, [128, 384], F32)
                k.dma("sp", qnbc[:, :], self.W["mla_q_norm"].t[0, :].partition_broadcast(128), w=[qnbc])
                kvnbc = k.sb(es, [128, 256], F32)
                k.dma("sp", kvnbc[:, :], self.W["mla_kv_norm"].t[0, :].partition_broadcast(128), w=[kvnbc])
                RD, HF = 32, 8
                rope_d = self.rope_mla
            else:
                qnbc = k.sb(es, [128, 64], F32)
                k.dma("sp", qnbc[:, :], self.W["gqa_q_norm"].t[0, :].partition_broadcast(128), w=[qnbc])
                knbc = k.sb(es, [128, 64], F32)
                k.dma("sp", knbc[:, :], self.W["gqa_k_norm"].t[0, :].partition_broadcast(128), w=[knbc])
                RD, HF = 64, 16
                rope_d = self.rope_gqa
            xr = Ring([k.sb(es, [128, D], F32) for _ in range(2)])
            hTr = Ring([k.sb(es, [128, 8, 512], BF16) for _ in range(2)])
            scr = (k.sb(es, [128, D], BF16), k.sb(es, [128, 1], F32), k.sb(es, [128, 1], F32),
                   k.sb(es, [128, D], F32), k.sb(es, [128, D], BF16))
            qsb = k.sb(es, [128, H * DQ], F32)
            qb = k.sb(es, [128, H, DQ], BF16)
            kb = k.sb(es, [128, HK, DQ], BF16)
            ksb = k.sb(es, [128, HK * DQ if kind == "gqa" else 32], F32)
            vblk = Ring([k.sb(es, [128, 4, HK, 65], BF16) for _ in range(1)])
            for vb_ in vblk.items:
                k.op("pool", lambda g, vb_=vb_: g.memset(vb_[:, :, :, :], 1.0), w=[vb_])
            qTb = Ring([k.sb(es, [DQ, H, 512], BF16) for _ in range(1)])
            kTb = Ring([k.sb(es, [DQ, HK, 512], BF16) for _ in range(1)])
            rtab = Ring([k.sb(es, [128, 2, RD], F32) for _ in range(2)])
            ra = k.sb(es, [128, H, RD], F32)
            rb_ = k.sb(es, [128, H, RD], F32)
            ss2 = k.sb(es, [128, 32], F32)
            rs2 = k.sb(es, [128, 32], F32)
            sq = k.sb(es, [128, H * DQ], F32)
            if kind == "mla":
                cqn = k.sb(es, [128, 640], BF16)
                cT = k.sb(es, [128, 5, 128], BF16)
            gsr = Ring([k.sb(es, [128, 512], BF16) for _ in range(2)])

            def rope(xv, nh, dst, tab):
                cosb = tab[:, 0:1, :].broadcast_to([128, nh, RD])
                k.op("dve", lambda v: v.tensor_tensor(out=ra[:, 0:nh, :], in0=xv, in1=cosb, op=ALU.mult),
                     r=[tab, qsb, ksb], w=[ra])
                x5 = xv.rearrange("p h (g s f) -> p h g s f", g=2, s=2)
                b5 = rb_[:, 0:nh, :].rearrange("p h (g s f) -> p h g s f", g=2, s=2)
                s5 = tab[:, 1, :].rearrange("p (g s f) -> p g s f", g=2, s=2)
                for g in range(2):
                    for s in range(2):
                        sinb = s5[:, g:g + 1, s, :].broadcast_to([128, nh, HF])
                        k.op("dve", lambda v, g=g, s=s, sinb=sinb: v.tensor_tensor(
                            out=b5[:, :, g, s, :], in0=x5[:, :, g, 1 - s, :], in1=sinb, op=ALU.mult),
                            r=[tab, qsb, ksb], w=[rb_])
                k.op("dve", lambda v: v.tensor_tensor(out=dst, in0=ra[:, 0:nh, :], in1=rb_[:, 0:nh, :], op=ALU.add),
                     r=[ra, rb_], w=[qb, kb])

            import os
            for bi, (t0, nt, cond) in enumerate(BLOCKS[:int(os.environ.get('DBG_P1_BLOCKS', NB))]):
                hT = hTr.next()
                self.norm_block(es, bi, xsrc, xr, hT, scr)
                ntile = nt // 128
                vb4 = vblk.next()
                qT = qTb.next()
                kT = kTb.next()
                for j in range(ntile):
                    tt0 = t0 + j * 128
                    kt = tt0 // 128
                    tab = rtab.next()
                    k.dma("sp", tab[:, :, :], rope_d.t[tt0:tt0 + 128, :, :], w=[tab])
                    hTj = lambda kk: hT[:, kk, j * 128:(j + 1) * 128]
                    if kind == "mla":
                        psA = self.psg.next()
                        psB = self.psg.next()
                        for kk in range(8):
                            k.op("pe", lambda pe, kk=kk: pe.matmul(psA[:, 0:384], lhsT=hTj(kk), rhs=win[:, kk, 0:384],
                                                                  start=(kk == 0), stop=(kk == 7)), r=[hT, win], w=[psA])
                        for kk in range(8):
                            k.op("pe", lambda pe, kk=kk: pe.matmul(psB[:, 0:288], lhsT=hTj(kk), rhs=win[:, kk, 384:672],
                                                                  start=(kk == 0), stop=(kk == 7)), r=[hT, win], w=[psB])
                        for (ps_, n_, gb_, o_) in ((psA, 384, qnbc, 0), (psB, 256, kvnbc, 384)):
                            k.op("act", lambda a, ps_=ps_, n_=n_: a.activation(out=sq[:, 0:n_], in_=ps_[:, 0:n_], func=AF.Square,
                                                                              accum_out=ss2[:, 0:1]), r=[ps_], w=[sq, ss2])
                            k.op("act", lambda a, n_=n_: a.activation(out=rs2[:, 0:1], in_=ss2[:, 0:1], func=AF.Sqrt,
                                                                     scale=1.0 / n_, bias=self.epsb[:, :]),
                                 r=[ss2, self.epsb], w=[rs2])
                            k.op("dve", lambda v: v.reciprocal(out=rs2[:, 0:1], in_=rs2[:, 0:1]), r=[rs2], w=[rs2])
                            k.op("dve", lambda v, ps_=ps_, n_=n_, gb_=gb_, o_=o_: v.scalar_tensor_tensor(
                                out=cqn[:, o_:o_ + n_], in0=ps_[:, 0:n_], scalar=rs2[:, 0:1], in1=gb_[:, :],
                                op0=ALU.mult, op1=ALU.mult), r=[ps_, rs2, gb_], w=[cqn])
                        k.op("act", lambda a: a.copy(out=ksb[:, 0:32], in_=psB[:, 256:288]), r=[psB], w=[ksb])
                        pst = self.psg.next()
                        ptv = pst[:, :].bitcast(BF16).rearrange("p (c t) -> p c t", c=8)
                        for c in range(5):
                            k.op("pe", lambda pe, c=c: pe.transpose(out=ptv[:, c, :], in_=cqn[:, c * 128:(c + 1) * 128],
                                                                    identity=self.identb[:, :]), r=[cqn, self.identb], w=[pst])
                        k.op("act", lambda a: a.copy(out=cT[:, :, :], in_=ptv[:, 0:5, :]), r=[pst], w=[cT])
                        for cb in range(3):
                            ps = self.psg.next()
                            for kk in range(3):
                                k.op("pe", lambda pe, kk=kk, cb=cb, ps=ps: pe.matmul(
                                    ps[:, :], lhsT=cT[:, kk, :], rhs=wuq[:, kk, cb * 512:(cb + 1) * 512],
                                    start=(kk == 0), stop=(kk == 2)), r=[cT, wuq], w=[ps])
                            k.op("act", lambda a, cb=cb, ps=ps: a.copy(out=qsb[:, cb * 512:(cb + 1) * 512], in_=ps[:, :]),
                                 r=[ps], w=[qsb])
                        q3 = qsb[:, :].rearrange("p (h d) -> p h d", h=16)
                        k.op("pool", lambda g: g.tensor_copy(out=qb[:, :, 0:64], in_=q3[:, :, 0:64]), r=[qsb], w=[qb])
                        rope(q3[:, :, 64:96], 16, qb[:, :, 64:96], tab)
                        krv = ksb[:, 0:32].rearrange("p (h d) -> p h d", h=1)
                        rope(krv, 1, kb[:, 0:1, 64:96], tab)
                        k.op("pool", lambda g: g.tensor_copy(out=kb[:, 1:16, 64:96],
                                                             in_=kb[:, 0:1, 64:96].broadcast_to([128, 15, 32])),
                             r=[kb], w=[kb])
                        for cb in range(4):
                            ps = self.psg.next()
                            for kk in range(2):
                                k.op("pe", lambda pe, kk=kk, cb=cb, ps=ps: pe.matmul(
                                    ps[:, :], lhsT=cT[:, 3 + kk, :], rhs=wukv[:, kk, cb * 512:(cb + 1) * 512],
                                    start=(kk == 0), stop=(kk == 1)), r=[cT, wukv], w=[ps])
                            p3 = ps[:, :].rearrange("p (h d) -> p h d", h=4)
                            k.op("act", lambda a, cb=cb, p3=p3: a.copy(out=kb[:, cb * 4:(cb + 1) * 4, 0:64], in_=p3[:, :, 0:64]),
                                 r=[ps], w=[kb])
                            k.op("dve", lambda v, cb=cb, p3=p3: v.tensor_copy(out=vb4[:, j, cb * 4:(cb + 1) * 4, 0:64],
                                                                              in_=p3[:, :, 64:128]), r=[ps], w=[vb4])
                    else:
                        import os
                        for cb in range(3 if int(os.environ.get('DBG_STEP', 9)) >= 1 else 0):
                            ps = self.psg.next()
                            for kk in range(8):
                                k.op("pe", lambda pe, kk=kk, cb=cb, ps=ps: pe.matmul(
                                    ps[:, :], lhsT=hTj(kk), rhs=win[:, kk, cb * 512:(cb + 1) * 512],
                                    start=(kk == 0), stop=(kk == 7)), r=[hT, win], w=[ps])
                            SUB = os.environ.get('DBG_SUB', 'abc')
                            if cb < 2:
                                if 'a' in SUB:
                                    k.op("act", lambda a, cb=cb, ps=ps: a.copy(out=qsb[:, cb * 512:(cb + 1) * 512], in_=ps[:, :]),
                                         r=[ps], w=[qsb])
                            elif 'b' in SUB:
                                k.op("act", lambda a, ps=ps: a.copy(out=ksb[:, 0:256], in_=ps[:, 0:256]), r=[ps], w=[ksb])
                                p3 = ps[:, 256:512].rearrange("p (h d) -> p h d", h=4)
                                if 'c' in SUB:
                                    for hh in range(4):
                                        k.op("act", lambda a, hh=hh: a.copy(out=vb4[:, j, hh, 0:64], in_=ps[:, 256 + hh * 64:256 + (hh + 1) * 64]),
                                             r=[ps], w=[vb4])
                        import os
                        DS = int(os.environ.get('DBG_STEP', 9))
                        for (src_, nh, gb_, dstb) in ((qsb, 16, qnbc, qb), (ksb, 4, knbc, kb)) if DS >= 2 else ():
                            s3 = src_[:, 0:nh * 64].rearrange("p (h d) -> p h d", h=nh)
                            sq3 = sq[:, 0:nh * 64].rearrange("p (h d) -> p h d", h=nh)
                            k.op("dve", lambda v, s3=s3, sq3=sq3: v.tensor_tensor(out=sq3, in0=s3, in1=s3, op=ALU.mult),
                                 r=[src_], w=[sq])
                            k.op("dve", lambda v, sq3=sq3, nh=nh: v.tensor_reduce(out=ss2[:, 0:nh], in_=sq3, axis=AX.X, op=ALU.add),
                                 r=[sq], w=[ss2])
                            k.op("act", lambda a, nh=nh: a.activation(out=rs2[:, 0:nh], in_=ss2[:, 0:nh], func=AF.Sqrt,
                                                                     scale=1.0 / 64, bias=self.epsb[:, :]),
                                 r=[ss2, self.epsb], w=[rs2])
                            k.op("dve", lambda v, nh=nh: v.reciprocal(out=rs2[:, 0:nh], in_=rs2[:, 0:nh]), r=[rs2], w=[rs2])
                            k.op("dve", lambda v, s3=s3, nh=nh: v.tensor_tensor(
                                out=s3, in0=s3, in1=rs2[:, 0:nh].unsqueeze(2).broadcast_to([128, nh, 64]), op=ALU.mult),
                                r=[src_, rs2], w=[src_])
                            k.op("dve", lambda v, s3=s3, nh=nh, gb_=gb_: v.tensor_tensor(
                                out=s3, in0=s3, in1=gb_[:, :].unsqueeze(1).broadcast_to([128, nh, 64]), op=ALU.mult),
                                r=[src_, gb_], w=[src_])
                            if DS >= 3:
                                rope(s3, nh, dstb[:, :, :], tab)
                    import os
                    for (srcb, nh, dstT) in ((qb, H, qT), (kb, HK, kT)) if int(os.environ.get('DBG_STEP', 9)) >= 4 else ():
                        for h0 in range(0, nh, 8):
                            hn = min(8, nh - h0)
                            ps = self.psg.next()
                            ptv = ps[:, :].bitcast(BF16).rearrange("p (c t) -> p c t", c=8)
                            for hh in range(hn):
                                k.op("pe", lambda pe, hh=hh, h0=h0, ptv=ptv, srcb=srcb: pe.transpose(
                                    out=ptv[0:DQ, hh, :], in_=srcb[:, h0 + hh, :], identity=self.identb[:, :]),
                                    r=[srcb, self.identb], w=[ps])
                            k.op("act", lambda a, h0=h0, hn=hn, ptv=ptv, dstT=dstT: a.copy(
                                out=dstT[:, h0:h0 + hn, j * 128:(j + 1) * 128], in_=ptv[0:DQ, 0:hn, :]), r=[ps], w=[dstT])
                for hp in range(8):
                    ps = self.psg.next()
                    for kk in range(8):
                        k.op("pe", lambda pe, kk=kk, hp=hp, ps=ps: pe.matmul(
                            ps[:, 0:nt], lhsT=win[:, kk, GOFF + hp * 128:GOFF + (hp + 1) * 128], rhs=hT[:, kk, 0:nt],
                            start=(kk == 0), stop=(kk == 7)), r=[hT, win], w=[ps])
                    gs = gsr.next()
                    k.op("act", lambda a, ps=ps, gs=gs: a.activation(out=gs[:, 0:nt], in_=ps[:, 0:nt], func=AF.Silu),
                         r=[ps], w=[gs])
                    k.dma("pool", GS.t[hp, :, t0:t0 + nt], gs[:, 0:nt], r=[gs], w=[GS])
                for h0 in range(0, H, 4):
                    k.dma("act", QT.t[h0:h0 + 4, :, t0:t0 + nt].rearrange("h d t -> d h t"), qT[:, h0:h0 + 4, 0:nt], r=[qT], w=[QT])
                for h0 in range(0, HK, 4):
                    k.dma("act", KT.t[h0:h0 + 4, :, t0:t0 + nt].rearrange("h d t -> d h t"), kT[:, h0:h0 + 4, 0:nt], r=[kT], w=[KT])
                kt0 = t0 // 128
                for j in range(ntile):
                    for h0 in range(0, HK, 4):
                        k.dma("act", VV.t[h0:h0 + 4, :, kt0 + j, :].rearrange("h p e -> p h e"), vb4[:, j, h0:h0 + 4, :],
                              r=[vb4], w=[VV], allow_slow_non_contiguous=True)
            k.barrier()

        if self.stop == "p1":
            return
        with ExitStack() as es:
            DQP = 128 if DQ == 64 else DQ
            KTs = Ring([k.sb(es, [DQP, T], BF16) for _ in range(2)])
            Vs = Ring([k.sb(es, [128, NT, 65], BF16) for _ in range(2)])
            Qs = Ring([k.sb(es, [DQP, T], BF16) for _ in range(2)])
            if DQP != DQ:
                for b_ in KTs.items + Qs.items:
                    k.op("pool", lambda g, b_=b_: g.memset(b_[DQ:DQP, :], 0.0), w=[b_])
            Gs = Ring([k.sb(es, [64, T], BF16) for _ in range(2)])
            Ps = Ring([k.sb(es, [128, 512], BF16) for _ in range(6)])
            rr = k.sb(es, [65, 512], F32)
            tmp = k.sb(es, [64, 512], F32)
            ogr = Ring([k.sb(es, [64, 512], BF16) for _ in range(2)])
            pss = Ring(self.psb[0:5])
            psm = Ring(self.psb[5:6])
            import os
            for hk in range(int(os.environ.get('DBG_P2_HEADS', HK))):
                Kt = KTs.next()
                Vt = Vs.next()
                k.dma("sp", Kt[0:DQ, :], KT.t[hk], r=[KT], w=[Kt])
                k.dma("sp", Vt[:, :, :], VV.t[hk], r=[VV], w=[Vt])
                for hr in range(REP):
                    h = hk * REP + hr
                    Qt = Qs.next()
                    Gt = Gs.next()
                    k.dma("sp", Qt[0:DQ, :], QT.t[h], r=[QT], w=[Qt])
                    k.dma("sp", Gt[:, :], GS.t[h // 2, (h % 2) * 64:(h % 2) * 64 + 64, :], r=[GS], w=[Gt])
                    for bi, (t0, nt, cond) in enumerate(BLOCKS):
                        nkt = 2 if cond == 0 else NT
                        po = self.pso.next()
                        pend = []

                        def s_mm(kt):
                            ps = pss.next()
                            k.op("pe", lambda pe: pe.matmul(ps[:, 0:nt], lhsT=Kt[:, kt * 128:(kt + 1) * 128], rhs=Qt[:, t0:t0 + nt],
                                                            start=True, stop=True), r=[Kt, Qt], w=[ps])
                            pt = Ps.next()
                            k.op("act", lambda a: a.activation(out=pt[:, 0:nt], in_=ps[:, 0:nt], func=AF.Exp, scale=scale),
                                 r=[ps], w=[pt])
                            return pt

                        def pv_mm(kt, pt):
                            k.op("pe", lambda pe: pe.matmul(po[0:65, 0:nt], lhsT=Vt[:, kt, :], rhs=pt[:, 0:nt],
                                                            start=(kt == 0), stop=(kt == nkt - 1)), r=[Vt, pt], w=[po])

                        SK = 3
                        for kt in range(nkt + SK):
                            if kt < nkt:
                                pend.append((kt, s_mm(kt)))
                            if kt >= SK:
                                a_, b_ = pend.pop(0)
                                pv_mm(a_, b_)
                        k.op("dve", lambda v: v.reciprocal(out=rr[64:65, 0:nt], in_=po[64:65, 0:nt]), r=[po], w=[rr])
                        pm = psm.next()
                        k.op("pe", lambda pe: pe.matmul(pm[0:64, 0:nt], lhsT=self.ones_f[64:65, 0:64], rhs=rr[64:65, 0:nt],
                                                        start=True, stop=True), r=[rr, self.ones_f], w=[pm])
                        k.op("dve", lambda v: v.tensor_tensor(out=tmp[:, 0:nt], in0=pm[0:64, 0:nt], in1=Gt[:, t0:t0 + nt], op=ALU.mult),
                             r=[pm, Gt], w=[tmp])
                        og = ogr.next()
                        k.op("dve", lambda v: v.tensor_tensor(out=og[:, 0:nt], in0=po[0:64, 0:nt], in1=tmp[:, 0:nt], op=ALU.mult),
                             r=[po, tmp], w=[og])
                        k.dma("pool", OG.t[bi, :, h, 0:nt], og[:, 0:nt], r=[og], w=[OG])
            k.barrier()

        if self.stop == "p2":
            return
        with ExitStack() as es:
            stg = Ring([k.sb(es, [128, 1024], F32) for _ in range(2)])
            wo = k.sb(es, [64, 16, D], BF16)
            self.load_w(es, wo, wo, w_out, 16, 64, D, stg)
            ogb = Ring([k.sb(es, [64, 16, 512], BF16) for _ in range(2)])
            xr = Ring([k.sb(es, [128, D], F32) for _ in range(3)])
            tm = k.sb(es, [128, 512], F32)
            self.outproj(es, lambda bi, nt: (OG.t[bi, :, :, 0:nt], OG), ogb, 64, 16, wo, xsrc, xr, tm)
            k.barrier()


    def layer_gla(self, L, xsrc):
        k = self.k
        w_in = self.W["gla_w_in"].t[0]
        w_out = self.W["gla_w_out"].t[0]
        REC = k.dram([NT, 128, 3584], BF16, f"GREC{L}")
        GG = k.dram([NT, 128, 1024], F32, f"GGG{L}")
        GOF = k.dram([NT, 128, 1024], F32, f"GOF{L}")
        recT = [TT(REC.t) for _ in range(NT)]
        ggT = [TT(GG.t) for _ in range(NT)]
        ofT = [TT(GOF.t) for _ in range(NT)]
        ps8 = self.ps8
        with ExitStack() as es:
            self.epsb = k.sb(es, [128, 1], F32)
            k.op("pool", lambda g: g.memset(self.epsb[:, :], EPS), w=[self.epsb])
            onec = k.sb(es, [128, 1], F32)
            k.op("pool", lambda g: g.memset(onec[:, :], 1.0), w=[onec])
            stg = Ring([k.sb(es, [128, 1024], F32) for _ in range(2)])
            win = k.sb(es, [128, 8, 3104], BF16)
            self.load_w(es, win, win, w_in, 8, 128, 3104, stg)
            wg = k.sb(es, [16, 2, 512], F32)
            k.dma("sp", wg[:, 0, :], self.W["gla_w_gf"].t[0], w=[wg])
            k.dma("sp", wg[:, 1, :], self.W["gla_w_gb"].t[0], w=[wg])
            bg = k.sb(es, [128, 2, 512], F32)
            k.dma("sp", bg[:, 0, :], self.W["gla_b_gf"].t[0, :].partition_broadcast(128), w=[bg])
            k.dma("sp", bg[:, 1, :], self.W["gla_b_gb"].t[0, :].partition_broadcast(128), w=[bg])
            xr = Ring([k.sb(es, [128, D], F32) for _ in range(2)])
            hTr = Ring([k.sb(es, [128, 8, 512], BF16) for _ in range(2)])
            scr = (k.sb(es, [128, D], BF16), k.sb(es, [128, 1], F32), k.sb(es, [128, 1], F32),
                   k.sb(es, [128, D], F32), k.sb(es, [128, D], BF16))
            recr = Ring([k.sb(es, [128, 3584], BF16) for _ in range(2)])
            ggr = Ring([k.sb(es, [128, 1024], F32) for _ in range(2)])
            rT = k.sb(es, [16, 2, 128], F32)
            zt = k.sb(es, [128, 512], F32)
            for bi, (t0, nt, cond) in enumerate(BLOCKS):
                hT = hTr.next()
                self.norm_block(es, bi, xsrc, xr, hT, scr)
                for j in range(nt // 128):
                    t = (t0 + j * 128) // 128
                    rec = recr.next()
                    gg = ggr.next()
                    hTj = lambda kk: hT[:, kk, j * 128:(j + 1) * 128]

                    def tokmm(c0, n):
                        ps = ps8.next()
                        for kk in range(8):
                            k.op("pe", lambda pe: pe.matmul(ps[:, 0:n], lhsT=hTj(kk), rhs=win[:, kk, c0:c0 + n],
                                                            start=(kk == 0), stop=(kk == 7)), r=[hT, win], w=[ps])
                        return ps

                    ps = tokmm(512, 512)
                    k.op("act", lambda a: a.copy(out=rec[:, 1024:1536], in_=ps[:, :]), r=[ps], w=[rec])
                    for cb in range(2):
                        ps = tokmm(1024 + cb * 512, 512)
                        k.op("dve", lambda v: v.tensor_copy(out=rec[:, 1536 + cb * 512:2048 + cb * 512], in_=ps[:, :]),
                             r=[ps], w=[rec])
                    for cb in range(2):
                        ps = tokmm(2048 + cb * 512, 512)
                        k.op("act", lambda a: a.activation(out=rec[:, 2560 + cb * 512:3072 + cb * 512], in_=ps[:, :],
                                                           func=AF.Silu), r=[ps], w=[rec])
                    for qk in range(2):
                        ps = ps8.next()
                        for h in range(4):
                            for kk in range(8):
                                k.op("pe", lambda pe: pe.matmul(
                                    ps[:, h * 128:(h + 1) * 128], lhsT=win[:, kk, qk * 512 + h * 128:qk * 512 + (h + 1) * 128],
                                    rhs=hTj(kk), start=(kk == 0), stop=(kk == 7)), r=[hT, win], w=[ps])
                        if qk == 0:
                            k.op("act", lambda a: a.mul(out=rec[:, 0:512], in_=ps[:, :], mul=128 ** -0.5), r=[ps], w=[rec])
                        else:
                            k.op("dve", lambda v: v.tensor_copy(out=rec[:, 512:1024], in_=ps[:, :]), r=[ps], w=[rec])
                    ps = ps8.next()
                    for d in range(2):
                        for kk in range(8):
                            k.op("pe", lambda pe: pe.matmul(
                                ps[0:16, d * 128:(d + 1) * 128], lhsT=win[:, kk, 3072 + 16 * d:3088 + 16 * d],
                                rhs=hTj(kk), start=(kk == 0), stop=(kk == 7)), r=[hT, win], w=[ps])
                    k.op("act", lambda a: a.copy(out=rT[:, :, :], in_=ps[0:16, 0:256].rearrange("p (d t) -> p d t", d=2)),
                         r=[ps], w=[rT])
                    for d in range(2):
                        ps = ps8.next()
                        k.op("pe", lambda pe: pe.matmul(ps[:, :], lhsT=rT[:, d, :], rhs=wg[:, d, :], start=True, stop=True),
                             r=[rT, wg], w=[ps])
                        k.op("dve", lambda v: v.tensor_tensor(out=zt[:, :], in0=ps[:, :], in1=bg[:, d, :], op=ALU.add),
                             r=[ps, bg], w=[zt])
                        k.op("act", lambda a: a.activation(out=zt[:, :], in_=zt[:, :], func=AF.Exp, scale=-1.0), r=[zt], w=[zt])
                        k.op("act", lambda a: a.activation(out=zt[:, :], in_=zt[:, :], func=AF.Ln, bias=onec[:, :]),
                             r=[zt, onec], w=[zt])
                        k.op("dve", lambda v: v.tensor_scalar(out=gg[:, d * 512:(d + 1) * 512], in0=zt[:, :],
                                                              scalar1=-1.0 / 16.0, scalar2=None, op0=ALU.mult),
                             r=[zt], w=[gg])
                    k.dma("pool", REC.t[t], rec[:, :], r=[rec], w=[recT[t]])
                    k.dma("pool", GG.t[t], gg[:, :], r=[gg], w=[ggT[t]])
            k.barrier()

        with ExitStack() as es:
            self.epsb = k.sb(es, [128, 1], F32)
            k.op("pool", lambda g: g.memset(self.epsb[:, :], EPS), w=[self.epsb])
            tri = k.sb(es, [128, 4, 128], F32)
            k.dma("sp", tri[:, :, :], self.tri_d.t.rearrange("m j i -> j m i"), w=[tri])
            S = [k.sb(es, [128, 256], F32) for _ in range(4)]
            Sb = [k.sb(es, [128, 256], BF16) for _ in range(4)]
            recr = Ring([k.sb(es, [128, 3584], BF16) for _ in range(3)])
            ggr = Ring([k.sb(es, [128, 1024], F32) for _ in range(3)])
            E1r = Ring([k.sb(es, [128, 512], F32) for _ in range(2)])
            E2r = Ring([k.sb(es, [128, 512], F32) for _ in range(2)])
            E3r = Ring([k.sb(es, [128, 512], F32) for _ in range(2)])
            qtr = Ring([k.sb(es, [128, 512], BF16) for _ in range(2)])
            ktr = Ring([k.sb(es, [128, 512], BF16) for _ in range(2)])
            khr = Ring([k.sb(es, [128, 512], BF16) for _ in range(2)])
            Amr = Ring([k.sb(es, [128, 512], BF16) for _ in range(2)])
            osb = Ring([k.sb(es, [128, 1024], F32) for _ in range(2)])
            stg = Ring([k.sb(es, [128, 1024], F32) for _ in range(2)])
            wo = k.sb(es, [128, 8, D], BF16)
            self.load_w(es, wo, wo, w_out, 8, 128, D, stg)
            onbc = k.sb(es, [128, 256], F32)
            k.dma("sp", onbc[:, :], self.W["gla_o_norm"].t[0, :].partition_broadcast(128), w=[onbc])
            ofr = Ring([k.sb(es, [128, 1024], F32) for _ in range(2)])
            xr = Ring([k.sb(es, [128, D], F32) for _ in range(2)])
            osum = k.sb(es, [128, 1024], F32)
            ogf = k.sb(es, [128, 1024], F32)
            ogb = k.sb(es, [128, 1024], BF16)
            ogT = k.sb(es, [128, 8, 128], BF16)
            junk = k.sb(es, [128, 256], BF16)
            ss4 = k.sb(es, [128, 4], F32)
            rs4 = k.sb(es, [128, 4], F32)
            tm = k.sb(es, [128, 512], F32)

            def stageA(d, rec, gg):
                cm = tri[:, 0 if d == 0 else 2, :]
                sm = tri[:, 1 if d == 0 else 3, :]
                g = lambda a_, b_: gg[:, d * 512 + a_:d * 512 + b_]
                E1, E2, E3 = E1r.next(), E2r.next(), E3r.next()
                qt, kt_, kh, Am = qtr.next(), ktr.next(), khr.next(), Amr.next()
                psA = ps8.next()
                for h in range(4):
                    k.op("pe", lambda pe: pe.matmul(psA[:, h * 128:(h + 1) * 128], lhsT=g(h * 128, (h + 1) * 128), rhs=cm,
                                                    start=True, stop=True), r=[gg, tri], w=[psA])
                psB = ps8.next()
                k.op("pe", lambda pe: pe.matmul(psB[:, :], lhsT=sm, rhs=g(0, 512), start=True, stop=True), r=[gg, tri], w=[psB])
                k.op("act", lambda a: a.activation(out=E1[:, :], in_=psA[:, :], func=AF.Exp), r=[psA], w=[E1])
                k.op("act", lambda a: a.activation(out=E2[:, :], in_=psA[:, :], func=AF.Exp, scale=-1.0), r=[psA], w=[E2])
                k.op("act", lambda a: a.activation(out=E3[:, :], in_=psB[:, :], func=AF.Exp), r=[psB], w=[E3])
                k.op("dve", lambda v: v.tensor_tensor(out=qt[:, :], in0=rec[:, 0:512], in1=E1[:, :], op=ALU.mult), r=[rec, E1], w=[qt])
                k.op("pool", lambda v: v.tensor_tensor(out=kt_[:, :], in0=rec[:, 512:1024], in1=E2[:, :], op=ALU.mult), r=[rec, E2], w=[kt_])
                k.op("pool", lambda v: v.tensor_tensor(out=kh[:, :], in0=rec[:, 1024:1536], in1=E3[:, :], op=ALU.mult), r=[rec, E3], w=[kh])
                return dict(d=d, rec=rec, E1=E1, qt=qt, kh=kh, Am=Am, kt_=kt_, cm=cm)

            def stageA2(c):
                qt, kt_, Am, cm = c["qt"], c["kt_"], c["Am"], c["cm"]
                psD = ps8.next()
                for h in range(4):
                    hs = slice(h * 128, (h + 1) * 128)
                    k.op("pe", lambda pe: pe.matmul(psD[:, hs], lhsT=kt_[:, hs], rhs=qt[:, hs], start=True, stop=True),
                         r=[kt_, qt], w=[psD])
                k.op("dve", lambda v: v.tensor_tensor(out=Am[:, :].rearrange("p (h i) -> p h i", h=4),
                                                      in0=psD[:, :].rearrange("p (h i) -> p h i", h=4),
                                                      in1=cm.unsqueeze(1).broadcast_to([128, 4, 128]), op=ALU.mult),
                     r=[psD, tri], w=[Am])

            def stageB(c):
                d, rec, E1, qt, kh, Am = (c[n] for n in ("d", "rec", "E1", "qt", "kh", "Am"))
                ecol = 127 if d == 0 else 0
                po = [ps8.next(), ps8.next()]
                for h in range(4):
                    hs = slice(h * 128, (h + 1) * 128)
                    bank = po[h // 2]
                    cs = slice((h % 2) * 256, (h % 2) * 256 + 256)
                    vs = slice(1536 + h * 256, 1536 + (h + 1) * 256)
                    k.op("pe", lambda pe: pe.matmul(bank[:, cs], lhsT=qt[:, hs], rhs=Sb[h][:, :], start=True, stop=False),
                         r=[qt, Sb[h]], w=[bank])
                    k.op("pe", lambda pe: pe.matmul(bank[:, cs], lhsT=Am[:, hs], rhs=rec[:, vs], start=False, stop=True),
                         r=[Am, rec], w=[bank])
                pss = [ps8.next(), ps8.next()]
                for h in range(4):
                    hs = slice(h * 128, (h + 1) * 128)
                    bank = pss[h // 2]
                    cs = slice((h % 2) * 256, (h % 2) * 256 + 256)
                    vs = slice(1536 + h * 256, 1536 + (h + 1) * 256)
                    k.op("pe", lambda pe: pe.matmul(bank[:, cs], lhsT=kh[:, hs], rhs=rec[:, vs], start=True, stop=True),
                         r=[kh, rec], w=[bank])
                    k.op("dve", lambda v: v.scalar_tensor_tensor(out=S[h][:, :], in0=S[h][:, :],
                                                                 scalar=E1[:, h * 128 + ecol:h * 128 + ecol + 1],
                                                                 in1=bank[:, cs], op0=ALU.mult, op1=ALU.add),
                         r=[S[h], E1, bank], w=[S[h]])
                    k.op("act", lambda gp: gp.copy(out=Sb[h][:, :], in_=S[h][:, :]), r=[S[h]], w=[Sb[h]])
                return po

            def reset_state():
                for h in range(4):
                    k.op("pool", lambda gp: gp.memset(S[h][:, :], 0.0), w=[S[h]])
                    k.op("pool", lambda gp: gp.memset(Sb[h][:, :], 0.0), w=[Sb[h]])

            def gl_loads(t):
                rec = recr.next()
                gg = ggr.next()
                k.dma("sp", rec[:, :], REC.t[t], r=[recT[t]], w=[rec])
                k.dma("sp", gg[:, :], GG.t[t], r=[ggT[t]], w=[gg])
                return rec, gg

            reset_state()
            cnext = stageA(0, *gl_loads(0))
            stageA2(cnext)
            for t in range(NT):
                c = cnext
                if t + 1 < NT:
                    cnext = stageA(0, *gl_loads(t + 1))
                po = stageB(c)
                if t + 1 < NT:
                    stageA2(cnext)
                ob = osb.next()
                for c in range(2):
                    k.op("act", lambda a: a.copy(out=ob[:, c * 512:(c + 1) * 512], in_=po[c][:, :]), r=[po[c]], w=[ob])
                k.dma("pool", GOF.t[t], ob[:, :], r=[ob], w=[ofT[t]])
            reset_state()
            order = [1, 0] + list(range(NT - 1, 1, -1))
            cnext = stageA(1, *gl_loads(order[0]))
            stageA2(cnext)
            for oi, t in enumerate(order):
                cond = 0 if t < 2 else 1
                bi = 0 if t < 2 else 1 + (t - 2) // 4
                c = cnext
                rec = c["rec"]
                if oi + 1 < NT:
                    cnext = stageA(1, *gl_loads(order[oi + 1]))
                of = ofr.next()
                xt = xr.next()
                k.dma("sp", of[:, :], GOF.t[t], r=[ofT[t]], w=[of])
                sap, rr = self.xsrc_ap(xsrc, t * 128, 128)
                k.dma("sp", xt[:, :], sap, r=(rr or [self.xblk[bi]]), w=[xt])
                po = stageB(c)
                if oi + 1 < NT:
                    stageA2(cnext)
                for c in range(2):
                    k.op("dve", lambda v: v.tensor_tensor(out=osum[:, c * 512:(c + 1) * 512], in0=po[c][:, :],
                                                          in1=of[:, c * 512:(c + 1) * 512], op=ALU.add),
                         r=[po[c], of], w=[osum])
                for h in range(4):
                    k.op("act", lambda a: a.activation(out=junk[:, :], in_=osum[:, h * 256:(h + 1) * 256], func=AF.Square,
                                                       accum_out=ss4[:, h:h + 1]), r=[osum], w=[junk, ss4])
                k.op("act", lambda a: a.activation(out=rs4[:, :], in_=ss4[:, :], func=AF.Sqrt, scale=1.0 / 256, bias=self.epsb[:, :]),
                     r=[ss4, self.epsb], w=[rs4])
                k.op("dve", lambda v: v.reciprocal(out=rs4[:, :], in_=rs4[:, :]), r=[rs4], w=[rs4])
                for h in range(4):
                    k.op("dve", lambda v: v.scalar_tensor_tensor(out=ogf[:, h * 256:(h + 1) * 256], in0=osum[:, h * 256:(h + 1) * 256],
                                                                 scalar=rs4[:, h:h + 1], in1=onbc[:, :], op0=ALU.mult, op1=ALU.mult),
                         r=[osum, rs4, onbc], w=[ogf])
                k.op("dve", lambda v: v.tensor_tensor(out=ogb[:, :], in0=ogf[:, :], in1=rec[:, 2560:3584], op=ALU.mult),
                     r=[ogf, rec], w=[ogb])
                self.tok_outproj(ogb, 8, ogT, wo, xt, tm, cond)
                k.dma("pool", self.xres.t[t * 128:(t + 1) * 128, :], xt[:, :], r=[xt], w=[self.xblk[bi]])
            k.barrier()


    def norm_rows(self, src, rtrk, n, cond, dst, scr, xt):
        k = self.k
        junk, ss, rs, hf, hb = scr
        k.dma("sp", xt[0:n, :], src, r=rtrk, w=[xt])
        k.op("act", lambda a: a.activation(out=junk[0:n, :], in_=xt[0:n, :], func=AF.Square, accum_out=ss[0:n, :]),
             r=[xt], w=[junk, ss])
        k.op("act", lambda a: a.activation(out=rs[0:n, :], in_=ss[0:n, :], func=AF.Sqrt, scale=1.0 / D, bias=self.epsb[0:n, :]),
             r=[ss, self.epsb], w=[rs])
        k.op("dve", lambda v: v.reciprocal(out=rs[0:n, :], in_=rs[0:n, :]), r=[rs], w=[rs])
        k.op("dve", lambda v: v.scalar_tensor_tensor(out=hf[0:n, :], in0=xt[0:n, :], scalar=rs[0:n, 0:1],
                                                     in1=self.gmod[cond][0:n, :], op0=ALU.mult, op1=ALU.mult),
             r=[xt, rs, self.gmod[cond]], w=[hf])
        k.op("dve", lambda v: v.tensor_tensor(out=hb[0:n, :], in0=hf[0:n, :], in1=self.bcm[cond][0:n, 0:D], op=ALU.add),
             r=[hf, self.bcm[cond]], w=[hb])
        ps = self.ps8.next()
        pv = ps[:, :].bitcast(BF16).rearrange("p (c t) -> p c t", c=8)
        for c in range(8):
            k.op("pe", lambda pe: pe.transpose(out=pv[:, c, 0:n], in_=hb[0:n, c * 128:(c + 1) * 128],
                                               identity=self.identb[0:n, 0:n]), r=[hb, self.identb], w=[ps])
        return ps, pv

    def layer_ssd(self, L, xsrc):
        k = self.k
        ps8 = self.ps8
        w_in = self.W["ssd_w_in"].t[0]
        w_out = self.W["ssd_w_out"].t[0]
        RX = k.dram([NT, 128, 2560], BF16, f"SRX{L}")
        RZ = k.dram([NT, 128, 2048], BF16, f"SRZ{L}")
        RBC = k.dram([NT, 128, 1024], BF16, f"SRBC{L}")
        DD = k.dram([NT, 128, 128], F32, f"SDD{L}")
        YF = k.dram([NT, 128, 2048], F32, f"SYF{L}")
        rxT = [TT(RX.t) for _ in range(NT)]
        rzT = [TT(RZ.t) for _ in range(NT)]
        rbcT = [TT(RBC.t) for _ in range(NB)]
        ddT = [TT(DD.t) for _ in range(NT)]
        yfT = [TT(YF.t) for _ in range(NT)]
        with ExitStack() as es:
            self.epsb = k.sb(es, [128, 1], F32)
            k.op("pool", lambda g: g.memset(self.epsb[:, :], EPS), w=[self.epsb])
            onec = k.sb(es, [128, 1], F32)
            k.op("pool", lambda g: g.memset(onec[:, :], 1.0), w=[onec])
            stg = Ring([k.sb(es, [128, 1024], F32) for _ in range(2)])
            wz = k.sb(es, [128, 8, 2048], BF16)
            self.load_w(es, wz, wz, w_in[:, 0:2048], 8, 128, 2048, stg)
            wx = k.sb(es, [128, 8, 3072], BF16)
            self.load_w(es, wx, wx, w_in[:, 2048:5120], 8, 128, 3072, stg)
            wdt = k.sb(es, [128, 8, 64], BF16)
            self.load_w(es, wdt, wdt, w_in[:, 5120:5184], 8, 128, 64, stg)
            cw = k.sb(es, [128, 24, 5], F32)
            for kk in range(5):
                k.dma("sp", cw[:, :, kk], self.W["ssd_conv_w"].t[0, kk, :].rearrange("(c p) -> p c", p=128), w=[cw],
                      allow_slow_non_contiguous=True)
            cbias = k.sb(es, [128, 24], F32)
            k.dma("sp", cbias[:, :], self.W["ssd_conv_b"].t[0, :].rearrange("(c p) -> p c", p=128), w=[cbias],
                  allow_slow_non_contiguous=True)
            dtb = k.sb(es, [128, 64], F32)
            k.dma("sp", dtb[:, 0:32], self.W["ssd_dt_bias_f"].t[0, :].partition_broadcast(128), w=[dtb])
            k.dma("sp", dtb[:, 32:64], self.W["ssd_dt_bias_b"].t[0, :].partition_broadcast(128), w=[dtb])
            abc = k.sb(es, [128, 64], F32)
            k.dma("sp", abc[:, 0:32], self.W["ssd_a_log_f"].t[0, :].partition_broadcast(128), w=[abc])
            k.dma("sp", abc[:, 32:64], self.W["ssd_a_log_b"].t[0, :].partition_broadcast(128), w=[abc])
            k.op("act", lambda a: a.activation(out=abc[:, :], in_=abc[:, :], func=AF.Exp), r=[abc], w=[abc])
            k.op("dve", lambda v: v.tensor_scalar(out=abc[:, :], in0=abc[:, :], scalar1=-1.0, scalar2=None, op0=ALU.mult),
                 r=[abc], w=[abc])
            xr = Ring([k.sb(es, [128, D], F32) for _ in range(2)])
            hT = k.sb(es, [128, 8, 516], BF16)
            scr = (k.sb(es, [128, D], BF16), k.sb(es, [128, 1], F32), k.sb(es, [128, 1], F32),
                   k.sb(es, [128, D], F32), k.sb(es, [128, D], BF16))
            prer = Ring([k.sb(es, [128, 516], BF16) for _ in range(3)])
            accr = Ring([k.sb(es, [128, 512], F32) for _ in range(2)])
            xc = k.sb(es, [128, 24, 512], BF16)
            rxr = Ring([k.sb(es, [128, 2560], BF16) for _ in range(2)])
            rzr = Ring([k.sb(es, [128, 2048], BF16) for _ in range(2)])
            ddr = Ring([k.sb(es, [128, 128], F32) for _ in range(2)])
            dtt = k.sb(es, [128, 64], F32)
            for bi, (t0, nt, cond) in enumerate(BLOCKS):
                ntile = nt // 128
                for j in range(ntile):
                    src, rr = self.xsrc_ap(xsrc, t0 + j * 128, 128)
                    ps, pv = self.norm_rows(src, rr or [self.xblk[bi]], 128, cond, None, scr, xr.next())
                    k.op("act", lambda a: a.copy(out=hT[:, :, j * 128:(j + 1) * 128], in_=pv), r=[ps], w=[hT])
                for side in range(2):
                    col = 512 + 2 * side
                    has = (bi >= 2) if side == 0 else (1 <= bi < NB - 1)
                    if not has:
                        k.op("pool", lambda g: g.memset(hT[:, :, col:col + 2], 0.0), w=[hT])
                    else:
                        r0 = t0 - 2 if side == 0 else t0 + nt
                        nb_ = bi - 1 if side == 0 else bi + 1
                        src, rr = self.xsrc_ap(xsrc, r0, 2)
                        ps, pv = self.norm_rows(src, rr or [self.xblk[nb_]], 2, cond, None, scr, xr.next())
                        k.op("act", lambda a: a.copy(out=hT[:, :, col:col + 2], in_=pv[:, :, 0:2]), r=[ps], w=[hT])
                for c in range(24):
                    ps = ps8.next()
                    for kk in range(8):
                        k.op("pe", lambda pe: pe.matmul(ps[:, 0:nt], lhsT=wx[:, kk, c * 128:(c + 1) * 128], rhs=hT[:, kk, 0:nt],
                                                        start=(kk == 0), stop=(kk == 7)), r=[wx, hT], w=[ps])
                    ps2 = ps8.next()
                    for kk in range(8):
                        k.op("pe", lambda pe: pe.matmul(ps2[:, 0:4], lhsT=wx[:, kk, c * 128:(c + 1) * 128], rhs=hT[:, kk, 512:516],
                                                        start=(kk == 0), stop=(kk == 7)), r=[wx, hT], w=[ps2])
                    pre = prer.next()
                    k.op("act", lambda a: a.copy(out=pre[:, 2:2 + nt], in_=ps[:, 0:nt]), r=[ps], w=[pre])
                    k.op("act", lambda a: a.copy(out=pre[:, 0:2], in_=ps2[:, 0:2]), r=[ps2], w=[pre])
                    k.op("act", lambda a: a.copy(out=pre[:, 2 + nt:4 + nt], in_=ps2[:, 2:4]), r=[ps2], w=[pre])
                    acc = accr.next()
                    k.op("dve", lambda v: v.tensor_scalar(out=acc[:, 0:nt], in0=pre[:, 0:nt], scalar1=cw[:, c, 0:1], scalar2=None,
                                                          op0=ALU.mult), r=[pre, cw], w=[acc])
                    for kk in range(1, 5):
                        k.op("dve", lambda v: v.scalar_tensor_tensor(out=acc[:, 0:nt], in0=pre[:, kk:kk + nt], scalar=cw[:, c, kk:kk + 1],
                                                                     in1=acc[:, 0:nt], op0=ALU.mult, op1=ALU.add),
                             r=[pre, cw, acc], w=[acc])
                    k.op("act", lambda a: a.activation(out=xc[:, c, 0:nt], in_=acc[:, 0:nt], func=AF.Silu, bias=cbias[:, c:c + 1]),
                         r=[acc, cbias], w=[xc])
                kt0 = t0 // 128
                for j in range(ntile):
                    for s_ in range(2):
                        k.dma("act", RBC.t[kt0 + j, :, s_ * 512:(s_ + 1) * 512].rearrange("p (g t) -> p g t", g=4),
                              xc[:, 16 + 4 * s_:20 + 4 * s_, j * 128:(j + 1) * 128], r=[xc], w=[rbcT[bi]])
                for j in range(ntile):
                    t = kt0 + j
                    rx = rxr.next()
                    for c0 in (0, 8, 16):
                        cn = 8 if c0 < 16 else 4
                        ps = ps8.next()
                        pv = ps[:, :].bitcast(BF16).rearrange("p (c t) -> p c t", c=8)
                        for c in range(cn):
                            k.op("pe", lambda pe: pe.transpose(out=pv[:, c, :], in_=xc[:, c0 + c, j * 128:(j + 1) * 128],
                                                               identity=self.identb[:, :]), r=[xc, self.identb], w=[ps])
                        k.op("act" if c0 != 8 else "dve",
                             (lambda a: a.copy(out=rx[:, c0 * 128:(c0 + cn) * 128].rearrange("p (c t) -> p c t", c=cn), in_=pv[:, 0:cn, :]))
                             if c0 != 8 else
                             (lambda v: v.tensor_copy(out=rx[:, c0 * 128:(c0 + cn) * 128].rearrange("p (c t) -> p c t", c=cn), in_=pv[:, 0:cn, :])),
                             r=[ps], w=[rx])
                    k.dma("pool", RX.t[t], rx[:, :], r=[rx], w=[rxT[t]])
                    rz = rzr.next()
                    for cb in range(4):
                        ps = ps8.next()
                        for kk in range(8):
                            k.op("pe", lambda pe: pe.matmul(ps[:, :], lhsT=hT[:, kk, j * 128:(j + 1) * 128],
                                                            rhs=wz[:, kk, cb * 512:(cb + 1) * 512],
                                                            start=(kk == 0), stop=(kk == 7)), r=[hT, wz], w=[ps])
                        k.op("act", lambda a: a.activation(out=rz[:, cb * 512:(cb + 1) * 512], in_=ps[:, :], func=AF.Silu),
                             r=[ps], w=[rz])
                    k.dma("pool", RZ.t[t], rz[:, :], r=[rz], w=[rzT[t]])
                    dd = ddr.next()
                    ps = ps8.next()
                    for kk in range(8):
                        k.op("pe", lambda pe: pe.matmul(ps[:, 0:64], lhsT=hT[:, kk, j * 128:(j + 1) * 128], rhs=wdt[:, kk, :],
                                                        start=(kk == 0), stop=(kk == 7)), r=[hT, wdt], w=[ps])
                    k.op("dve", lambda v: v.tensor_tensor(out=dtt[:, :], in0=ps[:, 0:64], in1=dtb[:, :], op=ALU.add),
                         r=[ps, dtb], w=[dtt])
                    k.op("act", lambda a: a.activation(out=dtt[:, :], in_=dtt[:, :], func=AF.Exp), r=[dtt], w=[dtt])
                    k.op("act", lambda a: a.activation(out=dd[:, 0:64], in_=dtt[:, :], func=AF.Ln, bias=onec[:, :]),
                         r=[dtt, onec], w=[dd])
                    k.op("dve", lambda v: v.tensor_tensor(out=dd[:, 64:128], in0=dd[:, 0:64], in1=abc[:, :], op=ALU.mult),
                         r=[dd, abc], w=[dd])
                    k.dma("pool", DD.t[t], dd[:, :], r=[dd], w=[ddT[t]])
            k.barrier()

        with ExitStack() as es:
            self.epsb = k.sb(es, [128, 1], F32)
            k.op("pool", lambda g: g.memset(self.epsb[:, :], EPS), w=[self.epsb])
            tri = k.sb(es, [128, 4, 128], F32)
            k.dma("sp", tri[:, :, :], self.tri_d.t.rearrange("m j i -> j m i"), w=[tri])
            negm = k.sb(es, [128, 2, 128], F32)
            k.dma("sp", negm[:, :, :], self.negm_d.t.rearrange("m j i -> j m i"), w=[negm])
            onehot = k.sb(es, [96, 32, 128], BF16)
            k.dma("sp", onehot[:, :, :], self.onehot_d.t, w=[onehot])
            negmb = k.sb(es, [128, 2, 128], BF16)
            k.dma("sp", negmb[:, :, :], self.negmb_d.t.rearrange("m j i -> j m i"), w=[negmb])
            identf = k.sb(es, [128, 128], F32)
            k.dma("sp", identf[:, :], self.identf_d.t, w=[identf])
            Hs = [k.sb(es, [128, 512], F32) for _ in range(4)]
            Hb = [k.sb(es, [128, 512], BF16) for _ in range(4)]
            rxr = Ring([k.sb(es, [128, 2560], BF16) for _ in range(3)])
            rbcr = Ring([k.sb(es, [128, 1024], BF16) for _ in range(2)])
            ddr = Ring([k.sb(es, [128, 128], F32) for _ in range(3)])
            bankY = Ring(self.psb[0:2])
            bankLb = Ring(self.psb[2:5])
            bankM = Ring(self.psb[5:8])
            def ring2(shape, dt):
                return Ring([k.sb(es, shape, dt) for _ in range(2)])
            ela_r = ring2([128, 32], F32)
            edl_r = ring2([128, 32], F32)
            etot_r = ring2([128, 32], F32)
            laT_r = ring2([96, 128], BF16)
            larep_r = ring2([128, 96], F32)
            Abf_r = ring2([96, 128], BF16)
            Bbf_r = ring2([96, 128], BF16)
            R1_r = ring2([96, 128], F32)
            nla_r = ring2([128, 32], F32)
            xdt_r = ring2([128, 2048], BF16)
            xdl_r = ring2([128, 2048], BF16)
            CBm_r = ring2([128, 4, 128], BF16)
            Eh_r = Ring([k.sb(es, [128, 4, 128], BF16) for _ in range(3)])
            Mh = Ring([k.sb(es, [128, 4, 128], BF16) for _ in range(3)])
            ysr = Ring([k.sb(es, [128, 2048], F32) for _ in range(3)])
            wo = k.sb(es, [128, 16, D], BF16)
            with ExitStack() as es2:
                stg = Ring([k.sb(es2, [128, 1024], F32) for _ in range(2)])
                self.load_w(es2, wo, wo, w_out, 16, 128, D, stg)
                k.barrier()
            gnbc = k.sb(es, [128, 2048], F32)
            k.dma("sp", gnbc[:, :], self.W["ssd_norm"].t[0, :].partition_broadcast(128), w=[gnbc])
            dsk = k.sb(es, [128, 32], F32)
            k.dma("sp", dsk[:, :], self.W["ssd_d"].t[0, :].partition_broadcast(128), w=[dsk])
            rzr = Ring([k.sb(es, [128, 2048], BF16) for _ in range(1)])
            yfr = Ring([k.sb(es, [128, 2048], F32) for _ in range(1)])
            xr = Ring([k.sb(es, [128, D], F32) for _ in range(2)])
            ynb = k.sb(es, [128, 2048], BF16)
            ogT = k.sb(es, [128, 16, 128], BF16)
            junk = k.sb(es, [128, 512], BF16)
            ss4 = k.sb(es, [128, 4], F32)
            rs4 = k.sb(es, [128, 4], F32)
            tm = k.sb(es, [128, 512], F32)

            def stageA(d, rx, rbc, dd):
                c = dict(d=d, rx=rx, rbc=rbc, dd=dd)
                cm = tri[:, 0 if d == 0 else 2, :]
                sm = tri[:, 1 if d == 0 else 3, :]
                dA = dd[:, 64 + 32 * d:96 + 32 * d]
                dt = dd[:, 32 * d:32 * d + 32]
                larep = larep_r.next()
                la_sb = larep
                ela, edl, etot, laT = ela_r.next(), edl_r.next(), etot_r.next(), laT_r.next()
                Abf, Bbf, R1 = Abf_r.next(), Bbf_r.next(), R1_r.next()
                xdt, xdl, CBm = xdt_r.next(), xdl_r.next(), CBm_r.next()
                nla = nla_r.next()
                c.update(la_sb=la_sb, ela=ela, etot=etot, laT=laT, xdt=xdt, xdl=xdl, CBm=CBm, nla=nla)
                psL = bankM.next()
                k.op("pe", lambda pe: pe.matmul(psL[:, 0:32], lhsT=cm, rhs=dA, start=True, stop=True), r=[tri, dd], w=[psL])
                k.op("pe", lambda pe: pe.matmul(psL[:, 32:64], lhsT=sm, rhs=dA, start=True, stop=True), r=[tri, dd], w=[psL])
                k.op("pe", lambda pe: pe.matmul(psL[:, 64:96], lhsT=self.ones_f[:, :], rhs=dA, start=True, stop=True),
                     r=[self.ones_f, dd], w=[psL])
                for rep in range(3):
                    k.op("act", lambda a: a.copy(out=larep[:, rep * 32:(rep + 1) * 32], in_=psL[:, 0:32]), r=[psL], w=[larep])
                k.op("act", lambda a: a.mul(out=nla[:, :], in_=psL[:, 0:32], mul=-1.0), r=[psL], w=[nla])
                k.op("act", lambda a: a.activation(out=ela[:, :], in_=psL[:, 0:32], func=AF.Exp), r=[psL], w=[ela])
                k.op("act", lambda a: a.activation(out=edl[:, :], in_=psL[:, 32:64], func=AF.Exp), r=[psL], w=[edl])
                k.op("act", lambda a: a.activation(out=etot[:, :], in_=psL[:, 64:96], func=AF.Exp), r=[psL], w=[etot])
                xs3 = rx[:, 0:2048].rearrange("p (h e) -> p h e", h=32)
                k.op("pool", lambda g: g.tensor_tensor(out=xdt[:, :].rearrange("p (h e) -> p h e", h=32), in0=xs3,
                                                       in1=dt.unsqueeze(2).broadcast_to([128, 32, 64]), op=ALU.mult),
                     r=[rx, dd], w=[xdt])
                k.op("pool", lambda g: g.tensor_tensor(out=xdl[:, :].rearrange("p (h e) -> p h e", h=32),
                                                       in0=xdt[:, :].rearrange("p (h e) -> p h e", h=32),
                                                       in1=edl[:, :].unsqueeze(2).broadcast_to([128, 32, 64]), op=ALU.mult),
                     r=[xdt, edl], w=[xdl])
                psT = bankM.next()
                k.op("pe", lambda pe: pe.transpose(out=psT[0:96, 0:128], in_=larep[:, 0:96], identity=identf[:, :]),
                     r=[larep, identf], w=[psT])
                k.op("act", lambda a: a.copy(out=Abf[:, :], in_=psT[0:96, 0:128]), r=[psT], w=[Abf])
                k.op("dve", lambda v: v.tensor_tensor(out=R1[:, :], in0=psT[0:96, 0:128], in1=Abf[:, :], op=ALU.subtract),
                     r=[psT, Abf], w=[R1])
                k.op("act", lambda a: a.copy(out=Bbf[:, :], in_=R1[:, :]), r=[R1], w=[Bbf])
                k.op("dve", lambda v: v.tensor_tensor(out=R1[:, :], in0=R1[:, :], in1=Bbf[:, :], op=ALU.subtract),
                     r=[R1, Bbf], w=[R1])
                k.op("pool", lambda g: g.tensor_copy(out=laT[0:32, :], in_=Abf[0:32, :]), r=[Abf], w=[laT])
                k.op("pool", lambda g: g.tensor_copy(out=laT[32:64, :], in_=Bbf[32:64, :]), r=[Bbf], w=[laT])
                k.op("pool", lambda g: g.tensor_copy(out=laT[64:96, :], in_=R1[64:96, :]), r=[R1], w=[laT])
                psCB = bankM.next()
                for g in range(4):
                    k.op("pe", lambda pe: pe.matmul(psCB[:, g * 128:(g + 1) * 128], lhsT=rbc[:, g * 128:(g + 1) * 128],
                                                    rhs=rbc[:, 512 + g * 128:512 + (g + 1) * 128], start=True, stop=True),
                         r=[rbc], w=[psCB])
                k.op("dve", lambda v: v.tensor_tensor(out=CBm[:, :, :], in0=psCB[:, :].rearrange("p (g i) -> p g i", g=4),
                                                      in1=cm.unsqueeze(1).broadcast_to([128, 4, 128]), op=ALU.mult),
                     r=[psCB, tri], w=[CBm])
                return c

            def stageB(c, ys):
                d, rx, rbc = c["d"], c["rx"], c["rbc"]
                la_sb, ela, etot, laT, xdt, xdl, CBm = (c[n] for n in ("la_sb", "ela", "etot", "laT", "xdt", "xdl", "CBm"))
                ng = negm[:, d, :]
                nla = c["nla"]

                def lb_mm(i):
                    h0 = i * 4
                    psLb = bankLb.next()
                    for hh in range(4):
                        k.op("pe", lambda pe: pe.matmul(psLb[:, hh * 128:(hh + 1) * 128], lhsT=onehot[:, h0 + hh, :],
                                                        rhs=laT[:, :], start=True, stop=False), r=[onehot, laT], w=[psLb])
                        k.op("pe", lambda pe: pe.matmul(psLb[:, hh * 128:(hh + 1) * 128], lhsT=self.identb[:, :],
                                                        rhs=negmb[:, d, :], start=False, stop=True), r=[self.identb, negmb], w=[psLb])
                    return psLb

                pend = [lb_mm(0), lb_mm(1)]

                def heads(g):
                    psY = bankY.next()
                    for hq in range(2):
                        i = g * 2 + hq
                        h0 = i * 4
                        psLb = pend.pop(0)
                        if i + 2 < 8:
                            pend.append(lb_mm(i + 2))
                        Eh = Eh_r.next()
                        for hh in range(4):
                            k.op("act", lambda a: a.activation(out=Eh[:, hh, :], in_=psLb[:, hh * 128:(hh + 1) * 128], func=AF.Exp,
                                                               bias=nla[:, h0 + hh:h0 + hh + 1]), r=[psLb, nla], w=[Eh])
                        mh = Mh.next()
                        k.op("dve", lambda v: v.tensor_tensor(out=mh[:, :, :], in0=Eh[:, :, :],
                                                              in1=CBm[:, g:g + 1, :].broadcast_to([128, 4, 128]), op=ALU.mult),
                             r=[Eh, CBm], w=[mh])
                        for hh in range(4):
                            h = h0 + hh
                            k.op("pe", lambda pe: pe.matmul(psY[:, (h % 8) * 64:(h % 8 + 1) * 64], lhsT=mh[:, hh, :],
                                                            rhs=xdt[:, h * 64:(h + 1) * 64], start=True, stop=True),
                                 r=[mh, xdt], w=[psY])
                    return psY

                def tail(g, psY):
                    psYi = bankM.next()
                    k.op("pe", lambda pe: pe.matmul(psYi[:, :], lhsT=rbc[:, 512 + g * 128:512 + (g + 1) * 128], rhs=Hb[g][:, :],
                                                    start=True, stop=True), r=[rbc, Hb[g]], w=[psYi])
                    yv = ys[:, g * 512:(g + 1) * 512]
                    k.op("dve", lambda v: v.tensor_tensor(out=yv.rearrange("p (h e) -> p h e", h=8),
                                                          in0=psYi[:, :].rearrange("p (h e) -> p h e", h=8),
                                                          in1=ela[:, g * 8:(g + 1) * 8].unsqueeze(2).broadcast_to([128, 8, 64]),
                                                          op=ALU.mult), r=[psYi, ela], w=[ys])
                    k.op("dve", lambda v: v.tensor_tensor(out=yv, in0=yv, in1=psY[:, :], op=ALU.add), r=[ys, psY], w=[ys])
                    psH = bankM.next()
                    k.op("pe", lambda pe: pe.matmul(psH[:, :], lhsT=rx[:, 2048 + g * 128:2048 + (g + 1) * 128],
                                                    rhs=xdl[:, g * 512:(g + 1) * 512], start=True, stop=True), r=[rx, xdl], w=[psH])
                    k.op("pool", lambda v: v.tensor_tensor(out=Hs[g][:, :].rearrange("p (h e) -> p h e", h=8),
                                                           in0=Hs[g][:, :].rearrange("p (h e) -> p h e", h=8),
                                                           in1=etot[:, g * 8:(g + 1) * 8].unsqueeze(2).broadcast_to([128, 8, 64]),
                                                           op=ALU.mult), r=[Hs[g], etot], w=[Hs[g]])
                    k.op("dve", lambda v: v.tensor_tensor(out=Hs[g][:, :], in0=Hs[g][:, :], in1=psH[:, :], op=ALU.add),
                         r=[Hs[g], psH], w=[Hs[g]])
                    k.op("pool", lambda gp: gp.tensor_copy(out=Hb[g][:, :], in_=Hs[g][:, :]), r=[Hs[g]], w=[Hb[g]])

                prev = None
                for g in range(4):
                    py = heads(g)
                    if prev is not None:
                        tail(*prev)
                    prev = (g, py)
                tail(*prev)

            def reset_state():
                for g in range(4):
                    k.op("pool", lambda gp: gp.memset(Hs[g][:, :], 0.0), w=[Hs[g]])
                    k.op("pool", lambda gp: gp.memset(Hb[g][:, :], 0.0), w=[Hb[g]])

            def loads(t):
                bi = 0 if t < 2 else 1 + (t - 2) // 4
                rx = rxr.next()
                rbc = rbcr.next()
                dd = ddr.next()
                k.dma("sp", rx[:, :], RX.t[t], r=[rxT[t]], w=[rx])
                k.dma("sp", rbc[:, :], RBC.t[t], r=[rbcT[bi]], w=[rbc])
                k.dma("sp", dd[:, :], DD.t[t], r=[ddT[t]], w=[dd])
                return rx, rbc, dd

            reset_state()
            cnext = stageA(0, *loads(0))
            for t in range(NT):
                c = cnext
                if t + 1 < NT:
                    cnext = stageA(0, *loads(t + 1))
                ys = ysr.next()
                stageB(c, ys)
                k.dma("pool", YF.t[t], ys[:, :], r=[ys], w=[yfT[t]])
            reset_state()
            order = [1, 0] + list(range(NT - 1, 1, -1))
            cnext = stageA(1, *loads(order[0]))

            def out_stage(t, rx, ys):
                cond = 0 if t < 2 else 1
                bi = 0 if t < 2 else 1 + (t - 2) // 4
                rz = rzr.next()
                yf = yfr.next()
                xt = xr.next()
                k.dma("sp", rz[:, :], RZ.t[t], r=[rzT[t]], w=[rz])
                k.dma("sp", yf[:, :], YF.t[t], r=[yfT[t]], w=[yf])
                sap, rr = self.xsrc_ap(xsrc, t * 128, 128)
                k.dma("sp", xt[:, :], sap, r=(rr or [self.xblk[bi]]), w=[xt])
                k.op("dve", lambda v: v.tensor_tensor(out=ys[:, :], in0=ys[:, :], in1=yf[:, :], op=ALU.add), r=[ys, yf], w=[ys])
                ytmp = yf
                k.op("pool", lambda g: g.tensor_tensor(out=ytmp[:, :].rearrange("p (h e) -> p h e", h=32),
                                                       in0=rx[:, 0:2048].rearrange("p (h e) -> p h e", h=32),
                                                       in1=dsk[:, :].unsqueeze(2).broadcast_to([128, 32, 64]), op=ALU.mult),
                     r=[rx, dsk, yf], w=[ytmp])
                k.op("dve", lambda v: v.tensor_tensor(out=ys[:, :], in0=ys[:, :], in1=ytmp[:, :], op=ALU.add), r=[ys, ytmp], w=[ys])
                k.op("dve", lambda v: v.tensor_tensor(out=ys[:, :], in0=ys[:, :], in1=rz[:, :], op=ALU.mult), r=[ys, rz], w=[ys])
                for g in range(4):
                    k.op("act", lambda a: a.activation(out=junk[:, :], in_=ys[:, g * 512:(g + 1) * 512], func=AF.Square,
                                                       accum_out=ss4[:, g:g + 1]), r=[ys], w=[junk, ss4])
                k.op("act", lambda a: a.activation(out=rs4[:, :], in_=ss4[:, :], func=AF.Sqrt, scale=1.0 / 512, bias=self.epsb[:, :]),
                     r=[ss4, self.epsb], w=[rs4])
                k.op("dve", lambda v: v.reciprocal(out=rs4[:, :], in_=rs4[:, :]), r=[rs4], w=[rs4])
                for g in range(4):
                    k.op("dve", lambda v: v.scalar_tensor_tensor(out=ynb[:, g * 512:(g + 1) * 512], in0=ys[:, g * 512:(g + 1) * 512],
                                                                 scalar=rs4[:, g:g + 1], in1=gnbc[:, g * 512:(g + 1) * 512],
                                                                 op0=ALU.mult, op1=ALU.mult), r=[ys, rs4, gnbc], w=[ynb])
                self.tok_outproj(ynb, 16, ogT, wo, xt, tm, cond)
                k.dma("pool", self.xres.t[t * 128:(t + 1) * 128, :], xt[:, :], r=[xt], w=[self.xblk[bi]])

            pending_out = None
            for oi, t in enumerate(order):
                c = cnext
                if oi + 1 < NT:
                    cnext = stageA(1, *loads(order[oi + 1]))
                ys = ysr.next()
                stageB(c, ys)
                if pending_out is not None:
                    out_stage(*pending_out)
                pending_out = (t, c["rx"], ys)
            out_stage(*pending_out)
            k.barrier()

    def tok_outproj(self, ogb, kc, ogT, wo, xt, tm, cond):
        k = self.k
        for c0 in range(0, kc, 8):
            ps = self.ps8.next()
            pv = ps[:, :].bitcast(BF16).rearrange("p (c t) -> p c t", c=8)
            for c in range(8):
                k.op("pe", lambda pe: pe.transpose(out=pv[:, c, :], in_=ogb[:, (c0 + c) * 128:(c0 + c + 1) * 128],
                                                   identity=self.identb[:, :]), r=[ogb, self.identb], w=[ps])
            k.op("act", lambda a: a.copy(out=ogT[:, c0:c0 + 8, :], in_=pv), r=[ps], w=[ogT])
        for cb in range(2):
            ps = self.ps8.next()
            for kk in range(kc):
                k.op("pe", lambda pe: pe.matmul(ps[:, :], lhsT=ogT[:, kk, :], rhs=wo[:, kk, cb * 512:(cb + 1) * 512],
                                                start=(kk == 0), stop=(kk == kc - 1)), r=[ogT, wo], w=[ps])
            k.op("dve", lambda v: v.tensor_tensor(out=tm[:, :], in0=ps[:, :],
                                                  in1=self.bcm[cond][:, 2 * D + cb * 512:2 * D + (cb + 1) * 512], op=ALU.mult),
                 r=[ps, self.bcm[cond]], w=[tm])
            k.op("dve", lambda v: v.tensor_tensor(out=xt[:, cb * 512:(cb + 1) * 512], in0=tm[:, :],
                                                  in1=xt[:, cb * 512:(cb + 1) * 512], op=ALU.add), r=[tm, xt], w=[xt])

    def outproj(self, es, og_src, ogb, kp, kc, wo, xsrc, xr, tm):
        k = self.k
        for bi, (t0, nt, cond) in enumerate(BLOCKS):
            ob = ogb.next()
            src, trk = og_src(bi, nt)
            k.dma("sp", ob[:, :, 0:nt], src, r=[trk], w=[ob])
            for j in range(nt // 128):
                xt = xr.next()
                sap, rr = self.xsrc_ap(xsrc, t0 + j * 128, 128)
                k.dma("sp", xt[:, :], sap, r=(rr or [self.xblk[bi]]), w=[xt])
                import os
                for cb in range(2 if not os.environ.get("DBG_SKIPMM") else 0):
                    ps = self.psg.next()
                    for kk in range(kc):
                        k.op("pe", lambda pe, kk=kk, cb=cb, ps=ps: pe.matmul(
                            ps[:, :], lhsT=ob[0:kp, kk, j * 128:(j + 1) * 128], rhs=wo[0:kp, kk, cb * 512:(cb + 1) * 512],
                            start=(kk == 0), stop=(kk == kc - 1)), r=[ob, wo], w=[ps])
                    k.op("dve", lambda v, cb=cb, ps=ps: v.tensor_tensor(
                        out=tm[:, :], in0=ps[:, :], in1=self.bcm[cond][:, 2 * D + cb * 512:2 * D + (cb + 1) * 512], op=ALU.mult),
                        r=[ps, self.bcm[cond]], w=[tm])
                    k.op("dve", lambda v, cb=cb, xt=xt: v.tensor_tensor(
                        out=xt[:, cb * 512:(cb + 1) * 512], in0=tm[:, :], in1=xt[:, cb * 512:(cb + 1) * 512], op=ALU.add),
                        r=[tm, xt], w=[xt])
                k.dma("pool", self.xres.t[t0 + j * 128:t0 + (j + 1) * 128, :], xt[:, :], r=[xt], w=[self.xblk[bi]])

    def final(self, xsrc):
        k = self.k
        with ExitStack() as es:
            xr = Ring([k.sb(es, [128, D], F32) for _ in range(3)])
            junk = k.sb(es, [128, D], BF16)
            ss = k.sb(es, [128, 1], F32)
            rs = k.sb(es, [128, 1], F32)
            epsb = k.sb(es, [128, 1], F32)
            k.op("pool", lambda g: g.memset(epsb[:, :], EPS), w=[epsb])
            fg = k.sb(es, [128, D], F32)
            k.dma("sp", fg[:, :], self.final_g.t.partition_broadcast(128), w=[fg])
            for bi, (t0, nt, cond) in enumerate(BLOCKS):
                if cond == 0 and not self.debug_x:
                    continue
                for j in range(nt // 128):
                    xt = xr.next()
                    tt0 = t0 + j * 128
                    sap, rr = self.xsrc_ap(xsrc, tt0, 128)
                    k.dma("sp", xt[:, :], sap, r=(rr or [self.xblk[bi]]), w=[xt])
                    if self.debug_x:
                        k.dma("pool", self.out.t[tt0:tt0 + 128, :], xt[:, :], r=[xt], w=[self.out])
                        continue
                    k.op("act", lambda a, xt=xt: a.activation(out=junk[:, :], in_=xt[:, :], func=AF.Square, accum_out=ss[:, :]),
                         r=[xt], w=[junk, ss])
                    k.op("act", lambda a: a.activation(out=rs[:, :], in_=ss[:, :], func=AF.Sqrt, scale=1.0 / D, bias=epsb[:, :]),
                         r=[ss, epsb], w=[rs])
                    k.op("dve", lambda v: v.reciprocal(out=rs[:, :], in_=rs[:, :]), r=[rs], w=[rs])
                    k.op("dve", lambda v, xt=xt: v.scalar_tensor_tensor(out=xt[:, :], in0=xt[:, :], scalar=rs[:, 0:1],
                                                                       in1=fg[:, :], op0=ALU.mult, op1=ALU.mult),
                         r=[xt, rs, fg], w=[xt])
                    k.dma("pool", self.out.t[tt0 - CTX:tt0 - CTX + 128, :], xt[:, :], r=[xt], w=[self.out])


WSHAPES = {
    "mla_w_in": [1, 1024, 1696], "mla_q_norm": [1, 384], "mla_w_uq": [1, 384, 1536], "mla_kv_norm": [1, 256],
    "mla_w_ukv": [1, 256, 2048], "mla_w_out": [1, 1024, 1024],
    "gla_w_in": [1, 1024, 3104], "gla_w_gf": [1, 16, 512], "gla_b_gf": [1, 512], "gla_w_gb": [1, 16, 512],
    "gla_b_gb": [1, 512], "gla_o_norm": [1, 256], "gla_w_out": [1, 1024, 1024],
    "gqa_w_in": [1, 1024, 2560], "gqa_q_norm": [1, 64], "gqa_k_norm": [1, 64], "gqa_w_out": [1, 1024, 1024],
    "ssd_w_in": [1, 1024, 5184], "ssd_conv_w": [1, 5, 3072], "ssd_conv_b": [1, 3072], "ssd_dt_bias_f": [1, 32],
    "ssd_dt_bias_b": [1, 32], "ssd_a_log_f": [1, 32], "ssd_a_log_b": [1, 32], "ssd_d": [1, 32], "ssd_norm": [1, 2048],
    "ssd_w_out": [1, 2048, 1024],
}


def tri_consts():
    j = np.arange(128)[:, None]
    i = np.arange(128)[None, :]
    return np.stack([(j <= i), (j > i), (j >= i), (j < i)]).astype(np.float32)


def rope_tables(rd):
    hf = rd // 4
    inv = 10000.0 ** (-np.arange(hf, dtype=np.float64) / hf)
    p = np.arange(SEQ)
    row = (p // 64).astype(np.float64)[:, None] * inv[None, :]
    col = (p % 64).astype(np.float64)[:, None] * inv[None, :]
    cos = np.concatenate([np.cos(row), np.cos(row), np.cos(col), np.cos(col)], axis=1)
    sin = np.concatenate([-np.sin(row), np.sin(row), -np.sin(col), np.sin(col)], axis=1)
    tab = np.zeros((T, 2, rd), np.float32)
    tab[:CTX, 0, :] = 1.0
    tab[CTX:, 0, :] = cos
    tab[CTX:, 1, :] = sin
    return tab


def run(inputs, layers=(0, 1, 2, 3), debug_x=False, cores=(0, 1), stop=None):
    nc = bass.Bass("TRN2", target_bir_lowering=False)
    Prog(nc, layers=layers, debug_x=debug_x, stop=stop).build()
    f = lambda a: np.ascontiguousarray(np.asarray(a, dtype=np.float32))
    common = {nm: f(inputs[nm]) for nm in WSHAPES}
    for nm in ("ada_w", "ada_b", "norm_g", "final_g"):
        common[nm] = f(inputs[nm])
    common["ident_bf"] = np.eye(128, dtype=np.float32).astype(ml_dtypes.bfloat16)
    common["rope_mla"] = rope_tables(32)
    common["rope_gqa"] = rope_tables(64)
    common["tri"] = tri_consts()
    common["negm"] = ((1.0 - tri_consts()[[0, 2]]) * -1e30).astype(np.float32)
    oh = np.zeros((3, 32, 32, 128), np.float32)
    oh[:, np.arange(32), np.arange(32), :] = 1.0
    common["onehot3"] = oh.reshape(96, 32, 128).astype(ml_dtypes.bfloat16)
    common["negmb"] = common["negm"].astype(ml_dtypes.bfloat16)
    common["ident_f"] = np.eye(128, dtype=np.float32)
    in_maps = []
    for b in cores:
        m = dict(common)
        m["xin"] = np.ascontiguousarray(np.concatenate([f(inputs["ctx"])[b], f(inputs["x"])[b]], axis=0))
        m["c2"] = np.ascontiguousarray(np.stack([f(inputs["c_ctx"]), f(inputs["c"])[b]], axis=0))
        in_maps.append(m)
    res = run_bass_kernel_spmd(nc, in_maps, core_ids=list(range(len(cores))))
    return [r["y"] for r in res.results]


FUSED = True


def kernel(**inputs):
    if FUSED:
        outs = run(inputs)
        return np.stack(outs, axis=0).astype(np.float32)
    cur = dict(inputs)
    for L in (0, 1, 2):
        outs = run(cur, layers=(L,), debug_x=True)
        st = np.stack(outs, axis=0)
        cur["ctx"] = np.ascontiguousarray(st[:, :CTX])
        cur["x"] = np.ascontiguousarray(st[:, CTX:])
    outs = run(cur, layers=(3,), debug_x=False)
    return np.stack(outs, axis=0).astype(np.float32)
```

```python
import math
from contextlib import ExitStack

import numpy as np
import ml_dtypes
import concourse.bass as bass
import concourse.mybir as mybir
from concourse.bass_utils import run_bass_kernel_spmd

F32 = mybir.dt.float32
BF16 = mybir.dt.bfloat16
AF = mybir.ActivationFunctionType
ALU = mybir.AluOpType
AX = mybir.AxisListType

D = 1024
SEQ = 8192
CTX = 256
T = SEQ + CTX
NT = T // 128
EPS = 1e-6
EPOCH = 30000

BLOCKS = [(0, 256, 0)] + [(256 + 512 * i, 512, 1) for i in range(16)]
NB = len(BLOCKS)


class Buf:
    __slots__ = ("w", "r")

    def __init__(self):
        self.w = None
        self.r = {}


class TT:
    def __init__(self, t):
        self.t = t
        self.b = Buf()

    def __getitem__(self, idx):
        return self.t[idx]


class Ring:
    def __init__(self, items):
        self.items = items
        self.i = 0

    def next(self):
        it = self.items[self.i % len(self.items)]
        self.i += 1
        return it


class KB:
    def __init__(self, nc, es):
        self.nc = nc
        self.es = es
        self.eng = {"pe": nc.tensor, "act": nc.scalar, "dve": nc.vector, "pool": nc.gpsimd, "sp": nc.sync}
        self.sems = {e: [] for e in self.eng}
        self.cnt = {e: 0 for e in self.eng}
        self.seen = {e: {} for e in self.eng}
        self.last = {e: None for e in self.eng}
        self.slots = {}
        self.slot_i = {}
        for q in ("sp", "pool", "act"):
            self.slots[q] = [[es.enter_context(nc.semaphore(f"d_{q}_{i}")), 0, f"d_{q}_{i}"] for i in range(12)]
            self.slot_i[q] = 0
        self.nsb = 0

    def sb(self, es, shape, dt, name=None):
        self.nsb += 1
        return TT(es.enter_context(self.nc.sbuf_tensor(name or f"sb{self.nsb}", list(shape), dt)))

    def dram(self, shape, dt, name):
        h = self.nc.dram_tensor(name, list(shape), dt, kind="Internal")
        return TT(h.ap())

    def _wait(self, e, deps):
        seen = self.seen[e]
        for ev in deps:
            key, sem, val, src = ev
            if src == "pe" and e == "pe":
                continue
            if seen.get(key, 0) >= val:
                continue
            self.eng[e].wait_ge(sem, val)
            seen[key] = val

    def _deps(self, reads, writes):
        deps = []
        for t in reads:
            if t.b.w is not None:
                deps.append(t.b.w)
        for t in writes:
            if t.b.w is not None:
                deps.append(t.b.w)
            deps.extend(t.b.r.values())
        return deps

    def _mark(self, ev, reads, writes):
        for t in reads:
            t.b.r[ev[0]] = ev
        for t in writes:
            t.b.w = ev
            t.b.r = {}

    def op(self, e, fn, r=(), w=()):
        self._wait(e, self._deps(r, w))
        ins = fn(self.eng[e])
        epoch = self.cnt[e] // EPOCH
        while len(self.sems[e]) <= epoch:
            self.sems[e].append(self.es.enter_context(self.nc.semaphore(f"s_{e}_{len(self.sems[e])}")))
        sem = self.sems[e][epoch]
        val = self.cnt[e] % EPOCH + 1
        ins.then_inc(sem, 1)
        self.cnt[e] += 1
        ev = ((e, epoch), sem, val, e)
        self.last[e] = ev
        self._mark(ev, r, w)
        return ev

    def dma(self, q, out, in_, r=(), w=(), **kw):
        deps = self._deps(r, w)
        slots = self.slots[q]
        si = self.slot_i[q] % len(slots)
        self.slot_i[q] += 1
        slot = slots[si]
        if slot[1] > 0:
            deps.append((slot[2], slot[0], 16 * slot[1], "dma"))
        self._wait(q, deps)
        ins = self.eng[q].dma_start(out=out, in_=in_, **kw)
        ins.then_inc(slot[0], 16)
        slot[1] += 1
        ev = (slot[2], slot[0], 16 * slot[1], "dma")
        self._mark(ev, r, w)
        return ev

    def barrier(self):
        evs = [self.last[e] for e in self.eng if self.last[e] is not None]
        for q in self.slots:
            for slot in self.slots[q]:
                if slot[1] > 0:
                    evs.append((slot[2], slot[0], 16 * slot[1], "dma"))
        for e in self.eng:
            seen = self.seen[e]
            for ev in evs:
                key, sem, val, src = ev
                if src == e:
                    continue
                if seen.get(key, 0) >= val:
                    continue
                self.eng[e].wait_ge(sem, val)
                seen[key] = val


class Prog:
    def __init__(self, nc, layers=(0, 1, 2, 3), debug_x=False, stop=None):
        self.nc = nc
        self.stop = stop
        self.layers = layers
        self.debug_x = debug_x

    def din(self, name, shape, dt=F32):
        return TT(self.nc.dram_tensor(name, list(shape), dt, kind="ExternalInput").ap())

    def build(self):
        nc = self.nc
        with ExitStack() as es:
            self.k = k = KB(nc, es)
            self.es = es
            self.xin = self.din("xin", [T, D])
            self.c2 = self.din("c2", [2, D])
            self.ada_w = self.din("ada_w", [4, D, 3 * D])
            self.ada_b = self.din("ada_b", [4, 3 * D])
            self.norm_g = self.din("norm_g", [4, D])
            self.final_g = self.din("final_g", [D])
            self.W = {}
            for nm, shp in WSHAPES.items():
                self.W[nm] = self.din(nm, shp)
            self.identb_d = self.din("ident_bf", [128, 128], BF16)
            self.rope_mla = self.din("rope_mla", [T, 2, 32])
            self.rope_gqa = self.din("rope_gqa", [T, 2, 64])
            self.tri_d = self.din("tri", [4, 128, 128])
            self.negm_d = self.din("negm", [2, 128, 128])
            self.onehot_d = self.din("onehot3", [96, 32, 128], BF16)
            self.negmb_d = self.din("negmb", [2, 128, 128], BF16)
            self.identf_d = self.din("ident_f", [128, 128])
            if self.debug_x:
                self.out = TT(nc.dram_tensor("y", [T, D], F32, kind="ExternalOutput").ap())
            else:
                self.out = TT(nc.dram_tensor("y", [SEQ, D], F32, kind="ExternalOutput").ap())
            self.xres = k.dram([T, D], F32, "xres")
            self.xblk = [TT(self.xres.t) for _ in range(NB)]
            self.modd = k.dram([4, 2, 3 * D], F32, "modd")
            self.identb = k.sb(es, [128, 128], BF16, "identb")
            k.dma("sp", self.identb[:, :], self.identb_d.t[:, :], w=[self.identb])
            self.ones_f = k.sb(es, [128, 128], F32, "ones_f")
            k.op("pool", lambda g: g.memset(self.ones_f[:, :], 1.0), w=[self.ones_f])
            self.psb = [TT(es.enter_context(nc.psum_tensor(f"ps{i}", [128, 512], F32))) for i in range(8)]
            self.psg = Ring(self.psb[0:6])
            self.pso = Ring(self.psb[6:8])
            self.ps8 = Ring(self.psb)
            self.bcm = [k.sb(es, [128, 3 * D], F32, f"bcm{c}") for c in range(2)]
            self.gmod = [k.sb(es, [128, D], F32, f"gmod{c}") for c in range(2)]
            self.ngbc = k.sb(es, [128, D], F32, "ngbc")

            first = True
            for L in self.layers:
                self.modulation(L)
                if self.stop == "mod":
                    break
                xsrc = self.xin if first else None
                if L == 0:
                    self.layer_attn(L, "mla", xsrc)
                elif L == 2:
                    self.layer_attn(L, "gqa", xsrc)
                elif L == 1:
                    self.layer_gla(L, xsrc)
                elif L == 3:
                    self.layer_ssd(L, xsrc)
                first = False
                k.barrier()
            self.final(self.xin if first else None)
            k.barrier()
        return nc

    def xsrc_ap(self, xsrc, t0, n):
        if xsrc is not None:
            return xsrc.t[t0:t0 + n, :], [xsrc]
        return self.xres.t[t0:t0 + n, :], None

    def modulation(self, L):
        k = self.k
        with ExitStack() as es:
            cT = k.sb(es, [128, 8, 2], F32)
            sT = k.sb(es, [128, 8, 2], F32)
            for kk in range(8):
                k.dma("sp", cT[:, kk, :], self.c2.t[:, kk * 128:(kk + 1) * 128].rearrange("c p -> p c"), w=[cT],
                      allow_slow_non_contiguous=True)
            k.op("act", lambda a: a.activation(out=sT[:, :, :], in_=cT[:, :, :], func=AF.Silu), r=[cT], w=[sT])
            msb = k.sb(es, [2, 3 * D], F32)
            bb = k.sb(es, [2, 3 * D], F32)
            k.dma("sp", bb[:, :], self.ada_b.t[L, :].partition_broadcast(2), w=[bb])
            wr = Ring([k.sb(es, [128, 8, 512], F32) for _ in range(2)])
            for cb in range(6):
                wt = wr.next()
                k.dma("sp", wt[:, :, :],
                      self.ada_w.t[L, :, cb * 512:(cb + 1) * 512].rearrange("(k p) n -> p k n", p=128), w=[wt])
                ps = self.psg.next()
                for kk in range(8):
                    k.op("pe", lambda pe, kk=kk: pe.matmul(ps[0:2, :], lhsT=sT[:, kk, :], rhs=wt[:, kk, :],
                                                          start=(kk == 0), stop=(kk == 7)), r=[sT, wt], w=[ps])
                k.op("dve", lambda v: v.tensor_tensor(out=msb[:, cb * 512:(cb + 1) * 512], in0=ps[0:2, :],
                                                      in1=bb[:, cb * 512:(cb + 1) * 512], op=ALU.add),
                     r=[ps, bb], w=[msb])
            md = TT(self.modd.t)
            k.dma("sp", self.modd.t[L, :, :], msb[:, :], r=[msb], w=[md])
            for c in range(2):
                k.dma("sp", self.bcm[c][:, :], self.modd.t[L, c, :].partition_broadcast(128), r=[md], w=[self.bcm[c]])
            k.dma("sp", self.ngbc[:, :], self.norm_g.t[L, :].partition_broadcast(128), w=[self.ngbc])
            for c in range(2):
                k.op("dve", lambda v, c=c: v.scalar_tensor_tensor(out=self.gmod[c][:, :], in0=self.bcm[c][:, D:2 * D],
                                                                 scalar=1.0, in1=self.ngbc[:, :], op0=ALU.add,
                                                                 op1=ALU.mult),
                     r=[self.bcm[c], self.ngbc], w=[self.gmod[c]])
            k.barrier()

    def load_w(self, es_stage, dst, dview, src_ap, kc, kp, n, stg):
        k = self.k
        CH = 1024
        for kk in range(kc):
            for c0 in range(0, n, CH):
                cn = min(CH, n - c0)
                st = stg.next()
                k.dma("sp", st[0:kp, 0:cn], src_ap[kk * kp:(kk + 1) * kp, c0:c0 + cn], w=[st])
                k.op("pool", lambda g, st=st, kk=kk, c0=c0, cn=cn: g.tensor_copy(out=dview[0:kp, kk, c0:c0 + cn],
                                                                                in_=st[0:kp, 0:cn]),
                     r=[st], w=[dst])

    def norm_block(self, es, bi, xsrc, xr, hT, scr):
        k = self.k
        t0, nt, cond = BLOCKS[bi]
        junk, ss, rs, hf, hb = scr
        for j in range(nt // 128):
            xt = xr.next()
            src, rr = self.xsrc_ap(xsrc, t0 + j * 128, 128)
            k.dma("sp", xt[:, :], src, r=(rr or [self.xblk[bi]]), w=[xt])
            k.op("act", lambda a, xt=xt: a.activation(out=junk[:, :], in_=xt[:, :], func=AF.Square, accum_out=ss[:, :]),
                 r=[xt], w=[junk, ss])
            k.op("act", lambda a: a.activation(out=rs[:, :], in_=ss[:, :], func=AF.Sqrt, scale=1.0 / D, bias=self.epsb[:, :]),
                 r=[ss, self.epsb], w=[rs])
            k.op("dve", lambda v: v.reciprocal(out=rs[:, :], in_=rs[:, :]), r=[rs], w=[rs])
            k.op("dve", lambda v, xt=xt: v.scalar_tensor_tensor(out=hf[:, :], in0=xt[:, :], scalar=rs[:, 0:1],
                                                               in1=self.gmod[cond][:, :], op0=ALU.mult, op1=ALU.mult),
                 r=[xt, rs, self.gmod[cond]], w=[hf])
            k.op("dve", lambda v: v.tensor_tensor(out=hb[:, :], in0=hf[:, :], in1=self.bcm[cond][:, 0:D], op=ALU.add),
                 r=[hf, self.bcm[cond]], w=[hb])
            ps = self.psg.next()
            pv = ps[:, :].bitcast(BF16).rearrange("p (c t) -> p c t", c=8)
            for c in range(8):
                k.op("pe", lambda pe, c=c: pe.transpose(out=pv[:, c, :], in_=hb[:, c * 128:(c + 1) * 128],
                                                        identity=self.identb[:, :]),
                     r=[hb, self.identb], w=[ps])
            k.op("act", lambda a, j=j: a.copy(out=hT[:, :, j * 128:(j + 1) * 128], in_=pv), r=[ps], w=[hT])

    def layer_attn(self, L, kind, xsrc):
        k = self.k
        nc = self.nc
        if kind == "mla":
            H, HK, DQ = 16, 16, 96
            w_in = self.W["mla_w_in"].t[0]
            w_out = self.W["mla_w_out"].t[0]
            GOFF = 672
            scale = 96 ** -0.5
        else:
            H, HK, DQ = 16, 4, 64
            w_in = self.W["gqa_w_in"].t[0]
            w_out = self.W["gqa_w_out"].t[0]
            GOFF = 1536
            scale = 64 ** -0.5
        REP = H // HK
        QT = k.dram([H, DQ, T], BF16, f"QT{L}")
        KT = k.dram([HK, DQ, T], BF16, f"KT{L}")
        VV = k.dram([HK, 128, NT, 65], BF16, f"VV{L}")
        GS = k.dram([8, 128, T], BF16, f"GS{L}")
        OG = k.dram([NB, 64, 16, 512], BF16, f"OG{L}")

        with ExitStack() as es:
            self.epsb = k.sb(es, [128, 1], F32)
            k.op("pool", lambda g: g.memset(self.epsb[:, :], EPS), w=[self.epsb])
            stg = Ring([k.sb(es, [128, 1024], F32) for _ in range(2)])
            NIN = 1696 if kind == "mla" else 2560
            win = k.sb(es, [128, 8, NIN], BF16)
            self.load_w(es, win, win, w_in, 8, 128, NIN, stg)
            if kind == "mla":
                wuq = k.sb(es, [128, 3, 1536], BF16)
                self.load_w(es, wuq, wuq, self.W["mla_w_uq"].t[0], 3, 128, 1536, stg)
                wukv = k.sb(es, [128, 2, 2048], BF16)
                self.load_w(es, wukv, wukv, self.W["mla_w_ukv"].t[0], 2, 128, 2048, stg)
                qnbc = k.sb(es, [128, 384], F32)
                k.dma("sp", qnbc[:, :], self.W["mla_q_norm"].t[0, :].partition_broadcast(128), w=[qnbc])
                kvnbc = k.sb(es, [128, 256], F32)
                k.dma("sp", kvnbc[:, :], self.W["mla_kv_norm"].t[0, :].partition_broadcast(128), w=[kvnbc])
                RD, HF = 32, 8
                rope_d = self.rope_mla
            else:
                qnbc = k.sb(es, [128, 64], F32)
                k.dma("sp", qnbc[:, :], self.W["gqa_q_norm"].t[0, :].partition_broadcast(128), w=[qnbc])
                knbc = k.sb(es, [128, 64], F32)
                k.dma("sp", knbc[:, :], self.W["gqa_k_norm"].t[0, :].partition_broadcast(128), w=[knbc])
                RD, HF = 64, 16
                rope_d = self.rope_gqa
            xr = Ring([k.sb(es, [128, D], F32) for _ in range(2)])
            hTr = Ring([k.sb(es, [128, 8, 512], BF16) for _ in range(2)])
            scr = (k.sb(es, [128, D], BF16), k.sb(es, [128, 1], F32), k.sb(es, [128, 1], F32),
                   k.sb(es, [128, D], F32), k.sb(es, [128, D], BF16))
            qsb = k.sb(es, [128, H * DQ], F32)
            qb = k.sb(es, [128, H, DQ], BF16)
            kb = k.sb(es, [128, HK, DQ], BF16)
            ksb = k.sb(es, [128, HK * DQ if kind == "gqa" else 32], F32)
            vblk = Ring([k.sb(es, [128, 4, HK, 65], BF16) for _ in range(1)])
            for vb_ in vblk.items:
                k.op("pool", lambda g, vb_=vb_: g.memset(vb_[:, :, :, :], 1.0), w=[vb_])
            qTb = Ring([k.sb(es, [DQ, H, 512], BF16) for _ in range(1)])
            kTb = Ring([k.sb(es, [DQ, HK, 512], BF16) for _ in range(1)])
            rtab = Ring([k.sb(es, [128, 2, RD], F32) for _ in range(2)])
            ra = k.sb(es, [128, H, RD], F32)
            rb_ = k.sb(es, [128, H, RD], F32)
            ss2 = k.sb(es, [128, 32], F32)
            rs2 = k.sb(es, [128, 32], F32)
            sq = k.sb(es, [128, H * DQ], F32)
            if kind == "mla":
                cqn = k.sb(es, [128, 640], BF16)
                cT = k.sb(es, [128, 5, 128], BF16)
            gsr = Ring([k.sb(es, [128, 512], BF16) for _ in range(2)])

            def rope(xv, nh, dst, tab):
                cosb = tab[:, 0:1, :].broadcast_to([128, nh, RD])
                k.op("dve", lambda v: v.tensor_tensor(out=ra[:, 0:nh, :], in0=xv, in1=cosb, op=ALU.mult),
                     r=[tab, qsb, ksb], w=[ra])
                x5 = xv.rearrange("p h (g s f) -> p h g s f", g=2, s=2)
                b5 = rb_[:, 0:nh, :].rearrange("p h (g s f) -> p h g s f", g=2, s=2)
                s5 = tab[:, 1, :].rearrange("p (g s f) -> p g s f", g=2, s=2)
                for g in range(2):
                    for s in range(2):
                        sinb = s5[:, g:g + 1, s, :].broadcast_to([128, nh, HF])
                        k.op("dve", lambda v, g=g, s=s, sinb=sinb: v.tensor_tensor(
                            out=b5[:, :, g, s, :], in0=x5[:, :, g, 1 - s, :], in1=sinb, op=ALU.mult),
                            r=[tab, qsb, ksb], w=[rb_])
                k.op("dve", lambda v: v.tensor_tensor(out=dst, in0=ra[:, 0:nh, :], in1=rb_[:, 0:nh, :], op=ALU.add),
                     r=[ra, rb_], w=[qb, kb])

            import os
            for bi, (t0, nt, cond) in enumerate(BLOCKS[:int(os.environ.get('DBG_P1_BLOCKS', NB))]):
                hT = hTr.next()
                self.norm_block(es, bi, xsrc, xr, hT, scr)
                ntile = nt // 128
                vb4 = vblk.next()
                qT = qTb.next()
                kT = kTb.next()
                for j in range(ntile):
                    tt0 = t0 + j * 128
                    kt = tt0 // 128
                    tab = rtab.next()
                    k.dma("sp", tab[:, :, :], rope_d.t[tt0:tt0 + 128, :, :], w=[tab])
                    hTj = lambda kk: hT[:, kk, j * 128:(j + 1) * 128]
                    if kind == "mla":
                        psA = self.psg.next()
                        psB = self.psg.next()
                        for kk in range(8):
                            k.op("pe", lambda pe, kk=kk: pe.matmul(psA[:, 0:384], lhsT=hTj(kk), rhs=win[:, kk, 0:384],
                                                                  start=(kk == 0), stop=(kk == 7)), r=[hT, win], w=[psA])
                        for kk in range(8):
                            k.op("pe", lambda pe, kk=kk: pe.matmul(psB[:, 0:288], lhsT=hTj(kk), rhs=win[:, kk, 384:672],
                                                                  start=(kk == 0), stop=(kk == 7)), r=[hT, win], w=[psB])
                        for (ps_, n_, gb_, o_) in ((psA, 384, qnbc, 0), (psB, 256, kvnbc, 384)):
                            k.op("act", lambda a, ps_=ps_, n_=n_: a.activation(out=sq[:, 0:n_], in_=ps_[:, 0:n_], func=AF.Square,
                                                                              accum_out=ss2[:, 0:1]), r=[ps_], w=[sq, ss2])
                            k.op("act", lambda a, n_=n_: a.activation(out=rs2[:, 0:1], in_=ss2[:, 0:1], func=AF.Sqrt,
                                                                     scale=1.0 / n_, bias=self.epsb[:, :]),
                                 r=[ss2, self.epsb], w=[rs2])
                            k.op("dve", lambda v: v.reciprocal(out=rs2[:, 0:1], in_=rs2[:, 0:1]), r=[rs2], w=[rs2])
                            k.op("dve", lambda v, ps_=ps_, n_=n_, gb_=gb_, o_=o_: v.scalar_tensor_tensor(
                                out=cqn[:, o_:o_ + n_], in0=ps_[:, 0:n_], scalar=rs2[:, 0:1], in1=gb_[:, :],
                                op0=ALU.mult, op1=ALU.mult), r=[ps_, rs2, gb_], w=[cqn])
                        k.op("act", lambda a: a.copy(out=ksb[:, 0:32], in_=psB[:, 256:288]), r=[psB], w=[ksb])
                        pst = self.psg.next()
                        ptv = pst[:, :].bitcast(BF16).rearrange("p (c t) -> p c t", c=8)
                        for c in range(5):
                            k.op("pe", lambda pe, c=c: pe.transpose(out=ptv[:, c, :], in_=cqn[:, c * 128:(c + 1) * 128],
                                                                    identity=self.identb[:, :]), r=[cqn, self.identb], w=[pst])
                        k.op("act", lambda a: a.copy(out=cT[:, :, :], in_=ptv[:, 0:5, :]), r=[pst], w=[cT])
                        for cb in range(3):
                            ps = self.psg.next()
                            for kk in range(3):
                                k.op("pe", lambda pe, kk=kk, cb=cb, ps=ps: pe.matmul(
                                    ps[:, :], lhsT=cT[:, kk, :], rhs=wuq[:, kk, cb * 512:(cb + 1) * 512],
                                    start=(kk == 0), stop=(kk == 2)), r=[cT, wuq], w=[ps])
                            k.op("act", lambda a, cb=cb, ps=ps: a.copy(out=qsb[:, cb * 512:(cb + 1) * 512], in_=ps[:, :]),
                                 r=[ps], w=[qsb])
                        q3 = qsb[:, :].rearrange("p (h d) -> p h d", h=16)
                        k.op("pool", lambda g: g.tensor_copy(out=qb[:, :, 0:64], in_=q3[:, :, 0:64]), r=[qsb], w=[qb])
                        rope(q3[:, :, 64:96], 16, qb[:, :, 64:96], tab)
                        krv = ksb[:, 0:32].rearrange("p (h d) -> p h d", h=1)
                        rope(krv, 1, kb[:, 0:1, 64:96], tab)
                        k.op("pool", lambda g: g.tensor_copy(out=kb[:, 1:16, 64:96],
                                                             in_=kb[:, 0:1, 64:96].broadcast_to([128, 15, 32])),
                             r=[kb], w=[kb])
                        for cb in range(4):
                            ps = self.psg.next()
                            for kk in range(2):
                                k.op("pe", lambda pe, kk=kk, cb=cb, ps=ps: pe.matmul(
                                    ps[:, :], lhsT=cT[:, 3 + kk, :], rhs=wukv[:, kk, cb * 512:(cb + 1) * 512],
                                    start=(kk == 0), stop=(kk == 1)), r=[cT, wukv], w=[ps])
                            p3 = ps[:, :].rearrange("p (h d) -> p h d", h=4)
                            k.op("act", lambda a, cb=cb, p3=p3: a.copy(out=kb[:, cb * 4:(cb + 1) * 4, 0:64], in_=p3[:, :, 0:64]),
                                 r=[ps], w=[kb])
                            k.op("dve", lambda v, cb=cb, p3=p3: v.tensor_copy(out=vb4[:, j, cb * 4:(cb + 1) * 4, 0:64],
                                                                              in_=p3[:, :, 64:128]), r=[ps], w=[vb4])
                    else:
                        import os
                        for cb in range(3 if int(os.environ.get('DBG_STEP', 9)) >= 1 else 0):
                            ps = self.psg.next()
                            for kk in range(8):
                                k.op("pe", lambda pe, kk=kk, cb=cb, ps=ps: pe.matmul(
                                    ps[:, :], lhsT=hTj(kk), rhs=win[:, kk, cb * 512:(cb + 1) * 512],
                                    start=(kk == 0), stop=(kk == 7)), r=[hT, win], w=[ps])
                            SUB = os.environ.get('DBG_SUB', 'abc')
                            if cb < 2:
                                if 'a' in SUB:
                                    k.op("act", lambda a, cb=cb, ps=ps: a.copy(out=qsb[:, cb * 512:(cb + 1) * 512], in_=ps[:, :]),
                                         r=[ps], w=[qsb])
                            elif 'b' in SUB:
                                k.op("act", lambda a, ps=ps: a.copy(out=ksb[:, 0:256], in_=ps[:, 0:256]), r=[ps], w=[ksb])
                                p3 = ps[:, 256:512].rearrange("p (h d) -> p h d", h=4)
                                if 'c' in SUB:
                                    for hh in range(4):
                                        k.op("act", lambda a, hh=hh: a.copy(out=vb4[:, j, hh, 0:64], in_=ps[:, 256 + hh * 64:256 + (hh + 1) * 64]),
                                             r=[ps], w=[vb4])
                        import os
                        DS = int(os.environ.get('DBG_STEP', 9))
                        for (src_, nh, gb_, dstb) in ((qsb, 16, qnbc, qb), (ksb, 4, knbc, kb)) if DS >= 2 else ():
                            s3 = src_[:, 0:nh * 64].rearrange("p (h d) -> p h d", h=nh)
                            sq3 = sq[:, 0:nh * 64].rearrange("p (h d) -> p h d", h=nh)
                            k.op("dve", lambda v, s3=s3, sq3=sq3: v.tensor_tensor(out=sq3, in0=s3, in1=s3, op=ALU.mult),
                                 r=[src_], w=[sq])
                            k.op("dve", lambda v, sq3=sq3, nh=nh: v.tensor_reduce(out=ss2[:, 0:nh], in_=sq3, axis=AX.X, op=ALU.add),
                                 r=[sq], w=[ss2])
                            k.op("act", lambda a, nh=nh: a.activation(out=rs2[:, 0:nh], in_=ss2[:, 0:nh], func=AF.Sqrt,
                                                                     scale=1.0 / 64, bias=self.epsb[:, :]),
                                 r=[ss2, self.epsb], w=[rs2])
                            k.op("dve", lambda v, nh=nh: v.reciprocal(out=rs2[:, 0:nh], in_=rs2[:, 0:nh]), r=[rs2], w=[rs2])
                            k.op("dve", lambda v, s3=s3, nh=nh: v.tensor_tensor(
                                out=s3, in0=s3, in1=rs2[:, 0:nh].unsqueeze(2).broadcast_to([128, nh, 64]), op=ALU.mult),
                                r=[src_, rs2], w=[src_])
                            k.op("dve", lambda v, s3=s3, nh=nh, gb_=gb_: v.tensor_tensor(
                                out=s3, in0=s3, in1=gb_[:, :].unsqueeze(1).broadcast_to([128, nh, 64]), op=ALU.mult),
                                r=[src_, gb_], w=[src_])
                            if DS >= 3:
                                rope(s3, nh, dstb[:, :, :], tab)
                    import os
                    for (srcb, nh, dstT) in ((qb, H, qT), (kb, HK, kT)) if int(os.environ.get('DBG_STEP', 9)) >= 4 else ():
                        for h0 in range(0, nh, 8):
                            hn = min(8, nh - h0)
                            ps = self.psg.next()
                            ptv = ps[:, :].bitcast(BF16).rearrange("p (c t) -> p c t", c=8)
                            for hh in range(hn):
                                k.op("pe", lambda pe, hh=hh, h0=h0, ptv=ptv, srcb=srcb: pe.transpose(
                                    out=ptv[0:DQ, hh, :], in_=srcb[:, h0 + hh, :], identity=self.identb[:, :]),
                                    r=[srcb, self.identb], w=[ps])
                            k.op("act", lambda a, h0=h0, hn=hn, ptv=ptv, dstT=dstT: a.copy(
                                out=dstT[:, h0:h0 + hn, j * 128:(j + 1) * 128], in_=ptv[0:DQ, 0:hn, :]), r=[ps], w=[dstT])
                for hp in range(8):
                    ps = self.psg.next()
                    for kk in range(8):
                        k.op("pe", lambda pe, kk=kk, hp=hp, ps=ps: pe.matmul(
                            ps[:, 0:nt], lhsT=win[:, kk, GOFF + hp * 128:GOFF + (hp + 1) * 128], rhs=hT[:, kk, 0:nt],
                            start=(kk == 0), stop=(kk == 7)), r=[hT, win], w=[ps])
                    gs = gsr.next()
                    k.op("act", lambda a, ps=ps, gs=gs: a.activation(out=gs[:, 0:nt], in_=ps[:, 0:nt], func=AF.Silu),
                         r=[ps], w=[gs])
                    k.dma("pool", GS.t[hp, :, t0:t0 + nt], gs[:, 0:nt], r=[gs], w=[GS])
                for h0 in range(0, H, 4):
                    k.dma("act", QT.t[h0:h0 + 4, :, t0:t0 + nt].rearrange("h d t -> d h t"), qT[:, h0:h0 + 4, 0:nt], r=[qT], w=[QT])
                for h0 in range(0, HK, 4):
                    k.dma("act", KT.t[h0:h0 + 4, :, t0:t0 + nt].rearrange("h d t -> d h t"), kT[:, h0:h0 + 4, 0:nt], r=[kT], w=[KT])
                kt0 = t0 // 128
                for j in range(ntile):
                    for h0 in range(0, HK, 4):
                        k.dma("act", VV.t[h0:h0 + 4, :, kt0 + j, :].rearrange("h p e -> p h e"), vb4[:, j, h0:h0 + 4, :],
                              r=[vb4], w=[VV], allow_slow_non_contiguous=True)
            k.barrier()

        if self.stop == "p1":
            return
        with ExitStack() as es:
            DQP = 128 if DQ == 64 else DQ
            KTs = Ring([k.sb(es, [DQP, T], BF16) for _ in range(2)])
            Vs = Ring([k.sb(es, [128, NT, 65], BF16) for _ in range(2)])
            Qs = Ring([k.sb(es, [DQP, T], BF16) for _ in range(2)])
            if DQP != DQ:
                for b_ in KTs.items + Qs.items:
                    k.op("pool", lambda g, b_=b_: g.memset(b_[DQ:DQP, :], 0.0), w=[b_])
            Gs = Ring([k.sb(es, [64, T], BF16) for _ in range(2)])
            Ps = Ring([k.sb(es, [128, 512], BF16) for _ in range(6)])
            rr = k.sb(es, [65, 512], F32)
            tmp = k.sb(es, [64, 512], F32)
            ogr = Ring([k.sb(es, [64, 512], BF16) for _ in range(2)])
            pss = Ring(self.psb[0:5])
            psm = Ring(self.psb[5:6])
            import os
            pending_epi = [None]
            for hk in range(int(os.environ.get('DBG_P2_HEADS', HK))):
                Kt = KTs.next()
                Vt = Vs.next()
                k.dma("sp", Kt[0:DQ, :], KT.t[hk], r=[KT], w=[Kt])
                k.dma("sp", Vt[:, :, :], VV.t[hk], r=[VV], w=[Vt])
                for hr in range(REP):
                    h = hk * REP + hr
                    Qt = Qs.next()
                    Gt = Gs.next()
                    k.dma("sp", Qt[0:DQ, :], QT.t[h], r=[QT], w=[Qt])
                    k.dma("sp", Gt[:, :], GS.t[h // 2, (h % 2) * 64:(h % 2) * 64 + 64, :], r=[GS], w=[Gt])
                    for bi, (t0, nt, cond) in enumerate(BLOCKS):
                        nkt = 2 if cond == 0 else NT
                        po = self.pso.next()
                        pend = []

                        def s_mm(kt):
                            ps = pss.next()
                            k.op("pe", lambda pe: pe.matmul(ps[:, 0:nt], lhsT=Kt[:, kt * 128:(kt + 1) * 128], rhs=Qt[:, t0:t0 + nt],
                                                            start=True, stop=True), r=[Kt, Qt], w=[ps])
                            pt = Ps.next()
                            k.op("act", lambda a: a.activation(out=pt[:, 0:nt], in_=ps[:, 0:nt], func=AF.Exp, scale=scale),
                                 r=[ps], w=[pt])
                            return pt

                        def pv_mm(kt, pt):
                            k.op("pe", lambda pe: pe.matmul(po[0:65, 0:nt], lhsT=Vt[:, kt, :], rhs=pt[:, 0:nt],
                                                            start=(kt == 0), stop=(kt == nkt - 1)), r=[Vt, pt], w=[po])

                        SK = 3
                        for kt in range(nkt + SK):
                            if kt < nkt:
                                pend.append((kt, s_mm(kt)))
                            if kt >= SK:
                                a_, b_ = pend.pop(0)
                                pv_mm(a_, b_)
                            if kt == 10 and pending_epi[0] is not None:
                                pending_epi[0]()
                                pending_epi[0] = None
                        if pending_epi[0] is not None:
                            pending_epi[0]()
                            pending_epi[0] = None

                        def make_epi(po, Gt, t0, nt, bi, h):
                            def epi():
                                k.op("dve", lambda v: v.reciprocal(out=rr[64:65, 0:nt], in_=po[64:65, 0:nt]), r=[po], w=[rr])
                                pm = psm.next()
                                k.op("pe", lambda pe: pe.matmul(pm[0:64, 0:nt], lhsT=self.ones_f[64:65, 0:64], rhs=rr[64:65, 0:nt],
                                                                start=True, stop=True), r=[rr, self.ones_f], w=[pm])
                                k.op("dve", lambda v: v.tensor_tensor(out=tmp[:, 0:nt], in0=pm[0:64, 0:nt], in1=Gt[:, t0:t0 + nt],
                                                                      op=ALU.mult), r=[pm, Gt], w=[tmp])
                                og = ogr.next()
                                k.op("dve", lambda v: v.tensor_tensor(out=og[:, 0:nt], in0=po[0:64, 0:nt], in1=tmp[:, 0:nt], op=ALU.mult),
                                     r=[po, tmp], w=[og])
                                k.dma("pool", OG.t[bi, :, h, 0:nt], og[:, 0:nt], r=[og], w=[OG])
                            return epi
                        pending_epi[0] = make_epi(po, Gt, t0, nt, bi, h)
            if pending_epi[0] is not None:
                pending_epi[0]()
                pending_epi[0] = None
            k.barrier()

        if self.stop == "p2":
            return
        with ExitStack() as es:
            stg = Ring([k.sb(es, [128, 1024], F32) for _ in range(2)])
            wo = k.sb(es, [64, 16, D], BF16)
            self.load_w(es, wo, wo, w_out, 16, 64, D, stg)
            ogb = Ring([k.sb(es, [64, 16, 512], BF16) for _ in range(2)])
            xr = Ring([k.sb(es, [128, D], F32) for _ in range(3)])
            tm = k.sb(es, [128, 512], F32)
            self.outproj(es, lambda bi, nt: (OG.t[bi, :, :, 0:nt], OG), ogb, 64, 16, wo, xsrc, xr, tm)
            k.barrier()


    def layer_gla(self, L, xsrc):
        k = self.k
        w_in = self.W["gla_w_in"].t[0]
        w_out = self.W["gla_w_out"].t[0]
        REC = k.dram([NT, 128, 3584], BF16, f"GREC{L}")
        GG = k.dram([NT, 128, 1024], F32, f"GGG{L}")
        GOF = k.dram([NT, 128, 1024], F32, f"GOF{L}")
        recT = [TT(REC.t) for _ in range(NT)]
        ggT = [TT(GG.t) for _ in range(NT)]
        ofT = [TT(GOF.t) for _ in range(NT)]
        ps8 = self.ps8
        with ExitStack() as es:
            self.epsb = k.sb(es, [128, 1], F32)
            k.op("pool", lambda g: g.memset(self.epsb[:, :], EPS), w=[self.epsb])
            onec = k.sb(es, [128, 1], F32)
            k.op("pool", lambda g: g.memset(onec[:, :], 1.0), w=[onec])
            stg = Ring([k.sb(es, [128, 1024], F32) for _ in range(2)])
            win = k.sb(es, [128, 8, 3104], BF16)
            self.load_w(es, win, win, w_in, 8, 128, 3104, stg)
            wg = k.sb(es, [16, 2, 512], F32)
            k.dma("sp", wg[:, 0, :], self.W["gla_w_gf"].t[0], w=[wg])
            k.dma("sp", wg[:, 1, :], self.W["gla_w_gb"].t[0], w=[wg])
            bg = k.sb(es, [128, 2, 512], F32)
            k.dma("sp", bg[:, 0, :], self.W["gla_b_gf"].t[0, :].partition_broadcast(128), w=[bg])
            k.dma("sp", bg[:, 1, :], self.W["gla_b_gb"].t[0, :].partition_broadcast(128), w=[bg])
            xr = Ring([k.sb(es, [128, D], F32) for _ in range(2)])
            hTr = Ring([k.sb(es, [128, 8, 512], BF16) for _ in range(2)])
            scr = (k.sb(es, [128, D], BF16), k.sb(es, [128, 1], F32), k.sb(es, [128, 1], F32),
                   k.sb(es, [128, D], F32), k.sb(es, [128, D], BF16))
            recr = Ring([k.sb(es, [128, 3584], BF16) for _ in range(2)])
            ggr = Ring([k.sb(es, [128, 1024], F32) for _ in range(2)])
            rT = k.sb(es, [16, 2, 128], F32)
            zt = k.sb(es, [128, 512], F32)
            for bi, (t0, nt, cond) in enumerate(BLOCKS):
                hT = hTr.next()
                self.norm_block(es, bi, xsrc, xr, hT, scr)
                for j in range(nt // 128):
                    t = (t0 + j * 128) // 128
                    rec = recr.next()
                    gg = ggr.next()
                    hTj = lambda kk: hT[:, kk, j * 128:(j + 1) * 128]

                    def tokmm(c0, n):
                        ps = ps8.next()
                        for kk in range(8):
                            k.op("pe", lambda pe: pe.matmul(ps[:, 0:n], lhsT=hTj(kk), rhs=win[:, kk, c0:c0 + n],
                                                            start=(kk == 0), stop=(kk == 7)), r=[hT, win], w=[ps])
                        return ps

                    ps = tokmm(512, 512)
                    k.op("act", lambda a: a.copy(out=rec[:, 1024:1536], in_=ps[:, :]), r=[ps], w=[rec])
                    for cb in range(2):
                        ps = tokmm(1024 + cb * 512, 512)
                        k.op("dve", lambda v: v.tensor_copy(out=rec[:, 1536 + cb * 512:2048 + cb * 512], in_=ps[:, :]),
                             r=[ps], w=[rec])
                    for cb in range(2):
                        ps = tokmm(2048 + cb * 512, 512)
                        k.op("act", lambda a: a.activation(out=rec[:, 2560 + cb * 512:3072 + cb * 512], in_=ps[:, :],
                                                           func=AF.Silu), r=[ps], w=[rec])
                    for qk in range(2):
                        ps = ps8.next()
                        for h in range(4):
                            for kk in range(8):
                                k.op("pe", lambda pe: pe.matmul(
                                    ps[:, h * 128:(h + 1) * 128], lhsT=win[:, kk, qk * 512 + h * 128:qk * 512 + (h + 1) * 128],
                                    rhs=hTj(kk), start=(kk == 0), stop=(kk == 7)), r=[hT, win], w=[ps])
                        if qk == 0:
                            k.op("act", lambda a: a.mul(out=rec[:, 0:512], in_=ps[:, :], mul=128 ** -0.5), r=[ps], w=[rec])
                        else:
                            k.op("dve", lambda v: v.tensor_copy(out=rec[:, 512:1024], in_=ps[:, :]), r=[ps], w=[rec])
                    ps = ps8.next()
                    for d in range(2):
                        for kk in range(8):
                            k.op("pe", lambda pe: pe.matmul(
                                ps[0:16, d * 128:(d + 1) * 128], lhsT=win[:, kk, 3072 + 16 * d:3088 + 16 * d],
                                rhs=hTj(kk), start=(kk == 0), stop=(kk == 7)), r=[hT, win], w=[ps])
                    k.op("act", lambda a: a.copy(out=rT[:, :, :], in_=ps[0:16, 0:256].rearrange("p (d t) -> p d t", d=2)),
                         r=[ps], w=[rT])
                    for d in range(2):
                        ps = ps8.next()
                        k.op("pe", lambda pe: pe.matmul(ps[:, :], lhsT=rT[:, d, :], rhs=wg[:, d, :], start=True, stop=True),
                             r=[rT, wg], w=[ps])
                        k.op("dve", lambda v: v.tensor_tensor(out=zt[:, :], in0=ps[:, :], in1=bg[:, d, :], op=ALU.add),
                             r=[ps, bg], w=[zt])
                        k.op("act", lambda a: a.activation(out=zt[:, :], in_=zt[:, :], func=AF.Exp, scale=-1.0), r=[zt], w=[zt])
                        k.op("act", lambda a: a.activation(out=zt[:, :], in_=zt[:, :], func=AF.Ln, bias=onec[:, :]),
                             r=[zt, onec], w=[zt])
                        k.op("dve", lambda v: v.tensor_scalar(out=gg[:, d * 512:(d + 1) * 512], in0=zt[:, :],
                                                              scalar1=-1.0 / 16.0, scalar2=None, op0=ALU.mult),
                             r=[zt], w=[gg])
                    k.dma("pool", REC.t[t], rec[:, :], r=[rec], w=[recT[t]])
                    k.dma("pool", GG.t[t], gg[:, :], r=[gg], w=[ggT[t]])
            k.barrier()

        with ExitStack() as es:
            self.epsb = k.sb(es, [128, 1], F32)
            k.op("pool", lambda g: g.memset(self.epsb[:, :], EPS), w=[self.epsb])
            tri = k.sb(es, [128, 4, 128], F32)
            k.dma("sp", tri[:, :, :], self.tri_d.t.rearrange("m j i -> j m i"), w=[tri])
            S = [k.sb(es, [128, 256], F32) for _ in range(4)]
            Sb = [k.sb(es, [128, 256], BF16) for _ in range(4)]
            recr = Ring([k.sb(es, [128, 3584], BF16) for _ in range(3)])
            ggr = Ring([k.sb(es, [128, 1024], F32) for _ in range(3)])
            E1r = Ring([k.sb(es, [128, 512], F32) for _ in range(2)])
            E2r = Ring([k.sb(es, [128, 512], F32) for _ in range(2)])
            E3r = Ring([k.sb(es, [128, 512], F32) for _ in range(2)])
            qtr = Ring([k.sb(es, [128, 512], BF16) for _ in range(2)])
            ktr = Ring([k.sb(es, [128, 512], BF16) for _ in range(2)])
            khr = Ring([k.sb(es, [128, 512], BF16) for _ in range(2)])
            Amr = Ring([k.sb(es, [128, 512], BF16) for _ in range(2)])
            osb = Ring([k.sb(es, [128, 1024], F32) for _ in range(2)])
            stg = Ring([k.sb(es, [128, 1024], F32) for _ in range(2)])
            wo = k.sb(es, [128, 8, D], BF16)
            self.load_w(es, wo, wo, w_out, 8, 128, D, stg)
            onbc = k.sb(es, [128, 256], F32)
            k.dma("sp", onbc[:, :], self.W["gla_o_norm"].t[0, :].partition_broadcast(128), w=[onbc])
            ofr = Ring([k.sb(es, [128, 1024], F32) for _ in range(2)])
            xr = Ring([k.sb(es, [128, D], F32) for _ in range(2)])
            osum = k.sb(es, [128, 1024], F32)
            ogf = k.sb(es, [128, 1024], F32)
            ogb = k.sb(es, [128, 1024], BF16)
            ogT = k.sb(es, [128, 8, 128], BF16)
            junk = k.sb(es, [128, 256], BF16)
            ss4 = k.sb(es, [128, 4], F32)
            rs4 = k.sb(es, [128, 4], F32)
            tm = k.sb(es, [128, 512], F32)

            def stageA(d, rec, gg):
                cm = tri[:, 0 if d == 0 else 2, :]
                sm = tri[:, 1 if d == 0 else 3, :]
                g = lambda a_, b_: gg[:, d * 512 + a_:d * 512 + b_]
                E1, E2, E3 = E1r.next(), E2r.next(), E3r.next()
                qt, kt_, kh, Am = qtr.next(), ktr.next(), khr.next(), Amr.next()
                psA = ps8.next()
                for h in range(4):
                    k.op("pe", lambda pe: pe.matmul(psA[:, h * 128:(h + 1) * 128], lhsT=g(h * 128, (h + 1) * 128), rhs=cm,
                                                    start=True, stop=True), r=[gg, tri], w=[psA])
                psB = ps8.next()
                k.op("pe", lambda pe: pe.matmul(psB[:, :], lhsT=sm, rhs=g(0, 512), start=True, stop=True), r=[gg, tri], w=[psB])
                k.op("act", lambda a: a.activation(out=E1[:, :], in_=psA[:, :], func=AF.Exp), r=[psA], w=[E1])
                k.op("act", lambda a: a.activation(out=E2[:, :], in_=psA[:, :], func=AF.Exp, scale=-1.0), r=[psA], w=[E2])
                k.op("act", lambda a: a.activation(out=E3[:, :], in_=psB[:, :], func=AF.Exp), r=[psB], w=[E3])
                k.op("dve", lambda v: v.tensor_tensor(out=qt[:, :], in0=rec[:, 0:512], in1=E1[:, :], op=ALU.mult), r=[rec, E1], w=[qt])
                k.op("pool", lambda v: v.tensor_tensor(out=kt_[:, :], in0=rec[:, 512:1024], in1=E2[:, :], op=ALU.mult), r=[rec, E2], w=[kt_])
                k.op("pool", lambda v: v.tensor_tensor(out=kh[:, :], in0=rec[:, 1024:1536], in1=E3[:, :], op=ALU.mult), r=[rec, E3], w=[kh])
                return dict(d=d, rec=rec, E1=E1, qt=qt, kh=kh, Am=Am, kt_=kt_, cm=cm)

            def stageA2(c):
                qt, kt_, Am, cm = c["qt"], c["kt_"], c["Am"], c["cm"]
                psD = ps8.next()
                for h in range(4):
                    hs = slice(h * 128, (h + 1) * 128)
                    k.op("pe", lambda pe: pe.matmul(psD[:, hs], lhsT=kt_[:, hs], rhs=qt[:, hs], start=True, stop=True),
                         r=[kt_, qt], w=[psD])
                k.op("dve", lambda v: v.tensor_tensor(out=Am[:, :].rearrange("p (h i) -> p h i", h=4),
                                                      in0=psD[:, :].rearrange("p (h i) -> p h i", h=4),
                                                      in1=cm.unsqueeze(1).broadcast_to([128, 4, 128]), op=ALU.mult),
                     r=[psD, tri], w=[Am])

            def stageB(c):
                d, rec, E1, qt, kh, Am = (c[n] for n in ("d", "rec", "E1", "qt", "kh", "Am"))
                ecol = 127 if d == 0 else 0
                po = [ps8.next(), ps8.next()]
                for h in range(4):
                    hs = slice(h * 128, (h + 1) * 128)
                    bank = po[h // 2]
                    cs = slice((h % 2) * 256, (h % 2) * 256 + 256)
                    vs = slice(1536 + h * 256, 1536 + (h + 1) * 256)
                    k.op("pe", lambda pe: pe.matmul(bank[:, cs], lhsT=qt[:, hs], rhs=Sb[h][:, :], start=True, stop=False),
                         r=[qt, Sb[h]], w=[bank])
                    k.op("pe", lambda pe: pe.matmul(bank[:, cs], lhsT=Am[:, hs], rhs=rec[:, vs], start=False, stop=True),
                         r=[Am, rec], w=[bank])
                pss = [ps8.next(), ps8.next()]
                for h in range(4):
                    hs = slice(h * 128, (h + 1) * 128)
                    bank = pss[h // 2]
                    cs = slice((h % 2) * 256, (h % 2) * 256 + 256)
                    vs = slice(1536 + h * 256, 1536 + (h + 1) * 256)
                    k.op("pe", lambda pe: pe.matmul(bank[:, cs], lhsT=kh[:, hs], rhs=rec[:, vs], start=True, stop=True),
                         r=[kh, rec], w=[bank])
                    k.op("dve", lambda v: v.scalar_tensor_tensor(out=S[h][:, :], in0=S[h][:, :],
                                                                 scalar=E1[:, h * 128 + ecol:h * 128 + ecol + 1],
                                                                 in1=bank[:, cs], op0=ALU.mult, op1=ALU.add),
                         r=[S[h], E1, bank], w=[S[h]])
                    k.op("act", lambda gp: gp.copy(out=Sb[h][:, :], in_=S[h][:, :]), r=[S[h]], w=[Sb[h]])
                return po

            def reset_state():
                for h in range(4):
                    k.op("pool", lambda gp: gp.memset(S[h][:, :], 0.0), w=[S[h]])
                    k.op("pool", lambda gp: gp.memset(Sb[h][:, :], 0.0), w=[Sb[h]])

            def gl_loads(t):
                rec = recr.next()
                gg = ggr.next()
                k.dma("sp", rec[:, :], REC.t[t], r=[recT[t]], w=[rec])
                k.dma("sp", gg[:, :], GG.t[t], r=[ggT[t]], w=[gg])
                return rec, gg

            reset_state()
            cnext = stageA(0, *gl_loads(0))
            stageA2(cnext)
            for t in range(NT):
                c = cnext
                if t + 1 < NT:
                    cnext = stageA(0, *gl_loads(t + 1))
                po = stageB(c)
                if t + 1 < NT:
                    stageA2(cnext)
                ob = osb.next()
                for c in range(2):
                    k.op("act", lambda a: a.copy(out=ob[:, c * 512:(c + 1) * 512], in_=po[c][:, :]), r=[po[c]], w=[ob])
                k.dma("pool", GOF.t[t], ob[:, :], r=[ob], w=[ofT[t]])
            reset_state()
            order = [1, 0] + list(range(NT - 1, 1, -1))
            cnext = stageA(1, *gl_loads(order[0]))
            stageA2(cnext)
            for oi, t in enumerate(order):
                cond = 0 if t < 2 else 1
                bi = 0 if t < 2 else 1 + (t - 2) // 4
                c = cnext
                rec = c["rec"]
                if oi + 1 < NT:
                    cnext = stageA(1, *gl_loads(order[oi + 1]))
                of = ofr.next()
                xt = xr.next()
                k.dma("sp", of[:, :], GOF.t[t], r=[ofT[t]], w=[of])
                sap, rr = self.xsrc_ap(xsrc, t * 128, 128)
                k.dma("sp", xt[:, :], sap, r=(rr or [self.xblk[bi]]), w=[xt])
                po = stageB(c)
                if oi + 1 < NT:
                    stageA2(cnext)
                for c in range(2):
                    k.op("dve", lambda v: v.tensor_tensor(out=osum[:, c * 512:(c + 1) * 512], in0=po[c][:, :],
                                                          in1=of[:, c * 512:(c + 1) * 512], op=ALU.add),
                         r=[po[c], of], w=[osum])
                for h in range(4):
                    k.op("act", lambda a: a.activation(out=junk[:, :], in_=osum[:, h * 256:(h + 1) * 256], func=AF.Square,
                                                       accum_out=ss4[:, h:h + 1]), r=[osum], w=[junk, ss4])
                k.op("act", lambda a: a.activation(out=rs4[:, :], in_=ss4[:, :], func=AF.Sqrt, scale=1.0 / 256, bias=self.epsb[:, :]),
                     r=[ss4, self.epsb], w=[rs4])
                k.op("dve", lambda v: v.reciprocal(out=rs4[:, :], in_=rs4[:, :]), r=[rs4], w=[rs4])
                for h in range(4):
                    k.op("dve", lambda v: v.scalar_tensor_tensor(out=ogf[:, h * 256:(h + 1) * 256], in0=osum[:, h * 256:(h + 1) * 256],
                                                                 scalar=rs4[:, h:h + 1], in1=onbc[:, :], op0=ALU.mult, op1=ALU.mult),
                         r=[osum, rs4, onbc], w=[ogf])
                k.op("dve", lambda v: v.tensor_tensor(out=ogb[:, :], in0=ogf[:, :], in1=rec[:, 2560:3584], op=ALU.mult),
                     r=[ogf, rec], w=[ogb])
                self.tok_outproj(ogb, 8, ogT, wo, xt, tm, cond)
                k.dma("pool", self.xres.t[t * 128:(t + 1) * 128, :], xt[:, :], r=[xt], w=[self.xblk[bi]])
            k.barrier()


    def norm_rows(self, src, rtrk, n, cond, dst, scr, xt):
        k = self.k
        junk, ss, rs, hf, hb = scr
        k.dma("sp", xt[0:n, :], src, r=rtrk, w=[xt])
        k.op("act", lambda a: a.activation(out=junk[0:n, :], in_=xt[0:n, :], func=AF.Square, accum_out=ss[0:n, :]),
             r=[xt], w=[junk, ss])
        k.op("act", lambda a: a.activation(out=rs[0:n, :], in_=ss[0:n, :], func=AF.Sqrt, scale=1.0 / D, bias=self.epsb[0:n, :]),
             r=[ss, self.epsb], w=[rs])
        k.op("dve", lambda v: v.reciprocal(out=rs[0:n, :], in_=rs[0:n, :]), r=[rs], w=[rs])
        k.op("dve", lambda v: v.scalar_tensor_tensor(out=hf[0:n, :], in0=xt[0:n, :], scalar=rs[0:n, 0:1],
                                                     in1=self.gmod[cond][0:n, :], op0=ALU.mult, op1=ALU.mult),
             r=[xt, rs, self.gmod[cond]], w=[hf])
        k.op("dve", lambda v: v.tensor_tensor(out=hb[0:n, :], in0=hf[0:n, :], in1=self.bcm[cond][0:n, 0:D], op=ALU.add),
             r=[hf, self.bcm[cond]], w=[hb])
        ps = self.ps8.next()
        pv = ps[:, :].bitcast(BF16).rearrange("p (c t) -> p c t", c=8)
        for c in range(8):
            k.op("pe", lambda pe: pe.transpose(out=pv[:, c, 0:n], in_=hb[0:n, c * 128:(c + 1) * 128],
                                               identity=self.identb[0:n, 0:n]), r=[hb, self.identb], w=[ps])
        return ps, pv

    def layer_ssd(self, L, xsrc):
        k = self.k
        ps8 = self.ps8
        w_in = self.W["ssd_w_in"].t[0]
        w_out = self.W["ssd_w_out"].t[0]
        RX = k.dram([NT, 128, 2560], BF16, f"SRX{L}")
        RZ = k.dram([NT, 128, 2048], BF16, f"SRZ{L}")
        RBC = k.dram([NT, 128, 1024], BF16, f"SRBC{L}")
        DD = k.dram([NT, 128, 128], F32, f"SDD{L}")
        YF = k.dram([NT, 128, 2048], F32, f"SYF{L}")
        rxT = [TT(RX.t) for _ in range(NT)]
        rzT = [TT(RZ.t) for _ in range(NT)]
        rbcT = [TT(RBC.t) for _ in range(NB)]
        ddT = [TT(DD.t) for _ in range(NT)]
        yfT = [TT(YF.t) for _ in range(NT)]
        with ExitStack() as es:
            self.epsb = k.sb(es, [128, 1], F32)
            k.op("pool", lambda g: g.memset(self.epsb[:, :], EPS), w=[self.epsb])
            onec = k.sb(es, [128, 1], F32)
            k.op("pool", lambda g: g.memset(onec[:, :], 1.0), w=[onec])
            stg = Ring([k.sb(es, [128, 1024], F32) for _ in range(2)])
            wz = k.sb(es, [128, 8, 2048], BF16)
            self.load_w(es, wz, wz, w_in[:, 0:2048], 8, 128, 2048, stg)
            wx = k.sb(es, [128, 8, 3072], BF16)
            self.load_w(es, wx, wx, w_in[:, 2048:5120], 8, 128, 3072, stg)
            wdt = k.sb(es, [128, 8, 64], BF16)
            self.load_w(es, wdt, wdt, w_in[:, 5120:5184], 8, 128, 64, stg)
            cw = k.sb(es, [128, 24, 5], F32)
            for kk in range(5):
                k.dma("sp", cw[:, :, kk], self.W["ssd_conv_w"].t[0, kk, :].rearrange("(c p) -> p c", p=128), w=[cw],
                      allow_slow_non_contiguous=True)
            cbias = k.sb(es, [128, 24], F32)
            k.dma("sp", cbias[:, :], self.W["ssd_conv_b"].t[0, :].rearrange("(c p) -> p c", p=128), w=[cbias],
                  allow_slow_non_contiguous=True)
            dtb = k.sb(es, [128, 64], F32)
            k.dma("sp", dtb[:, 0:32], self.W["ssd_dt_bias_f"].t[0, :].partition_broadcast(128), w=[dtb])
            k.dma("sp", dtb[:, 32:64], self.W["ssd_dt_bias_b"].t[0, :].partition_broadcast(128), w=[dtb])
            abc = k.sb(es, [128, 64], F32)
            k.dma("sp", abc[:, 0:32], self.W["ssd_a_log_f"].t[0, :].partition_broadcast(128), w=[abc])
            k.dma("sp", abc[:, 32:64], self.W["ssd_a_log_b"].t[0, :].partition_broadcast(128), w=[abc])
            k.op("act", lambda a: a.activation(out=abc[:, :], in_=abc[:, :], func=AF.Exp), r=[abc], w=[abc])
            k.op("dve", lambda v: v.tensor_scalar(out=abc[:, :], in0=abc[:, :], scalar1=-1.0, scalar2=None, op0=ALU.mult),
                 r=[abc], w=[abc])
            xr = Ring([k.sb(es, [128, D], F32) for _ in range(2)])
            hT = k.sb(es, [128, 8, 516], BF16)
            scr = (k.sb(es, [128, D], BF16), k.sb(es, [128, 1], F32), k.sb(es, [128, 1], F32),
                   k.sb(es, [128, D], F32), k.sb(es, [128, D], BF16))
            prer = Ring([k.sb(es, [128, 516], BF16) for _ in range(3)])
            accr = Ring([k.sb(es, [128, 512], F32) for _ in range(2)])
            xc = k.sb(es, [128, 24, 512], BF16)
            rxr = Ring([k.sb(es, [128, 2560], BF16) for _ in range(2)])
            rzr = Ring([k.sb(es, [128, 2048], BF16) for _ in range(2)])
            ddr = Ring([k.sb(es, [128, 128], F32) for _ in range(2)])
            dtt = k.sb(es, [128, 64], F32)
            for bi, (t0, nt, cond) in enumerate(BLOCKS):
                ntile = nt // 128
                for j in range(ntile):
                    src, rr = self.xsrc_ap(xsrc, t0 + j * 128, 128)
                    ps, pv = self.norm_rows(src, rr or [self.xblk[bi]], 128, cond, None, scr, xr.next())
                    k.op("act", lambda a: a.copy(out=hT[:, :, j * 128:(j + 1) * 128], in_=pv), r=[ps], w=[hT])
                for side in range(2):
                    col = 512 + 2 * side
                    has = (bi >= 2) if side == 0 else (1 <= bi < NB - 1)
                    if not has:
                        k.op("pool", lambda g: g.memset(hT[:, :, col:col + 2], 0.0), w=[hT])
                    else:
                        r0 = t0 - 2 if side == 0 else t0 + nt
                        nb_ = bi - 1 if side == 0 else bi + 1
                        src, rr = self.xsrc_ap(xsrc, r0, 2)
                        ps, pv = self.norm_rows(src, rr or [self.xblk[nb_]], 2, cond, None, scr, xr.next())
                        k.op("act", lambda a: a.copy(out=hT[:, :, col:col + 2], in_=pv[:, :, 0:2]), r=[ps], w=[hT])
                for c in range(24):
                    ps = ps8.next()
                    for kk in range(8):
                        k.op("pe", lambda pe: pe.matmul(ps[:, 0:nt], lhsT=wx[:, kk, c * 128:(c + 1) * 128], rhs=hT[:, kk, 0:nt],
                                                        start=(kk == 0), stop=(kk == 7)), r=[wx, hT], w=[ps])
                    ps2 = ps8.next()
                    for kk in range(8):
                        k.op("pe", lambda pe: pe.matmul(ps2[:, 0:4], lhsT=wx[:, kk, c * 128:(c + 1) * 128], rhs=hT[:, kk, 512:516],
                                                        start=(kk == 0), stop=(kk == 7)), r=[wx, hT], w=[ps2])
                    pre = prer.next()
                    k.op("act", lambda a: a.copy(out=pre[:, 2:2 + nt], in_=ps[:, 0:nt]), r=[ps], w=[pre])
                    k.op("act", lambda a: a.copy(out=pre[:, 0:2], in_=ps2[:, 0:2]), r=[ps2], w=[pre])
                    k.op("act", lambda a: a.copy(out=pre[:, 2 + nt:4 + nt], in_=ps2[:, 2:4]), r=[ps2], w=[pre])
                    acc = accr.next()
                    k.op("dve", lambda v: v.tensor_scalar(out=acc[:, 0:nt], in0=pre[:, 0:nt], scalar1=cw[:, c, 0:1], scalar2=None,
                                                          op0=ALU.mult), r=[pre, cw], w=[acc])
                    for kk in range(1, 5):
                        k.op("dve", lambda v: v.scalar_tensor_tensor(out=acc[:, 0:nt], in0=pre[:, kk:kk + nt], scalar=cw[:, c, kk:kk + 1],
                                                                     in1=acc[:, 0:nt], op0=ALU.mult, op1=ALU.add),
                             r=[pre, cw, acc], w=[acc])
                    k.op("act", lambda a: a.activation(out=xc[:, c, 0:nt], in_=acc[:, 0:nt], func=AF.Silu, bias=cbias[:, c:c + 1]),
                         r=[acc, cbias], w=[xc])
                kt0 = t0 // 128
                for j in range(ntile):
                    for s_ in range(2):
                        k.dma("act", RBC.t[kt0 + j, :, s_ * 512:(s_ + 1) * 512].rearrange("p (g t) -> p g t", g=4),
                              xc[:, 16 + 4 * s_:20 + 4 * s_, j * 128:(j + 1) * 128], r=[xc], w=[rbcT[bi]])
                for j in range(ntile):
                    t = kt0 + j
                    rx = rxr.next()
                    for c0 in (0, 8, 16):
                        cn = 8 if c0 < 16 else 4
                        ps = ps8.next()
                        pv = ps[:, :].bitcast(BF16).rearrange("p (c t) -> p c t", c=8)
                        for c in range(cn):
                            k.op("pe", lambda pe: pe.transpose(out=pv[:, c, :], in_=xc[:, c0 + c, j * 128:(j + 1) * 128],
                                                               identity=self.identb[:, :]), r=[xc, self.identb], w=[ps])
                        k.op("act" if c0 != 8 else "dve",
                             (lambda a: a.copy(out=rx[:, c0 * 128:(c0 + cn) * 128].rearrange("p (c t) -> p c t", c=cn), in_=pv[:, 0:cn, :]))
                             if c0 != 8 else
                             (lambda v: v.tensor_copy(out=rx[:, c0 * 128:(c0 + cn) * 128].rearrange("p (c t) -> p c t", c=cn), in_=pv[:, 0:cn, :])),
                             r=[ps], w=[rx])
                    k.dma("pool", RX.t[t], rx[:, :], r=[rx], w=[rxT[t]])
                    rz = rzr.next()
                    for cb in range(4):
                        ps = ps8.next()
                        for kk in range(8):
                            k.op("pe", lambda pe: pe.matmul(ps[:, :], lhsT=hT[:, kk, j * 128:(j + 1) * 128],
                                                            rhs=wz[:, kk, cb * 512:(cb + 1) * 512],
                                                            start=(kk == 0), stop=(kk == 7)), r=[hT, wz], w=[ps])
                        k.op("act", lambda a: a.activation(out=rz[:, cb * 512:(cb + 1) * 512], in_=ps[:, :], func=AF.Silu),
                             r=[ps], w=[rz])
                    k.dma("pool", RZ.t[t], rz[:, :], r=[rz], w=[rzT[t]])
                    dd = ddr.next()
                    ps = ps8.next()
                    for kk in range(8):
                        k.op("pe", lambda pe: pe.matmul(ps[:, 0:64], lhsT=hT[:, kk, j * 128:(j + 1) * 128], rhs=wdt[:, kk, :],
                                                        start=(kk == 0), stop=(kk == 7)), r=[hT, wdt], w=[ps])
                    k.op("dve", lambda v: v.tensor_tensor(out=dtt[:, :], in0=ps[:, 0:64], in1=dtb[:, :], op=ALU.add),
                         r=[ps, dtb], w=[dtt])
                    k.op("act", lambda a: a.activation(out=dtt[:, :], in_=dtt[:, :], func=AF.Exp), r=[dtt], w=[dtt])
                    k.op("act", lambda a: a.activation(out=dd[:, 0:64], in_=dtt[:, :], func=AF.Ln, bias=onec[:, :]),
                         r=[dtt, onec], w=[dd])
                    k.op("dve", lambda v: v.tensor_tensor(out=dd[:, 64:128], in0=dd[:, 0:64], in1=abc[:, :], op=ALU.mult),
                         r=[dd, abc], w=[dd])
                    k.dma("pool", DD.t[t], dd[:, :], r=[dd], w=[ddT[t]])
            k.barrier()

        with ExitStack() as es:
            self.epsb = k.sb(es, [128, 1], F32)
            k.op("pool", lambda g: g.memset(self.epsb[:, :], EPS), w=[self.epsb])
            tri = k.sb(es, [128, 4, 128], F32)
            k.dma("sp", tri[:, :, :], self.tri_d.t.rearrange("m j i -> j m i"), w=[tri])
            negm = k.sb(es, [128, 2, 128], F32)
            k.dma("sp", negm[:, :, :], self.negm_d.t.rearrange("m j i -> j m i"), w=[negm])
            onehot = k.sb(es, [96, 32, 128], BF16)
            k.dma("sp", onehot[:, :, :], self.onehot_d.t, w=[onehot])
            negmb = k.sb(es, [128, 2, 128], BF16)
            k.dma("sp", negmb[:, :, :], self.negmb_d.t.rearrange("m j i -> j m i"), w=[negmb])
            identf = k.sb(es, [128, 128], F32)
            k.dma("sp", identf[:, :], self.identf_d.t, w=[identf])
            Hs = [k.sb(es, [128, 512], F32) for _ in range(4)]
            Hb = [k.sb(es, [128, 512], BF16) for _ in range(4)]
            rxr = Ring([k.sb(es, [128, 2560], BF16) for _ in range(3)])
            rbcr = Ring([k.sb(es, [128, 1024], BF16) for _ in range(2)])
            ddr = Ring([k.sb(es, [128, 128], F32) for _ in range(3)])
            bankY = Ring(self.psb[0:2])
            bankLb = Ring(self.psb[2:5])
            bankM = Ring(self.psb[5:8])
            def ring2(shape, dt):
                return Ring([k.sb(es, shape, dt) for _ in range(2)])
            ela_r = ring2([128, 32], F32)
            edl_r = ring2([128, 32], F32)
            etot_r = ring2([128, 32], F32)
            laT_r = ring2([96, 128], BF16)
            larep_r = ring2([128, 96], F32)
            Abf_r = ring2([96, 128], BF16)
            Bbf_r = ring2([96, 128], BF16)
            R1_r = ring2([96, 128], F32)
            nla_r = ring2([128, 32], F32)
            xdt_r = ring2([128, 2048], BF16)
            xdl_r = ring2([128, 2048], BF16)
            CBm_r = ring2([128, 4, 128], BF16)
            Eh_r = Ring([k.sb(es, [128, 4, 128], BF16) for _ in range(3)])
            Mh = Ring([k.sb(es, [128, 4, 128], BF16) for _ in range(3)])
            ysr = Ring([k.sb(es, [128, 2048], F32) for _ in range(3)])
            wo = k.sb(es, [128, 16, D], BF16)
            with ExitStack() as es2:
                stg = Ring([k.sb(es2, [128, 1024], F32) for _ in range(2)])
                self.load_w(es2, wo, wo, w_out, 16, 128, D, stg)
                k.barrier()
            gnbc = k.sb(es, [128, 2048], F32)
            k.dma("sp", gnbc[:, :], self.W["ssd_norm"].t[0, :].partition_broadcast(128), w=[gnbc])
            dsk = k.sb(es, [128, 32], F32)
            k.dma("sp", dsk[:, :], self.W["ssd_d"].t[0, :].partition_broadcast(128), w=[dsk])
            rzr = Ring([k.sb(es, [128, 2048], BF16) for _ in range(1)])
            yfr = Ring([k.sb(es, [128, 2048], F32) for _ in range(1)])
            xr = Ring([k.sb(es, [128, D], F32) for _ in range(2)])
            ynb = k.sb(es, [128, 2048], BF16)
            ogT = k.sb(es, [128, 16, 128], BF16)
            junk = k.sb(es, [128, 512], BF16)
            ss4 = k.sb(es, [128, 4], F32)
            rs4 = k.sb(es, [128, 4], F32)
            tm = k.sb(es, [128, 512], F32)

            def stageA(d, rx, rbc, dd):
                c = dict(d=d, rx=rx, rbc=rbc, dd=dd)
                cm = tri[:, 0 if d == 0 else 2, :]
                sm = tri[:, 1 if d == 0 else 3, :]
                dA = dd[:, 64 + 32 * d:96 + 32 * d]
                dt = dd[:, 32 * d:32 * d + 32]
                larep = larep_r.next()
                la_sb = larep
                ela, edl, etot, laT = ela_r.next(), edl_r.next(), etot_r.next(), laT_r.next()
                Abf, Bbf, R1 = Abf_r.next(), Bbf_r.next(), R1_r.next()
                xdt, xdl, CBm = xdt_r.next(), xdl_r.next(), CBm_r.next()
                nla = nla_r.next()
                c.update(la_sb=la_sb, ela=ela, etot=etot, laT=laT, xdt=xdt, xdl=xdl, CBm=CBm, nla=nla)
                psL = bankM.next()
                k.op("pe", lambda pe: pe.matmul(psL[:, 0:32], lhsT=cm, rhs=dA, start=True, stop=True), r=[tri, dd], w=[psL])
                k.op("pe", lambda pe: pe.matmul(psL[:, 32:64], lhsT=sm, rhs=dA, start=True, stop=True), r=[tri, dd], w=[psL])
                k.op("pe", lambda pe: pe.matmul(psL[:, 64:96], lhsT=self.ones_f[:, :], rhs=dA, start=True, stop=True),
                     r=[self.ones_f, dd], w=[psL])
                for rep in range(3):
                    k.op("act", lambda a: a.copy(out=larep[:, rep * 32:(rep + 1) * 32], in_=psL[:, 0:32]), r=[psL], w=[larep])
                k.op("act", lambda a: a.mul(out=nla[:, :], in_=psL[:, 0:32], mul=-1.0), r=[psL], w=[nla])
                k.op("act", lambda a: a.activation(out=ela[:, :], in_=psL[:, 0:32], func=AF.Exp), r=[psL], w=[ela])
                k.op("act", lambda a: a.activation(out=edl[:, :], in_=psL[:, 32:64], func=AF.Exp), r=[psL], w=[edl])
                k.op("act", lambda a: a.activation(out=etot[:, :], in_=psL[:, 64:96], func=AF.Exp), r=[psL], w=[etot])
                xs3 = rx[:, 0:2048].rearrange("p (h e) -> p h e", h=32)
                k.op("pool", lambda g: g.tensor_tensor(out=xdt[:, :].rearrange("p (h e) -> p h e", h=32), in0=xs3,
                                                       in1=dt.unsqueeze(2).broadcast_to([128, 32, 64]), op=ALU.mult),
                     r=[rx, dd], w=[xdt])
                k.op("pool", lambda g: g.tensor_tensor(out=xdl[:, :].rearrange("p (h e) -> p h e", h=32),
                                                       in0=xdt[:, :].rearrange("p (h e) -> p h e", h=32),
                                                       in1=edl[:, :].unsqueeze(2).broadcast_to([128, 32, 64]), op=ALU.mult),
                     r=[xdt, edl], w=[xdl])
                psT = bankM.next()
                k.op("pe", lambda pe: pe.transpose(out=psT[0:96, 0:128], in_=larep[:, 0:96], identity=identf[:, :]),
                     r=[larep, identf], w=[psT])
                k.op("act", lambda a: a.copy(out=Abf[:, :], in_=psT[0:96, 0:128]), r=[psT], w=[Abf])
                k.op("dve", lambda v: v.tensor_tensor(out=R1[:, :], in0=psT[0:96, 0:128], in1=Abf[:, :], op=ALU.subtract),
                     r=[psT, Abf], w=[R1])
                k.op("act", lambda a: a.copy(out=Bbf[:, :], in_=R1[:, :]), r=[R1], w=[Bbf])
                k.op("dve", lambda v: v.tensor_tensor(out=R1[:, :], in0=R1[:, :], in1=Bbf[:, :], op=ALU.subtract),
                     r=[R1, Bbf], w=[R1])
                k.op("pool", lambda g: g.tensor_copy(out=laT[0:32, :], in_=Abf[0:32, :]), r=[Abf], w=[laT])
                k.op("pool", lambda g: g.tensor_copy(out=laT[32:64, :], in_=Bbf[32:64, :]), r=[Bbf], w=[laT])
                k.op("pool", lambda g: g.tensor_copy(out=laT[64:96, :], in_=R1[64:96, :]), r=[R1], w=[laT])
                psCB = bankM.next()
                for g in range(4):
                    k.op("pe", lambda pe: pe.matmul(psCB[:, g * 128:(g + 1) * 128], lhsT=rbc[:, g * 128:(g + 1) * 128],
                                                    rhs=rbc[:, 512 + g * 128:512 + (g + 1) * 128], start=True, stop=True),
                         r=[rbc], w=[psCB])
                k.op("dve", lambda v: v.tensor_tensor(out=CBm[:, :, :], in0=psCB[:, :].rearrange("p (g i) -> p g i", g=4),
                                                      in1=cm.unsqueeze(1).broadcast_to([128, 4, 128]), op=ALU.mult),
                     r=[psCB, tri], w=[CBm])
                return c

            def stageB(c, ys):
                d, rx, rbc = c["d"], c["rx"], c["rbc"]
                la_sb, ela, etot, laT, xdt, xdl, CBm = (c[n] for n in ("la_sb", "ela", "etot", "laT", "xdt", "xdl", "CBm"))
                ng = negm[:, d, :]
                nla = c["nla"]

                def lb_mm(i):
                    h0 = i * 4
                    psLb = bankLb.next()
                    for hh in range(4):
                        k.op("pe", lambda pe: pe.matmul(psLb[:, hh * 128:(hh + 1) * 128], lhsT=onehot[:, h0 + hh, :],
                                                        rhs=laT[:, :], start=True, stop=False), r=[onehot, laT], w=[psLb])
                        k.op("pe", lambda pe: pe.matmul(psLb[:, hh * 128:(hh + 1) * 128], lhsT=self.identb[:, :],
                                                        rhs=negmb[:, d, :], start=False, stop=True), r=[self.identb, negmb], w=[psLb])
                    return psLb

                pend = [lb_mm(0), lb_mm(1)]

                def heads(g):
                    psY = bankY.next()
                    for hq in range(2):
                        i = g * 2 + hq
                        h0 = i * 4
                        psLb = pend.pop(0)
                        if i + 2 < 8:
                            pend.append(lb_mm(i + 2))
                        Eh = Eh_r.next()
                        for hh in range(4):
                            k.op("act", lambda a: a.activation(out=Eh[:, hh, :], in_=psLb[:, hh * 128:(hh + 1) * 128], func=AF.Exp,
                                                               bias=nla[:, h0 + hh:h0 + hh + 1]), r=[psLb, nla], w=[Eh])
                        mh = Mh.next()
                        k.op("dve", lambda v: v.tensor_tensor(out=mh[:, :, :], in0=Eh[:, :, :],
                                                              in1=CBm[:, g:g + 1, :].broadcast_to([128, 4, 128]), op=ALU.mult),
                             r=[Eh, CBm], w=[mh])
                        for hh in range(4):
                            h = h0 + hh
                            k.op("pe", lambda pe: pe.matmul(psY[:, (h % 8) * 64:(h % 8 + 1) * 64], lhsT=mh[:, hh, :],
                                                            rhs=xdt[:, h * 64:(h + 1) * 64], start=True, stop=True),
                                 r=[mh, xdt], w=[psY])
                    return psY

                def tail(g, psY):
                    psYi = bankM.next()
                    k.op("pe", lambda pe: pe.matmul(psYi[:, :], lhsT=rbc[:, 512 + g * 128:512 + (g + 1) * 128], rhs=Hb[g][:, :],
                                                    start=True, stop=True), r=[rbc, Hb[g]], w=[psYi])
                    yv = ys[:, g * 512:(g + 1) * 512]
                    k.op("dve", lambda v: v.tensor_tensor(out=yv.rearrange("p (h e) -> p h e", h=8),
                                                          in0=psYi[:, :].rearrange("p (h e) -> p h e", h=8),
                                                          in1=ela[:, g * 8:(g + 1) * 8].unsqueeze(2).broadcast_to([128, 8, 64]),
                                                          op=ALU.mult), r=[psYi, ela], w=[ys])
                    k.op("dve", lambda v: v.tensor_tensor(out=yv, in0=yv, in1=psY[:, :], op=ALU.add), r=[ys, psY], w=[ys])
                    psH = bankM.next()
                    k.op("pe", lambda pe: pe.matmul(psH[:, :], lhsT=rx[:, 2048 + g * 128:2048 + (g + 1) * 128],
                                                    rhs=xdl[:, g * 512:(g + 1) * 512], start=True, stop=True), r=[rx, xdl], w=[psH])
                    k.op("pool", lambda v: v.tensor_tensor(out=Hs[g][:, :].rearrange("p (h e) -> p h e", h=8),
                                                           in0=Hs[g][:, :].rearrange("p (h e) -> p h e", h=8),
                                                           in1=etot[:, g * 8:(g + 1) * 8].unsqueeze(2).broadcast_to([128, 8, 64]),
                                                           op=ALU.mult), r=[Hs[g], etot], w=[Hs[g]])
                    k.op("dve", lambda v: v.tensor_tensor(out=Hs[g][:, :], in0=Hs[g][:, :], in1=psH[:, :], op=ALU.add),
                         r=[Hs[g], psH], w=[Hs[g]])
                    k.op("pool", lambda gp: gp.tensor_copy(out=Hb[g][:, :], in_=Hs[g][:, :]), r=[Hs[g]], w=[Hb[g]])

                prev = None
                for g in range(4):
                    py = heads(g)
                    if prev is not None:
                        tail(*prev)
                    prev = (g, py)
                tail(*prev)

            def reset_state():
                for g in range(4):
                    k.op("pool", lambda gp: gp.memset(Hs[g][:, :], 0.0), w=[Hs[g]])
                    k.op("pool", lambda gp: gp.memset(Hb[g][:, :], 0.0), w=[Hb[g]])

            def loads(t):
                bi = 0 if t < 2 else 1 + (t - 2) // 4
                rx = rxr.next()
                rbc = rbcr.next()
                dd = ddr.next()
                k.dma("sp", rx[:, :], RX.t[t], r=[rxT[t]], w=[rx])
                k.dma("sp", rbc[:, :], RBC.t[t], r=[rbcT[bi]], w=[rbc])
                k.dma("sp", dd[:, :], DD.t[t], r=[ddT[t]], w=[dd])
                return rx, rbc, dd

            reset_state()
            cnext = stageA(0, *loads(0))
            for t in range(NT):
                c = cnext
                if t + 1 < NT:
                    cnext = stageA(0, *loads(t + 1))
                ys = ysr.next()
                stageB(c, ys)
                k.dma("pool", YF.t[t], ys[:, :], r=[ys], w=[yfT[t]])
            reset_state()
            order = [1, 0] + list(range(NT - 1, 1, -1))
            cnext = stageA(1, *loads(order[0]))

            def out_stage(t, rx, ys):
                cond = 0 if t < 2 else 1
                bi = 0 if t < 2 else 1 + (t - 2) // 4
                rz = rzr.next()
                yf = yfr.next()
                xt = xr.next()
                k.dma("sp", rz[:, :], RZ.t[t], r=[rzT[t]], w=[rz])
                k.dma("sp", yf[:, :], YF.t[t], r=[yfT[t]], w=[yf])
                sap, rr = self.xsrc_ap(xsrc, t * 128, 128)
                k.dma("sp", xt[:, :], sap, r=(rr or [self.xblk[bi]]), w=[xt])
                k.op("dve", lambda v: v.tensor_tensor(out=ys[:, :], in0=ys[:, :], in1=yf[:, :], op=ALU.add), r=[ys, yf], w=[ys])
                ytmp = yf
                k.op("pool", lambda g: g.tensor_tensor(out=ytmp[:, :].rearrange("p (h e) -> p h e", h=32),
                                                       in0=rx[:, 0:2048].rearrange("p (h e) -> p h e", h=32),
                                                       in1=dsk[:, :].unsqueeze(2).broadcast_to([128, 32, 64]), op=ALU.mult),
                     r=[rx, dsk, yf], w=[ytmp])
                k.op("dve", lambda v: v.tensor_tensor(out=ys[:, :], in0=ys[:, :], in1=ytmp[:, :], op=ALU.add), r=[ys, ytmp], w=[ys])
                k.op("dve", lambda v: v.tensor_tensor(out=ys[:, :], in0=ys[:, :], in1=rz[:, :], op=ALU.mult), r=[ys, rz], w=[ys])
                for g in range(4):
                    k.op("act", lambda a: a.activation(out=junk[:, :], in_=ys[:, g * 512:(g + 1) * 512], func=AF.Square,
                                                       accum_out=ss4[:, g:g + 1]), r=[ys], w=[junk, ss4])
                k.op("act", lambda a: a.activation(out=rs4[:, :], in_=ss4[:, :], func=AF.Sqrt, scale=1.0 / 512, bias=self.epsb[:, :]),
                     r=[ss4, self.epsb], w=[rs4])
                k.op("dve", lambda v: v.reciprocal(out=rs4[:, :], in_=rs4[:, :]), r=[rs4], w=[rs4])
                for g in range(4):
                    k.op("dve", lambda v: v.scalar_tensor_tensor(out=ynb[:, g * 512:(g + 1) * 512], in0=ys[:, g * 512:(g + 1) * 512],
                                                                 scalar=rs4[:, g:g + 1], in1=gnbc[:, g * 512:(g + 1) * 512],
                                                                 op0=ALU.mult, op1=ALU.mult), r=[ys, rs4, gnbc], w=[ynb])
                self.tok_outproj(ynb, 16, ogT, wo, xt, tm, cond)
                k.dma("pool", self.xres.t[t * 128:(t + 1) * 128, :], xt[:, :], r=[xt], w=[self.xblk[bi]])

            pending_out = None
            for oi, t in enumerate(order):
                c = cnext
                if oi + 1 < NT:
                    cnext = stageA(1, *loads(order[oi + 1]))
                ys = ysr.next()
                stageB(c, ys)
                if pending_out is not None:
                    out_stage(*pending_out)
                pending_out = (t, c["rx"], ys)
            out_stage(*pending_out)
            k.barrier()

    def tok_outproj(self, ogb, kc, ogT, wo, xt, tm, cond):
        k = self.k
        for c0 in range(0, kc, 8):
            ps = self.ps8.next()
            pv = ps[:, :].bitcast(BF16).rearrange("p (c t) -> p c t", c=8)
            for c in range(8):
                k.op("pe", lambda pe: pe.transpose(out=pv[:, c, :], in_=ogb[:, (c0 + c) * 128:(c0 + c + 1) * 128],
                                                   identity=self.identb[:, :]), r=[ogb, self.identb], w=[ps])
            k.op("act", lambda a: a.copy(out=ogT[:, c0:c0 + 8, :], in_=pv), r=[ps], w=[ogT])
        for cb in range(2):
            ps = self.ps8.next()
            for kk in range(kc):
                k.op("pe", lambda pe: pe.matmul(ps[:, :], lhsT=ogT[:, kk, :], rhs=wo[:, kk, cb * 512:(cb + 1) * 512],
                                                start=(kk == 0), stop=(kk == kc - 1)), r=[ogT, wo], w=[ps])
            k.op("dve", lambda v: v.tensor_tensor(out=tm[:, :], in0=ps[:, :],
                                                  in1=self.bcm[cond][:, 2 * D + cb * 512:2 * D + (cb + 1) * 512], op=ALU.mult),
                 r=[ps, self.bcm[cond]], w=[tm])
            k.op("dve", lambda v: v.tensor_tensor(out=xt[:, cb * 512:(cb + 1) * 512], in0=tm[:, :],
                                                  in1=xt[:, cb * 512:(cb + 1) * 512], op=ALU.add), r=[tm, xt], w=[xt])

    def outproj(self, es, og_src, ogb, kp, kc, wo, xsrc, xr, tm):
        k = self.k
        for bi, (t0, nt, cond) in enumerate(BLOCKS):
            ob = ogb.next()
            src, trk = og_src(bi, nt)
            k.dma("sp", ob[:, :, 0:nt], src, r=[trk], w=[ob])
            for j in range(nt // 128):
                xt = xr.next()
                sap, rr = self.xsrc_ap(xsrc, t0 + j * 128, 128)
                k.dma("sp", xt[:, :], sap, r=(rr or [self.xblk[bi]]), w=[xt])
                import os
                for cb in range(2 if not os.environ.get("DBG_SKIPMM") else 0):
                    ps = self.psg.next()
                    for kk in range(kc):
                        k.op("pe", lambda pe, kk=kk, cb=cb, ps=ps: pe.matmul(
                            ps[:, :], lhsT=ob[0:kp, kk, j * 128:(j + 1) * 128], rhs=wo[0:kp, kk, cb * 512:(cb + 1) * 512],
                            start=(kk == 0), stop=(kk == kc - 1)), r=[ob, wo], w=[ps])
                    k.op("dve", lambda v, cb=cb, ps=ps: v.tensor_tensor(
                        out=tm[:, :], in0=ps[:, :], in1=self.bcm[cond][:, 2 * D + cb * 512:2 * D + (cb + 1) * 512], op=ALU.mult),
                        r=[ps, self.bcm[cond]], w=[tm])
                    k.op("dve", lambda v, cb=cb, xt=xt: v.tensor_tensor(
                        out=xt[:, cb * 512:(cb + 1) * 512], in0=tm[:, :], in1=xt[:, cb * 512:(cb + 1) * 512], op=ALU.add),
                        r=[tm, xt], w=[xt])
                k.dma("pool", self.xres.t[t0 + j * 128:t0 + (j + 1) * 128, :], xt[:, :], r=[xt], w=[self.xblk[bi]])

    def final(self, xsrc):
        k = self.k
        with ExitStack() as es:
            xr = Ring([k.sb(es, [128, D], F32) for _ in range(3)])
            junk = k.sb(es, [128, D], BF16)
            ss = k.sb(es, [128, 1], F32)
            rs = k.sb(es, [128, 1], F32)
            epsb = k.sb(es, [128, 1], F32)
            k.op("pool", lambda g: g.memset(epsb[:, :], EPS), w=[epsb])
            fg = k.sb(es, [128, D], F32)
            k.dma("sp", fg[:, :], self.final_g.t.partition_broadcast(128), w=[fg])
            for bi, (t0, nt, cond) in enumerate(BLOCKS):
                if cond == 0 and not self.debug_x:
                    continue
                for j in range(nt // 128):
                    xt = xr.next()
                    tt0 = t0 + j * 128
                    sap, rr = self.xsrc_ap(xsrc, tt0, 128)
                    k.dma("sp", xt[:, :], sap, r=(rr or [self.xblk[bi]]), w=[xt])
                    if self.debug_x:
                        k.dma("pool", self.out.t[tt0:tt0 + 128, :], xt[:, :], r=[xt], w=[self.out])
                        continue
                    k.op("act", lambda a, xt=xt: a.activation(out=junk[:, :], in_=xt[:, :], func=AF.Square, accum_out=ss[:, :]),
                         r=[xt], w=[junk, ss])
                    k.op("act", lambda a: a.activation(out=rs[:, :], in_=ss[:, :], func=AF.Sqrt, scale=1.0 / D, bias=epsb[:, :]),
                         r=[ss, epsb], w=[rs])
                    k.op("dve", lambda v: v.reciprocal(out=rs[:, :], in_=rs[:, :]), r=[rs], w=[rs])
                    k.op("dve", lambda v, xt=xt: v.scalar_tensor_tensor(out=xt[:, :], in0=xt[:, :], scalar=rs[:, 0:1],
                                                                       in1=fg[:, :], op0=ALU.mult, op1=ALU.mult),
                         r=[xt, rs, fg], w=[xt])
                    k.dma("pool", self.out.t[tt0 - CTX:tt0 - CTX + 128, :], xt[:, :], r=[xt], w=[self.out])


WSHAPES = {
    "mla_w_in": [1, 1024, 1696], "mla_q_norm": [1, 384], "mla_w_uq": [1, 384, 1536], "mla_kv_norm": [1, 256],
    "mla_w_ukv": [1, 256, 2048], "mla_w_out": [1, 1024, 1024],
    "gla_w_in": [1, 1024, 3104], "gla_w_gf": [1, 16, 512], "gla_b_gf": [1, 512], "gla_w_gb": [1, 16, 512],
    "gla_b_gb": [1, 512], "gla_o_norm": [1, 256], "gla_w_out": [1, 1024, 1024],
    "gqa_w_in": [1, 1024, 2560], "gqa_q_norm": [1, 64], "gqa_k_norm": [1, 64], "gqa_w_out": [1, 1024, 1024],
    "ssd_w_in": [1, 1024, 5184], "ssd_conv_w": [1, 5, 3072], "ssd_conv_b": [1, 3072], "ssd_dt_bias_f": [1, 32],
    "ssd_dt_bias_b": [1, 32], "ssd_a_log_f": [1, 32], "ssd_a_log_b": [1, 32], "ssd_d": [1, 32], "ssd_norm": [1, 2048],
    "ssd_w_out": [1, 2048, 1024],
}


def tri_consts():
    j = np.arange(128)[:, None]
    i = np.arange(128)[None, :]
    return np.stack([(j <= i), (j > i), (j >= i), (j < i)]).astype(np.float32)


def rope_tables(rd):
    hf = rd // 4
    inv = 10000.0 ** (-np.arange(hf, dtype=np.float64) / hf)
    p = np.arange(SEQ)
    row = (p // 64).astype(np.float64)[:, None] * inv[None, :]
    col = (p % 64).astype(np.float64)[:, None] * inv[None, :]
    cos = np.concatenate([np.cos(row), np.cos(row), np.cos(col), np.cos(col)], axis=1)
    sin = np.concatenate([-np.sin(row), np.sin(row), -np.sin(col), np.sin(col)], axis=1)
    tab = np.zeros((T, 2, rd), np.float32)
    tab[:CTX, 0, :] = 1.0
    tab[CTX:, 0, :] = cos
    tab[CTX:, 1, :] = sin
    return tab


def run(inputs, layers=(0, 1, 2, 3), debug_x=False, cores=(0, 1), stop=None):
    nc = bass.Bass("TRN2", target_bir_lowering=False)
    Prog(nc, layers=layers, debug_x=debug_x, stop=stop).build()
    f = lambda a: np.ascontiguousarray(np.asarray(a, dtype=np.float32))
    common = {nm: f(inputs[nm]) for nm in WSHAPES}
    for nm in ("ada_w", "ada_b", "norm_g", "final_g"):
        common[nm] = f(inputs[nm])
    common["ident_bf"] = np.eye(128, dtype=np.float32).astype(ml_dtypes.bfloat16)
    common["rope_mla"] = rope_tables(32)
    common["rope_gqa"] = rope_tables(64)
    common["tri"] = tri_consts()
    common["negm"] = ((1.0 - tri_consts()[[0, 2]]) * -1e30).astype(np.float32)
    oh = np.zeros((3, 32, 32, 128), np.float32)
    oh[:, np.arange(32), np.arange(32), :] = 1.0
    common["onehot3"] = oh.reshape(96, 32, 128).astype(ml_dtypes.bfloat16)
    common["negmb"] = common["negm"].astype(ml_dtypes.bfloat16)
    common["ident_f"] = np.eye(128, dtype=np.float32)
    in_maps = []
    for b in cores:
        m = dict(common)
        m["xin"] = np.ascontiguousarray(np.concatenate([f(inputs["ctx"])[b], f(inputs["x"])[b]], axis=0))
        m["c2"] = np.ascontiguousarray(np.stack([f(inputs["c_ctx"]), f(inputs["c"])[b]], axis=0))
        in_maps.append(m)
    res = run_bass_kernel_spmd(nc, in_maps, core_ids=list(range(len(cores))))
    return [r["y"] for r in res.results]


FUSED = True


def kernel(**inputs):
    if FUSED:
        outs = run(inputs)
        return np.stack(outs, axis=0).astype(np.float32)
    cur = dict(inputs)
    for L in (0, 1, 2):
        outs = run(cur, layers=(L,), debug_x=True)
        st = np.stack(outs, axis=0)
        cur["ctx"] = np.ascontiguousarray(st[:, :CTX])
        cur["x"] = np.ascontiguousarray(st[:, CTX:])
    outs = run(cur, layers=(3,), debug_x=False)
    return np.stack(outs, axis=0).astype(np.float32)
```

```python
import math
from contextlib import ExitStack

import numpy as np
import ml_dtypes
import concourse.bass as bass
import concourse.mybir as mybir
from concourse.bass_utils import run_bass_kernel_spmd

F32 = mybir.dt.float32
BF16 = mybir.dt.bfloat16
AF = mybir.ActivationFunctionType
ALU = mybir.AluOpType
AX = mybir.AxisListType

D = 1024
SEQ = 8192
CTX = 256
T = SEQ + CTX
NT = T // 128
EPS = 1e-6
EPOCH = 30000

BLOCKS = [(0, 256, 0)] + [(256 + 512 * i, 512, 1) for i in range(16)]
NB = len(BLOCKS)


class Buf:
    __slots__ = ("w", "r")

    def __init__(self):
        self.w = None
        self.r = {}


class TT:
    def __init__(self, t):
        self.t = t
        self.b = Buf()

    def __getitem__(self, idx):
        return self.t[idx]


class Ring:
    def __init__(self, items):
        self.items = items
        self.i = 0

    def next(self):
        it = self.items[self.i % len(self.items)]
        self.i += 1
        return it


class KB:
    def __init__(self, nc, es):
        self.nc = nc
        self.es = es
        self.eng = {"pe": nc.tensor, "act": nc.scalar, "dve": nc.vector, "pool": nc.gpsimd, "sp": nc.sync}
        self.sems = {e: [] for e in self.eng}
        self.cnt = {e: 0 for e in self.eng}
        self.seen = {e: {} for e in self.eng}
        self.last = {e: None for e in self.eng}
        self.slots = {}
        self.slot_i = {}
        for q in ("sp", "pool", "act"):
            self.slots[q] = [[es.enter_context(nc.semaphore(f"d_{q}_{i}")), 0, f"d_{q}_{i}"] for i in range(12)]
            self.slot_i[q] = 0
        self.nsb = 0

    def sb(self, es, shape, dt, name=None):
        self.nsb += 1
        return TT(es.enter_context(self.nc.sbuf_tensor(name or f"sb{self.nsb}", list(shape), dt)))

    def dram(self, shape, dt, name):
        h = self.nc.dram_tensor(name, list(shape), dt, kind="Internal")
        return TT(h.ap())

    def _wait(self, e, deps):
        seen = self.seen[e]
        for ev in deps:
            key, sem, val, src = ev
            if src == "pe" and e == "pe":
                continue
            if seen.get(key, 0) >= val:
                continue
            self.eng[e].wait_ge(sem, val)
            seen[key] = val

    def _deps(self, reads, writes):
        deps = []
        for t in reads:
            if t.b.w is not None:
                deps.append(t.b.w)
        for t in writes:
            if t.b.w is not None:
                deps.append(t.b.w)
            deps.extend(t.b.r.values())
        return deps

    def _mark(self, ev, reads, writes):
        for t in reads:
            t.b.r[ev[0]] = ev
        for t in writes:
            t.b.w = ev
            t.b.r = {}

    def op(self, e, fn, r=(), w=()):
        self._wait(e, self._deps(r, w))
        ins = fn(self.eng[e])
        epoch = self.cnt[e] // EPOCH
        while len(self.sems[e]) <= epoch:
            self.sems[e].append(self.es.enter_context(self.nc.semaphore(f"s_{e}_{len(self.sems[e])}")))
        sem = self.sems[e][epoch]
        val = self.cnt[e] % EPOCH + 1
        ins.then_inc(sem, 1)
        self.cnt[e] += 1
        ev = ((e, epoch), sem, val, e)
        self.last[e] = ev
        self._mark(ev, r, w)
        return ev

    def dma(self, q, out, in_, r=(), w=(), **kw):
        deps = self._deps(r, w)
        slots = self.slots[q]
        si = self.slot_i[q] % len(slots)
        self.slot_i[q] += 1
        slot = slots[si]
        if slot[1] > 0:
            deps.append((slot[2], slot[0], 16 * slot[1], "dma"))
        self._wait(q, deps)
        ins = self.eng[q].dma_start(out=out, in_=in_, **kw)
        ins.then_inc(slot[0], 16)
        slot[1] += 1
        ev = (slot[2], slot[0], 16 * slot[1], "dma")
        self._mark(ev, r, w)
        return ev

    def barrier(self):
        evs = [self.last[e] for e in self.eng if self.last[e] is not None]
        for q in self.slots:
            for slot in self.slots[q]:
                if slot[1] > 0:
                    evs.append((slot[2], slot[0], 16 * slot[1], "dma"))
        for e in self.eng:
            seen = self.seen[e]
            for ev in evs:
                key, sem, val, src = ev
                if src == e:
                    continue
                if seen.get(key, 0) >= val:
                    continue
                self.eng[e].wait_ge(sem, val)
                seen[key] = val


class Prog:
    def __init__(self, nc, layers=(0, 1, 2, 3), debug_x=False, stop=None):
        self.nc = nc
        self.stop = stop
        self.layers = layers
        self.debug_x = debug_x

    def din(self, name, shape, dt=F32):
        return TT(self.nc.dram_tensor(name, list(shape), dt, kind="ExternalInput").ap())

    def build(self):
        nc = self.nc
        with ExitStack() as es:
            self.k = k = KB(nc, es)
            self.es = es
            self.xin = self.din("xin", [T, D])
            self.c2 = self.din("c2", [2, D])
            self.ada_w = self.din("ada_w", [4, D, 3 * D])
            self.ada_b = self.din("ada_b", [4, 3 * D])
            self.norm_g = self.din("norm_g", [4, D])
            self.final_g = self.din("final_g", [D])
            self.W = {}
            for nm, shp in WSHAPES.items():
                self.W[nm] = self.din(nm, shp)
            self.identb_d = self.din("ident_bf", [128, 128], BF16)
            self.rope_mla = self.din("rope_mla", [T, 2, 32])
            self.rope_gqa = self.din("rope_gqa", [T, 2, 64])
            self.tri_d = self.din("tri", [4, 128, 128])
            self.negm_d = self.din("negm", [2, 128, 128])
            self.onehot_d = self.din("onehot3", [96, 32, 128], BF16)
            self.negmb_d = self.din("negmb", [2, 128, 128], BF16)
            self.identf_d = self.din("ident_f", [128, 128])
            if self.debug_x:
                self.out = TT(nc.dram_tensor("y", [T, D], F32, kind="ExternalOutput").ap())
            else:
                self.out = TT(nc.dram_tensor("y", [SEQ, D], F32, kind="ExternalOutput").ap())
            self.xres = k.dram([T, D], F32, "xres")
            self.xblk = [TT(self.xres.t) for _ in range(NB)]
            self.modd = k.dram([4, 2, 3 * D], F32, "modd")
            self.identb = k.sb(es, [128, 128], BF16, "identb")
            k.dma("sp", self.identb[:, :], self.identb_d.t[:, :], w=[self.identb])
            self.ones_f = k.sb(es, [128, 128], F32, "ones_f")
            k.op("pool", lambda g: g.memset(self.ones_f[:, :], 1.0), w=[self.ones_f])
            self.psb = [TT(es.enter_context(nc.psum_tensor(f"ps{i}", [128, 512], F32))) for i in range(8)]
            self.psg = Ring(self.psb[0:6])
            self.pso = Ring(self.psb[6:8])
            self.ps8 = Ring(self.psb)
            self.bcm = [k.sb(es, [128, 3 * D], F32, f"bcm{c}") for c in range(2)]
            self.gmod = [k.sb(es, [128, D], F32, f"gmod{c}") for c in range(2)]
            self.ngbc = k.sb(es, [128, D], F32, "ngbc")

            first = True
            for L in self.layers:
                self.modulation(L)
                if self.stop == "mod":
                    break
                xsrc = self.xin if first else None
                if L == 0:
                    self.layer_attn(L, "mla", xsrc)
                elif L == 2:
                    self.layer_attn(L, "gqa", xsrc)
                elif L == 1:
                    self.layer_gla(L, xsrc)
                elif L == 3:
                    self.layer_ssd(L, xsrc)
                first = False
                k.barrier()
            self.final(self.xin if first else None)
            k.barrier()
        return nc

    def xsrc_ap(self, xsrc, t0, n):
        if xsrc is not None:
            return xsrc.t[t0:t0 + n, :], [xsrc]
        return self.xres.t[t0:t0 + n, :], None

    def modulation(self, L):
        k = self.k
        with ExitStack() as es:
            cT = k.sb(es, [128, 8, 2], F32)
            sT = k.sb(es, [128, 8, 2], F32)
            for kk in range(8):
                k.dma("sp", cT[:, kk, :], self.c2.t[:, kk * 128:(kk + 1) * 128].rearrange("c p -> p c"), w=[cT],
                      allow_slow_non_contiguous=True)
            k.op("act", lambda a: a.activation(out=sT[:, :, :], in_=cT[:, :, :], func=AF.Silu), r=[cT], w=[sT])
            msb = k.sb(es, [2, 3 * D], F32)
            bb = k.sb(es, [2, 3 * D], F32)
            k.dma("sp", bb[:, :], self.ada_b.t[L, :].partition_broadcast(2), w=[bb])
            wr = Ring([k.sb(es, [128, 8, 512], F32) for _ in range(2)])
            for cb in range(6):
                wt = wr.next()
                k.dma("sp", wt[:, :, :],
                      self.ada_w.t[L, :, cb * 512:(cb + 1) * 512].rearrange("(k p) n -> p k n", p=128), w=[wt])
                ps = self.psg.next()
                for kk in range(8):
                    k.op("pe", lambda pe, kk=kk: pe.matmul(ps[0:2, :], lhsT=sT[:, kk, :], rhs=wt[:, kk, :],
                                                          start=(kk == 0), stop=(kk == 7)), r=[sT, wt], w=[ps])
                k.op("dve", lambda v: v.tensor_tensor(out=msb[:, cb * 512:(cb + 1) * 512], in0=ps[0:2, :],
                                                      in1=bb[:, cb * 512:(cb + 1) * 512], op=ALU.add),
                     r=[ps, bb], w=[msb])
            md = TT(self.modd.t)
            k.dma("sp", self.modd.t[L, :, :], msb[:, :], r=[msb], w=[md])
            for c in range(2):
                k.dma("sp", self.bcm[c][:, :], self.modd.t[L, c, :].partition_broadcast(128), r=[md], w=[self.bcm[c]])
            k.dma("sp", self.ngbc[:, :], self.norm_g.t[L, :].partition_broadcast(128), w=[self.ngbc])
            for c in range(2):
                k.op("dve", lambda v, c=c: v.scalar_tensor_tensor(out=self.gmod[c][:, :], in0=self.bcm[c][:, D:2 * D],
                                                                 scalar=1.0, in1=self.ngbc[:, :], op0=ALU.add,
                                                                 op1=ALU.mult),
                     r=[self.bcm[c], self.ngbc], w=[self.gmod[c]])
            k.barrier()

    def load_w(self, es_stage, dst, dview, src_ap, kc, kp, n, stg):
        k = self.k
        CH = 1024
        for kk in range(kc):
            for c0 in range(0, n, CH):
                cn = min(CH, n - c0)
                st = stg.next()
                k.dma("sp", st[0:kp, 0:cn], src_ap[kk * kp:(kk + 1) * kp, c0:c0 + cn], w=[st])
                k.op("pool", lambda g, st=st, kk=kk, c0=c0, cn=cn: g.tensor_copy(out=dview[0:kp, kk, c0:c0 + cn],
                                                                                in_=st[0:kp, 0:cn]),
                     r=[st], w=[dst])

    def norm_block(self, es, bi, xsrc, xr, hT, scr):
        k = self.k
        t0, nt, cond = BLOCKS[bi]
        junk, ss, rs, hf, hb = scr
        for j in range(nt // 128):
            xt = xr.next()
            src, rr = self.xsrc_ap(xsrc, t0 + j * 128, 128)
            k.dma("sp", xt[:, :], src, r=(rr or [self.xblk[bi]]), w=[xt])
            k.op("act", lambda a, xt=xt: a.activation(out=junk[:, :], in_=xt[:, :], func=AF.Square, accum_out=ss[:, :]),
                 r=[xt], w=[junk, ss])
            k.op("act", lambda a: a.activation(out=rs[:, :], in_=ss[:, :], func=AF.Sqrt, scale=1.0 / D, bias=self.epsb[:, :]),
                 r=[ss, self.epsb], w=[rs])
            k.op("dve", lambda v: v.reciprocal(out=rs[:, :], in_=rs[:, :]), r=[rs], w=[rs])
            k.op("dve", lambda v, xt=xt: v.scalar_tensor_tensor(out=hf[:, :], in0=xt[:, :], scalar=rs[:, 0:1],
                                                               in1=self.gmod[cond][:, :], op0=ALU.mult, op1=ALU.mult),
                 r=[xt, rs, self.gmod[cond]], w=[hf])
            k.op("dve", lambda v: v.tensor_tensor(out=hb[:, :], in0=hf[:, :], in1=self.bcm[cond][:, 0:D], op=ALU.add),
                 r=[hf, self.bcm[cond]], w=[hb])
            ps = self.psg.next()
            pv = ps[:, :].bitcast(BF16).rearrange("p (c t) -> p c t", c=8)
            for c in range(8):
                k.op("pe", lambda pe, c=c: pe.transpose(out=pv[:, c, :], in_=hb[:, c * 128:(c + 1) * 128],
                                                        identity=self.identb[:, :]),
                     r=[hb, self.identb], w=[ps])
            k.op("act", lambda a, j=j: a.copy(out=hT[:, :, j * 128:(j + 1) * 128], in_=pv), r=[ps], w=[hT])

    def layer_attn(self, L, kind, xsrc):
        k = self.k
        nc = self.nc
        if kind == "mla":
            H, HK, DQ = 16, 16, 96
            w_in = self.W["mla_w_in"].t[0]
            w_out = self.W["mla_w_out"].t[0]
            GOFF = 672
            scale = 96 ** -0.5
        else:
            H, HK, DQ = 16, 4, 64
            w_in = self.W["gqa_w_in"].t[0]
            w_out = self.W["gqa_w_out"].t[0]
            GOFF = 1536
            scale = 64 ** -0.5
        REP = H // HK
        QT = k.dram([H, DQ, T], BF16, f"QT{L}")
        KT = k.dram([HK, DQ, T], BF16, f"KT{L}")
        VV = k.dram([HK, 128, NT, 65], BF16, f"VV{L}")
        GS = k.dram([8, 128, T], BF16, f"GS{L}")
        OG = k.dram([NB, 64, 16, 512], BF16, f"OG{L}")

        with ExitStack() as es:
            self.epsb = k.sb(es, [128, 1], F32)
            k.op("pool", lambda g: g.memset(self.epsb[:, :], EPS), w=[self.epsb])
            stg = Ring([k.sb(es, [128, 1024], F32) for _ in range(2)])
            NIN = 1696 if kind == "mla" else 2560
            win = k.sb(es, [128, 8, NIN], BF16)
            self.load_w(es, win, win, w_in, 8, 128, NIN, stg)
            if kind == "mla":
                wuq = k.sb(es, [128, 3, 1536], BF16)
                self.load_w(es, wuq, wuq, self.W["mla_w_uq"].t[0], 3, 128, 1536, stg)
                wukv = k.sb(es, [128, 2, 2048], BF16)
                self.load_w(es, wukv, wukv, self.W["mla_w_ukv"].t[0], 2, 128, 2048, stg)
                qnbc = k.sb(es, [128, 384], F32)
                k.dma("sp", qnbc[:, :], self.W["mla_q_norm"].t[0, :].partition_broadcast(128), w=[qnbc])
                kvnbc = k.sb(es, [128, 256], F32)
                k.dma("sp", kvnbc[:, :], self.W["mla_kv_norm"].t[0, :].partition_broadcast(128), w=[kvnbc])
                RD, HF = 32, 8
                rope_d = self.rope_mla
            else:
                qnbc = k.sb(es, [128, 64], F32)
                k.dma("sp", qnbc[:, :], self.W["gqa_q_norm"].t[0, :].partition_broadcast(128), w=[qnbc])
                knbc = k.sb(es, [128, 64], F32)
                k.dma("sp", knbc[:, :], self.W["gqa_k_norm"].t[0, :].partition_broadcast(128), w=[knbc])
                RD, HF = 64, 16
                rope_d = self.rope_gqa
            xr = Ring([k.sb(es, [128, D], F32) for _ in range(2)])
            hTr = Ring([k.sb(es, [128, 8, 512], BF16) for _ in range(2)])
            scr = (k.sb(es, [128, D], BF16), k.sb(es, [128, 1], F32), k.sb(es, [128, 1], F32),
                   k.sb(es, [128, D], F32), k.sb(es, [128, D], BF16))
            qsb = k.sb(es, [128, H * DQ], F32)
            qb = k.sb(es, [128, H, DQ], BF16)
            kb = k.sb(es, [128, HK, DQ], BF16)
            ksb = k.sb(es, [128, HK * DQ if kind == "gqa" else 32], F32)
            vblk = Ring([k.sb(es, [128, 4, HK, 65], BF16) for _ in range(1)])
            for vb_ in vblk.items:
                k.op("pool", lambda g, vb_=vb_: g.memset(vb_[:, :, :, :], 1.0), w=[vb_])
            qTb = Ring([k.sb(es, [DQ, H, 512], BF16) for _ in range(1)])
            kTb = Ring([k.sb(es, [DQ, HK, 512], BF16) for _ in range(1)])
            rtab = Ring([k.sb(es, [128, 2, RD], F32) for _ in range(2)])
            ra = k.sb(es, [128, H, RD], F32)
            rb_ = k.sb(es, [128, H, RD], F32)
            ss2 = k.sb(es, [128, 32], F32)
            rs2 = k.sb(es, [128, 32], F32)
            sq = k.sb(es, [128, H * DQ], F32)
            if kind == "mla":
                cqn = k.sb(es, [128, 640], BF16)
                cT = k.sb(es, [128, 5, 128], BF16)
            gsr = Ring([k.sb(es, [128, 512], BF16) for _ in range(2)])

            def rope(xv, nh, dst, tab):
                cosb = tab[:, 0:1, :].broadcast_to([128, nh, RD])
                k.op("dve", lambda v: v.tensor_tensor(out=ra[:, 0:nh, :], in0=xv, in1=cosb, op=ALU.mult),
                     r=[tab, qsb, ksb], w=[ra])
                x5 = xv.rearrange("p h (g s f) -> p h g s f", g=2, s=2)
                b5 = rb_[:, 0:nh, :].rearrange("p h (g s f) -> p h g s f", g=2, s=2)
                s5 = tab[:, 1, :].rearrange("p (g s f) -> p g s f", g=2, s=2)
                for g in range(2):
                    for s in range(2):
                        sinb = s5[:, g:g + 1, s, :].broadcast_to([128, nh, HF])
                        k.op("dve", lambda v, g=g, s=s, sinb=sinb: v.tensor_tensor(
                            out=b5[:, :, g, s, :], in0=x5[:, :, g, 1 - s, :], in1=sinb, op=ALU.mult),
                            r=[tab, qsb, ksb], w=[rb_])
                k.op("dve", lambda v: v.tensor_tensor(out=dst, in0=ra[:, 0:nh, :], in1=rb_[:, 0:nh, :], op=ALU.add),
                     r=[ra, rb_], w=[qb, kb])

            import os
            for bi, (t0, nt, cond) in enumerate(BLOCKS[:int(os.environ.get('DBG_P1_BLOCKS', NB))]):
                hT = hTr.next()
                self.norm_block(es, bi, xsrc, xr, hT, scr)
                ntile = nt // 128
                vb4 = vblk.next()
                qT = qTb.next()
                kT = kTb.next()
                for j in range(ntile):
                    tt0 = t0 + j * 128
                    kt = tt0 // 128
                    tab = rtab.next()
                    k.dma("sp", tab[:, :, :], rope_d.t[tt0:tt0 + 128, :, :], w=[tab])
                    hTj = lambda kk: hT[:, kk, j * 128:(j + 1) * 128]
                    if kind == "mla":
                        psA = self.psg.next()
                        psB = self.psg.next()
                        for kk in range(8):
                            k.op("pe", lambda pe, kk=kk: pe.matmul(psA[:, 0:384], lhsT=hTj(kk), rhs=win[:, kk, 0:384],
                                                                  start=(kk == 0), stop=(kk == 7)), r=[hT, win], w=[psA])
                        for kk in range(8):
                            k.op("pe", lambda pe, kk=kk: pe.matmul(psB[:, 0:288], lhsT=hTj(kk), rhs=win[:, kk, 384:672],
                                                                  start=(kk == 0), stop=(kk == 7)), r=[hT, win], w=[psB])
                        for (ps_, n_, gb_, o_) in ((psA, 384, qnbc, 0), (psB, 256, kvnbc, 384)):
                            k.op("act", lambda a, ps_=ps_, n_=n_: a.activation(out=sq[:, 0:n_], in_=ps_[:, 0:n_], func=AF.Square,
                                                                              accum_out=ss2[:, 0:1]), r=[ps_], w=[sq, ss2])
                            k.op("act", lambda a, n_=n_: a.activation(out=rs2[:, 0:1], in_=ss2[:, 0:1], func=AF.Sqrt,
                                                                     scale=1.0 / n_, bias=self.epsb[:, :]),
                                 r=[ss2, self.epsb], w=[rs2])
                            k.op("dve", lambda v: v.reciprocal(out=rs2[:, 0:1], in_=rs2[:, 0:1]), r=[rs2], w=[rs2])
                            k.op("dve", lambda v, ps_=ps_, n_=n_, gb_=gb_, o_=o_: v.scalar_tensor_tensor(
                                out=cqn[:, o_:o_ + n_], in0=ps_[:, 0:n_], scalar=rs2[:, 0:1], in1=gb_[:, :],
                                op0=ALU.mult, op1=ALU.mult), r=[ps_, rs2, gb_], w=[cqn])
                        k.op("act", lambda a: a.copy(out=ksb[:, 0:32], in_=psB[:, 256:288]), r=[psB], w=[ksb])
                        pst = self.psg.next()
                        ptv = pst[:, :].bitcast(BF16).rearrange("p (c t) -> p c t", c=8)
                        for c in range(5):
                            k.op("pe", lambda pe, c=c: pe.transpose(out=ptv[:, c, :], in_=cqn[:, c * 128:(c + 1) * 128],
                                                                    identity=self.identb[:, :]), r=[cqn, self.identb], w=[pst])
                        k.op("act", lambda a: a.copy(out=cT[:, :, :], in_=ptv[:, 0:5, :]), r=[pst], w=[cT])
                        for cb in range(3):
                            ps = self.psg.next()
                            for kk in range(3):
                                k.op("pe", lambda pe, kk=kk, cb=cb, ps=ps: pe.matmul(
                                    ps[:, :], lhsT=cT[:, kk, :], rhs=wuq[:, kk, cb * 512:(cb + 1) * 512],
                                    start=(kk == 0), stop=(kk == 2)), r=[cT, wuq], w=[ps])
                            k.op("act", lambda a, cb=cb, ps=ps: a.copy(out=qsb[:, cb * 512:(cb + 1) * 512], in_=ps[:, :]),
                                 r=[ps], w=[qsb])
                        q3 = qsb[:, :].rearrange("p (h d) -> p h d", h=16)
                        k.op("pool", lambda g: g.tensor_copy(out=qb[:, :, 0:64], in_=q3[:, :, 0:64]), r=[qsb], w=[qb])
                        rope(q3[:, :, 64:96], 16, qb[:, :, 64:96], tab)
                        krv = ksb[:, 0:32].rearrange("p (h d) -> p h d", h=1)
                        rope(krv, 1, kb[:, 0:1, 64:96], tab)
                        k.op("pool", lambda g: g.tensor_copy(out=kb[:, 1:16, 64:96],
                                                             in_=kb[:, 0:1, 64:96].broadcast_to([128, 15, 32])),
                             r=[kb], w=[kb])
                        for cb in range(4):
                            ps = self.psg.next()
                            for kk in range(2):
                                k.op("pe", lambda pe, kk=kk, cb=cb, ps=ps: pe.matmul(
                                    ps[:, :], lhsT=cT[:, 3 + kk, :], rhs=wukv[:, kk, cb * 512:(cb + 1) * 512],
                                    start=(kk == 0), stop=(kk == 1)), r=[cT, wukv], w=[ps])
                            p3 = ps[:, :].rearrange("p (h d) -> p h d", h=4)
                            k.op("act", lambda a, cb=cb, p3=p3: a.copy(out=kb[:, cb * 4:(cb + 1) * 4, 0:64], in_=p3[:, :, 0:64]),
                                 r=[ps], w=[kb])
                            k.op("dve", lambda v, cb=cb, p3=p3: v.tensor_copy(out=vb4[:, j, cb * 4:(cb + 1) * 4, 0:64],
                                                                              in_=p3[:, :, 64:128]), r=[ps], w=[vb4])
                    else:
                        import os
                        for cb in range(3 if int(os.environ.get('DBG_STEP', 9)) >= 1 else 0):
                            ps = self.psg.next()
                            for kk in range(8):
                                k.op("pe", lambda pe, kk=kk, cb=cb, ps=ps: pe.matmul(
                                    ps[:, :], lhsT=hTj(kk), rhs=win[:, kk, cb * 512:(cb + 1) * 512],
                                    start=(kk == 0), stop=(kk == 7)), r=[hT, win], w=[ps])
                            SUB = os.environ.get('DBG_SUB', 'abc')
                            if cb < 2:
                                if 'a' in SUB:
                                    k.op("act", lambda a, cb=cb, ps=ps: a.copy(out=qsb[:, cb * 512:(cb + 1) * 512], in_=ps[:, :]),
                                         r=[ps], w=[qsb])
                            elif 'b' in SUB:
                                k.op("act", lambda a, ps=ps: a.copy(out=ksb[:, 0:256], in_=ps[:, 0:256]), r=[ps], w=[ksb])
                                p3 = ps[:, 256:512].rearrange("p (h d) -> p h d", h=4)
                                if 'c' in SUB:
                                    for hh in range(4):
                                        k.op("act", lambda a, hh=hh: a.copy(out=vb4[:, j, hh, 0:64], in_=ps[:, 256 + hh * 64:256 + (hh + 1) * 64]),
                                             r=[ps], w=[vb4])
                        import os
                        DS = int(os.environ.get('DBG_STEP', 9))
                        for (src_, nh, gb_, dstb) in ((qsb, 16, qnbc, qb), (ksb, 4, knbc, kb)) if DS >= 2 else ():
                            s3 = src_[:, 0:nh * 64].rearrange("p (h d) -> p h d", h=nh)
                            sq3 = sq[:, 0:nh * 64].rearrange("p (h d) -> p h d", h=nh)
                            k.op("dve", lambda v, s3=s3, sq3=sq3: v.tensor_tensor(out=sq3, in0=s3, in1=s3, op=ALU.mult),
                                 r=[src_], w=[sq])
                            k.op("dve", lambda v, sq3=sq3, nh=nh: v.tensor_reduce(out=ss2[:, 0:nh], in_=sq3, axis=AX.X, op=ALU.add),
                                 r=[sq], w=[ss2])
                            k.op("act", lambda a, nh=nh: a.activation(out=rs2[:, 0:nh], in_=ss2[:, 0:nh], func=AF.Sqrt,
                                                                     scale=1.0 / 64, bias=self.epsb[:, :]),
                                 r=[ss2, self.epsb], w=[rs2])
                            k.op("dve", lambda v, nh=nh: v.reciprocal(out=rs2[:, 0:nh], in_=rs2[:, 0:nh]), r=[rs2], w=[rs2])
                            k.op("dve", lambda v, s3=s3, nh=nh: v.tensor_tensor(
                                out=s3, in0=s3, in1=rs2[:, 0:nh].unsqueeze(2).broadcast_to([128, nh, 64]), op=ALU.mult),
                                r=[src_, rs2], w=[src_])
                            k.op("dve", lambda v, s3=s3, nh=nh, gb_=gb_: v.tensor_tensor(
                                out=s3, in0=s3, in1=gb_[:, :].unsqueeze(1).broadcast_to([128, nh, 64]), op=ALU.mult),
                                r=[src_, gb_], w=[src_])
                            if DS >= 3:
                                rope(s3, nh, dstb[:, :, :], tab)
                    import os
                    for (srcb, nh, dstT) in ((qb, H, qT), (kb, HK, kT)) if int(os.environ.get('DBG_STEP', 9)) >= 4 else ():
                        for h0 in range(0, nh, 8):
                            hn = min(8, nh - h0)
                            ps = self.psg.next()
                            ptv = ps[:, :].bitcast(BF16).rearrange("p (c t) -> p c t", c=8)
                            for hh in range(hn):
                                k.op("pe", lambda pe, hh=hh, h0=h0, ptv=ptv, srcb=srcb: pe.transpose(
                                    out=ptv[0:DQ, hh, :], in_=srcb[:, h0 + hh, :], identity=self.identb[:, :]),
                                    r=[srcb, self.identb], w=[ps])
                            k.op("act", lambda a, h0=h0, hn=hn, ptv=ptv, dstT=dstT: a.copy(
                                out=dstT[:, h0:h0 + hn, j * 128:(j + 1) * 128], in_=ptv[0:DQ, 0:hn, :]), r=[ps], w=[dstT])
                for hp in range(8):
                    ps = self.psg.next()
                    for kk in range(8):
                        k.op("pe", lambda pe, kk=kk, hp=hp, ps=ps: pe.matmul(
                            ps[:, 0:nt], lhsT=win[:, kk, GOFF + hp * 128:GOFF + (hp + 1) * 128], rhs=hT[:, kk, 0:nt],
                            start=(kk == 0), stop=(kk == 7)), r=[hT, win], w=[ps])
                    gs = gsr.next()
                    k.op("act", lambda a, ps=ps, gs=gs: a.activation(out=gs[:, 0:nt], in_=ps[:, 0:nt], func=AF.Silu),
                         r=[ps], w=[gs])
                    k.dma("pool", GS.t[hp, :, t0:t0 + nt], gs[:, 0:nt], r=[gs], w=[GS])
                for h0 in range(0, H, 4):
                    k.dma("act", QT.t[h0:h0 + 4, :, t0:t0 + nt].rearrange("h d t -> d h t"), qT[:, h0:h0 + 4, 0:nt], r=[qT], w=[QT])
                for h0 in range(0, HK, 4):
                    k.dma("act", KT.t[h0:h0 + 4, :, t0:t0 + nt].rearrange("h d t -> d h t"), kT[:, h0:h0 + 4, 0:nt], r=[kT], w=[KT])
                kt0 = t0 // 128
                for j in range(ntile):
                    for h0 in range(0, HK, 4):
                        k.dma("act", VV.t[h0:h0 + 4, :, kt0 + j, :].rearrange("h p e -> p h e"), vb4[:, j, h0:h0 + 4, :],
                              r=[vb4], w=[VV], allow_slow_non_contiguous=True)
            k.barrier()

        if self.stop == "p1":
            return
        with ExitStack() as es:
            DQP = 128 if DQ == 64 else DQ
            KTs = Ring([k.sb(es, [DQP, T], BF16) for _ in range(2)])
            Vs = Ring([k.sb(es, [128, NT, 65], BF16) for _ in range(2)])
            Qs = Ring([k.sb(es, [DQP, T], BF16) for _ in range(2)])
            if DQP != DQ:
                for b_ in KTs.items + Qs.items:
                    k.op("pool", lambda g, b_=b_: g.memset(b_[DQ:DQP, :], 0.0), w=[b_])
            Gs = Ring([k.sb(es, [64, T], BF16) for _ in range(2)])
            Ps = Ring([k.sb(es, [128, 512], BF16) for _ in range(6)])
            rr = k.sb(es, [65, 512], F32)
            tmp = k.sb(es, [64, 512], F32)
            ogr = Ring([k.sb(es, [64, 512], BF16) for _ in range(2)])
            pss = Ring(self.psb[0:5])
            psm = Ring(self.psb[5:6])
            import os
            pending_epi = [None]
            for hk in range(int(os.environ.get('DBG_P2_HEADS', HK))):
                Kt = KTs.next()
                Vt = Vs.next()
                k.dma("sp", Kt[0:DQ, :], KT.t[hk], r=[KT], w=[Kt])
                k.dma("sp", Vt[:, :, :], VV.t[hk], r=[VV], w=[Vt])
                for hr in range(REP):
                    h = hk * REP + hr
                    Qt = Qs.next()
                    Gt = Gs.next()
                    k.dma("sp", Qt[0:DQ, :], QT.t[h], r=[QT], w=[Qt])
                    k.dma("sp", Gt[:, :], GS.t[h // 2, (h % 2) * 64:(h % 2) * 64 + 64, :], r=[GS], w=[Gt])
                    for bi, (t0, nt, cond) in enumerate(BLOCKS):
                        nkt = 2 if cond == 0 else NT
                        po = self.pso.next()
                        pend = []

                        def s_mm(kt):
                            ps = pss.next()
                            k.op("pe", lambda pe: pe.matmul(ps[:, 0:nt], lhsT=Kt[:, kt * 128:(kt + 1) * 128], rhs=Qt[:, t0:t0 + nt],
                                                            start=True, stop=True), r=[Kt, Qt], w=[ps])
                            pt = Ps.next()
                            k.op("act", lambda a: a.activation(out=pt[:, 0:nt], in_=ps[:, 0:nt], func=AF.Exp, scale=scale),
                                 r=[ps], w=[pt])
                            return pt

                        def pv_mm(kt, pt):
                            k.op("pe", lambda pe: pe.matmul(po[0:65, 0:nt], lhsT=Vt[:, kt, :], rhs=pt[:, 0:nt],
                                                            start=(kt == 0), stop=(kt == nkt - 1)), r=[Vt, pt], w=[po])

                        SK = 3
                        for kt in range(nkt + SK):
                            if kt < nkt:
                                pend.append((kt, s_mm(kt)))
                            if kt >= SK:
                                a_, b_ = pend.pop(0)
                                pv_mm(a_, b_)
                            if kt == 10 and pending_epi[0] is not None:
                                pending_epi[0]()
                                pending_epi[0] = None
                        if pending_epi[0] is not None:
                            pending_epi[0]()
                            pending_epi[0] = None

                        def make_epi(po, Gt, t0, nt, bi, h):
                            def epi():
                                k.op("dve", lambda v: v.reciprocal(out=rr[64:65, 0:nt], in_=po[64:65, 0:nt]), r=[po], w=[rr])
                                pm = psm.next()
                                k.op("pe", lambda pe: pe.matmul(pm[0:64, 0:nt], lhsT=self.ones_f[64:65, 0:64], rhs=rr[64:65, 0:nt],
                                                                start=True, stop=True), r=[rr, self.ones_f], w=[pm])
                                k.op("dve", lambda v: v.tensor_tensor(out=tmp[:, 0:nt], in0=pm[0:64, 0:nt], in1=Gt[:, t0:t0 + nt],
                                                                      op=ALU.mult), r=[pm, Gt], w=[tmp])
                                og = ogr.next()
                                k.op("dve", lambda v: v.tensor_tensor(out=og[:, 0:nt], in0=po[0:64, 0:nt], in1=tmp[:, 0:nt], op=ALU.mult),
                                     r=[po, tmp], w=[og])
                                k.dma("pool", OG.t[bi, :, h, 0:nt], og[:, 0:nt], r=[og], w=[OG])
                            return epi
                        pending_epi[0] = make_epi(po, Gt, t0, nt, bi, h)
            if pending_epi[0] is not None:
                pending_epi[0]()
                pending_epi[0] = None
            k.barrier()

        if self.stop == "p2":
            return
        with ExitStack() as es:
            stg = Ring([k.sb(es, [128, 1024], F32) for _ in range(2)])
            wo = k.sb(es, [128, 8, D], BF16)
            self.load_w(es, wo, wo, w_out, 8, 128, D, stg)
            ogb = Ring([k.sb(es, [128, 8, 512], BF16) for _ in range(2)])
            xr = Ring([k.sb(es, [128, D], F32) for _ in range(3)])
            tm = k.sb(es, [128, 512], F32)
            self.outproj(es, lambda bi, nt: (OG.t[bi, :, :, 0:nt], OG), ogb, 128, 8, wo, xsrc, xr, tm)
            k.barrier()


    def layer_gla(self, L, xsrc):
        k = self.k
        w_in = self.W["gla_w_in"].t[0]
        w_out = self.W["gla_w_out"].t[0]
        REC = k.dram([NT, 128, 3584], BF16, f"GREC{L}")
        GG = k.dram([NT, 128, 1024], F32, f"GGG{L}")
        GOF = k.dram([NT, 128, 1024], F32, f"GOF{L}")
        recT = [TT(REC.t) for _ in range(NT)]
        ggT = [TT(GG.t) for _ in range(NT)]
        ofT = [TT(GOF.t) for _ in range(NT)]
        ps8 = self.ps8
        with ExitStack() as es:
            self.epsb = k.sb(es, [128, 1], F32)
            k.op("pool", lambda g: g.memset(self.epsb[:, :], EPS), w=[self.epsb])
            onec = k.sb(es, [128, 1], F32)
            k.op("pool", lambda g: g.memset(onec[:, :], 1.0), w=[onec])
            stg = Ring([k.sb(es, [128, 1024], F32) for _ in range(2)])
            win = k.sb(es, [128, 8, 3104], BF16)
            self.load_w(es, win, win, w_in, 8, 128, 3104, stg)
            wg = k.sb(es, [16, 2, 512], F32)
            k.dma("sp", wg[:, 0, :], self.W["gla_w_gf"].t[0], w=[wg])
            k.dma("sp", wg[:, 1, :], self.W["gla_w_gb"].t[0], w=[wg])
            bg = k.sb(es, [128, 2, 512], F32)
            k.dma("sp", bg[:, 0, :], self.W["gla_b_gf"].t[0, :].partition_broadcast(128), w=[bg])
            k.dma("sp", bg[:, 1, :], self.W["gla_b_gb"].t[0, :].partition_broadcast(128), w=[bg])
            xr = Ring([k.sb(es, [128, D], F32) for _ in range(2)])
            hTr = Ring([k.sb(es, [128, 8, 512], BF16) for _ in range(2)])
            scr = (k.sb(es, [128, D], BF16), k.sb(es, [128, 1], F32), k.sb(es, [128, 1], F32),
                   k.sb(es, [128, D], F32), k.sb(es, [128, D], BF16))
            recr = Ring([k.sb(es, [128, 3584], BF16) for _ in range(2)])
            ggr = Ring([k.sb(es, [128, 1024], F32) for _ in range(2)])
            rT = k.sb(es, [16, 2, 128], F32)
            zt = k.sb(es, [128, 512], F32)
            for bi, (t0, nt, cond) in enumerate(BLOCKS):
                hT = hTr.next()
                self.norm_block(es, bi, xsrc, xr, hT, scr)
                for j in range(nt // 128):
                    t = (t0 + j * 128) // 128
                    rec = recr.next()
                    gg = ggr.next()
                    hTj = lambda kk: hT[:, kk, j * 128:(j + 1) * 128]

                    def tokmm(c0, n):
                        ps = ps8.next()
                        for kk in range(8):
                            k.op("pe", lambda pe: pe.matmul(ps[:, 0:n], lhsT=hTj(kk), rhs=win[:, kk, c0:c0 + n],
                                                            start=(kk == 0), stop=(kk == 7)), r=[hT, win], w=[ps])
                        return ps

                    ps = tokmm(512, 512)
                    k.op("act", lambda a: a.copy(out=rec[:, 1024:1536], in_=ps[:, :]), r=[ps], w=[rec])
                    for cb in range(2):
                        ps = tokmm(1024 + cb * 512, 512)
                        k.op("dve", lambda v: v.tensor_copy(out=rec[:, 1536 + cb * 512:2048 + cb * 512], in_=ps[:, :]),
                             r=[ps], w=[rec])
                    for cb in range(2):
                        ps = tokmm(2048 + cb * 512, 512)
                        k.op("act", lambda a: a.activation(out=rec[:, 2560 + cb * 512:3072 + cb * 512], in_=ps[:, :],
                                                           func=AF.Silu), r=[ps], w=[rec])
                    for qk in range(2):
                        ps = ps8.next()
                        for h in range(4):
                            for kk in range(8):
                                k.op("pe", lambda pe: pe.matmul(
                                    ps[:, h * 128:(h + 1) * 128], lhsT=win[:, kk, qk * 512 + h * 128:qk * 512 + (h + 1) * 128],
                                    rhs=hTj(kk), start=(kk == 0), stop=(kk == 7)), r=[hT, win], w=[ps])
                        if qk == 0:
                            k.op("act", lambda a: a.mul(out=rec[:, 0:512], in_=ps[:, :], mul=128 ** -0.5), r=[ps], w=[rec])
                        else:
                            k.op("dve", lambda v: v.tensor_copy(out=rec[:, 512:1024], in_=ps[:, :]), r=[ps], w=[rec])
                    ps = ps8.next()
                    for d in range(2):
                        for kk in range(8):
                            k.op("pe", lambda pe: pe.matmul(
                                ps[0:16, d * 128:(d + 1) * 128], lhsT=win[:, kk, 3072 + 16 * d:3088 + 16 * d],
                                rhs=hTj(kk), start=(kk == 0), stop=(kk == 7)), r=[hT, win], w=[ps])
                    k.op("act", lambda a: a.copy(out=rT[:, :, :], in_=ps[0:16, 0:256].rearrange("p (d t) -> p d t", d=2)),
                         r=[ps], w=[rT])
                    for d in range(2):
                        ps = ps8.next()
                        k.op("pe", lambda pe: pe.matmul(ps[:, :], lhsT=rT[:, d, :], rhs=wg[:, d, :], start=True, stop=True),
                             r=[rT, wg], w=[ps])
                        k.op("dve", lambda v: v.tensor_tensor(out=zt[:, :], in0=ps[:, :], in1=bg[:, d, :], op=ALU.add),
                             r=[ps, bg], w=[zt])
                        k.op("act", lambda a: a.activation(out=zt[:, :], in_=zt[:, :], func=AF.Exp, scale=-1.0), r=[zt], w=[zt])
                        k.op("act", lambda a: a.activation(out=zt[:, :], in_=zt[:, :], func=AF.Ln, bias=onec[:, :]),
                             r=[zt, onec], w=[zt])
                        k.op("dve", lambda v: v.tensor_scalar(out=gg[:, d * 512:(d + 1) * 512], in0=zt[:, :],
                                                              scalar1=-1.0 / 16.0, scalar2=None, op0=ALU.mult),
                             r=[zt], w=[gg])
                    k.dma("pool", REC.t[t], rec[:, :], r=[rec], w=[recT[t]])
                    k.dma("pool", GG.t[t], gg[:, :], r=[gg], w=[ggT[t]])
            k.barrier()

        with ExitStack() as es:
            self.epsb = k.sb(es, [128, 1], F32)
            k.op("pool", lambda g: g.memset(self.epsb[:, :], EPS), w=[self.epsb])
            tri = k.sb(es, [128, 4, 128], F32)
            k.dma("sp", tri[:, :, :], self.tri_d.t.rearrange("m j i -> j m i"), w=[tri])
            S = [k.sb(es, [128, 256], F32) for _ in range(4)]
            Sb = [k.sb(es, [128, 256], BF16) for _ in range(4)]
            recr = Ring([k.sb(es, [128, 3584], BF16) for _ in range(3)])
            ggr = Ring([k.sb(es, [128, 1024], F32) for _ in range(3)])
            E1r = Ring([k.sb(es, [128, 512], F32) for _ in range(2)])
            E2r = Ring([k.sb(es, [128, 512], F32) for _ in range(2)])
            E3r = Ring([k.sb(es, [128, 512], F32) for _ in range(2)])
            qtr = Ring([k.sb(es, [128, 512], BF16) for _ in range(2)])
            ktr = Ring([k.sb(es, [128, 512], BF16) for _ in range(2)])
            khr = Ring([k.sb(es, [128, 512], BF16) for _ in range(2)])
            Amr = Ring([k.sb(es, [128, 512], BF16) for _ in range(2)])
            osb = Ring([k.sb(es, [128, 1024], F32) for _ in range(2)])
            stg = Ring([k.sb(es, [128, 1024], F32) for _ in range(2)])
            wo = k.sb(es, [128, 8, D], BF16)
            self.load_w(es, wo, wo, w_out, 8, 128, D, stg)
            onbc = k.sb(es, [128, 256], F32)
            k.dma("sp", onbc[:, :], self.W["gla_o_norm"].t[0, :].partition_broadcast(128), w=[onbc])
            ofr = Ring([k.sb(es, [128, 1024], F32) for _ in range(2)])
            xr = Ring([k.sb(es, [128, D], F32) for _ in range(2)])
            ogf = k.sb(es, [128, 1024], F32)
            ogb = k.sb(es, [128, 1024], BF16)
            ogT = k.sb(es, [128, 8, 128], BF16)
            junk = k.sb(es, [128, 256], BF16)
            ss4 = k.sb(es, [128, 4], F32)
            rs4 = k.sb(es, [128, 4], F32)
            tm = k.sb(es, [128, 512], F32)

            def stageA(d, rec, gg):
                cm = tri[:, 0 if d == 0 else 2, :]
                sm = tri[:, 1 if d == 0 else 3, :]
                g = lambda a_, b_: gg[:, d * 512 + a_:d * 512 + b_]
                E1, E2, E3 = E1r.next(), E2r.next(), E3r.next()
                qt, kt_, kh, Am = qtr.next(), ktr.next(), khr.next(), Amr.next()
                psA = ps8.next()
                for h in range(4):
                    k.op("pe", lambda pe: pe.matmul(psA[:, h * 128:(h + 1) * 128], lhsT=g(h * 128, (h + 1) * 128), rhs=cm,
                                                    start=True, stop=True), r=[gg, tri], w=[psA])
                psB = ps8.next()
                k.op("pe", lambda pe: pe.matmul(psB[:, :], lhsT=sm, rhs=g(0, 512), start=True, stop=True), r=[gg, tri], w=[psB])
                k.op("act", lambda a: a.activation(out=E1[:, :], in_=psA[:, :], func=AF.Exp), r=[psA], w=[E1])
                k.op("act", lambda a: a.activation(out=E2[:, :], in_=psA[:, :], func=AF.Exp, scale=-1.0), r=[psA], w=[E2])
                k.op("act", lambda a: a.activation(out=E3[:, :], in_=psB[:, :], func=AF.Exp), r=[psB], w=[E3])
                k.op("dve", lambda v: v.tensor_tensor(out=qt[:, :], in0=rec[:, 0:512], in1=E1[:, :], op=ALU.mult), r=[rec, E1], w=[qt])
                k.op("pool", lambda v: v.tensor_tensor(out=kt_[:, :], in0=rec[:, 512:1024], in1=E2[:, :], op=ALU.mult), r=[rec, E2], w=[kt_])
                k.op("pool", lambda v: v.tensor_tensor(out=kh[:, :], in0=rec[:, 1024:1536], in1=E3[:, :], op=ALU.mult), r=[rec, E3], w=[kh])
                return dict(d=d, rec=rec, E1=E1, qt=qt, kh=kh, Am=Am, kt_=kt_, cm=cm)

            def stageA2(c):
                qt, kt_, Am, cm = c["qt"], c["kt_"], c["Am"], c["cm"]
                psD = ps8.next()
                for h in range(4):
                    hs = slice(h * 128, (h + 1) * 128)
                    k.op("pe", lambda pe: pe.matmul(psD[:, hs], lhsT=kt_[:, hs], rhs=qt[:, hs], start=True, stop=True),
                         r=[kt_, qt], w=[psD])
                k.op("dve", lambda v: v.tensor_tensor(out=Am[:, :].rearrange("p (h i) -> p h i", h=4),
                                                      in0=psD[:, :].rearrange("p (h i) -> p h i", h=4),
                                                      in1=cm.unsqueeze(1).broadcast_to([128, 4, 128]), op=ALU.mult),
                     r=[psD, tri], w=[Am])

            def stageB(c):
                d, rec, E1, qt, kh, Am = (c[n] for n in ("d", "rec", "E1", "qt", "kh", "Am"))
                ecol = 127 if d == 0 else 0
                po = [ps8.next(), ps8.next()]
                for h in range(4):
                    hs = slice(h * 128, (h + 1) * 128)
                    bank = po[h // 2]
                    cs = slice((h % 2) * 256, (h % 2) * 256 + 256)
                    vs = slice(1536 + h * 256, 1536 + (h + 1) * 256)
                    k.op("pe", lambda pe: pe.matmul(bank[:, cs], lhsT=qt[:, hs], rhs=Sb[h][:, :], start=True, stop=False),
                         r=[qt, Sb[h]], w=[bank])
                    k.op("pe", lambda pe: pe.matmul(bank[:, cs], lhsT=Am[:, hs], rhs=rec[:, vs], start=False, stop=True),
                         r=[Am, rec], w=[bank])
                pss = [ps8.next(), ps8.next()]
                for h in range(4):
                    hs = slice(h * 128, (h + 1) * 128)
                    bank = pss[h // 2]
                    cs = slice((h % 2) * 256, (h % 2) * 256 + 256)
                    vs = slice(1536 + h * 256, 1536 + (h + 1) * 256)
                    k.op("pe", lambda pe: pe.matmul(bank[:, cs], lhsT=kh[:, hs], rhs=rec[:, vs], start=True, stop=True),
                         r=[kh, rec], w=[bank])
                    k.op("dve", lambda v: v.scalar_tensor_tensor(out=S[h][:, :], in0=S[h][:, :],
                                                                 scalar=E1[:, h * 128 + ecol:h * 128 + ecol + 1],
                                                                 in1=bank[:, cs], op0=ALU.mult, op1=ALU.add),
                         r=[S[h], E1, bank], w=[S[h]])
                    k.op("act", lambda gp: gp.copy(out=Sb[h][:, :], in_=S[h][:, :]), r=[S[h]], w=[Sb[h]])
                return po

            def reset_state():
                for h in range(4):
                    k.op("pool", lambda gp: gp.memset(S[h][:, :], 0.0), w=[S[h]])
                    k.op("pool", lambda gp: gp.memset(Sb[h][:, :], 0.0), w=[Sb[h]])

            def gl_loads(t):
                rec = recr.next()
                gg = ggr.next()
                k.dma("sp", rec[:, :], REC.t[t], r=[recT[t]], w=[rec])
                k.dma("sp", gg[:, :], GG.t[t], r=[ggT[t]], w=[gg])
                return rec, gg

            reset_state()
            cnext = stageA(0, *gl_loads(0))
            stageA2(cnext)
            for t in range(NT):
                c = cnext
                if t + 1 < NT:
                    cnext = stageA(0, *gl_loads(t + 1))
                po = stageB(c)
                if t + 1 < NT:
                    stageA2(cnext)
                ob = osb.next()
                for c in range(2):
                    k.op("act", lambda a: a.copy(out=ob[:, c * 512:(c + 1) * 512], in_=po[c][:, :]), r=[po[c]], w=[ob])
                k.dma("pool", GOF.t[t], ob[:, :], r=[ob], w=[ofT[t]])
            reset_state()
            order = [1, 0] + list(range(NT - 1, 1, -1))
            cnext = stageA(1, *gl_loads(order[0]))
            stageA2(cnext)
            osr = Ring([k.sb(es, [128, 1024], F32) for _ in range(2)])

            def gl_out(t, rec, osum):
                cond = 0 if t < 2 else 1
                bi = 0 if t < 2 else 1 + (t - 2) // 4
                xt = xr.next()
                sap, rr = self.xsrc_ap(xsrc, t * 128, 128)
                k.dma("sp", xt[:, :], sap, r=(rr or [self.xblk[bi]]), w=[xt])
                for h in range(4):
                    k.op("act", lambda a: a.activation(out=junk[:, :], in_=osum[:, h * 256:(h + 1) * 256], func=AF.Square,
                                                       accum_out=ss4[:, h:h + 1]), r=[osum], w=[junk, ss4])
                k.op("act", lambda a: a.activation(out=rs4[:, :], in_=ss4[:, :], func=AF.Sqrt, scale=1.0 / 256, bias=self.epsb[:, :]),
                     r=[ss4, self.epsb], w=[rs4])
                k.op("dve", lambda v: v.reciprocal(out=rs4[:, :], in_=rs4[:, :]), r=[rs4], w=[rs4])
                for h in range(4):
                    k.op("dve", lambda v: v.scalar_tensor_tensor(out=ogf[:, h * 256:(h + 1) * 256], in0=osum[:, h * 256:(h + 1) * 256],
                                                                 scalar=rs4[:, h:h + 1], in1=onbc[:, :], op0=ALU.mult, op1=ALU.mult),
                         r=[osum, rs4, onbc], w=[ogf])
                k.op("dve", lambda v: v.tensor_tensor(out=ogb[:, :], in0=ogf[:, :], in1=rec[:, 2560:3584], op=ALU.mult),
                     r=[ogf, rec], w=[ogb])
                self.tok_outproj(ogb, 8, ogT, wo, xt, tm, cond)
                k.dma("pool", self.xres.t[t * 128:(t + 1) * 128, :], xt[:, :], r=[xt], w=[self.xblk[bi]])

            pending = None
            for oi, t in enumerate(order):
                c = cnext
                rec = c["rec"]
                if oi + 1 < NT:
                    cnext = stageA(1, *gl_loads(order[oi + 1]))
                of = ofr.next()
                k.dma("sp", of[:, :], GOF.t[t], r=[ofT[t]], w=[of])
                po = stageB(c)
                if oi + 1 < NT:
                    stageA2(cnext)
                osum = osr.next()
                for cc in range(2):
                    k.op("dve", lambda v: v.tensor_tensor(out=osum[:, cc * 512:(cc + 1) * 512], in0=po[cc][:, :],
                                                          in1=of[:, cc * 512:(cc + 1) * 512], op=ALU.add),
                         r=[po[cc], of], w=[osum])
                if pending is not None:
                    gl_out(*pending)
                pending = (t, rec, osum)
            gl_out(*pending)
            k.barrier()


    def norm_rows(self, src, rtrk, n, cond, dst, scr, xt):
        k = self.k
        junk, ss, rs, hf, hb = scr
        k.dma("sp", xt[0:n, :], src, r=rtrk, w=[xt])
        k.op("act", lambda a: a.activation(out=junk[0:n, :], in_=xt[0:n, :], func=AF.Square, accum_out=ss[0:n, :]),
             r=[xt], w=[junk, ss])
        k.op("act", lambda a: a.activation(out=rs[0:n, :], in_=ss[0:n, :], func=AF.Sqrt, scale=1.0 / D, bias=self.epsb[0:n, :]),
             r=[ss, self.epsb], w=[rs])
        k.op("dve", lambda v: v.reciprocal(out=rs[0:n, :], in_=rs[0:n, :]), r=[rs], w=[rs])
        k.op("dve", lambda v: v.scalar_tensor_tensor(out=hf[0:n, :], in0=xt[0:n, :], scalar=rs[0:n, 0:1],
                                                     in1=self.gmod[cond][0:n, :], op0=ALU.mult, op1=ALU.mult),
             r=[xt, rs, self.gmod[cond]], w=[hf])
        k.op("dve", lambda v: v.tensor_tensor(out=hb[0:n, :], in0=hf[0:n, :], in1=self.bcm[cond][0:n, 0:D], op=ALU.add),
             r=[hf, self.bcm[cond]], w=[hb])
        ps = self.ps8.next()
        pv = ps[:, :].bitcast(BF16).rearrange("p (c t) -> p c t", c=8)
        for c in range(8):
            k.op("pe", lambda pe: pe.transpose(out=pv[:, c, 0:n], in_=hb[0:n, c * 128:(c + 1) * 128],
                                               identity=self.identb[0:n, 0:n]), r=[hb, self.identb], w=[ps])
        return ps, pv

    def layer_ssd(self, L, xsrc):
        k = self.k
        ps8 = self.ps8
        w_in = self.W["ssd_w_in"].t[0]
        w_out = self.W["ssd_w_out"].t[0]
        RX = k.dram([NT, 128, 2560], BF16, f"SRX{L}")
        RZ = k.dram([NT, 128, 2048], BF16, f"SRZ{L}")
        RBC = k.dram([NT, 128, 1024], BF16, f"SRBC{L}")
        DD = k.dram([NT, 128, 128], F32, f"SDD{L}")
        YF = k.dram([NT, 128, 2048], F32, f"SYF{L}")
        rxT = [TT(RX.t) for _ in range(NT)]
        rzT = [TT(RZ.t) for _ in range(NT)]
        rbcT = [TT(RBC.t) for _ in range(NB)]
        ddT = [TT(DD.t) for _ in range(NT)]
        yfT = [TT(YF.t) for _ in range(NT)]
        with ExitStack() as es:
            self.epsb = k.sb(es, [128, 1], F32)
            k.op("pool", lambda g: g.memset(self.epsb[:, :], EPS), w=[self.epsb])
            onec = k.sb(es, [128, 1], F32)
            k.op("pool", lambda g: g.memset(onec[:, :], 1.0), w=[onec])
            stg = Ring([k.sb(es, [128, 1024], F32) for _ in range(2)])
            wz = k.sb(es, [128, 8, 2048], BF16)
            self.load_w(es, wz, wz, w_in[:, 0:2048], 8, 128, 2048, stg)
            wx = k.sb(es, [128, 8, 3072], BF16)
            self.load_w(es, wx, wx, w_in[:, 2048:5120], 8, 128, 3072, stg)
            wdt = k.sb(es, [128, 8, 64], BF16)
            self.load_w(es, wdt, wdt, w_in[:, 5120:5184], 8, 128, 64, stg)
            cw = k.sb(es, [128, 24, 5], F32)
            for kk in range(5):
                k.dma("sp", cw[:, :, kk], self.W["ssd_conv_w"].t[0, kk, :].rearrange("(c p) -> p c", p=128), w=[cw],
                      allow_slow_non_contiguous=True)
            cbias = k.sb(es, [128, 24], F32)
            k.dma("sp", cbias[:, :], self.W["ssd_conv_b"].t[0, :].rearrange("(c p) -> p c", p=128), w=[cbias],
                  allow_slow_non_contiguous=True)
            dtb = k.sb(es, [128, 64], F32)
            k.dma("sp", dtb[:, 0:32], self.W["ssd_dt_bias_f"].t[0, :].partition_broadcast(128), w=[dtb])
            k.dma("sp", dtb[:, 32:64], self.W["ssd_dt_bias_b"].t[0, :].partition_broadcast(128), w=[dtb])
            abc = k.sb(es, [128, 64], F32)
            k.dma("sp", abc[:, 0:32], self.W["ssd_a_log_f"].t[0, :].partition_broadcast(128), w=[abc])
            k.dma("sp", abc[:, 32:64], self.W["ssd_a_log_b"].t[0, :].partition_broadcast(128), w=[abc])
            k.op("act", lambda a: a.activation(out=abc[:, :], in_=abc[:, :], func=AF.Exp), r=[abc], w=[abc])
            k.op("dve", lambda v: v.tensor_scalar(out=abc[:, :], in0=abc[:, :], scalar1=-1.0, scalar2=None, op0=ALU.mult),
                 r=[abc], w=[abc])
            xr = Ring([k.sb(es, [128, D], F32) for _ in range(2)])
            hT = k.sb(es, [128, 8, 516], BF16)
            scr = (k.sb(es, [128, D], BF16), k.sb(es, [128, 1], F32), k.sb(es, [128, 1], F32),
                   k.sb(es, [128, D], F32), k.sb(es, [128, D], BF16))
            prer = Ring([k.sb(es, [128, 516], BF16) for _ in range(3)])
            accr = Ring([k.sb(es, [128, 512], F32) for _ in range(2)])
            xc = k.sb(es, [128, 24, 512], BF16)
            rxr = Ring([k.sb(es, [128, 2560], BF16) for _ in range(2)])
            rzr = Ring([k.sb(es, [128, 2048], BF16) for _ in range(2)])
            ddr = Ring([k.sb(es, [128, 128], F32) for _ in range(2)])
            dtt = k.sb(es, [128, 64], F32)
            for bi, (t0, nt, cond) in enumerate(BLOCKS):
                ntile = nt // 128
                for j in range(ntile):
                    src, rr = self.xsrc_ap(xsrc, t0 + j * 128, 128)
                    ps, pv = self.norm_rows(src, rr or [self.xblk[bi]], 128, cond, None, scr, xr.next())
                    k.op("act", lambda a: a.copy(out=hT[:, :, j * 128:(j + 1) * 128], in_=pv), r=[ps], w=[hT])
                for side in range(2):
                    col = 512 + 2 * side
                    has = (bi >= 2) if side == 0 else (1 <= bi < NB - 1)
                    if not has:
                        k.op("pool", lambda g: g.memset(hT[:, :, col:col + 2], 0.0), w=[hT])
                    else:
                        r0 = t0 - 2 if side == 0 else t0 + nt
                        nb_ = bi - 1 if side == 0 else bi + 1
                        src, rr = self.xsrc_ap(xsrc, r0, 2)
                        ps, pv = self.norm_rows(src, rr or [self.xblk[nb_]], 2, cond, None, scr, xr.next())
                        k.op("act", lambda a: a.copy(out=hT[:, :, col:col + 2], in_=pv[:, :, 0:2]), r=[ps], w=[hT])
                for c in range(24):
                    ps = ps8.next()
                    for kk in range(8):
                        k.op("pe", lambda pe: pe.matmul(ps[:, 0:nt], lhsT=wx[:, kk, c * 128:(c + 1) * 128], rhs=hT[:, kk, 0:nt],
                                                        start=(kk == 0), stop=(kk == 7)), r=[wx, hT], w=[ps])
                    ps2 = ps8.next()
                    for kk in range(8):
                        k.op("pe", lambda pe: pe.matmul(ps2[:, 0:4], lhsT=wx[:, kk, c * 128:(c + 1) * 128], rhs=hT[:, kk, 512:516],
                                                        start=(kk == 0), stop=(kk == 7)), r=[wx, hT], w=[ps2])
                    pre = prer.next()
                    k.op("act", lambda a: a.copy(out=pre[:, 2:2 + nt], in_=ps[:, 0:nt]), r=[ps], w=[pre])
                    k.op("act", lambda a: a.copy(out=pre[:, 0:2], in_=ps2[:, 0:2]), r=[ps2], w=[pre])
                    k.op("act", lambda a: a.copy(out=pre[:, 2 + nt:4 + nt], in_=ps2[:, 2:4]), r=[ps2], w=[pre])
                    acc = accr.next()
                    k.op("dve", lambda v: v.tensor_scalar(out=acc[:, 0:nt], in0=pre[:, 0:nt], scalar1=cw[:, c, 0:1], scalar2=None,
                                                          op0=ALU.mult), r=[pre, cw], w=[acc])
                    for kk in range(1, 5):
                        k.op("dve", lambda v: v.scalar_tensor_tensor(out=acc[:, 0:nt], in0=pre[:, kk:kk + nt], scalar=cw[:, c, kk:kk + 1],
                                                                     in1=acc[:, 0:nt], op0=ALU.mult, op1=ALU.add),
                             r=[pre, cw, acc], w=[acc])
                    k.op("act", lambda a: a.activation(out=xc[:, c, 0:nt], in_=acc[:, 0:nt], func=AF.Silu, bias=cbias[:, c:c + 1]),
                         r=[acc, cbias], w=[xc])
                kt0 = t0 // 128
                for j in range(ntile):
                    for s_ in range(2):
                        k.dma("act", RBC.t[kt0 + j, :, s_ * 512:(s_ + 1) * 512].rearrange("p (g t) -> p g t", g=4),
                              xc[:, 16 + 4 * s_:20 + 4 * s_, j * 128:(j + 1) * 128], r=[xc], w=[rbcT[bi]])
                for j in range(ntile):
                    t = kt0 + j
                    rx = rxr.next()
                    for c0 in (0, 8, 16):
                        cn = 8 if c0 < 16 else 4
                        ps = ps8.next()
                        pv = ps[:, :].bitcast(BF16).rearrange("p (c t) -> p c t", c=8)
                        for c in range(cn):
                            k.op("pe", lambda pe: pe.transpose(out=pv[:, c, :], in_=xc[:, c0 + c, j * 128:(j + 1) * 128],
                                                               identity=self.identb[:, :]), r=[xc, self.identb], w=[ps])
                        k.op("act" if c0 != 8 else "dve",
                             (lambda a: a.copy(out=rx[:, c0 * 128:(c0 + cn) * 128].rearrange("p (c t) -> p c t", c=cn), in_=pv[:, 0:cn, :]))
                             if c0 != 8 else
                             (lambda v: v.tensor_copy(out=rx[:, c0 * 128:(c0 + cn) * 128].rearrange("p (c t) -> p c t", c=cn), in_=pv[:, 0:cn, :])),
                             r=[ps], w=[rx])
                    k.dma("pool", RX.t[t], rx[:, :], r=[rx], w=[rxT[t]])
                    rz = rzr.next()
                    for cb in range(4):
                        ps = ps8.next()
                        for kk in range(8):
                            k.op("pe", lambda pe: pe.matmul(ps[:, :], lhsT=hT[:, kk, j * 128:(j + 1) * 128],
                                                            rhs=wz[:, kk, cb * 512:(cb + 1) * 512],
                                                            start=(kk == 0), stop=(kk == 7)), r=[hT, wz], w=[ps])
                        k.op("act", lambda a: a.activation(out=rz[:, cb * 512:(cb + 1) * 512], in_=ps[:, :], func=AF.Silu),
                             r=[ps], w=[rz])
                    k.dma("pool", RZ.t[t], rz[:, :], r=[rz], w=[rzT[t]])
                    dd = ddr.next()
                    ps = ps8.next()
                    for kk in range(8):
                        k.op("pe", lambda pe: pe.matmul(ps[:, 0:64], lhsT=hT[:, kk, j * 128:(j + 1) * 128], rhs=wdt[:, kk, :],
                                                        start=(kk == 0), stop=(kk == 7)), r=[hT, wdt], w=[ps])
                    k.op("dve", lambda v: v.tensor_tensor(out=dtt[:, :], in0=ps[:, 0:64], in1=dtb[:, :], op=ALU.add),
                         r=[ps, dtb], w=[dtt])
                    k.op("act", lambda a: a.activation(out=dtt[:, :], in_=dtt[:, :], func=AF.Exp), r=[dtt], w=[dtt])
                    k.op("act", lambda a: a.activation(out=dd[:, 0:64], in_=dtt[:, :], func=AF.Ln, bias=onec[:, :]),
                         r=[dtt, onec], w=[dd])
                    k.op("dve", lambda v: v.tensor_tensor(out=dd[:, 64:128], in0=dd[:, 0:64], in1=abc[:, :], op=ALU.mult),
                         r=[dd, abc], w=[dd])
                    k.dma("pool", DD.t[t], dd[:, :], r=[dd], w=[ddT[t]])
            k.barrier()

        with ExitStack() as es:
            self.epsb = k.sb(es, [128, 1], F32)
            k.op("pool", lambda g: g.memset(self.epsb[:, :], EPS), w=[self.epsb])
            tri = k.sb(es, [128, 4, 128], F32)
            k.dma("sp", tri[:, :, :], self.tri_d.t.rearrange("m j i -> j m i"), w=[tri])
            negm = k.sb(es, [128, 2, 128], F32)
            k.dma("sp", negm[:, :, :], self.negm_d.t.rearrange("m j i -> j m i"), w=[negm])
            onehot = k.sb(es, [96, 32, 128], BF16)
            k.dma("sp", onehot[:, :, :], self.onehot_d.t, w=[onehot])
            negmb = k.sb(es, [128, 2, 128], BF16)
            k.dma("sp", negmb[:, :, :], self.negmb_d.t.rearrange("m j i -> j m i"), w=[negmb])
            identf = k.sb(es, [128, 128], F32)
            k.dma("sp", identf[:, :], self.identf_d.t, w=[identf])
            Hs = [k.sb(es, [128, 512], F32) for _ in range(4)]
            Hb = [k.sb(es, [128, 512], BF16) for _ in range(4)]
            rxr = Ring([k.sb(es, [128, 2560], BF16) for _ in range(3)])
            rbcr = Ring([k.sb(es, [128, 1024], BF16) for _ in range(2)])
            ddr = Ring([k.sb(es, [128, 128], F32) for _ in range(3)])
            bankY = Ring(self.psb[0:2])
            bankLb = Ring(self.psb[2:5])
            bankM = Ring(self.psb[5:8])
            def ring2(shape, dt):
                return Ring([k.sb(es, shape, dt) for _ in range(2)])
            ela_r = ring2([128, 32], F32)
            edl_r = ring2([128, 32], F32)
            etot_r = ring2([128, 32], F32)
            laT_r = ring2([96, 128], BF16)
            larep_r = ring2([128, 96], F32)
            Abf_r = ring2([96, 128], BF16)
            Bbf_r = ring2([96, 128], BF16)
            R1_r = ring2([96, 128], F32)
            nla_r = ring2([128, 32], F32)
            xdt_r = ring2([128, 2048], BF16)
            xdl_r = ring2([128, 2048], BF16)
            CBm_r = ring2([128, 4, 128], BF16)
            Eh_r = Ring([k.sb(es, [128, 4, 128], BF16) for _ in range(3)])
            Mh = Ring([k.sb(es, [128, 4, 128], BF16) for _ in range(3)])
            ysr = Ring([k.sb(es, [128, 2048], F32) for _ in range(3)])
            wo = k.sb(es, [128, 16, D], BF16)
            with ExitStack() as es2:
                stg = Ring([k.sb(es2, [128, 1024], F32) for _ in range(2)])
                self.load_w(es2, wo, wo, w_out, 16, 128, D, stg)
                k.barrier()
            gnbc = k.sb(es, [128, 2048], F32)
            k.dma("sp", gnbc[:, :], self.W["ssd_norm"].t[0, :].partition_broadcast(128), w=[gnbc])
            dsk = k.sb(es, [128, 32], F32)
            k.dma("sp", dsk[:, :], self.W["ssd_d"].t[0, :].partition_broadcast(128), w=[dsk])
            rzr = Ring([k.sb(es, [128, 2048], BF16) for _ in range(1)])
            yfr = Ring([k.sb(es, [128, 2048], F32) for _ in range(1)])
            xr = Ring([k.sb(es, [128, D], F32) for _ in range(2)])
            ynb = k.sb(es, [128, 2048], BF16)
            ogT = k.sb(es, [128, 16, 128], BF16)
            junk = k.sb(es, [128, 512], BF16)
            ss4 = k.sb(es, [128, 4], F32)
            rs4 = k.sb(es, [128, 4], F32)
            tm = k.sb(es, [128, 512], F32)

            def stageA(d, rx, rbc, dd):
                c = dict(d=d, rx=rx, rbc=rbc, dd=dd)
                cm = tri[:, 0 if d == 0 else 2, :]
                sm = tri[:, 1 if d == 0 else 3, :]
                dA = dd[:, 64 + 32 * d:96 + 32 * d]
                dt = dd[:, 32 * d:32 * d + 32]
                larep = larep_r.next()
                la_sb = larep
                ela, edl, etot, laT = ela_r.next(), edl_r.next(), etot_r.next(), laT_r.next()
                Abf, Bbf, R1 = Abf_r.next(), Bbf_r.next(), R1_r.next()
                xdt, xdl, CBm = xdt_r.next(), xdl_r.next(), CBm_r.next()
                nla = nla_r.next()
                c.update(la_sb=la_sb, ela=ela, etot=etot, laT=laT, xdt=xdt, xdl=xdl, CBm=CBm, nla=nla)
                psL = bankM.next()
                k.op("pe", lambda pe: pe.matmul(psL[:, 0:32], lhsT=cm, rhs=dA, start=True, stop=True), r=[tri, dd], w=[psL])
                k.op("pe", lambda pe: pe.matmul(psL[:, 32:64], lhsT=sm, rhs=dA, start=True, stop=True), r=[tri, dd], w=[psL])
                k.op("pe", lambda pe: pe.matmul(psL[:, 64:96], lhsT=self.ones_f[:, :], rhs=dA, start=True, stop=True),
                     r=[self.ones_f, dd], w=[psL])
                for rep in range(3):
                    k.op("act", lambda a: a.copy(out=larep[:, rep * 32:(rep + 1) * 32], in_=psL[:, 0:32]), r=[psL], w=[larep])
                k.op("act", lambda a: a.mul(out=nla[:, :], in_=psL[:, 0:32], mul=-1.0), r=[psL], w=[nla])
                k.op("act", lambda a: a.activation(out=ela[:, :], in_=psL[:, 0:32], func=AF.Exp), r=[psL], w=[ela])
                k.op("act", lambda a: a.activation(out=edl[:, :], in_=psL[:, 32:64], func=AF.Exp), r=[psL], w=[edl])
                k.op("act", lambda a: a.activation(out=etot[:, :], in_=psL[:, 64:96], func=AF.Exp), r=[psL], w=[etot])
                xs3 = rx[:, 0:2048].rearrange("p (h e) -> p h e", h=32)
                k.op("pool", lambda g: g.tensor_tensor(out=xdt[:, :].rearrange("p (h e) -> p h e", h=32), in0=xs3,
                                                       in1=dt.unsqueeze(2).broadcast_to([128, 32, 64]), op=ALU.mult),
                     r=[rx, dd], w=[xdt])
                k.op("pool", lambda g: g.tensor_tensor(out=xdl[:, :].rearrange("p (h e) -> p h e", h=32),
                                                       in0=xdt[:, :].rearrange("p (h e) -> p h e", h=32),
                                                       in1=edl[:, :].unsqueeze(2).broadcast_to([128, 32, 64]), op=ALU.mult),
                     r=[xdt, edl], w=[xdl])
                psT = bankM.next()
                k.op("pe", lambda pe: pe.transpose(out=psT[0:96, 0:128], in_=larep[:, 0:96], identity=identf[:, :]),
                     r=[larep, identf], w=[psT])
                k.op("act", lambda a: a.copy(out=Abf[:, :], in_=psT[0:96, 0:128]), r=[psT], w=[Abf])
                k.op("dve", lambda v: v.tensor_tensor(out=R1[:, :], in0=psT[0:96, 0:128], in1=Abf[:, :], op=ALU.subtract),
                     r=[psT, Abf], w=[R1])
                k.op("act", lambda a: a.copy(out=Bbf[:, :], in_=R1[:, :]), r=[R1], w=[Bbf])
                k.op("dve", lambda v: v.tensor_tensor(out=R1[:, :], in0=R1[:, :], in1=Bbf[:, :], op=ALU.subtract),
                     r=[R1, Bbf], w=[R1])
                k.op("pool", lambda g: g.tensor_copy(out=laT[0:32, :], in_=Abf[0:32, :]), r=[Abf], w=[laT])
                k.op("pool", lambda g: g.tensor_copy(out=laT[32:64, :], in_=Bbf[32:64, :]), r=[Bbf], w=[laT])
                k.op("pool", lambda g: g.tensor_copy(out=laT[64:96, :], in_=R1[64:96, :]), r=[R1], w=[laT])
                psCB = bankM.next()
                for g in range(4):
                    k.op("pe", lambda pe: pe.matmul(psCB[:, g * 128:(g + 1) * 128], lhsT=rbc[:, g * 128:(g + 1) * 128],
                                                    rhs=rbc[:, 512 + g * 128:512 + (g + 1) * 128], start=True, stop=True),
                         r=[rbc], w=[psCB])
                k.op("dve", lambda v: v.tensor_tensor(out=CBm[:, :, :], in0=psCB[:, :].rearrange("p (g i) -> p g i", g=4),
                                                      in1=cm.unsqueeze(1).broadcast_to([128, 4, 128]), op=ALU.mult),
                     r=[psCB, tri], w=[CBm])
                return c

            def stageB(c, ys):
                d, rx, rbc = c["d"], c["rx"], c["rbc"]
                la_sb, ela, etot, laT, xdt, xdl, CBm = (c[n] for n in ("la_sb", "ela", "etot", "laT", "xdt", "xdl", "CBm"))
                ng = negm[:, d, :]
                nla = c["nla"]

                def lb_mm(i):
                    h0 = i * 4
                    psLb = bankLb.next()
                    for hh in range(4):
                        k.op("pe", lambda pe: pe.matmul(psLb[:, hh * 128:(hh + 1) * 128], lhsT=onehot[:, h0 + hh, :],
                                                        rhs=laT[:, :], start=True, stop=False), r=[onehot, laT], w=[psLb])
                        k.op("pe", lambda pe: pe.matmul(psLb[:, hh * 128:(hh + 1) * 128], lhsT=self.identb[:, :],
                                                        rhs=negmb[:, d, :], start=False, stop=True), r=[self.identb, negmb], w=[psLb])
                    return psLb

                pend = [lb_mm(0), lb_mm(1)]

                def heads(g):
                    psY = bankY.next()
                    for hq in range(2):
                        i = g * 2 + hq
                        h0 = i * 4
                        psLb = pend.pop(0)
                        if i + 2 < 8:
                            pend.append(lb_mm(i + 2))
                        Eh = Eh_r.next()
                        for hh in range(4):
                            k.op("act", lambda a: a.activation(out=Eh[:, hh, :], in_=psLb[:, hh * 128:(hh + 1) * 128], func=AF.Exp,
                                                               bias=nla[:, h0 + hh:h0 + hh + 1]), r=[psLb, nla], w=[Eh])
                        mh = Mh.next()
                        k.op("dve", lambda v: v.tensor_tensor(out=mh[:, :, :], in0=Eh[:, :, :],
                                                              in1=CBm[:, g:g + 1, :].broadcast_to([128, 4, 128]), op=ALU.mult),
                             r=[Eh, CBm], w=[mh])
                        for hh in range(4):
                            h = h0 + hh
                            k.op("pe", lambda pe: pe.matmul(psY[:, (h % 8) * 64:(h % 8 + 1) * 64], lhsT=mh[:, hh, :],
                                                            rhs=xdt[:, h * 64:(h + 1) * 64], start=True, stop=True),
                                 r=[mh, xdt], w=[psY])
                    return psY

                def tail(g, psY):
                    psYi = bankM.next()
                    k.op("pe", lambda pe: pe.matmul(psYi[:, :], lhsT=rbc[:, 512 + g * 128:512 + (g + 1) * 128], rhs=Hb[g][:, :],
                                                    start=True, stop=True), r=[rbc, Hb[g]], w=[psYi])
                    yv = ys[:, g * 512:(g + 1) * 512]
                    k.op("dve", lambda v: v.tensor_tensor(out=yv.rearrange("p (h e) -> p h e", h=8),
                                                          in0=psYi[:, :].rearrange("p (h e) -> p h e", h=8),
                                                          in1=ela[:, g * 8:(g + 1) * 8].unsqueeze(2).broadcast_to([128, 8, 64]),
                                                          op=ALU.mult), r=[psYi, ela], w=[ys])
                    k.op("dve", lambda v: v.tensor_tensor(out=yv, in0=yv, in1=psY[:, :], op=ALU.add), r=[ys, psY], w=[ys])
                    psH = bankM.next()
                    k.op("pe", lambda pe: pe.matmul(psH[:, :], lhsT=rx[:, 2048 + g * 128:2048 + (g + 1) * 128],
                                                    rhs=xdl[:, g * 512:(g + 1) * 512], start=True, stop=True), r=[rx, xdl], w=[psH])
                    k.op("pool", lambda v: v.tensor_tensor(out=Hs[g][:, :].rearrange("p (h e) -> p h e", h=8),
                                                           in0=Hs[g][:, :].rearrange("p (h e) -> p h e", h=8),
                                                           in1=etot[:, g * 8:(g + 1) * 8].unsqueeze(2).broadcast_to([128, 8, 64]),
                                                           op=ALU.mult), r=[Hs[g], etot], w=[Hs[g]])
                    k.op("dve", lambda v: v.tensor_tensor(out=Hs[g][:, :], in0=Hs[g][:, :], in1=psH[:, :], op=ALU.add),
                         r=[Hs[g], psH], w=[Hs[g]])
                    k.op("pool", lambda gp: gp.tensor_copy(out=Hb[g][:, :], in_=Hs[g][:, :]), r=[Hs[g]], w=[Hb[g]])

                prev = None
                for g in range(4):
                    py = heads(g)
                    if prev is not None:
                        tail(*prev)
                    prev = (g, py)
                tail(*prev)

            def reset_state():
                for g in range(4):
                    k.op("pool", lambda gp: gp.memset(Hs[g][:, :], 0.0), w=[Hs[g]])
                    k.op("pool", lambda gp: gp.memset(Hb[g][:, :], 0.0), w=[Hb[g]])

            def loads(t):
                bi = 0 if t < 2 else 1 + (t - 2) // 4
                rx = rxr.next()
                rbc = rbcr.next()
                dd = ddr.next()
                k.dma("sp", rx[:, :], RX.t[t], r=[rxT[t]], w=[rx])
                k.dma("sp", rbc[:, :], RBC.t[t], r=[rbcT[bi]], w=[rbc])
                k.dma("sp", dd[:, :], DD.t[t], r=[ddT[t]], w=[dd])
                return rx, rbc, dd

            reset_state()
            cnext = stageA(0, *loads(0))
            for t in range(NT):
                c = cnext
                if t + 1 < NT:
                    cnext = stageA(0, *loads(t + 1))
                ys = ysr.next()
                stageB(c, ys)
                k.dma("pool", YF.t[t], ys[:, :], r=[ys], w=[yfT[t]])
            reset_state()
            order = [1, 0] + list(range(NT - 1, 1, -1))
            cnext = stageA(1, *loads(order[0]))

            def out_stage(t, rx, ys):
                cond = 0 if t < 2 else 1
                bi = 0 if t < 2 else 1 + (t - 2) // 4
                rz = rzr.next()
                yf = yfr.next()
                xt = xr.next()
                k.dma("sp", rz[:, :], RZ.t[t], r=[rzT[t]], w=[rz])
                k.dma("sp", yf[:, :], YF.t[t], r=[yfT[t]], w=[yf])
                sap, rr = self.xsrc_ap(xsrc, t * 128, 128)
                k.dma("sp", xt[:, :], sap, r=(rr or [self.xblk[bi]]), w=[xt])
                k.op("dve", lambda v: v.tensor_tensor(out=ys[:, :], in0=ys[:, :], in1=yf[:, :], op=ALU.add), r=[ys, yf], w=[ys])
                ytmp = yf
                k.op("pool", lambda g: g.tensor_tensor(out=ytmp[:, :].rearrange("p (h e) -> p h e", h=32),
                                                       in0=rx[:, 0:2048].rearrange("p (h e) -> p h e", h=32),
                                                       in1=dsk[:, :].unsqueeze(2).broadcast_to([128, 32, 64]), op=ALU.mult),
                     r=[rx, dsk, yf], w=[ytmp])
                k.op("dve", lambda v: v.tensor_tensor(out=ys[:, :], in0=ys[:, :], in1=ytmp[:, :], op=ALU.add), r=[ys, ytmp], w=[ys])
                k.op("dve", lambda v: v.tensor_tensor(out=ys[:, :], in0=ys[:, :], in1=rz[:, :], op=ALU.mult), r=[ys, rz], w=[ys])
                for g in range(4):
                    k.op("act", lambda a: a.activation(out=junk[:, :], in_=ys[:, g * 512:(g + 1) * 512], func=AF.Square,
                                                       accum_out=ss4[:, g:g + 1]), r=[ys], w=[junk, ss4])
                k.op("act", lambda a: a.activation(out=rs4[:, :], in_=ss4[:, :], func=AF.Sqrt, scale=1.0 / 512, bias=self.epsb[:, :]),
                     r=[ss4, self.epsb], w=[rs4])
                k.op("dve", lambda v: v.reciprocal(out=rs4[:, :], in_=rs4[:, :]), r=[rs4], w=[rs4])
                for g in range(4):
                    k.op("dve", lambda v: v.scalar_tensor_tensor(out=ynb[:, g * 512:(g + 1) * 512], in0=ys[:, g * 512:(g + 1) * 512],
                                                                 scalar=rs4[:, g:g + 1], in1=gnbc[:, g * 512:(g + 1) * 512],
                                                                 op0=ALU.mult, op1=ALU.mult), r=[ys, rs4, gnbc], w=[ynb])
                self.tok_outproj(ynb, 16, ogT, wo, xt, tm, cond)
                k.dma("pool", self.xres.t[t * 128:(t + 1) * 128, :], xt[:, :], r=[xt], w=[self.xblk[bi]])

            pending_out = None
            for oi, t in enumerate(order):
                c = cnext
                if oi + 1 < NT:
                    cnext = stageA(1, *loads(order[oi + 1]))
                ys = ysr.next()
                stageB(c, ys)
                if pending_out is not None:
                    out_stage(*pending_out)
                pending_out = (t, c["rx"], ys)
            out_stage(*pending_out)
            k.barrier()

    def tok_outproj(self, ogb, kc, ogT, wo, xt, tm, cond):
        k = self.k
        for c0 in range(0, kc, 8):
            ps = self.ps8.next()
            pv = ps[:, :].bitcast(BF16).rearrange("p (c t) -> p c t", c=8)
            for c in range(8):
                k.op("pe", lambda pe: pe.transpose(out=pv[:, c, :], in_=ogb[:, (c0 + c) * 128:(c0 + c + 1) * 128],
                                                   identity=self.identb[:, :]), r=[ogb, self.identb], w=[ps])
            k.op("act", lambda a: a.copy(out=ogT[:, c0:c0 + 8, :], in_=pv), r=[ps], w=[ogT])
        for cb in range(2):
            ps = self.ps8.next()
            for kk in range(kc):
                k.op("pe", lambda pe: pe.matmul(ps[:, :], lhsT=ogT[:, kk, :], rhs=wo[:, kk, cb * 512:(cb + 1) * 512],
                                                start=(kk == 0), stop=(kk == kc - 1)), r=[ogT, wo], w=[ps])
            k.op("dve", lambda v: v.tensor_tensor(out=tm[:, :], in0=ps[:, :],
                                                  in1=self.bcm[cond][:, 2 * D + cb * 512:2 * D + (cb + 1) * 512], op=ALU.mult),
                 r=[ps, self.bcm[cond]], w=[tm])
            k.op("dve", lambda v: v.tensor_tensor(out=xt[:, cb * 512:(cb + 1) * 512], in0=tm[:, :],
                                                  in1=xt[:, cb * 512:(cb + 1) * 512], op=ALU.add), r=[tm, xt], w=[xt])

    def outproj(self, es, og_src, ogb, kp, kc, wo, xsrc, xr, tm):
        k = self.k
        for bi, (t0, nt, cond) in enumerate(BLOCKS):
            ob = ogb.next()
            src, trk = og_src(bi, nt)
            if kp == 128:
                s4 = src.rearrange("d (k two) t -> d two k t", two=2)
                for hp in range(2):
                    k.dma("sp", ob[hp * 64:(hp + 1) * 64, :, 0:nt], s4[:, hp, :, :], r=[trk], w=[ob])
            else:
                k.dma("sp", ob[:, :, 0:nt], src, r=[trk], w=[ob])
            for j in range(nt // 128):
                xt = xr.next()
                sap, rr = self.xsrc_ap(xsrc, t0 + j * 128, 128)
                k.dma("sp", xt[:, :], sap, r=(rr or [self.xblk[bi]]), w=[xt])
                import os
                for cb in range(2 if not os.environ.get("DBG_SKIPMM") else 0):
                    ps = self.psg.next()
                    for kk in range(kc):
                        k.op("pe", lambda pe, kk=kk, cb=cb, ps=ps: pe.matmul(
                            ps[:, :], lhsT=ob[0:kp, kk, j * 128:(j + 1) * 128], rhs=wo[0:kp, kk, cb * 512:(cb + 1) * 512],
                            start=(kk == 0), stop=(kk == kc - 1)), r=[ob, wo], w=[ps])
                    k.op("dve", lambda v, cb=cb, ps=ps: v.tensor_tensor(
                        out=tm[:, :], in0=ps[:, :], in1=self.bcm[cond][:, 2 * D + cb * 512:2 * D + (cb + 1) * 512], op=ALU.mult),
                        r=[ps, self.bcm[cond]], w=[tm])
                    k.op("dve", lambda v, cb=cb, xt=xt: v.tensor_tensor(
                        out=xt[:, cb * 512:(cb + 1) * 512], in0=tm[:, :], in1=xt[:, cb * 512:(cb + 1) * 512], op=ALU.add),
                        r=[tm, xt], w=[xt])
                k.dma("pool", self.xres.t[t0 + j * 128:t0 + (j + 1) * 128, :], xt[:, :], r=[xt], w=[self.xblk[bi]])

    def final(self, xsrc):
        k = self.k
        with ExitStack() as es:
            xr = Ring([k.sb(es, [128, D], F32) for _ in range(3)])
            junk = k.sb(es, [128, D], BF16)
            ss = k.sb(es, [128, 1], F32)
            rs = k.sb(es, [128, 1], F32)
            epsb = k.sb(es, [128, 1], F32)
            k.op("pool", lambda g: g.memset(epsb[:, :], EPS), w=[epsb])
            fg = k.sb(es, [128, D], F32)
            k.dma("sp", fg[:, :], self.final_g.t.partition_broadcast(128), w=[fg])
            for bi, (t0, nt, cond) in enumerate(BLOCKS):
                if cond == 0 and not self.debug_x:
                    continue
                for j in range(nt // 128):
                    xt = xr.next()
                    tt0 = t0 + j * 128
                    sap, rr = self.xsrc_ap(xsrc, tt0, 128)
                    k.dma("sp", xt[:, :], sap, r=(rr or [self.xblk[bi]]), w=[xt])
                    if self.debug_x:
                        k.dma("pool", self.out.t[tt0:tt0 + 128, :], xt[:, :], r=[xt], w=[self.out])
                        continue
                    k.op("act", lambda a, xt=xt: a.activation(out=junk[:, :], in_=xt[:, :], func=AF.Square, accum_out=ss[:, :]),
                         r=[xt], w=[junk, ss])
                    k.op("act", lambda a: a.activation(out=rs[:, :], in_=ss[:, :], func=AF.Sqrt, scale=1.0 / D, bias=epsb[:, :]),
                         r=[ss, epsb], w=[rs])
                    k.op("dve", lambda v: v.reciprocal(out=rs[:, :], in_=rs[:, :]), r=[rs], w=[rs])
                    k.op("dve", lambda v, xt=xt: v.scalar_tensor_tensor(out=xt[:, :], in0=xt[:, :], scalar=rs[:, 0:1],
                                                                       in1=fg[:, :], op0=ALU.mult, op1=ALU.mult),
                         r=[xt, rs, fg], w=[xt])
                    k.dma("pool", self.out.t[tt0 - CTX:tt0 - CTX + 128, :], xt[:, :], r=[xt], w=[self.out])


WSHAPES = {
    "mla_w_in": [1, 1024, 1696], "mla_q_norm": [1, 384], "mla_w_uq": [1, 384, 1536], "mla_kv_norm": [1, 256],
    "mla_w_ukv": [1, 256, 2048], "mla_w_out": [1, 1024, 1024],
    "gla_w_in": [1, 1024, 3104], "gla_w_gf": [1, 16, 512], "gla_b_gf": [1, 512], "gla_w_gb": [1, 16, 512],
    "gla_b_gb": [1, 512], "gla_o_norm": [1, 256], "gla_w_out": [1, 1024, 1024],
    "gqa_w_in": [1, 1024, 2560], "gqa_q_norm": [1, 64], "gqa_k_norm": [1, 64], "gqa_w_out": [1, 1024, 1024],
    "ssd_w_in": [1, 1024, 5184], "ssd_conv_w": [1, 5, 3072], "ssd_conv_b": [1, 3072], "ssd_dt_bias_f": [1, 32],
    "ssd_dt_bias_b": [1, 32], "ssd_a_log_f": [1, 32], "ssd_a_log_b": [1, 32], "ssd_d": [1, 32], "ssd_norm": [1, 2048],
    "ssd_w_out": [1, 2048, 1024],
}


def tri_consts():
    j = np.arange(128)[:, None]
    i = np.arange(128)[None, :]
    return np.stack([(j <= i), (j > i), (j >= i), (j < i)]).astype(np.float32)


def rope_tables(rd):
    hf = rd // 4
    inv = 10000.0 ** (-np.arange(hf, dtype=np.float64) / hf)
    p = np.arange(SEQ)
    row = (p // 64).astype(np.float64)[:, None] * inv[None, :]
    col = (p % 64).astype(np.float64)[:, None] * inv[None, :]
    cos = np.concatenate([np.cos(row), np.cos(row), np.cos(col), np.cos(col)], axis=1)
    sin = np.concatenate([-np.sin(row), np.sin(row), -np.sin(col), np.sin(col)], axis=1)
    tab = np.zeros((T, 2, rd), np.float32)
    tab[:CTX, 0, :] = 1.0
    tab[CTX:, 0, :] = cos
    tab[CTX:, 1, :] = sin
    return tab


def run(inputs, layers=(0, 1, 2, 3), debug_x=False, cores=(0, 1), stop=None):
    nc = bass.Bass("TRN2", target_bir_lowering=False)
    Prog(nc, layers=layers, debug_x=debug_x, stop=stop).build()
    f = lambda a: np.ascontiguousarray(np.asarray(a, dtype=np.float32))
    common = {nm: f(inputs[nm]) for nm in WSHAPES}
    for nm in ("ada_w", "ada_b", "norm_g", "final_g"):
        common[nm] = f(inputs[nm])
    common["ident_bf"] = np.eye(128, dtype=np.float32).astype(ml_dtypes.bfloat16)
    common["rope_mla"] = rope_tables(32)
    common["rope_gqa"] = rope_tables(64)
    common["tri"] = tri_consts()
    common["negm"] = ((1.0 - tri_consts()[[0, 2]]) * -1e30).astype(np.float32)
    oh = np.zeros((3, 32, 32, 128), np.float32)
    oh[:, np.arange(32), np.arange(32), :] = 1.0
    common["onehot3"] = oh.reshape(96, 32, 128).astype(ml_dtypes.bfloat16)
    common["negmb"] = common["negm"].astype(ml_dtypes.bfloat16)
    common["ident_f"] = np.eye(128, dtype=np.float32)
    in_maps = []
    for b in cores:
        m = dict(common)
        m["xin"] = np.ascontiguousarray(np.concatenate([f(inputs["ctx"])[b], f(inputs["x"])[b]], axis=0))
        m["c2"] = np.ascontiguousarray(np.stack([f(inputs["c_ctx"]), f(inputs["c"])[b]], axis=0))
        in_maps.append(m)
    res = run_bass_kernel_spmd(nc, in_maps, core_ids=list(range(len(cores))))
    return [r["y"] for r in res.results]


FUSED = True


def kernel(**inputs):
    if FUSED:
        outs = run(inputs)
        return np.stack(outs, axis=0).astype(np.float32)
    cur = dict(inputs)
    for L in (0, 1, 2):
        outs = run(cur, layers=(L,), debug_x=True)
        st = np.stack(outs, axis=0)
        cur["ctx"] = np.ascontiguousarray(st[:, :CTX])
        cur["x"] = np.ascontiguousarray(st[:, CTX:])
    outs = run(cur, layers=(3,), debug_x=False)
    return np.stack(outs, axis=0).astype(np.float32)
```

```python
import math
from contextlib import ExitStack

import numpy as np
import ml_dtypes
import concourse.bass as bass
import concourse.mybir as mybir
from concourse.bass_utils import run_bass_kernel_spmd

F32 = mybir.dt.float32
BF16 = mybir.dt.bfloat16
AF = mybir.ActivationFunctionType
ALU = mybir.AluOpType
AX = mybir.AxisListType

D = 1024
SEQ = 8192
CTX = 256
T = SEQ + CTX
NT = T // 128
EPS = 1e-6
EPOCH = 30000

BLOCKS = [(0, 256, 0)] + [(256 + 512 * i, 512, 1) for i in range(16)]
NB = len(BLOCKS)


class Buf:
    __slots__ = ("w", "r")

    def __init__(self):
        self.w = None
        self.r = {}


class TT:
    def __init__(self, t):
        self.t = t
        self.b = Buf()

    def __getitem__(self, idx):
        return self.t[idx]


class Ring:
    def __init__(self, items):
        self.items = items
        self.i = 0

    def next(self):
        it = self.items[self.i % len(self.items)]
        self.i += 1
        return it


class KB:
    def __init__(self, nc, es):
        self.nc = nc
        self.es = es
        self.eng = {"pe": nc.tensor, "act": nc.scalar, "dve": nc.vector, "pool": nc.gpsimd, "sp": nc.sync}
        self.sems = {e: [] for e in self.eng}
        self.cnt = {e: 0 for e in self.eng}
        self.seen = {e: {} for e in self.eng}
        self.last = {e: None for e in self.eng}
        self.slots = {}
        self.slot_i = {}
        for q in ("sp", "pool", "act"):
            self.slots[q] = [[es.enter_context(nc.semaphore(f"d_{q}_{i}")), 0, f"d_{q}_{i}"] for i in range(12)]
            self.slot_i[q] = 0
        self.nsb = 0

    def sb(self, es, shape, dt, name=None):
        self.nsb += 1
        return TT(es.enter_context(self.nc.sbuf_tensor(name or f"sb{self.nsb}", list(shape), dt)))

    def dram(self, shape, dt, name):
        h = self.nc.dram_tensor(name, list(shape), dt, kind="Internal")
        return TT(h.ap())

    def _wait(self, e, deps):
        seen = self.seen[e]
        for ev in deps:
            key, sem, val, src = ev
            if src == "pe" and e == "pe":
                continue
            if seen.get(key, 0) >= val:
                continue
            self.eng[e].wait_ge(sem, val)
            seen[key] = val

    def _deps(self, reads, writes):
        deps = []
        for t in reads:
            if t.b.w is not None:
                deps.append(t.b.w)
        for t in writes:
            if t.b.w is not None:
                deps.append(t.b.w)
            deps.extend(t.b.r.values())
        return deps

    def _mark(self, ev, reads, writes):
        for t in reads:
            t.b.r[ev[0]] = ev
        for t in writes:
            t.b.w = ev
            t.b.r = {}

    def op(self, e, fn, r=(), w=()):
        self._wait(e, self._deps(r, w))
        ins = fn(self.eng[e])
        epoch = self.cnt[e] // EPOCH
        while len(self.sems[e]) <= epoch:
            self.sems[e].append(self.es.enter_context(self.nc.semaphore(f"s_{e}_{len(self.sems[e])}")))
        sem = self.sems[e][epoch]
        val = self.cnt[e] % EPOCH + 1
        ins.then_inc(sem, 1)
        self.cnt[e] += 1
        ev = ((e, epoch), sem, val, e)
        self.last[e] = ev
        self._mark(ev, r, w)
        return ev

    def dma(self, q, out, in_, r=(), w=(), **kw):
        deps = self._deps(r, w)
        slots = self.slots[q]
        si = self.slot_i[q] % len(slots)
        self.slot_i[q] += 1
        slot = slots[si]
        if slot[1] > 0:
            deps.append((slot[2], slot[0], 16 * slot[1], "dma"))
        self._wait(q, deps)
        ins = self.eng[q].dma_start(out=out, in_=in_, **kw)
        ins.then_inc(slot[0], 16)
        slot[1] += 1
        ev = (slot[2], slot[0], 16 * slot[1], "dma")
        self._mark(ev, r, w)
        return ev

    def barrier(self):
        evs = [self.last[e] for e in self.eng if self.last[e] is not None]
        for q in self.slots:
            for slot in self.slots[q]:
                if slot[1] > 0:
                    evs.append((slot[2], slot[0], 16 * slot[1], "dma"))
        for e in self.eng:
            seen = self.seen[e]
            for ev in evs:
                key, sem, val, src = ev
                if src == e:
                    continue
                if seen.get(key, 0) >= val:
                    continue
                self.eng[e].wait_ge(sem, val)
                seen[key] = val


class Prog:
    def __init__(self, nc, layers=(0, 1, 2, 3), debug_x=False, stop=None):
        self.nc = nc
        self.stop = stop
        self.layers = layers
        self.debug_x = debug_x

    def din(self, name, shape, dt=F32):
        return TT(self.nc.dram_tensor(name, list(shape), dt, kind="ExternalInput").ap())

    def build(self):
        nc = self.nc
        with ExitStack() as es:
            self.k = k = KB(nc, es)
            self.es = es
            self.xin = self.din("xin", [T, D])
            self.c2 = self.din("c2", [2, D])
            self.ada_w = self.din("ada_w", [4, D, 3 * D])
            self.ada_b = self.din("ada_b", [4, 3 * D])
            self.norm_g = self.din("norm_g", [4, D])
            self.final_g = self.din("final_g", [D])
            self.W = {}
            for nm, shp in WSHAPES.items():
                self.W[nm] = self.din(nm, shp)
            self.identb_d = self.din("ident_bf", [128, 128], BF16)
            self.rope_mla = self.din("rope_mla", [T, 2, 32])
            self.rope_gqa = self.din("rope_gqa", [T, 2, 64])
            self.tri_d = self.din("tri", [4, 128, 128])
            self.negm_d = self.din("negm", [2, 128, 128])
            self.onehot_d = self.din("onehot3", [96, 32, 128], BF16)
            self.negmb_d = self.din("negmb", [2, 128, 128], BF16)
            self.identf_d = self.din("ident_f", [128, 128])
            if self.debug_x:
                self.out = TT(nc.dram_tensor("y", [T, D], F32, kind="ExternalOutput").ap())
            else:
                self.out = TT(nc.dram_tensor("y", [SEQ, D], F32, kind="ExternalOutput").ap())
            self.xres = k.dram([T, D], F32, "xres")
            self.xblk = [TT(self.xres.t) for _ in range(NB)]
            self.modd = k.dram([4, 2, 3 * D], F32, "modd")
            self.identb = k.sb(es, [128, 128], BF16, "identb")
            k.dma("sp", self.identb[:, :], self.identb_d.t[:, :], w=[self.identb])
            self.ones_f = k.sb(es, [128, 128], F32, "ones_f")
            k.op("pool", lambda g: g.memset(self.ones_f[:, :], 1.0), w=[self.ones_f])
            self.psb = [TT(es.enter_context(nc.psum_tensor(f"ps{i}", [128, 512], F32))) for i in range(8)]
            self.psg = Ring(self.psb[0:6])
            self.pso = Ring(self.psb[6:8])
            self.ps8 = Ring(self.psb)
            self.bcm = [k.sb(es, [128, 3 * D], F32, f"bcm{c}") for c in range(2)]
            self.gmod = [k.sb(es, [128, D], F32, f"gmod{c}") for c in range(2)]
            self.ngbc = k.sb(es, [128, D], F32, "ngbc")

            first = True
            for L in self.layers:
                self.modulation(L)
                if self.stop == "mod":
                    break
                xsrc = self.xin if first else None
                if L == 0:
                    self.layer_attn(L, "mla", xsrc)
                elif L == 2:
                    self.layer_attn(L, "gqa", xsrc)
                elif L == 1:
                    self.layer_gla(L, xsrc)
                elif L == 3:
                    self.layer_ssd(L, xsrc)
                first = False
                k.barrier()
            self.final(self.xin if first else None)
            k.barrier()
        return nc

    def xsrc_ap(self, xsrc, t0, n):
        if xsrc is not None:
            return xsrc.t[t0:t0 + n, :], [xsrc]
        return self.xres.t[t0:t0 + n, :], None

    def modulation(self, L):
        k = self.k
        with ExitStack() as es:
            cT = k.sb(es, [128, 8, 2], F32)
            sT = k.sb(es, [128, 8, 2], F32)
            for kk in range(8):
                k.dma("sp", cT[:, kk, :], self.c2.t[:, kk * 128:(kk + 1) * 128].rearrange("c p -> p c"), w=[cT],
                      allow_slow_non_contiguous=True)
            k.op("act", lambda a: a.activation(out=sT[:, :, :], in_=cT[:, :, :], func=AF.Silu), r=[cT], w=[sT])
            msb = k.sb(es, [2, 3 * D], F32)
            bb = k.sb(es, [2, 3 * D], F32)
            k.dma("sp", bb[:, :], self.ada_b.t[L, :].partition_broadcast(2), w=[bb])
            wr = Ring([k.sb(es, [128, 8, 512], F32) for _ in range(2)])
            for cb in range(6):
                wt = wr.next()
                k.dma("sp", wt[:, :, :],
                      self.ada_w.t[L, :, cb * 512:(cb + 1) * 512].rearrange("(k p) n -> p k n", p=128), w=[wt])
                ps = self.psg.next()
                for kk in range(8):
                    k.op("pe", lambda pe, kk=kk: pe.matmul(ps[0:2, :], lhsT=sT[:, kk, :], rhs=wt[:, kk, :],
                                                          start=(kk == 0), stop=(kk == 7)), r=[sT, wt], w=[ps])
                k.op("dve", lambda v: v.tensor_tensor(out=msb[:, cb * 512:(cb + 1) * 512], in0=ps[0:2, :],
                                                      in1=bb[:, cb * 512:(cb + 1) * 512], op=ALU.add),
                     r=[ps, bb], w=[msb])
            md = TT(self.modd.t)
            k.dma("sp", self.modd.t[L, :, :], msb[:, :], r=[msb], w=[md])
            for c in range(2):
                k.dma("sp", self.bcm[c][:, :], self.modd.t[L, c, :].partition_broadcast(128), r=[md], w=[self.bcm[c]])
            k.dma("sp", self.ngbc[:, :], self.norm_g.t[L, :].partition_broadcast(128), w=[self.ngbc])
            for c in range(2):
                k.op("dve", lambda v, c=c: v.scalar_tensor_tensor(out=self.gmod[c][:, :], in0=self.bcm[c][:, D:2 * D],
                                                                 scalar=1.0, in1=self.ngbc[:, :], op0=ALU.add,
                                                                 op1=ALU.mult),
                     r=[self.bcm[c], self.ngbc], w=[self.gmod[c]])
            k.barrier()

    def load_w(self, es_stage, dst, dview, src_ap, kc, kp, n, stg):
        k = self.k
        CH = 1024
        for kk in range(kc):
            for c0 in range(0, n, CH):
                cn = min(CH, n - c0)
                st = stg.next()
                k.dma("sp", st[0:kp, 0:cn], src_ap[kk * kp:(kk + 1) * kp, c0:c0 + cn], w=[st])
                k.op("pool", lambda g, st=st, kk=kk, c0=c0, cn=cn: g.tensor_copy(out=dview[0:kp, kk, c0:c0 + cn],
                                                                                in_=st[0:kp, 0:cn]),
                     r=[st], w=[dst])

    def norm_block(self, es, bi, xsrc, xr, hT, scr):
        k = self.k
        t0, nt, cond = BLOCKS[bi]
        junk, _ss, _rs, hf0, hb0 = scr
        key = id(es)
        if getattr(self, "_nb_key", None) != key:
            self._nb_key = key
            self._nb = dict(ss=k.sb(es, [128, 4], F32), rs=k.sb(es, [128, 4], F32),
                            hf=Ring([hf0, k.sb(es, [128, D], F32)]),
                            hb=Ring([hb0] + [k.sb(es, [128, D], BF16) for _ in range(3)]))
        nb = self._nb
        ss, rs = nb["ss"], nb["rs"]
        ntile = nt // 128
        xts = []
        for j in range(ntile):
            xt = xr.next()
            xts.append(xt)
            src, rr = self.xsrc_ap(xsrc, t0 + j * 128, 128)
            k.dma("sp", xt[:, :], src, r=(rr or [self.xblk[bi]]), w=[xt])
            k.op("act", lambda a: a.activation(out=junk[:, :], in_=xt[:, :], func=AF.Square, accum_out=ss[:, j:j + 1]),
                 r=[xt], w=[junk, ss])
            if len(xr.items) < ntile and j % len(xr.items) == len(xr.items) - 1:
                pass
        k.op("act", lambda a: a.activation(out=rs[:, 0:ntile], in_=ss[:, 0:ntile], func=AF.Sqrt, scale=1.0 / D, bias=self.epsb[:, :]),
             r=[ss, self.epsb], w=[rs])
        k.op("dve", lambda v: v.reciprocal(out=rs[:, 0:ntile], in_=rs[:, 0:ntile]), r=[rs], w=[rs])
        hbs = []
        for j in range(ntile):
            xt = xts[j]
            hf = nb["hf"].next()
            hb = nb["hb"].next()
            hbs.append(hb)
            k.op("dve", lambda v: v.scalar_tensor_tensor(out=hf[:, :], in0=xt[:, :], scalar=rs[:, j:j + 1],
                                                         in1=self.gmod[cond][:, :], op0=ALU.mult, op1=ALU.mult),
                 r=[xt, rs, self.gmod[cond]], w=[hf])
            k.op("dve", lambda v: v.tensor_tensor(out=hb[:, :], in0=hf[:, :], in1=self.bcm[cond][:, 0:D], op=ALU.add),
                 r=[hf, self.bcm[cond]], w=[hb])
        for j in range(ntile):
            hb = hbs[j]
            ps = self.psg.next()
            pv = ps[:, :].bitcast(BF16).rearrange("p (c t) -> p c t", c=8)
            for c in range(8):
                k.op("pe", lambda pe: pe.transpose(out=pv[:, c, :], in_=hb[:, c * 128:(c + 1) * 128],
                                                   identity=self.identb[:, :]), r=[hb, self.identb], w=[ps])
            k.op("act", lambda a: a.copy(out=hT[:, :, j * 128:(j + 1) * 128], in_=pv), r=[ps], w=[hT])

    def layer_attn(self, L, kind, xsrc):
        k = self.k
        nc = self.nc
        if kind == "mla":
            H, HK, DQ = 16, 16, 96
            w_in = self.W["mla_w_in"].t[0]
            w_out = self.W["mla_w_out"].t[0]
            GOFF = 672
            scale = 96 ** -0.5
        else:
            H, HK, DQ = 16, 4, 64
            w_in = self.W["gqa_w_in"].t[0]
            w_out = self.W["gqa_w_out"].t[0]
            GOFF = 1536
            scale = 64 ** -0.5
        REP = H // HK
        QT = k.dram([H, DQ, T], BF16, f"QT{L}")
        KT = k.dram([HK, DQ, T], BF16, f"KT{L}")
        VV = k.dram([HK, 128, NT, 65], BF16, f"VV{L}")
        GS = k.dram([8, 128, T], BF16, f"GS{L}")
        OG = k.dram([NB, 64, 16, 512], BF16, f"OG{L}")

        with ExitStack() as es:
            self.epsb = k.sb(es, [128, 1], F32)
            k.op("pool", lambda g: g.memset(self.epsb[:, :], EPS), w=[self.epsb])
            NIN = 1696 if kind == "mla" else 2560
            win = k.sb(es, [128, 8, NIN], BF16)
            if kind == "mla":
                wuq = k.sb(es, [128, 3, 1536], BF16)
                wukv = k.sb(es, [128, 2, 2048], BF16)
            with ExitStack() as es2:
                stg = Ring([k.sb(es2, [128, 1024], F32) for _ in range(2)])
                self.load_w(es2, win, win, w_in, 8, 128, NIN, stg)
                if kind == "mla":
                    self.load_w(es2, wuq, wuq, self.W["mla_w_uq"].t[0], 3, 128, 1536, stg)
                    self.load_w(es2, wukv, wukv, self.W["mla_w_ukv"].t[0], 2, 128, 2048, stg)
                k.barrier()
            if kind == "mla":
                qnbc = k.sb(es, [128, 384], F32)
                k.dma("sp", qnbc[:, :], self.W["mla_q_norm"].t[0, :].partition_broadcast(128), w=[qnbc])
                kvnbc = k.sb(es, [128, 256], F32)
                k.dma("sp", kvnbc[:, :], self.W["mla_kv_norm"].t[0, :].partition_broadcast(128), w=[kvnbc])
                RD, HF = 32, 8
                rope_d = self.rope_mla
            else:
                qnbc = k.sb(es, [128, 64], F32)
                k.dma("sp", qnbc[:, :], self.W["gqa_q_norm"].t[0, :].partition_broadcast(128), w=[qnbc])
                knbc = k.sb(es, [128, 64], F32)
                k.dma("sp", knbc[:, :], self.W["gqa_k_norm"].t[0, :].partition_broadcast(128), w=[knbc])
                RD, HF = 64, 16
                rope_d = self.rope_gqa
            xr = Ring([k.sb(es, [128, D], F32) for _ in range(4)])
            hTr = Ring([k.sb(es, [128, 8, 512], BF16) for _ in range(2)])
            scr = (k.sb(es, [128, D], BF16), k.sb(es, [128, 1], F32), k.sb(es, [128, 1], F32),
                   k.sb(es, [128, D], F32), k.sb(es, [128, D], BF16))
            qsb = k.sb(es, [128, H * DQ], F32)
            qb = k.sb(es, [128, H, DQ], BF16)
            kb = k.sb(es, [128, HK, DQ], BF16)
            ksb = k.sb(es, [128, HK * DQ if kind == "gqa" else 32], F32)
            vblk = Ring([k.sb(es, [128, 4, HK, 65], BF16) for _ in range(1)])
            for vb_ in vblk.items:
                k.op("pool", lambda g, vb_=vb_: g.memset(vb_[:, :, :, :], 1.0), w=[vb_])
            qTb = Ring([k.sb(es, [DQ, H, 512], BF16) for _ in range(1)])
            kTb = Ring([k.sb(es, [DQ, HK, 512], BF16) for _ in range(1)])
            rtab = Ring([k.sb(es, [128, 2, RD], F32) for _ in range(2)])
            ra = k.sb(es, [128, H, RD], F32)
            rb_ = k.sb(es, [128, H, RD], F32)
            ss2 = k.sb(es, [128, 32], F32)
            rs2 = k.sb(es, [128, 32], F32)
            sq = k.sb(es, [128, H * DQ], F32)
            if kind == "mla":
                cqn = k.sb(es, [128, 640], BF16)
                cT = k.sb(es, [128, 5, 128], BF16)
            gsr = Ring([k.sb(es, [128, 512], BF16) for _ in range(2)])

            def rope(xv, nh, dst, tab):
                cosb = tab[:, 0:1, :].broadcast_to([128, nh, RD])
                k.op("dve", lambda v: v.tensor_tensor(out=ra[:, 0:nh, :], in0=xv, in1=cosb, op=ALU.mult),
                     r=[tab, qsb, ksb], w=[ra])
                x5 = xv.rearrange("p h (g s f) -> p h g s f", g=2, s=2)
                b5 = rb_[:, 0:nh, :].rearrange("p h (g s f) -> p h g s f", g=2, s=2)
                s5 = tab[:, 1, :].rearrange("p (g s f) -> p g s f", g=2, s=2)
                for g in range(2):
                    for s in range(2):
                        sinb = s5[:, g:g + 1, s, :].broadcast_to([128, nh, HF])
                        k.op("dve", lambda v, g=g, s=s, sinb=sinb: v.tensor_tensor(
                            out=b5[:, :, g, s, :], in0=x5[:, :, g, 1 - s, :], in1=sinb, op=ALU.mult),
                            r=[tab, qsb, ksb], w=[rb_])
                k.op("dve", lambda v: v.tensor_tensor(out=dst, in0=ra[:, 0:nh, :], in1=rb_[:, 0:nh, :], op=ALU.add),
                     r=[ra, rb_], w=[qb, kb])

            import os
            for bi, (t0, nt, cond) in enumerate(BLOCKS[:int(os.environ.get('DBG_P1_BLOCKS', NB))]):
                hT = hTr.next()
                self.norm_block(es, bi, xsrc, xr, hT, scr)
                ntile = nt // 128
                vb4 = vblk.next()
                qT = qTb.next()
                kT = kTb.next()
                for j in range(ntile):
                    tt0 = t0 + j * 128
                    kt = tt0 // 128
                    tab = rtab.next()
                    k.dma("sp", tab[:, :, :], rope_d.t[tt0:tt0 + 128, :, :], w=[tab])
                    hTj = lambda kk: hT[:, kk, j * 128:(j + 1) * 128]
                    if kind == "mla":
                        psA = self.psg.next()
                        psB = self.psg.next()
                        for kk in range(8):
                            k.op("pe", lambda pe, kk=kk: pe.matmul(psA[:, 0:384], lhsT=hTj(kk), rhs=win[:, kk, 0:384],
                                                                  start=(kk == 0), stop=(kk == 7)), r=[hT, win], w=[psA])
                        for kk in range(8):
                            k.op("pe", lambda pe, kk=kk: pe.matmul(psB[:, 0:288], lhsT=hTj(kk), rhs=win[:, kk, 384:672],
                                                                  start=(kk == 0), stop=(kk == 7)), r=[hT, win], w=[psB])
                        for (ps_, n_, gb_, o_) in ((psA, 384, qnbc, 0), (psB, 256, kvnbc, 384)):
                            k.op("act", lambda a, ps_=ps_, n_=n_: a.activation(out=sq[:, 0:n_], in_=ps_[:, 0:n_], func=AF.Square,
                                                                              accum_out=ss2[:, 0:1]), r=[ps_], w=[sq, ss2])
                            k.op("act", lambda a, n_=n_: a.activation(out=rs2[:, 0:1], in_=ss2[:, 0:1], func=AF.Sqrt,
                                                                     scale=1.0 / n_, bias=self.epsb[:, :]),
                                 r=[ss2, self.epsb], w=[rs2])
                            k.op("dve", lambda v: v.reciprocal(out=rs2[:, 0:1], in_=rs2[:, 0:1]), r=[rs2], w=[rs2])
                            k.op("dve", lambda v, ps_=ps_, n_=n_, gb_=gb_, o_=o_: v.scalar_tensor_tensor(
                                out=cqn[:, o_:o_ + n_], in0=ps_[:, 0:n_], scalar=rs2[:, 0:1], in1=gb_[:, :],
                                op0=ALU.mult, op1=ALU.mult), r=[ps_, rs2, gb_], w=[cqn])
                        k.op("act", lambda a: a.copy(out=ksb[:, 0:32], in_=psB[:, 256:288]), r=[psB], w=[ksb])
                        pst = self.psg.next()
                        ptv = pst[:, :].bitcast(BF16).rearrange("p (c t) -> p c t", c=8)
                        for c in range(5):
                            k.op("pe", lambda pe, c=c: pe.transpose(out=ptv[:, c, :], in_=cqn[:, c * 128:(c + 1) * 128],
                                                                    identity=self.identb[:, :]), r=[cqn, self.identb], w=[pst])
                        k.op("act", lambda a: a.copy(out=cT[:, :, :], in_=ptv[:, 0:5, :]), r=[pst], w=[cT])
                        for cb in range(3):
                            ps = self.psg.next()
                            for kk in range(3):
                                k.op("pe", lambda pe, kk=kk, cb=cb, ps=ps: pe.matmul(
                                    ps[:, :], lhsT=cT[:, kk, :], rhs=wuq[:, kk, cb * 512:(cb + 1) * 512],
                                    start=(kk == 0), stop=(kk == 2)), r=[cT, wuq], w=[ps])
                            k.op("act", lambda a, cb=cb, ps=ps: a.copy(out=qsb[:, cb * 512:(cb + 1) * 512], in_=ps[:, :]),
                                 r=[ps], w=[qsb])
                        q3 = qsb[:, :].rearrange("p (h d) -> p h d", h=16)
                        k.op("pool", lambda g: g.tensor_copy(out=qb[:, :, 0:64], in_=q3[:, :, 0:64]), r=[qsb], w=[qb])
                        rope(q3[:, :, 64:96], 16, qb[:, :, 64:96], tab)
                        krv = ksb[:, 0:32].rearrange("p (h d) -> p h d", h=1)
                        rope(krv, 1, kb[:, 0:1, 64:96], tab)
                        k.op("pool", lambda g: g.tensor_copy(out=kb[:, 1:16, 64:96],
                                                             in_=kb[:, 0:1, 64:96].broadcast_to([128, 15, 32])),
                             r=[kb], w=[kb])
                        for cb in range(4):
                            ps = self.psg.next()
                            for kk in range(2):
                                k.op("pe", lambda pe, kk=kk, cb=cb, ps=ps: pe.matmul(
                                    ps[:, :], lhsT=cT[:, 3 + kk, :], rhs=wukv[:, kk, cb * 512:(cb + 1) * 512],
                                    start=(kk == 0), stop=(kk == 1)), r=[cT, wukv], w=[ps])
                            p3 = ps[:, :].rearrange("p (h d) -> p h d", h=4)
                            k.op("act", lambda a, cb=cb, p3=p3: a.copy(out=kb[:, cb * 4:(cb + 1) * 4, 0:64], in_=p3[:, :, 0:64]),
                                 r=[ps], w=[kb])
                            k.op("dve", lambda v, cb=cb, p3=p3: v.tensor_copy(out=vb4[:, j, cb * 4:(cb + 1) * 4, 0:64],
                                                                              in_=p3[:, :, 64:128]), r=[ps], w=[vb4])
                    else:
                        import os
                        for cb in range(3 if int(os.environ.get('DBG_STEP', 9)) >= 1 else 0):
                            ps = self.psg.next()
                            for kk in range(8):
                                k.op("pe", lambda pe, kk=kk, cb=cb, ps=ps: pe.matmul(
                                    ps[:, :], lhsT=hTj(kk), rhs=win[:, kk, cb * 512:(cb + 1) * 512],
                                    start=(kk == 0), stop=(kk == 7)), r=[hT, win], w=[ps])
                            SUB = os.environ.get('DBG_SUB', 'abc')
                            if cb < 2:
                                if 'a' in SUB:
                                    k.op("act", lambda a, cb=cb, ps=ps: a.copy(out=qsb[:, cb * 512:(cb + 1) * 512], in_=ps[:, :]),
                                         r=[ps], w=[qsb])
                            elif 'b' in SUB:
                                k.op("act", lambda a, ps=ps: a.copy(out=ksb[:, 0:256], in_=ps[:, 0:256]), r=[ps], w=[ksb])
                                p3 = ps[:, 256:512].rearrange("p (h d) -> p h d", h=4)
                                if 'c' in SUB:
                                    for hh in range(4):
                                        k.op("act", lambda a, hh=hh: a.copy(out=vb4[:, j, hh, 0:64], in_=ps[:, 256 + hh * 64:256 + (hh + 1) * 64]),
                                             r=[ps], w=[vb4])
                        import os
                        DS = int(os.environ.get('DBG_STEP', 9))
                        for (src_, nh, gb_, dstb) in ((qsb, 16, qnbc, qb), (ksb, 4, knbc, kb)) if DS >= 2 else ():
                            s3 = src_[:, 0:nh * 64].rearrange("p (h d) -> p h d", h=nh)
                            sq3 = sq[:, 0:nh * 64].rearrange("p (h d) -> p h d", h=nh)
                            k.op("dve", lambda v, s3=s3, sq3=sq3: v.tensor_tensor(out=sq3, in0=s3, in1=s3, op=ALU.mult),
                                 r=[src_], w=[sq])
                            k.op("dve", lambda v, sq3=sq3, nh=nh: v.tensor_reduce(out=ss2[:, 0:nh], in_=sq3, axis=AX.X, op=ALU.add),
                                 r=[sq], w=[ss2])
                            k.op("act", lambda a, nh=nh: a.activation(out=rs2[:, 0:nh], in_=ss2[:, 0:nh], func=AF.Sqrt,
                                                                     scale=1.0 / 64, bias=self.epsb[:, :]),
                                 r=[ss2, self.epsb], w=[rs2])
                            k.op("dve", lambda v, nh=nh: v.reciprocal(out=rs2[:, 0:nh], in_=rs2[:, 0:nh]), r=[rs2], w=[rs2])
                            k.op("dve", lambda v, s3=s3, nh=nh: v.tensor_tensor(
                                out=s3, in0=s3, in1=rs2[:, 0:nh].unsqueeze(2).broadcast_to([128, nh, 64]), op=ALU.mult),
                                r=[src_, rs2], w=[src_])
                            k.op("dve", lambda v, s3=s3, nh=nh, gb_=gb_: v.tensor_tensor(
                                out=s3, in0=s3, in1=gb_[:, :].unsqueeze(1).broadcast_to([128, nh, 64]), op=ALU.mult),
                                r=[src_, gb_], w=[src_])
                            if DS >= 3:
                                rope(s3, nh, dstb[:, :, :], tab)
                    import os
                    for (srcb, nh, dstT) in ((qb, H, qT), (kb, HK, kT)) if int(os.environ.get('DBG_STEP', 9)) >= 4 else ():
                        for h0 in range(0, nh, 8):
                            hn = min(8, nh - h0)
                            ps = self.psg.next()
                            ptv = ps[:, :].bitcast(BF16).rearrange("p (c t) -> p c t", c=8)
                            for hh in range(hn):
                                k.op("pe", lambda pe, hh=hh, h0=h0, ptv=ptv, srcb=srcb: pe.transpose(
                                    out=ptv[0:DQ, hh, :], in_=srcb[:, h0 + hh, :], identity=self.identb[:, :]),
                                    r=[srcb, self.identb], w=[ps])
                            k.op("act", lambda a, h0=h0, hn=hn, ptv=ptv, dstT=dstT: a.copy(
                                out=dstT[:, h0:h0 + hn, j * 128:(j + 1) * 128], in_=ptv[0:DQ, 0:hn, :]), r=[ps], w=[dstT])
                for hp in range(8):
                    ps = self.psg.next()
                    for kk in range(8):
                        k.op("pe", lambda pe, kk=kk, hp=hp, ps=ps: pe.matmul(
                            ps[:, 0:nt], lhsT=win[:, kk, GOFF + hp * 128:GOFF + (hp + 1) * 128], rhs=hT[:, kk, 0:nt],
                            start=(kk == 0), stop=(kk == 7)), r=[hT, win], w=[ps])
                    gs = gsr.next()
                    k.op("act", lambda a, ps=ps, gs=gs: a.activation(out=gs[:, 0:nt], in_=ps[:, 0:nt], func=AF.Silu),
                         r=[ps], w=[gs])
                    k.dma("pool", GS.t[hp, :, t0:t0 + nt], gs[:, 0:nt], r=[gs], w=[GS])
                for h0 in range(0, H, 4):
                    k.dma("act", QT.t[h0:h0 + 4, :, t0:t0 + nt].rearrange("h d t -> d h t"), qT[:, h0:h0 + 4, 0:nt], r=[qT], w=[QT])
                for h0 in range(0, HK, 4):
                    k.dma("act", KT.t[h0:h0 + 4, :, t0:t0 + nt].rearrange("h d t -> d h t"), kT[:, h0:h0 + 4, 0:nt], r=[kT], w=[KT])
                kt0 = t0 // 128
                for j in range(ntile):
                    for h0 in range(0, HK, 4):
                        k.dma("act", VV.t[h0:h0 + 4, :, kt0 + j, :].rearrange("h p e -> p h e"), vb4[:, j, h0:h0 + 4, :],
                              r=[vb4], w=[VV], allow_slow_non_contiguous=True)
            k.barrier()

        if self.stop == "p1":
            return
        with ExitStack() as es:
            DQP = 128 if DQ == 64 else DQ
            KTs = Ring([k.sb(es, [DQP, T], BF16) for _ in range(2)])
            Vs = Ring([k.sb(es, [128, NT, 65], BF16) for _ in range(2)])
            Qs = Ring([k.sb(es, [DQP, T], BF16) for _ in range(2)])
            if DQP != DQ:
                for b_ in KTs.items + Qs.items:
                    k.op("pool", lambda g, b_=b_: g.memset(b_[DQ:DQP, :], 0.0), w=[b_])
            Gs = Ring([k.sb(es, [64, T], BF16) for _ in range(2)])
            Ps = Ring([k.sb(es, [128, 512], BF16) for _ in range(6)])
            rr = k.sb(es, [65, 512], F32)
            tmp = k.sb(es, [64, 512], F32)
            ogr = Ring([k.sb(es, [64, 512], BF16) for _ in range(2)])
            pss = Ring(self.psb[0:5])
            psm = Ring(self.psb[5:6])
            import os
            pending_epi = [None]
            for hk in range(int(os.environ.get('DBG_P2_HEADS', HK))):
                Kt = KTs.next()
                Vt = Vs.next()
                k.dma("sp", Kt[0:DQ, :], KT.t[hk], r=[KT], w=[Kt])
                k.dma("sp", Vt[:, :, :], VV.t[hk], r=[VV], w=[Vt])
                for hr in range(REP):
                    h = hk * REP + hr
                    Qt = Qs.next()
                    Gt = Gs.next()
                    k.dma("sp", Qt[0:DQ, :], QT.t[h], r=[QT], w=[Qt])
                    k.dma("sp", Gt[:, :], GS.t[h // 2, (h % 2) * 64:(h % 2) * 64 + 64, :], r=[GS], w=[Gt])
                    for bi, (t0, nt, cond) in enumerate(BLOCKS):
                        nkt = 2 if cond == 0 else NT
                        po = self.pso.next()
                        pend = []

                        def s_mm(kt):
                            ps = pss.next()
                            k.op("pe", lambda pe: pe.matmul(ps[:, 0:nt], lhsT=Kt[:, kt * 128:(kt + 1) * 128], rhs=Qt[:, t0:t0 + nt],
                                                            start=True, stop=True), r=[Kt, Qt], w=[ps])
                            pt = Ps.next()
                            k.op("act", lambda a: a.activation(out=pt[:, 0:nt], in_=ps[:, 0:nt], func=AF.Exp, scale=scale),
                                 r=[ps], w=[pt])
                            return pt

                        def pv_mm(kt, pt):
                            k.op("pe", lambda pe: pe.matmul(po[0:65, 0:nt], lhsT=Vt[:, kt, :], rhs=pt[:, 0:nt],
                                                            start=(kt == 0), stop=(kt == nkt - 1)), r=[Vt, pt], w=[po])

                        SK = 3
                        for kt in range(nkt + SK):
                            if kt < nkt:
                                pend.append((kt, s_mm(kt)))
                            if kt >= SK:
                                a_, b_ = pend.pop(0)
                                pv_mm(a_, b_)
                            if kt == 10 and pending_epi[0] is not None:
                                pending_epi[0]()
                                pending_epi[0] = None
                        if pending_epi[0] is not None:
                            pending_epi[0]()
                            pending_epi[0] = None

                        def make_epi(po, Gt, t0, nt, bi, h):
                            def epi():
                                k.op("dve", lambda v: v.reciprocal(out=rr[64:65, 0:nt], in_=po[64:65, 0:nt]), r=[po], w=[rr])
                                pm = psm.next()
                                k.op("pe", lambda pe: pe.matmul(pm[0:64, 0:nt], lhsT=self.ones_f[64:65, 0:64], rhs=rr[64:65, 0:nt],
                                                                start=True, stop=True), r=[rr, self.ones_f], w=[pm])
                                k.op("dve", lambda v: v.tensor_tensor(out=tmp[:, 0:nt], in0=pm[0:64, 0:nt], in1=Gt[:, t0:t0 + nt],
                                                                      op=ALU.mult), r=[pm, Gt], w=[tmp])
                                og = ogr.next()
                                k.op("dve", lambda v: v.tensor_tensor(out=og[:, 0:nt], in0=po[0:64, 0:nt], in1=tmp[:, 0:nt], op=ALU.mult),
                                     r=[po, tmp], w=[og])
                                k.dma("pool", OG.t[bi, :, h, 0:nt], og[:, 0:nt], r=[og], w=[OG])
                            return epi
                        pending_epi[0] = make_epi(po, Gt, t0, nt, bi, h)
            if pending_epi[0] is not None:
                pending_epi[0]()
                pending_epi[0] = None
            k.barrier()

        if self.stop == "p2":
            return
        with ExitStack() as es:
            stg = Ring([k.sb(es, [128, 1024], F32) for _ in range(2)])
            wo = k.sb(es, [128, 8, D], BF16)
            self.load_w(es, wo, wo, w_out, 8, 128, D, stg)
            ogb = Ring([k.sb(es, [128, 8, 512], BF16) for _ in range(2)])
            xr = Ring([k.sb(es, [128, D], F32) for _ in range(3)])
            tm = k.sb(es, [128, 512], F32)
            self.outproj(es, lambda bi, nt: (OG.t[bi, :, :, 0:nt], OG), ogb, 128, 8, wo, xsrc, xr, tm)
            k.barrier()


    def layer_gla(self, L, xsrc):
        k = self.k
        w_in = self.W["gla_w_in"].t[0]
        w_out = self.W["gla_w_out"].t[0]
        REC = k.dram([NT, 128, 3584], BF16, f"GREC{L}")
        GG = k.dram([NT, 128, 1024], F32, f"GGG{L}")
        GOF = k.dram([NT, 128, 1024], F32, f"GOF{L}")
        recT = [TT(REC.t) for _ in range(NT)]
        ggT = [TT(GG.t) for _ in range(NT)]
        ofT = [TT(GOF.t) for _ in range(NT)]
        ps8 = self.ps8
        with ExitStack() as es:
            self.epsb = k.sb(es, [128, 1], F32)
            k.op("pool", lambda g: g.memset(self.epsb[:, :], EPS), w=[self.epsb])
            onec = k.sb(es, [128, 1], F32)
            k.op("pool", lambda g: g.memset(onec[:, :], 1.0), w=[onec])
            stg = Ring([k.sb(es, [128, 1024], F32) for _ in range(2)])
            win = k.sb(es, [128, 8, 3104], BF16)
            self.load_w(es, win, win, w_in, 8, 128, 3104, stg)
            wg = k.sb(es, [16, 2, 512], F32)
            k.dma("sp", wg[:, 0, :], self.W["gla_w_gf"].t[0], w=[wg])
            k.dma("sp", wg[:, 1, :], self.W["gla_w_gb"].t[0], w=[wg])
            bg = k.sb(es, [128, 2, 512], F32)
            k.dma("sp", bg[:, 0, :], self.W["gla_b_gf"].t[0, :].partition_broadcast(128), w=[bg])
            k.dma("sp", bg[:, 1, :], self.W["gla_b_gb"].t[0, :].partition_broadcast(128), w=[bg])
            xr = Ring([k.sb(es, [128, D], F32) for _ in range(4)])
            hTr = Ring([k.sb(es, [128, 8, 512], BF16) for _ in range(2)])
            scr = (k.sb(es, [128, D], BF16), k.sb(es, [128, 1], F32), k.sb(es, [128, 1], F32),
                   k.sb(es, [128, D], F32), k.sb(es, [128, D], BF16))
            recr = Ring([k.sb(es, [128, 3584], BF16) for _ in range(2)])
            ggr = Ring([k.sb(es, [128, 1024], F32) for _ in range(2)])
            rT = k.sb(es, [16, 2, 128], F32)
            zt = k.sb(es, [128, 512], F32)
            for bi, (t0, nt, cond) in enumerate(BLOCKS):
                hT = hTr.next()
                self.norm_block(es, bi, xsrc, xr, hT, scr)
                for j in range(nt // 128):
                    t = (t0 + j * 128) // 128
                    rec = recr.next()
                    gg = ggr.next()
                    hTj = lambda kk: hT[:, kk, j * 128:(j + 1) * 128]

                    def tokmm(c0, n):
                        ps = ps8.next()
                        for kk in range(8):
                            k.op("pe", lambda pe: pe.matmul(ps[:, 0:n], lhsT=hTj(kk), rhs=win[:, kk, c0:c0 + n],
                                                            start=(kk == 0), stop=(kk == 7)), r=[hT, win], w=[ps])
                        return ps

                    ps = tokmm(512, 512)
                    k.op("act", lambda a: a.copy(out=rec[:, 1024:1536], in_=ps[:, :]), r=[ps], w=[rec])
                    for cb in range(2):
                        ps = tokmm(1024 + cb * 512, 512)
                        k.op("dve", lambda v: v.tensor_copy(out=rec[:, 1536 + cb * 512:2048 + cb * 512], in_=ps[:, :]),
                             r=[ps], w=[rec])
                    for cb in range(2):
                        ps = tokmm(2048 + cb * 512, 512)
                        k.op("act", lambda a: a.activation(out=rec[:, 2560 + cb * 512:3072 + cb * 512], in_=ps[:, :],
                                                           func=AF.Silu), r=[ps], w=[rec])
                    for qk in range(2):
                        ps = ps8.next()
                        for h in range(4):
                            for kk in range(8):
                                k.op("pe", lambda pe: pe.matmul(
                                    ps[:, h * 128:(h + 1) * 128], lhsT=win[:, kk, qk * 512 + h * 128:qk * 512 + (h + 1) * 128],
                                    rhs=hTj(kk), start=(kk == 0), stop=(kk == 7)), r=[hT, win], w=[ps])
                        if qk == 0:
                            k.op("act", lambda a: a.mul(out=rec[:, 0:512], in_=ps[:, :], mul=128 ** -0.5), r=[ps], w=[rec])
                        else:
                            k.op("dve", lambda v: v.tensor_copy(out=rec[:, 512:1024], in_=ps[:, :]), r=[ps], w=[rec])
                    ps = ps8.next()
                    for d in range(2):
                        for kk in range(8):
                            k.op("pe", lambda pe: pe.matmul(
                                ps[0:16, d * 128:(d + 1) * 128], lhsT=win[:, kk, 3072 + 16 * d:3088 + 16 * d],
                                rhs=hTj(kk), start=(kk == 0), stop=(kk == 7)), r=[hT, win], w=[ps])
                    k.op("act", lambda a: a.copy(out=rT[:, :, :], in_=ps[0:16, 0:256].rearrange("p (d t) -> p d t", d=2)),
                         r=[ps], w=[rT])
                    for d in range(2):
                        ps = ps8.next()
                        k.op("pe", lambda pe: pe.matmul(ps[:, :], lhsT=rT[:, d, :], rhs=wg[:, d, :], start=True, stop=True),
                             r=[rT, wg], w=[ps])
                        k.op("dve", lambda v: v.tensor_tensor(out=zt[:, :], in0=ps[:, :], in1=bg[:, d, :], op=ALU.add),
                             r=[ps, bg], w=[zt])
                        k.op("act", lambda a: a.activation(out=zt[:, :], in_=zt[:, :], func=AF.Exp, scale=-1.0), r=[zt], w=[zt])
                        k.op("act", lambda a: a.activation(out=zt[:, :], in_=zt[:, :], func=AF.Ln, bias=onec[:, :]),
                             r=[zt, onec], w=[zt])
                        k.op("dve", lambda v: v.tensor_scalar(out=gg[:, d * 512:(d + 1) * 512], in0=zt[:, :],
                                                              scalar1=-1.0 / 16.0, scalar2=None, op0=ALU.mult),
                             r=[zt], w=[gg])
                    k.dma("pool", REC.t[t], rec[:, :], r=[rec], w=[recT[t]])
                    k.dma("pool", GG.t[t], gg[:, :], r=[gg], w=[ggT[t]])
            k.barrier()

        with ExitStack() as es:
            self.epsb = k.sb(es, [128, 1], F32)
            k.op("pool", lambda g: g.memset(self.epsb[:, :], EPS), w=[self.epsb])
            tri = k.sb(es, [128, 4, 128], F32)
            k.dma("sp", tri[:, :, :], self.tri_d.t.rearrange("m j i -> j m i"), w=[tri])
            S = [k.sb(es, [128, 256], F32) for _ in range(4)]
            Sb = [k.sb(es, [128, 256], BF16) for _ in range(4)]
            recr = Ring([k.sb(es, [128, 3584], BF16) for _ in range(3)])
            ggr = Ring([k.sb(es, [128, 1024], F32) for _ in range(3)])
            E1r = Ring([k.sb(es, [128, 512], F32) for _ in range(2)])
            E2r = Ring([k.sb(es, [128, 512], F32) for _ in range(2)])
            E3r = Ring([k.sb(es, [128, 512], F32) for _ in range(2)])
            qtr = Ring([k.sb(es, [128, 512], BF16) for _ in range(2)])
            ktr = Ring([k.sb(es, [128, 512], BF16) for _ in range(2)])
            khr = Ring([k.sb(es, [128, 512], BF16) for _ in range(2)])
            Amr = Ring([k.sb(es, [128, 512], BF16) for _ in range(2)])
            osb = Ring([k.sb(es, [128, 1024], F32) for _ in range(2)])
            stg = Ring([k.sb(es, [128, 1024], F32) for _ in range(2)])
            wo = k.sb(es, [128, 8, D], BF16)
            self.load_w(es, wo, wo, w_out, 8, 128, D, stg)
            onbc = k.sb(es, [128, 256], F32)
            k.dma("sp", onbc[:, :], self.W["gla_o_norm"].t[0, :].partition_broadcast(128), w=[onbc])
            ofr = Ring([k.sb(es, [128, 1024], F32) for _ in range(2)])
            xr = Ring([k.sb(es, [128, D], F32) for _ in range(2)])
            ogf = k.sb(es, [128, 1024], F32)
            ogb = k.sb(es, [128, 1024], BF16)
            ogT = k.sb(es, [128, 8, 128], BF16)
            junk = k.sb(es, [128, 256], BF16)
            ss4 = k.sb(es, [128, 4], F32)
            rs4 = k.sb(es, [128, 4], F32)
            tm = k.sb(es, [128, 512], F32)

            def stageA(d, rec, gg):
                cm = tri[:, 0 if d == 0 else 2, :]
                sm = tri[:, 1 if d == 0 else 3, :]
                g = lambda a_, b_: gg[:, d * 512 + a_:d * 512 + b_]
                E1, E2, E3 = E1r.next(), E2r.next(), E3r.next()
                qt, kt_, kh, Am = qtr.next(), ktr.next(), khr.next(), Amr.next()
                psA = ps8.next()
                for h in range(4):
                    k.op("pe", lambda pe: pe.matmul(psA[:, h * 128:(h + 1) * 128], lhsT=g(h * 128, (h + 1) * 128), rhs=cm,
                                                    start=True, stop=True), r=[gg, tri], w=[psA])
                psB = ps8.next()
                k.op("pe", lambda pe: pe.matmul(psB[:, :], lhsT=sm, rhs=g(0, 512), start=True, stop=True), r=[gg, tri], w=[psB])
                k.op("act", lambda a: a.activation(out=E1[:, :], in_=psA[:, :], func=AF.Exp), r=[psA], w=[E1])
                k.op("act", lambda a: a.activation(out=E2[:, :], in_=psA[:, :], func=AF.Exp, scale=-1.0), r=[psA], w=[E2])
                k.op("act", lambda a: a.activation(out=E3[:, :], in_=psB[:, :], func=AF.Exp), r=[psB], w=[E3])
                k.op("dve", lambda v: v.tensor_tensor(out=qt[:, :], in0=rec[:, 0:512], in1=E1[:, :], op=ALU.mult), r=[rec, E1], w=[qt])
                k.op("pool", lambda v: v.tensor_tensor(out=kt_[:, :], in0=rec[:, 512:1024], in1=E2[:, :], op=ALU.mult), r=[rec, E2], w=[kt_])
                k.op("pool", lambda v: v.tensor_tensor(out=kh[:, :], in0=rec[:, 1024:1536], in1=E3[:, :], op=ALU.mult), r=[rec, E3], w=[kh])
                return dict(d=d, rec=rec, E1=E1, qt=qt, kh=kh, Am=Am, kt_=kt_, cm=cm)

            def stageA2(c):
                qt, kt_, Am, cm = c["qt"], c["kt_"], c["Am"], c["cm"]
                psD = ps8.next()
                for h in range(4):
                    hs = slice(h * 128, (h + 1) * 128)
                    k.op("pe", lambda pe: pe.matmul(psD[:, hs], lhsT=kt_[:, hs], rhs=qt[:, hs], start=True, stop=True),
                         r=[kt_, qt], w=[psD])
                k.op("dve", lambda v: v.tensor_tensor(out=Am[:, :].rearrange("p (h i) -> p h i", h=4),
                                                      in0=psD[:, :].rearrange("p (h i) -> p h i", h=4),
                                                      in1=cm.unsqueeze(1).broadcast_to([128, 4, 128]), op=ALU.mult),
                     r=[psD, tri], w=[Am])

            def stageB(c):
                d, rec, E1, qt, kh, Am = (c[n] for n in ("d", "rec", "E1", "qt", "kh", "Am"))
                ecol = 127 if d == 0 else 0
                po = [ps8.next(), ps8.next()]
                for h in range(4):
                    hs = slice(h * 128, (h + 1) * 128)
                    bank = po[h // 2]
                    cs = slice((h % 2) * 256, (h % 2) * 256 + 256)
                    vs = slice(1536 + h * 256, 1536 + (h + 1) * 256)
                    k.op("pe", lambda pe: pe.matmul(bank[:, cs], lhsT=qt[:, hs], rhs=Sb[h][:, :], start=True, stop=False),
                         r=[qt, Sb[h]], w=[bank])
                    k.op("pe", lambda pe: pe.matmul(bank[:, cs], lhsT=Am[:, hs], rhs=rec[:, vs], start=False, stop=True),
                         r=[Am, rec], w=[bank])
                pss = [ps8.next(), ps8.next()]
                for h in range(4):
                    hs = slice(h * 128, (h + 1) * 128)
                    bank = pss[h // 2]
                    cs = slice((h % 2) * 256, (h % 2) * 256 + 256)
                    vs = slice(1536 + h * 256, 1536 + (h + 1) * 256)
                    k.op("pe", lambda pe: pe.matmul(bank[:, cs], lhsT=kh[:, hs], rhs=rec[:, vs], start=True, stop=True),
                         r=[kh, rec], w=[bank])
                    k.op("dve", lambda v: v.scalar_tensor_tensor(out=S[h][:, :], in0=S[h][:, :],
                                                                 scalar=E1[:, h * 128 + ecol:h * 128 + ecol + 1],
                                                                 in1=bank[:, cs], op0=ALU.mult, op1=ALU.add),
                         r=[S[h], E1, bank], w=[S[h]])
                    k.op("act", lambda gp: gp.copy(out=Sb[h][:, :], in_=S[h][:, :]), r=[S[h]], w=[Sb[h]])
                return po

            def reset_state():
                for h in range(4):
                    k.op("pool", lambda gp: gp.memset(S[h][:, :], 0.0), w=[S[h]])
                    k.op("pool", lambda gp: gp.memset(Sb[h][:, :], 0.0), w=[Sb[h]])

            def gl_loads(t):
                rec = recr.next()
                gg = ggr.next()
                k.dma("sp", rec[:, :], REC.t[t], r=[recT[t]], w=[rec])
                k.dma("sp", gg[:, :], GG.t[t], r=[ggT[t]], w=[gg])
                return rec, gg

            reset_state()
            cnext = stageA(0, *gl_loads(0))
            stageA2(cnext)
            for t in range(NT):
                c = cnext
                if t + 1 < NT:
                    cnext = stageA(0, *gl_loads(t + 1))
                po = stageB(c)
                if t + 1 < NT:
                    stageA2(cnext)
                ob = osb.next()
                for c in range(2):
                    k.op("act", lambda a: a.copy(out=ob[:, c * 512:(c + 1) * 512], in_=po[c][:, :]), r=[po[c]], w=[ob])
                k.dma("pool", GOF.t[t], ob[:, :], r=[ob], w=[ofT[t]])
            reset_state()
            order = [1, 0] + list(range(NT - 1, 1, -1))
            cnext = stageA(1, *gl_loads(order[0]))
            stageA2(cnext)
            osr = Ring([k.sb(es, [128, 1024], F32) for _ in range(2)])

            def gl_out(t, rec, osum):
                cond = 0 if t < 2 else 1
                bi = 0 if t < 2 else 1 + (t - 2) // 4
                xt = xr.next()
                sap, rr = self.xsrc_ap(xsrc, t * 128, 128)
                k.dma("sp", xt[:, :], sap, r=(rr or [self.xblk[bi]]), w=[xt])
                for h in range(4):
                    k.op("act", lambda a: a.activation(out=junk[:, :], in_=osum[:, h * 256:(h + 1) * 256], func=AF.Square,
                                                       accum_out=ss4[:, h:h + 1]), r=[osum], w=[junk, ss4])
                k.op("act", lambda a: a.activation(out=rs4[:, :], in_=ss4[:, :], func=AF.Sqrt, scale=1.0 / 256, bias=self.epsb[:, :]),
                     r=[ss4, self.epsb], w=[rs4])
                k.op("dve", lambda v: v.reciprocal(out=rs4[:, :], in_=rs4[:, :]), r=[rs4], w=[rs4])
                for h in range(4):
                    k.op("dve", lambda v: v.scalar_tensor_tensor(out=ogf[:, h * 256:(h + 1) * 256], in0=osum[:, h * 256:(h + 1) * 256],
                                                                 scalar=rs4[:, h:h + 1], in1=onbc[:, :], op0=ALU.mult, op1=ALU.mult),
                         r=[osum, rs4, onbc], w=[ogf])
                k.op("dve", lambda v: v.tensor_tensor(out=ogb[:, :], in0=ogf[:, :], in1=rec[:, 2560:3584], op=ALU.mult),
                     r=[ogf, rec], w=[ogb])
                self.tok_outproj(ogb, 8, ogT, wo, xt, tm, cond)
                k.dma("pool", self.xres.t[t * 128:(t + 1) * 128, :], xt[:, :], r=[xt], w=[self.xblk[bi]])

            pending = None
            for oi, t in enumerate(order):
                c = cnext
                rec = c["rec"]
                if oi + 1 < NT:
                    cnext = stageA(1, *gl_loads(order[oi + 1]))
                of = ofr.next()
                k.dma("sp", of[:, :], GOF.t[t], r=[ofT[t]], w=[of])
                po = stageB(c)
                if oi + 1 < NT:
                    stageA2(cnext)
                osum = osr.next()
                for cc in range(2):
                    k.op("dve", lambda v: v.tensor_tensor(out=osum[:, cc * 512:(cc + 1) * 512], in0=po[cc][:, :],
                                                          in1=of[:, cc * 512:(cc + 1) * 512], op=ALU.add),
                         r=[po[cc], of], w=[osum])
                if pending is not None:
                    gl_out(*pending)
                pending = (t, rec, osum)
            gl_out(*pending)
            k.barrier()


    def norm_rows(self, src, rtrk, n, cond, dst, scr, xt):
        k = self.k
        junk, ss, rs, hf, hb = scr
        k.dma("sp", xt[0:n, :], src, r=rtrk, w=[xt])
        k.op("act", lambda a: a.activation(out=junk[0:n, :], in_=xt[0:n, :], func=AF.Square, accum_out=ss[0:n, :]),
             r=[xt], w=[junk, ss])
        k.op("act", lambda a: a.activation(out=rs[0:n, :], in_=ss[0:n, :], func=AF.Sqrt, scale=1.0 / D, bias=self.epsb[0:n, :]),
             r=[ss, self.epsb], w=[rs])
        k.op("dve", lambda v: v.reciprocal(out=rs[0:n, :], in_=rs[0:n, :]), r=[rs], w=[rs])
        k.op("dve", lambda v: v.scalar_tensor_tensor(out=hf[0:n, :], in0=xt[0:n, :], scalar=rs[0:n, 0:1],
                                                     in1=self.gmod[cond][0:n, :], op0=ALU.mult, op1=ALU.mult),
             r=[xt, rs, self.gmod[cond]], w=[hf])
        k.op("dve", lambda v: v.tensor_tensor(out=hb[0:n, :], in0=hf[0:n, :], in1=self.bcm[cond][0:n, 0:D], op=ALU.add),
             r=[hf, self.bcm[cond]], w=[hb])
        ps = self.ps8.next()
        pv = ps[:, :].bitcast(BF16).rearrange("p (c t) -> p c t", c=8)
        for c in range(8):
            k.op("pe", lambda pe: pe.transpose(out=pv[:, c, 0:n], in_=hb[0:n, c * 128:(c + 1) * 128],
                                               identity=self.identb[0:n, 0:n]), r=[hb, self.identb], w=[ps])
        return ps, pv

    def layer_ssd(self, L, xsrc):
        k = self.k
        ps8 = self.ps8
        w_in = self.W["ssd_w_in"].t[0]
        w_out = self.W["ssd_w_out"].t[0]
        RX = k.dram([NT, 128, 2560], BF16, f"SRX{L}")
        RZ = k.dram([NT, 128, 2048], BF16, f"SRZ{L}")
        RBC = k.dram([NT, 128, 1024], BF16, f"SRBC{L}")
        DD = k.dram([NT, 128, 128], F32, f"SDD{L}")
        YF = k.dram([NT, 128, 2048], F32, f"SYF{L}")
        rxT = [TT(RX.t) for _ in range(NT)]
        rzT = [TT(RZ.t) for _ in range(NT)]
        rbcT = [TT(RBC.t) for _ in range(NB)]
        ddT = [TT(DD.t) for _ in range(NT)]
        yfT = [TT(YF.t) for _ in range(NT)]
        with ExitStack() as es:
            self.epsb = k.sb(es, [128, 1], F32)
            k.op("pool", lambda g: g.memset(self.epsb[:, :], EPS), w=[self.epsb])
            onec = k.sb(es, [128, 1], F32)
            k.op("pool", lambda g: g.memset(onec[:, :], 1.0), w=[onec])
            stg = Ring([k.sb(es, [128, 1024], F32) for _ in range(2)])
            wz = k.sb(es, [128, 8, 2048], BF16)
            self.load_w(es, wz, wz, w_in[:, 0:2048], 8, 128, 2048, stg)
            wx = k.sb(es, [128, 8, 3072], BF16)
            self.load_w(es, wx, wx, w_in[:, 2048:5120], 8, 128, 3072, stg)
            wdt = k.sb(es, [128, 8, 64], BF16)
            self.load_w(es, wdt, wdt, w_in[:, 5120:5184], 8, 128, 64, stg)
            cw = k.sb(es, [128, 24, 5], F32)
            for kk in range(5):
                k.dma("sp", cw[:, :, kk], self.W["ssd_conv_w"].t[0, kk, :].rearrange("(c p) -> p c", p=128), w=[cw],
                      allow_slow_non_contiguous=True)
            cbias = k.sb(es, [128, 24], F32)
            k.dma("sp", cbias[:, :], self.W["ssd_conv_b"].t[0, :].rearrange("(c p) -> p c", p=128), w=[cbias],
                  allow_slow_non_contiguous=True)
            dtb = k.sb(es, [128, 64], F32)
            k.dma("sp", dtb[:, 0:32], self.W["ssd_dt_bias_f"].t[0, :].partition_broadcast(128), w=[dtb])
            k.dma("sp", dtb[:, 32:64], self.W["ssd_dt_bias_b"].t[0, :].partition_broadcast(128), w=[dtb])
            abc = k.sb(es, [128, 64], F32)
            k.dma("sp", abc[:, 0:32], self.W["ssd_a_log_f"].t[0, :].partition_broadcast(128), w=[abc])
            k.dma("sp", abc[:, 32:64], self.W["ssd_a_log_b"].t[0, :].partition_broadcast(128), w=[abc])
            k.op("act", lambda a: a.activation(out=abc[:, :], in_=abc[:, :], func=AF.Exp), r=[abc], w=[abc])
            k.op("dve", lambda v: v.tensor_scalar(out=abc[:, :], in0=abc[:, :], scalar1=-1.0, scalar2=None, op0=ALU.mult),
                 r=[abc], w=[abc])
            xr = Ring([k.sb(es, [128, D], F32) for _ in range(2)])
            hT = k.sb(es, [128, 8, 516], BF16)
            scr = (k.sb(es, [128, D], BF16), k.sb(es, [128, 1], F32), k.sb(es, [128, 1], F32),
                   k.sb(es, [128, D], F32), k.sb(es, [128, D], BF16))
            prer = Ring([k.sb(es, [128, 516], BF16) for _ in range(3)])
            accr = Ring([k.sb(es, [128, 512], F32) for _ in range(2)])
            xc = k.sb(es, [128, 24, 512], BF16)
            rxr = Ring([k.sb(es, [128, 2560], BF16) for _ in range(2)])
            rzr = Ring([k.sb(es, [128, 2048], BF16) for _ in range(2)])
            ddr = Ring([k.sb(es, [128, 128], F32) for _ in range(2)])
            dtt = k.sb(es, [128, 64], F32)
            for bi, (t0, nt, cond) in enumerate(BLOCKS):
                ntile = nt // 128
                for j in range(ntile):
                    src, rr = self.xsrc_ap(xsrc, t0 + j * 128, 128)
                    ps, pv = self.norm_rows(src, rr or [self.xblk[bi]], 128, cond, None, scr, xr.next())
                    k.op("act", lambda a: a.copy(out=hT[:, :, j * 128:(j + 1) * 128], in_=pv), r=[ps], w=[hT])
                for side in range(2):
                    col = 512 + 2 * side
                    has = (bi >= 2) if side == 0 else (1 <= bi < NB - 1)
                    if not has:
                        k.op("pool", lambda g: g.memset(hT[:, :, col:col + 2], 0.0), w=[hT])
                    else:
                        r0 = t0 - 2 if side == 0 else t0 + nt
                        nb_ = bi - 1 if side == 0 else bi + 1
                        src, rr = self.xsrc_ap(xsrc, r0, 2)
                        ps, pv = self.norm_rows(src, rr or [self.xblk[nb_]], 2, cond, None, scr, xr.next())
                        k.op("act", lambda a: a.copy(out=hT[:, :, col:col + 2], in_=pv[:, :, 0:2]), r=[ps], w=[hT])
                for c in range(24):
                    ps = ps8.next()
                    for kk in range(8):
                        k.op("pe", lambda pe: pe.matmul(ps[:, 0:nt], lhsT=wx[:, kk, c * 128:(c + 1) * 128], rhs=hT[:, kk, 0:nt],
                                                        start=(kk == 0), stop=(kk == 7)), r=[wx, hT], w=[ps])
                    ps2 = ps8.next()
                    for kk in range(8):
                        k.op("pe", lambda pe: pe.matmul(ps2[:, 0:4], lhsT=wx[:, kk, c * 128:(c + 1) * 128], rhs=hT[:, kk, 512:516],
                                                        start=(kk == 0), stop=(kk == 7)), r=[wx, hT], w=[ps2])
                    pre = prer.next()
                    k.op("act", lambda a: a.copy(out=pre[:, 2:2 + nt], in_=ps[:, 0:nt]), r=[ps], w=[pre])
                    k.op("act", lambda a: a.copy(out=pre[:, 0:2], in_=ps2[:, 0:2]), r=[ps2], w=[pre])
                    k.op("act", lambda a: a.copy(out=pre[:, 2 + nt:4 + nt], in_=ps2[:, 2:4]), r=[ps2], w=[pre])
                    acc = accr.next()
                    k.op("dve", lambda v: v.tensor_scalar(out=acc[:, 0:nt], in0=pre[:, 0:nt], scalar1=cw[:, c, 0:1], scalar2=None,
                                                          op0=ALU.mult), r=[pre, cw], w=[acc])
                    for kk in range(1, 5):
                        k.op("dve", lambda v: v.scalar_tensor_tensor(out=acc[:, 0:nt], in0=pre[:, kk:kk + nt], scalar=cw[:, c, kk:kk + 1],
                                                                     in1=acc[:, 0:nt], op0=ALU.mult, op1=ALU.add),
                             r=[pre, cw, acc], w=[acc])
                    k.op("act", lambda a: a.activation(out=xc[:, c, 0:nt], in_=acc[:, 0:nt], func=AF.Silu, bias=cbias[:, c:c + 1]),
                         r=[acc, cbias], w=[xc])
                kt0 = t0 // 128
                for j in range(ntile):
                    for s_ in range(2):
                        k.dma("act", RBC.t[kt0 + j, :, s_ * 512:(s_ + 1) * 512].rearrange("p (g t) -> p g t", g=4),
                              xc[:, 16 + 4 * s_:20 + 4 * s_, j * 128:(j + 1) * 128], r=[xc], w=[rbcT[bi]])
                for j in range(ntile):
                    t = kt0 + j
                    rx = rxr.next()
                    for c0 in (0, 8, 16):
                        cn = 8 if c0 < 16 else 4
                        ps = ps8.next()
                        pv = ps[:, :].bitcast(BF16).rearrange("p (c t) -> p c t", c=8)
                        for c in range(cn):
                            k.op("pe", lambda pe: pe.transpose(out=pv[:, c, :], in_=xc[:, c0 + c, j * 128:(j + 1) * 128],
                                                               identity=self.identb[:, :]), r=[xc, self.identb], w=[ps])
                        k.op("act" if c0 != 8 else "dve",
                             (lambda a: a.copy(out=rx[:, c0 * 128:(c0 + cn) * 128].rearrange("p (c t) -> p c t", c=cn), in_=pv[:, 0:cn, :]))
                             if c0 != 8 else
                             (lambda v: v.tensor_copy(out=rx[:, c0 * 128:(c0 + cn) * 128].rearrange("p (c t) -> p c t", c=cn), in_=pv[:, 0:cn, :])),
                             r=[ps], w=[rx])
                    k.dma("pool", RX.t[t], rx[:, :], r=[rx], w=[rxT[t]])
                    rz = rzr.next()
                    for cb in range(4):
                        ps = ps8.next()
                        for kk in range(8):
                            k.op("pe", lambda pe: pe.matmul(ps[:, :], lhsT=hT[:, kk, j * 128:(j + 1) * 128],
                                                            rhs=wz[:, kk, cb * 512:(cb + 1) * 512],
                                                            start=(kk == 0), stop=(kk == 7)), r=[hT, wz], w=[ps])
                        k.op("act", lambda a: a.activation(out=rz[:, cb * 512:(cb + 1) * 512], in_=ps[:, :], func=AF.Silu),
                             r=[ps], w=[rz])
                    k.dma("pool", RZ.t[t], rz[:, :], r=[rz], w=[rzT[t]])
                    dd = ddr.next()
                    ps = ps8.next()
                    for kk in range(8):
                        k.op("pe", lambda pe: pe.matmul(ps[:, 0:64], lhsT=hT[:, kk, j * 128:(j + 1) * 128], rhs=wdt[:, kk, :],
                                                        start=(kk == 0), stop=(kk == 7)), r=[hT, wdt], w=[ps])
                    k.op("dve", lambda v: v.tensor_tensor(out=dtt[:, :], in0=ps[:, 0:64], in1=dtb[:, :], op=ALU.add),
                         r=[ps, dtb], w=[dtt])
                    k.op("act", lambda a: a.activation(out=dtt[:, :], in_=dtt[:, :], func=AF.Exp), r=[dtt], w=[dtt])
                    k.op("act", lambda a: a.activation(out=dd[:, 0:64], in_=dtt[:, :], func=AF.Ln, bias=onec[:, :]),
                         r=[dtt, onec], w=[dd])
                    k.op("dve", lambda v: v.tensor_tensor(out=dd[:, 64:128], in0=dd[:, 0:64], in1=abc[:, :], op=ALU.mult),
                         r=[dd, abc], w=[dd])
                    k.dma("pool", DD.t[t], dd[:, :], r=[dd], w=[ddT[t]])
            k.barrier()

        with ExitStack() as es:
            self.epsb = k.sb(es, [128, 1], F32)
            k.op("pool", lambda g: g.memset(self.epsb[:, :], EPS), w=[self.epsb])
            tri = k.sb(es, [128, 4, 128], F32)
            k.dma("sp", tri[:, :, :], self.tri_d.t.rearrange("m j i -> j m i"), w=[tri])
            negm = k.sb(es, [128, 2, 128], F32)
            k.dma("sp", negm[:, :, :], self.negm_d.t.rearrange("m j i -> j m i"), w=[negm])
            onehot = k.sb(es, [96, 32, 128], BF16)
            k.dma("sp", onehot[:, :, :], self.onehot_d.t, w=[onehot])
            negmb = k.sb(es, [128, 2, 128], BF16)
            k.dma("sp", negmb[:, :, :], self.negmb_d.t.rearrange("m j i -> j m i"), w=[negmb])
            identf = k.sb(es, [128, 128], F32)
            k.dma("sp", identf[:, :], self.identf_d.t, w=[identf])
            Hs = [k.sb(es, [128, 512], F32) for _ in range(4)]
            Hb = [k.sb(es, [128, 512], BF16) for _ in range(4)]
            rxr = Ring([k.sb(es, [128, 2560], BF16) for _ in range(3)])
            rbcr = Ring([k.sb(es, [128, 1024], BF16) for _ in range(2)])
            ddr = Ring([k.sb(es, [128, 128], F32) for _ in range(3)])
            bankY = Ring(self.psb[0:2])
            bankLb = Ring(self.psb[2:5])
            bankM = Ring(self.psb[5:8])
            def ring2(shape, dt):
                return Ring([k.sb(es, shape, dt) for _ in range(2)])
            ela_r = ring2([128, 32], F32)
            edl_r = ring2([128, 32], F32)
            etot_r = ring2([128, 32], F32)
            laT_r = ring2([96, 128], BF16)
            larep_r = ring2([128, 96], F32)
            Abf_r = ring2([96, 128], BF16)
            Bbf_r = ring2([96, 128], BF16)
            R1_r = ring2([96, 128], F32)
            nla_r = ring2([128, 32], F32)
            xdt_r = ring2([128, 2048], BF16)
            xdl_r = ring2([128, 2048], BF16)
            CBm_r = ring2([128, 4, 128], BF16)
            Eh_r = Ring([k.sb(es, [128, 4, 128], BF16) for _ in range(3)])
            Mh = Ring([k.sb(es, [128, 4, 128], BF16) for _ in range(3)])
            ysr = Ring([k.sb(es, [128, 2048], F32) for _ in range(3)])
            wo = k.sb(es, [128, 16, D], BF16)
            with ExitStack() as es2:
                stg = Ring([k.sb(es2, [128, 1024], F32) for _ in range(2)])
                self.load_w(es2, wo, wo, w_out, 16, 128, D, stg)
                k.barrier()
            gnbc = k.sb(es, [128, 2048], F32)
            k.dma("sp", gnbc[:, :], self.W["ssd_norm"].t[0, :].partition_broadcast(128), w=[gnbc])
            dsk = k.sb(es, [128, 32], F32)
            k.dma("sp", dsk[:, :], self.W["ssd_d"].t[0, :].partition_broadcast(128), w=[dsk])
            rzr = Ring([k.sb(es, [128, 2048], BF16) for _ in range(1)])
            yfr = Ring([k.sb(es, [128, 2048], F32) for _ in range(1)])
            xr = Ring([k.sb(es, [128, D], F32) for _ in range(2)])
            ynb = k.sb(es, [128, 2048], BF16)
            ogT = k.sb(es, [128, 16, 128], BF16)
            junk = k.sb(es, [128, 512], BF16)
            ss4 = k.sb(es, [128, 4], F32)
            rs4 = k.sb(es, [128, 4], F32)
            tm = k.sb(es, [128, 512], F32)

            def stageA(d, rx, rbc, dd):
                c = dict(d=d, rx=rx, rbc=rbc, dd=dd)
                cm = tri[:, 0 if d == 0 else 2, :]
                sm = tri[:, 1 if d == 0 else 3, :]
                dA = dd[:, 64 + 32 * d:96 + 32 * d]
                dt = dd[:, 32 * d:32 * d + 32]
                larep = larep_r.next()
                la_sb = larep
                ela, edl, etot, laT = ela_r.next(), edl_r.next(), etot_r.next(), laT_r.next()
                Abf, Bbf, R1 = Abf_r.next(), Bbf_r.next(), R1_r.next()
                xdt, xdl, CBm = xdt_r.next(), xdl_r.next(), CBm_r.next()
                nla = nla_r.next()
                c.update(la_sb=la_sb, ela=ela, etot=etot, laT=laT, xdt=xdt, xdl=xdl, CBm=CBm, nla=nla)
                psL = bankM.next()
                k.op("pe", lambda pe: pe.matmul(psL[:, 0:32], lhsT=cm, rhs=dA, start=True, stop=True), r=[tri, dd], w=[psL])
                k.op("pe", lambda pe: pe.matmul(psL[:, 32:64], lhsT=sm, rhs=dA, start=True, stop=True), r=[tri, dd], w=[psL])
                k.op("pe", lambda pe: pe.matmul(psL[:, 64:96], lhsT=self.ones_f[:, :], rhs=dA, start=True, stop=True),
                     r=[self.ones_f, dd], w=[psL])
                for rep in range(3):
                    k.op("act", lambda a: a.copy(out=larep[:, rep * 32:(rep + 1) * 32], in_=psL[:, 0:32]), r=[psL], w=[larep])
                k.op("act", lambda a: a.mul(out=nla[:, :], in_=psL[:, 0:32], mul=-1.0), r=[psL], w=[nla])
                k.op("act", lambda a: a.activation(out=ela[:, :], in_=psL[:, 0:32], func=AF.Exp), r=[psL], w=[ela])
                k.op("act", lambda a: a.activation(out=edl[:, :], in_=psL[:, 32:64], func=AF.Exp), r=[psL], w=[edl])
                k.op("act", lambda a: a.activation(out=etot[:, :], in_=psL[:, 64:96], func=AF.Exp), r=[psL], w=[etot])
                xs3 = rx[:, 0:2048].rearrange("p (h e) -> p h e", h=32)
                k.op("pool", lambda g: g.tensor_tensor(out=xdt[:, :].rearrange("p (h e) -> p h e", h=32), in0=xs3,
                                                       in1=dt.unsqueeze(2).broadcast_to([128, 32, 64]), op=ALU.mult),
                     r=[rx, dd], w=[xdt])
                k.op("pool", lambda g: g.tensor_tensor(out=xdl[:, :].rearrange("p (h e) -> p h e", h=32),
                                                       in0=xdt[:, :].rearrange("p (h e) -> p h e", h=32),
                                                       in1=edl[:, :].unsqueeze(2).broadcast_to([128, 32, 64]), op=ALU.mult),
                     r=[xdt, edl], w=[xdl])
                psT = bankM.next()
                k.op("pe", lambda pe: pe.transpose(out=psT[0:96, 0:128], in_=larep[:, 0:96], identity=identf[:, :]),
                     r=[larep, identf], w=[psT])
                k.op("act", lambda a: a.copy(out=Abf[:, :], in_=psT[0:96, 0:128]), r=[psT], w=[Abf])
                k.op("dve", lambda v: v.tensor_tensor(out=R1[:, :], in0=psT[0:96, 0:128], in1=Abf[:, :], op=ALU.subtract),
                     r=[psT, Abf], w=[R1])
                k.op("act", lambda a: a.copy(out=Bbf[:, :], in_=R1[:, :]), r=[R1], w=[Bbf])
                k.op("dve", lambda v: v.tensor_tensor(out=R1[:, :], in0=R1[:, :], in1=Bbf[:, :], op=ALU.subtract),
                     r=[R1, Bbf], w=[R1])
                k.op("pool", lambda g: g.tensor_copy(out=laT[0:32, :], in_=Abf[0:32, :]), r=[Abf], w=[laT])
                k.op("pool", lambda g: g.tensor_copy(out=laT[32:64, :], in_=Bbf[32:64, :]), r=[Bbf], w=[laT])
                k.op("pool", lambda g: g.tensor_copy(out=laT[64:96, :], in_=R1[64:96, :]), r=[R1], w=[laT])
                psCB = bankM.next()
                for g in range(4):
                    k.op("pe", lambda pe: pe.matmul(psCB[:, g * 128:(g + 1) * 128], lhsT=rbc[:, g * 128:(g + 1) * 128],
                                                    rhs=rbc[:, 512 + g * 128:512 + (g + 1) * 128], start=True, stop=True),
                         r=[rbc], w=[psCB])
                k.op("dve", lambda v: v.tensor_tensor(out=CBm[:, :, :], in0=psCB[:, :].rearrange("p (g i) -> p g i", g=4),
                                                      in1=cm.unsqueeze(1).broadcast_to([128, 4, 128]), op=ALU.mult),
                     r=[psCB, tri], w=[CBm])
                return c

            def stageB(c, ys):
                d, rx, rbc = c["d"], c["rx"], c["rbc"]
                la_sb, ela, etot, laT, xdt, xdl, CBm = (c[n] for n in ("la_sb", "ela", "etot", "laT", "xdt", "xdl", "CBm"))
                ng = negm[:, d, :]
                nla = c["nla"]

                def lb_mm(i):
                    h0 = i * 4
                    psLb = bankLb.next()
                    for hh in range(4):
                        k.op("pe", lambda pe: pe.matmul(psLb[:, hh * 128:(hh + 1) * 128], lhsT=onehot[:, h0 + hh, :],
                                                        rhs=laT[:, :], start=True, stop=False), r=[onehot, laT], w=[psLb])
                        k.op("pe", lambda pe: pe.matmul(psLb[:, hh * 128:(hh + 1) * 128], lhsT=self.identb[:, :],
                                                        rhs=negmb[:, d, :], start=False, stop=True), r=[self.identb, negmb], w=[psLb])
                    return psLb

                pend = [lb_mm(0), lb_mm(1)]

                def heads(g):
                    psY = bankY.next()
                    for hq in range(2):
                        i = g * 2 + hq
                        h0 = i * 4
                        psLb = pend.pop(0)
                        if i + 2 < 8:
                            pend.append(lb_mm(i + 2))
                        Eh = Eh_r.next()
                        for hh in range(4):
                            k.op("act", lambda a: a.activation(out=Eh[:, hh, :], in_=psLb[:, hh * 128:(hh + 1) * 128], func=AF.Exp,
                                                               bias=nla[:, h0 + hh:h0 + hh + 1]), r=[psLb, nla], w=[Eh])
                        mh = Mh.next()
                        k.op("dve", lambda v: v.tensor_tensor(out=mh[:, :, :], in0=Eh[:, :, :],
                                                              in1=CBm[:, g:g + 1, :].broadcast_to([128, 4, 128]), op=ALU.mult),
                             r=[Eh, CBm], w=[mh])
                        for hh in range(4):
                            h = h0 + hh
                            k.op("pe", lambda pe: pe.matmul(psY[:, (h % 8) * 64:(h % 8 + 1) * 64], lhsT=mh[:, hh, :],
                                                            rhs=xdt[:, h * 64:(h + 1) * 64], start=True, stop=True),
                                 r=[mh, xdt], w=[psY])
                    return psY

                def tail(g, psY):
                    psYi = bankM.next()
                    k.op("pe", lambda pe: pe.matmul(psYi[:, :], lhsT=rbc[:, 512 + g * 128:512 + (g + 1) * 128], rhs=Hb[g][:, :],
                                                    start=True, stop=True), r=[rbc, Hb[g]], w=[psYi])
                    yv = ys[:, g * 512:(g + 1) * 512]
                    k.op("dve", lambda v: v.tensor_tensor(out=yv.rearrange("p (h e) -> p h e", h=8),
                                                          in0=psYi[:, :].rearrange("p (h e) -> p h e", h=8),
                                                          in1=ela[:, g * 8:(g + 1) * 8].unsqueeze(2).broadcast_to([128, 8, 64]),
                                                          op=ALU.mult), r=[psYi, ela], w=[ys])
                    k.op("dve", lambda v: v.tensor_tensor(out=yv, in0=yv, in1=psY[:, :], op=ALU.add), r=[ys, psY], w=[ys])
                    psH = bankM.next()
                    k.op("pe", lambda pe: pe.matmul(psH[:, :], lhsT=rx[:, 2048 + g * 128:2048 + (g + 1) * 128],
                                                    rhs=xdl[:, g * 512:(g + 1) * 512], start=True, stop=True), r=[rx, xdl], w=[psH])
                    k.op("pool", lambda v: v.tensor_tensor(out=Hs[g][:, :].rearrange("p (h e) -> p h e", h=8),
                                                           in0=Hs[g][:, :].rearrange("p (h e) -> p h e", h=8),
                                                           in1=etot[:, g * 8:(g + 1) * 8].unsqueeze(2).broadcast_to([128, 8, 64]),
                                                           op=ALU.mult), r=[Hs[g], etot], w=[Hs[g]])
                    k.op("dve", lambda v: v.tensor_tensor(out=Hs[g][:, :], in0=Hs[g][:, :], in1=psH[:, :], op=ALU.add),
                         r=[Hs[g], psH], w=[Hs[g]])
                    k.op("pool", lambda gp: gp.tensor_copy(out=Hb[g][:, :], in_=Hs[g][:, :]), r=[Hs[g]], w=[Hb[g]])

                prev = None
                for g in range(4):
                    py = heads(g)
                    if prev is not None:
                        tail(*prev)
                    prev = (g, py)
                tail(*prev)

            def reset_state():
                for g in range(4):
                    k.op("pool", lambda gp: gp.memset(Hs[g][:, :], 0.0), w=[Hs[g]])
                    k.op("pool", lambda gp: gp.memset(Hb[g][:, :], 0.0), w=[Hb[g]])

            def loads(t):
                bi = 0 if t < 2 else 1 + (t - 2) // 4
                rx = rxr.next()
                rbc = rbcr.next()
                dd = ddr.next()
                k.dma("sp", rx[:, :], RX.t[t], r=[rxT[t]], w=[rx])
                k.dma("sp", rbc[:, :], RBC.t[t], r=[rbcT[bi]], w=[rbc])
                k.dma("sp", dd[:, :], DD.t[t], r=[ddT[t]], w=[dd])
                return rx, rbc, dd

            reset_state()
            cnext = stageA(0, *loads(0))
            for t in range(NT):
                c = cnext
                if t + 1 < NT:
                    cnext = stageA(0, *loads(t + 1))
                ys = ysr.next()
                stageB(c, ys)
                k.dma("pool", YF.t[t], ys[:, :], r=[ys], w=[yfT[t]])
            reset_state()
            order = [1, 0] + list(range(NT - 1, 1, -1))
            cnext = stageA(1, *loads(order[0]))

            def out_stage(t, rx, ys):
                cond = 0 if t < 2 else 1
                bi = 0 if t < 2 else 1 + (t - 2) // 4
                rz = rzr.next()
                yf = yfr.next()
                xt = xr.next()
                k.dma("sp", rz[:, :], RZ.t[t], r=[rzT[t]], w=[rz])
                k.dma("sp", yf[:, :], YF.t[t], r=[yfT[t]], w=[yf])
                sap, rr = self.xsrc_ap(xsrc, t * 128, 128)
                k.dma("sp", xt[:, :], sap, r=(rr or [self.xblk[bi]]), w=[xt])
                k.op("dve", lambda v: v.tensor_tensor(out=ys[:, :], in0=ys[:, :], in1=yf[:, :], op=ALU.add), r=[ys, yf], w=[ys])
                ytmp = yf
                k.op("pool", lambda g: g.tensor_tensor(out=ytmp[:, :].rearrange("p (h e) -> p h e", h=32),
                                                       in0=rx[:, 0:2048].rearrange("p (h e) -> p h e", h=32),
                                                       in1=dsk[:, :].unsqueeze(2).broadcast_to([128, 32, 64]), op=ALU.mult),
                     r=[rx, dsk, yf], w=[ytmp])
                k.op("dve", lambda v: v.tensor_tensor(out=ys[:, :], in0=ys[:, :], in1=ytmp[:, :], op=ALU.add), r=[ys, ytmp], w=[ys])
                k.op("dve", lambda v: v.tensor_tensor(out=ys[:, :], in0=ys[:, :], in1=rz[:, :], op=ALU.mult), r=[ys, rz], w=[ys])
                for g in range(4):
                    k.op("act", lambda a: a.activation(out=junk[:, :], in_=ys[:, g * 512:(g + 1) * 512], func=AF.Square,
                                                       accum_out=ss4[:, g:g + 1]), r=[ys], w=[junk, ss4])
                k.op("act", lambda a: a.activation(out=rs4[:, :], in_=ss4[:, :], func=AF.Sqrt, scale=1.0 / 512, bias=self.epsb[:, :]),
                     r=[ss4, self.epsb], w=[rs4])
                k.op("dve", lambda v: v.reciprocal(out=rs4[:, :], in_=rs4[:, :]), r=[rs4], w=[rs4])
                for g in range(4):
                    k.op("dve", lambda v: v.scalar_tensor_tensor(out=ynb[:, g * 512:(g + 1) * 512], in0=ys[:, g * 512:(g + 1) * 512],
                                                                 scalar=rs4[:, g:g + 1], in1=gnbc[:, g * 512:(g + 1) * 512],
                                                                 op0=ALU.mult, op1=ALU.mult), r=[ys, rs4, gnbc], w=[ynb])
                self.tok_outproj(ynb, 16, ogT, wo, xt, tm, cond)
                k.dma("pool", self.xres.t[t * 128:(t + 1) * 128, :], xt[:, :], r=[xt], w=[self.xblk[bi]])

            pending_out = None
            for oi, t in enumerate(order):
                c = cnext
                if oi + 1 < NT:
                    cnext = stageA(1, *loads(order[oi + 1]))
                ys = ysr.next()
                stageB(c, ys)
                if pending_out is not None:
                    out_stage(*pending_out)
                pending_out = (t, c["rx"], ys)
            out_stage(*pending_out)
            k.barrier()

    def tok_outproj(self, ogb, kc, ogT, wo, xt, tm, cond):
        k = self.k
        for c0 in range(0, kc, 8):
            ps = self.ps8.next()
            pv = ps[:, :].bitcast(BF16).rearrange("p (c t) -> p c t", c=8)
            for c in range(8):
                k.op("pe", lambda pe: pe.transpose(out=pv[:, c, :], in_=ogb[:, (c0 + c) * 128:(c0 + c + 1) * 128],
                                                   identity=self.identb[:, :]), r=[ogb, self.identb], w=[ps])
            k.op("act", lambda a: a.copy(out=ogT[:, c0:c0 + 8, :], in_=pv), r=[ps], w=[ogT])
        for cb in range(2):
            ps = self.ps8.next()
            for kk in range(kc):
                k.op("pe", lambda pe: pe.matmul(ps[:, :], lhsT=ogT[:, kk, :], rhs=wo[:, kk, cb * 512:(cb + 1) * 512],
                                                start=(kk == 0), stop=(kk == kc - 1)), r=[ogT, wo], w=[ps])
            k.op("dve", lambda v: v.tensor_tensor(out=tm[:, :], in0=ps[:, :],
                                                  in1=self.bcm[cond][:, 2 * D + cb * 512:2 * D + (cb + 1) * 512], op=ALU.mult),
                 r=[ps, self.bcm[cond]], w=[tm])
            k.op("dve", lambda v: v.tensor_tensor(out=xt[:, cb * 512:(cb + 1) * 512], in0=tm[:, :],
                                                  in1=xt[:, cb * 512:(cb + 1) * 512], op=ALU.add), r=[tm, xt], w=[xt])

    def outproj(self, es, og_src, ogb, kp, kc, wo, xsrc, xr, tm):
        k = self.k
        for bi, (t0, nt, cond) in enumerate(BLOCKS):
            ob = ogb.next()
            src, trk = og_src(bi, nt)
            if kp == 128:
                s4 = src.rearrange("d (k two) t -> d two k t", two=2)
                for hp in range(2):
                    k.dma("sp", ob[hp * 64:(hp + 1) * 64, :, 0:nt], s4[:, hp, :, :], r=[trk], w=[ob])
            else:
                k.dma("sp", ob[:, :, 0:nt], src, r=[trk], w=[ob])
            for j in range(nt // 128):
                xt = xr.next()
                sap, rr = self.xsrc_ap(xsrc, t0 + j * 128, 128)
                k.dma("sp", xt[:, :], sap, r=(rr or [self.xblk[bi]]), w=[xt])
                import os
                for cb in range(2 if not os.environ.get("DBG_SKIPMM") else 0):
                    ps = self.psg.next()
                    for kk in range(kc):
                        k.op("pe", lambda pe, kk=kk, cb=cb, ps=ps: pe.matmul(
                            ps[:, :], lhsT=ob[0:kp, kk, j * 128:(j + 1) * 128], rhs=wo[0:kp, kk, cb * 512:(cb + 1) * 512],
                            start=(kk == 0), stop=(kk == kc - 1)), r=[ob, wo], w=[ps])
                    k.op("dve", lambda v, cb=cb, ps=ps: v.tensor_tensor(
                        out=tm[:, :], in0=ps[:, :], in1=self.bcm[cond][:, 2 * D + cb * 512:2 * D + (cb + 1) * 512], op=ALU.mult),
                        r=[ps, self.bcm[cond]], w=[tm])
                    k.op("dve", lambda v, cb=cb, xt=xt: v.tensor_tensor(
                        out=xt[:, cb * 512:(cb + 1) * 512], in0=tm[:, :], in1=xt[:, cb * 512:(cb + 1) * 512], op=ALU.add),
                        r=[tm, xt], w=[xt])
                k.dma("pool", self.xres.t[t0 + j * 128:t0 + (j + 1) * 128, :], xt[:, :], r=[xt], w=[self.xblk[bi]])

    def final(self, xsrc):
        k = self.k
        with ExitStack() as es:
            xr = Ring([k.sb(es, [128, D], F32) for _ in range(3)])
            junk = k.sb(es, [128, D], BF16)
            ss = k.sb(es, [128, 1], F32)
            rs = k.sb(es, [128, 1], F32)
            epsb = k.sb(es, [128, 1], F32)
            k.op("pool", lambda g: g.memset(epsb[:, :], EPS), w=[epsb])
            fg = k.sb(es, [128, D], F32)
            k.dma("sp", fg[:, :], self.final_g.t.partition_broadcast(128), w=[fg])
            for bi, (t0, nt, cond) in enumerate(BLOCKS):
                if cond == 0 and not self.debug_x:
                    continue
                for j in range(nt // 128):
                    xt = xr.next()
                    tt0 = t0 + j * 128
                    sap, rr = self.xsrc_ap(xsrc, tt0, 128)
                    k.dma("sp", xt[:, :], sap, r=(rr or [self.xblk[bi]]), w=[xt])
                    if self.debug_x:
                        k.dma("pool", self.out.t[tt0:tt0 + 128, :], xt[:, :], r=[xt], w=[self.out])
                        continue
                    k.op("act", lambda a, xt=xt: a.activation(out=junk[:, :], in_=xt[:, :], func=AF.Square, accum_out=ss[:, :]),
                         r=[xt], w=[junk, ss])
                    k.op("act", lambda a: a.activation(out=rs[:, :], in_=ss[:, :], func=AF.Sqrt, scale=1.0 / D, bias=epsb[:, :]),
                         r=[ss, epsb], w=[rs])
                    k.op("dve", lambda v: v.reciprocal(out=rs[:, :], in_=rs[:, :]), r=[rs], w=[rs])
                    k.op("dve", lambda v, xt=xt: v.scalar_tensor_tensor(out=xt[:, :], in0=xt[:, :], scalar=rs[:, 0:1],
                                                                       in1=fg[:, :], op0=ALU.mult, op1=ALU.mult),
                         r=[xt, rs, fg], w=[xt])
                    k.dma("pool", self.out.t[tt0 - CTX:tt0 - CTX + 128, :], xt[:, :], r=[xt], w=[self.out])


WSHAPES = {
    "mla_w_in": [1, 1024, 1696], "mla_q_norm": [1, 384], "mla_w_uq": [1, 384, 1536], "mla_kv_norm": [1, 256],
    "mla_w_ukv": [1, 256, 2048], "mla_w_out": [1, 1024, 1024],
    "gla_w_in": [1, 1024, 3104], "gla_w_gf": [1, 16, 512], "gla_b_gf": [1, 512], "gla_w_gb": [1, 16, 512],
    "gla_b_gb": [1, 512], "gla_o_norm": [1, 256], "gla_w_out": [1, 1024, 1024],
    "gqa_w_in": [1, 1024, 2560], "gqa_q_norm": [1, 64], "gqa_k_norm": [1, 64], "gqa_w_out": [1, 1024, 1024],
    "ssd_w_in": [1, 1024, 5184], "ssd_conv_w": [1, 5, 3072], "ssd_conv_b": [1, 3072], "ssd_dt_bias_f": [1, 32],
    "ssd_dt_bias_b": [1, 32], "ssd_a_log_f": [1, 32], "ssd_a_log_b": [1, 32], "ssd_d": [1, 32], "ssd_norm": [1, 2048],
    "ssd_w_out": [1, 2048, 1024],
}


def tri_consts():
    j = np.arange(128)[:, None]
    i = np.arange(128)[None, :]
    return np.stack([(j <= i), (j > i), (j >= i), (j < i)]).astype(np.float32)


def rope_tables(rd):
    hf = rd // 4
    inv = 10000.0 ** (-np.arange(hf, dtype=np.float64) / hf)
    p = np.arange(SEQ)
    row = (p // 64).astype(np.float64)[:, None] * inv[None, :]
    col = (p % 64).astype(np.float64)[:, None] * inv[None, :]
    cos = np.concatenate([np.cos(row), np.cos(row), np.cos(col), np.cos(col)], axis=1)
    sin = np.concatenate([-np.sin(row), np.sin(row), -np.sin(col), np.sin(col)], axis=1)
    tab = np.zeros((T, 2, rd), np.float32)
    tab[:CTX, 0, :] = 1.0
    tab[CTX:, 0, :] = cos
    tab[CTX:, 1, :] = sin
    return tab


def run(inputs, layers=(0, 1, 2, 3), debug_x=False, cores=(0, 1), stop=None):
    nc = bass.Bass("TRN2", target_bir_lowering=False)
    Prog(nc, layers=layers, debug_x=debug_x, stop=stop).build()
    f = lambda a: np.ascontiguousarray(np.asarray(a, dtype=np.float32))
    common = {nm: f(inputs[nm]) for nm in WSHAPES}
    for nm in ("ada_w", "ada_b", "norm_g", "final_g"):
        common[nm] = f(inputs[nm])
    common["ident_bf"] = np.eye(128, dtype=np.float32).astype(ml_dtypes.bfloat16)
    common["rope_mla"] = rope_tables(32)
    common["rope_gqa"] = rope_tables(64)
    common["tri"] = tri_consts()
    common["negm"] = ((1.0 - tri_consts()[[0, 2]]) * -1e30).astype(np.float32)
    oh = np.zeros((3, 32, 32, 128), np.float32)
    oh[:, np.arange(32), np.arange(32), :] = 1.0
    common["onehot3"] = oh.reshape(96, 32, 128).astype(ml_dtypes.bfloat16)
    common["negmb"] = common["negm"].astype(ml_dtypes.bfloat16)
    common["ident_f"] = np.eye(128, dtype=np.float32)
    in_maps = []
    for b in cores:
        m = dict(common)
        m["xin"] = np.ascontiguousarray(np.concatenate([f(inputs["ctx"])[b], f(inputs["x"])[b]], axis=0))
        m["c2"] = np.ascontiguousarray(np.stack([f(inputs["c_ctx"]), f(inputs["c"])[b]], axis=0))
        in_maps.append(m)
    res = run_bass_kernel_spmd(nc, in_maps, core_ids=list(range(len(cores))))
    return [r["y"] for r in res.results]


FUSED = True


def kernel(**inputs):
    if FUSED:
        outs = run(inputs)
        return np.stack(outs, axis=0).astype(np.float32)
    cur = dict(inputs)
    for L in (0, 1, 2):
        outs = run(cur, layers=(L,), debug_x=True)
        st = np.stack(outs, axis=0)
        cur["ctx"] = np.ascontiguousarray(st[:, :CTX])
        cur["x"] = np.ascontiguousarray(st[:, CTX:])
    outs = run(cur, layers=(3,), debug_x=False)
    return np.stack(outs, axis=0).astype(np.float32)
```

```python
import math
from contextlib import ExitStack

import numpy as np
import ml_dtypes
import concourse.bass as bass
import concourse.mybir as mybir
from concourse.bass_utils import run_bass_kernel_spmd

F32 = mybir.dt.float32
BF16 = mybir.dt.bfloat16
AF = mybir.ActivationFunctionType
ALU = mybir.AluOpType
AX = mybir.AxisListType

D = 1024
SEQ = 8192
CTX = 256
T = SEQ + CTX
NT = T // 128
EPS = 1e-6
EPOCH = 30000

BLOCKS = [(0, 256, 0)] + [(256 + 512 * i, 512, 1) for i in range(16)]
NB = len(BLOCKS)


class Buf:
    __slots__ = ("w", "r")

    def __init__(self):
        self.w = None
        self.r = {}


class TT:
    def __init__(self, t):
        self.t = t
        self.b = Buf()

    def __getitem__(self, idx):
        return self.t[idx]


class Ring:
    def __init__(self, items):
        self.items = items
        self.i = 0

    def next(self):
        it = self.items[self.i % len(self.items)]
        self.i += 1
        return it


class KB:
    def __init__(self, nc, es):
        self.nc = nc
        self.es = es
        self.eng = {"pe": nc.tensor, "act": nc.scalar, "dve": nc.vector, "pool": nc.gpsimd, "sp": nc.sync}
        self.sems = {e: [] for e in self.eng}
        self.cnt = {e: 0 for e in self.eng}
        self.seen = {e: {} for e in self.eng}
        self.last = {e: None for e in self.eng}
        self.slots = {}
        self.slot_i = {}
        for q in ("sp", "pool", "act"):
            self.slots[q] = [[es.enter_context(nc.semaphore(f"d_{q}_{i}")), 0, f"d_{q}_{i}"] for i in range(12)]
            self.slot_i[q] = 0
        self.nsb = 0

    def sb(self, es, shape, dt, name=None):
        self.nsb += 1
        return TT(es.enter_context(self.nc.sbuf_tensor(name or f"sb{self.nsb}", list(shape), dt)))

    def dram(self, shape, dt, name):
        h = self.nc.dram_tensor(name, list(shape), dt, kind="Internal")
        return TT(h.ap())

    def _wait(self, e, deps):
        seen = self.seen[e]
        for ev in deps:
            key, sem, val, src = ev
            if src == "pe" and e == "pe":
                continue
            if seen.get(key, 0) >= val:
                continue
            self.eng[e].wait_ge(sem, val)
            seen[key] = val

    def _deps(self, reads, writes):
        deps = []
        for t in reads:
            if t.b.w is not None:
                deps.append(t.b.w)
        for t in writes:
            if t.b.w is not None:
                deps.append(t.b.w)
            deps.extend(t.b.r.values())
        return deps

    def _mark(self, ev, reads, writes):
        for t in reads:
            t.b.r[ev[0]] = ev
        for t in writes:
            t.b.w = ev
            t.b.r = {}

    def op(self, e, fn, r=(), w=()):
        self._wait(e, self._deps(r, w))
        ins = fn(self.eng[e])
        epoch = self.cnt[e] // EPOCH
        while len(self.sems[e]) <= epoch:
            self.sems[e].append(self.es.enter_context(self.nc.semaphore(f"s_{e}_{len(self.sems[e])}")))
        sem = self.sems[e][epoch]
        val = self.cnt[e] % EPOCH + 1
        ins.then_inc(sem, 1)
        self.cnt[e] += 1
        ev = ((e, epoch), sem, val, e)
        self.last[e] = ev
        self._mark(ev, r, w)
        return ev

    def dma(self, q, out, in_, r=(), w=(), **kw):
        deps = self._deps(r, w)
        slots = self.slots[q]
        si = self.slot_i[q] % len(slots)
        self.slot_i[q] += 1
        slot = slots[si]
        if slot[1] > 0:
            deps.append((slot[2], slot[0], 16 * slot[1], "dma"))
        self._wait(q, deps)
        ins = self.eng[q].dma_start(out=out, in_=in_, **kw)
        ins.then_inc(slot[0], 16)
        slot[1] += 1
        ev = (slot[2], slot[0], 16 * slot[1], "dma")
        self._mark(ev, r, w)
        return ev

    def barrier(self):
        evs = [self.last[e] for e in self.eng if self.last[e] is not None]
        for q in self.slots:
            for slot in self.slots[q]:
                if slot[1] > 0:
                    evs.append((slot[2], slot[0], 16 * slot[1], "dma"))
        for e in self.eng:
            seen = self.seen[e]
            for ev in evs:
                key, sem, val, src = ev
                if src == e:
                    continue
                if seen.get(key, 0) >= val:
                    continue
                self.eng[e].wait_ge(sem, val)
                seen[key] = val


class Prog:
    def __init__(self, nc, layers=(0, 1, 2, 3), debug_x=False, stop=None):
        self.nc = nc
        self.stop = stop
        self.layers = layers
        self.debug_x = debug_x

    def din(self, name, shape, dt=F32):
        return TT(self.nc.dram_tensor(name, list(shape), dt, kind="ExternalInput").ap())

    def build(self):
        nc = self.nc
        with ExitStack() as es:
            self.k = k = KB(nc, es)
            self.es = es
            self.xin = self.din("xin", [T, D])
            self.c2 = self.din("c2", [2, D])
            self.ada_w = self.din("ada_w", [4, D, 3 * D])
            self.ada_b = self.din("ada_b", [4, 3 * D])
            self.norm_g = self.din("norm_g", [4, D])
            self.final_g = self.din("final_g", [D])
            self.W = {}
            for nm, shp in WSHAPES.items():
                self.W[nm] = self.din(nm, shp)
            self.identb_d = self.din("ident_bf", [128, 128], BF16)
            self.rope_mla = self.din("rope_mla", [T, 2, 32])
            self.rope_gqa = self.din("rope_gqa", [T, 2, 64])
            self.tri_d = self.din("tri", [4, 128, 128])
            self.negm_d = self.din("negm", [2, 128, 128])
            self.onehot_d = self.din("onehot3", [96, 32, 128], BF16)
            self.negmb_d = self.din("negmb", [2, 128, 128], BF16)
            self.identf_d = self.din("ident_f", [128, 128])
            if self.debug_x:
                self.out = TT(nc.dram_tensor("y", [T, D], F32, kind="ExternalOutput").ap())
            else:
                self.out = TT(nc.dram_tensor("y", [SEQ, D], F32, kind="ExternalOutput").ap())
            self.xres = k.dram([T, D], F32, "xres")
            self.xblk = [TT(self.xres.t) for _ in range(NB)]
            self.modd = k.dram([4, 2, 3 * D], F32, "modd")
            self.identb = k.sb(es, [128, 128], BF16, "identb")
            k.dma("sp", self.identb[:, :], self.identb_d.t[:, :], w=[self.identb])
            self.ones_f = k.sb(es, [128, 128], F32, "ones_f")
            k.op("pool", lambda g: g.memset(self.ones_f[:, :], 1.0), w=[self.ones_f])
            self.psb = [TT(es.enter_context(nc.psum_tensor(f"ps{i}", [128, 512], F32))) for i in range(8)]
            self.psg = Ring(self.psb[0:6])
            self.pso = Ring(self.psb[6:8])
            self.ps8 = Ring(self.psb)
            self.bcm = [k.sb(es, [128, 3 * D], F32, f"bcm{c}") for c in range(2)]
            self.gmod = [k.sb(es, [128, D], F32, f"gmod{c}") for c in range(2)]
            self.ngbc = k.sb(es, [128, D], F32, "ngbc")

            first = True
            for L in self.layers:
                self.modulation(L)
                if self.stop == "mod":
                    break
                xsrc = self.xin if first else None
                if L == 0:
                    self.layer_attn(L, "mla", xsrc)
                elif L == 2:
                    self.layer_attn(L, "gqa", xsrc)
                elif L == 1:
                    self.layer_gla(L, xsrc)
                elif L == 3:
                    self.layer_ssd(L, xsrc)
                first = False
                k.barrier()
            self.final(self.xin if first else None)
            k.barrier()
        return nc

    def xsrc_ap(self, xsrc, t0, n):
        if xsrc is not None:
            return xsrc.t[t0:t0 + n, :], [xsrc]
        return self.xres.t[t0:t0 + n, :], None

    def modulation(self, L):
        k = self.k
        with ExitStack() as es:
            cT = k.sb(es, [128, 8, 2], F32)
            sT = k.sb(es, [128, 8, 2], F32)
            for kk in range(8):
                k.dma("sp", cT[:, kk, :], self.c2.t[:, kk * 128:(kk + 1) * 128].rearrange("c p -> p c"), w=[cT],
                      allow_slow_non_contiguous=True)
            k.op("act", lambda a: a.activation(out=sT[:, :, :], in_=cT[:, :, :], func=AF.Silu), r=[cT], w=[sT])
            msb = k.sb(es, [2, 3 * D], F32)
            bb = k.sb(es, [2, 3 * D], F32)
            k.dma("sp", bb[:, :], self.ada_b.t[L, :].partition_broadcast(2), w=[bb])
            wr = Ring([k.sb(es, [128, 8, 512], F32) for _ in range(2)])
            for cb in range(6):
                wt = wr.next()
                k.dma("sp", wt[:, :, :],
                      self.ada_w.t[L, :, cb * 512:(cb + 1) * 512].rearrange("(k p) n -> p k n", p=128), w=[wt])
                ps = self.psg.next()
                for kk in range(8):
                    k.op("pe", lambda pe, kk=kk: pe.matmul(ps[0:2, :], lhsT=sT[:, kk, :], rhs=wt[:, kk, :],
                                                          start=(kk == 0), stop=(kk == 7)), r=[sT, wt], w=[ps])
                k.op("dve", lambda v: v.tensor_tensor(out=msb[:, cb * 512:(cb + 1) * 512], in0=ps[0:2, :],
                                                      in1=bb[:, cb * 512:(cb + 1) * 512], op=ALU.add),
                     r=[ps, bb], w=[msb])
            md = TT(self.modd.t)
            k.dma("sp", self.modd.t[L, :, :], msb[:, :], r=[msb], w=[md])
            for c in range(2):
                k.dma("sp", self.bcm[c][:, :], self.modd.t[L, c, :].partition_broadcast(128), r=[md], w=[self.bcm[c]])
            k.dma("sp", self.ngbc[:, :], self.norm_g.t[L, :].partition_broadcast(128), w=[self.ngbc])
            for c in range(2):
                k.op("dve", lambda v, c=c: v.scalar_tensor_tensor(out=self.gmod[c][:, :], in0=self.bcm[c][:, D:2 * D],
                                                                 scalar=1.0, in1=self.ngbc[:, :], op0=ALU.add,
                                                                 op1=ALU.mult),
                     r=[self.bcm[c], self.ngbc], w=[self.gmod[c]])
            k.barrier()

    def load_w(self, es_stage, dst, dview, src_ap, kc, kp, n, stg):
        k = self.k
        CH = 1024
        for kk in range(kc):
            for c0 in range(0, n, CH):
                cn = min(CH, n - c0)
                st = stg.next()
                k.dma("sp", st[0:kp, 0:cn], src_ap[kk * kp:(kk + 1) * kp, c0:c0 + cn], w=[st])
                k.op("pool", lambda g, st=st, kk=kk, c0=c0, cn=cn: g.tensor_copy(out=dview[0:kp, kk, c0:c0 + cn],
                                                                                in_=st[0:kp, 0:cn]),
                     r=[st], w=[dst])

    def norm_block(self, es, bi, xsrc, xr, hT, scr):
        k = self.k
        t0, nt, cond = BLOCKS[bi]
        junk, _ss, _rs, hf0, hb0 = scr
        key = id(es)
        if getattr(self, "_nb_key", None) != key:
            self._nb_key = key
            self._nb = dict(ss=k.sb(es, [128, 4], F32), rs=k.sb(es, [128, 4], F32),
                            hf=Ring([hf0, k.sb(es, [128, D], F32)]),
                            hb=Ring([hb0] + [k.sb(es, [128, D], BF16) for _ in range(3)]))
        nb = self._nb
        ss, rs = nb["ss"], nb["rs"]
        ntile = nt // 128
        xts = []
        for j in range(ntile):
            xt = xr.next()
            xts.append(xt)
            src, rr = self.xsrc_ap(xsrc, t0 + j * 128, 128)
            k.dma("sp", xt[:, :], src, r=(rr or [self.xblk[bi]]), w=[xt])
            k.op("act", lambda a: a.activation(out=junk[:, :], in_=xt[:, :], func=AF.Square, accum_out=ss[:, j:j + 1]),
                 r=[xt], w=[junk, ss])
            if len(xr.items) < ntile and j % len(xr.items) == len(xr.items) - 1:
                pass
        k.op("act", lambda a: a.activation(out=rs[:, 0:ntile], in_=ss[:, 0:ntile], func=AF.Sqrt, scale=1.0 / D, bias=self.epsb[:, :]),
             r=[ss, self.epsb], w=[rs])
        k.op("dve", lambda v: v.reciprocal(out=rs[:, 0:ntile], in_=rs[:, 0:ntile]), r=[rs], w=[rs])
        hbs = []
        for j in range(ntile):
            xt = xts[j]
            hf = nb["hf"].next()
            hb = nb["hb"].next()
            hbs.append(hb)
            k.op("dve", lambda v: v.scalar_tensor_tensor(out=hf[:, :], in0=xt[:, :], scalar=rs[:, j:j + 1],
                                                         in1=self.gmod[cond][:, :], op0=ALU.mult, op1=ALU.mult),
                 r=[xt, rs, self.gmod[cond]], w=[hf])
            k.op("dve", lambda v: v.tensor_tensor(out=hb[:, :], in0=hf[:, :], in1=self.bcm[cond][:, 0:D], op=ALU.add),
                 r=[hf, self.bcm[cond]], w=[hb])
        for j in range(ntile):
            hb = hbs[j]
            ps = self.psg.next()
            pv = ps[:, :].bitcast(BF16).rearrange("p (c t) -> p c t", c=8)
            for c in range(8):
                k.op("pe", lambda pe: pe.transpose(out=pv[:, c, :], in_=hb[:, c * 128:(c + 1) * 128],
                                                   identity=self.identb[:, :]), r=[hb, self.identb], w=[ps])
            k.op("act", lambda a: a.copy(out=hT[:, :, j * 128:(j + 1) * 128], in_=pv), r=[ps], w=[hT])

    def layer_attn(self, L, kind, xsrc):
        k = self.k
        nc = self.nc
        if kind == "mla":
            H, HK, DQ = 16, 16, 96
            w_in = self.W["mla_w_in"].t[0]
            w_out = self.W["mla_w_out"].t[0]
            GOFF = 672
            scale = 96 ** -0.5
        else:
            H, HK, DQ = 16, 4, 64
            w_in = self.W["gqa_w_in"].t[0]
            w_out = self.W["gqa_w_out"].t[0]
            GOFF = 1536
            scale = 64 ** -0.5
        REP = H // HK
        QT = k.dram([H, DQ, T], BF16, f"QT{L}")
        KT = k.dram([HK, DQ, T], BF16, f"KT{L}")
        VV = k.dram([HK, 128, NT, 65], BF16, f"VV{L}")
        GS = k.dram([8, 128, T], BF16, f"GS{L}")
        OG = k.dram([NB, 64, 16, 512], BF16, f"OG{L}")

        with ExitStack() as es:
            self.epsb = k.sb(es, [128, 1], F32)
            k.op("pool", lambda g: g.memset(self.epsb[:, :], EPS), w=[self.epsb])
            NIN = 1696 if kind == "mla" else 2560
            win = k.sb(es, [128, 8, NIN], BF16)
            if kind == "mla":
                wuq = k.sb(es, [128, 3, 1536], BF16)
                wukv = k.sb(es, [128, 2, 2048], BF16)
            with ExitStack() as es2:
                stg = Ring([k.sb(es2, [128, 1024], F32) for _ in range(2)])
                self.load_w(es2, win, win, w_in, 8, 128, NIN, stg)
                if kind == "mla":
                    self.load_w(es2, wuq, wuq, self.W["mla_w_uq"].t[0], 3, 128, 1536, stg)
                    self.load_w(es2, wukv, wukv, self.W["mla_w_ukv"].t[0], 2, 128, 2048, stg)
                k.barrier()
            if kind == "mla":
                qnbc = k.sb(es, [128, 384], F32)
                k.dma("sp", qnbc[:, :], self.W["mla_q_norm"].t[0, :].partition_broadcast(128), w=[qnbc])
                kvnbc = k.sb(es, [128, 256], F32)
                k.dma("sp", kvnbc[:, :], self.W["mla_kv_norm"].t[0, :].partition_broadcast(128), w=[kvnbc])
                RD, HF = 32, 8
                rope_d = self.rope_mla
            else:
                qnbc = k.sb(es, [128, 64], F32)
                k.dma("sp", qnbc[:, :], self.W["gqa_q_norm"].t[0, :].partition_broadcast(128), w=[qnbc])
                knbc = k.sb(es, [128, 64], F32)
                k.dma("sp", knbc[:, :], self.W["gqa_k_norm"].t[0, :].partition_broadcast(128), w=[knbc])
                RD, HF = 64, 16
                rope_d = self.rope_gqa
            xr = Ring([k.sb(es, [128, D], F32) for _ in range(4)])
            hTr = Ring([k.sb(es, [128, 8, 512], BF16) for _ in range(2)])
            scr = (k.sb(es, [128, D], BF16), k.sb(es, [128, 1], F32), k.sb(es, [128, 1], F32),
                   k.sb(es, [128, D], F32), k.sb(es, [128, D], BF16))
            qsb = k.sb(es, [128, H * DQ], F32)
            qb = k.sb(es, [128, H, DQ], BF16)
            kb = k.sb(es, [128, HK, DQ], BF16)
            ksb = k.sb(es, [128, HK * DQ if kind == "gqa" else 32], F32)
            vblk = Ring([k.sb(es, [128, 4, HK, 65], BF16) for _ in range(1)])
            for vb_ in vblk.items:
                k.op("pool", lambda g, vb_=vb_: g.memset(vb_[:, :, :, :], 1.0), w=[vb_])
            qTb = Ring([k.sb(es, [DQ, H, 512], BF16) for _ in range(1)])
            kTb = Ring([k.sb(es, [DQ, HK, 512], BF16) for _ in range(1)])
            rtab = Ring([k.sb(es, [128, 2, RD], F32) for _ in range(2)])
            ra = k.sb(es, [128, H, RD], F32)
            rb_ = k.sb(es, [128, H, RD], F32)
            ss2 = k.sb(es, [128, 32], F32)
            rs2 = k.sb(es, [128, 32], F32)
            sq = k.sb(es, [128, H * DQ], F32)
            if kind == "mla":
                cqn = k.sb(es, [128, 640], BF16)
                cT = k.sb(es, [128, 5, 128], BF16)
            gsr = Ring([k.sb(es, [128, 512], BF16) for _ in range(2)])

            def rope(xv, nh, dst, tab):
                cosb = tab[:, 0:1, :].broadcast_to([128, nh, RD])
                k.op("dve", lambda v: v.tensor_tensor(out=ra[:, 0:nh, :], in0=xv, in1=cosb, op=ALU.mult),
                     r=[tab, qsb, ksb], w=[ra])
                x5 = xv.rearrange("p h (g s f) -> p h g s f", g=2, s=2)
                b5 = rb_[:, 0:nh, :].rearrange("p h (g s f) -> p h g s f", g=2, s=2)
                s5 = tab[:, 1, :].rearrange("p (g s f) -> p g s f", g=2, s=2)
                for g in range(2):
                    for s in range(2):
                        sinb = s5[:, g:g + 1, s, :].broadcast_to([128, nh, HF])
                        k.op("dve", lambda v, g=g, s=s, sinb=sinb: v.tensor_tensor(
                            out=b5[:, :, g, s, :], in0=x5[:, :, g, 1 - s, :], in1=sinb, op=ALU.mult),
                            r=[tab, qsb, ksb], w=[rb_])
                k.op("dve", lambda v: v.tensor_tensor(out=dst, in0=ra[:, 0:nh, :], in1=rb_[:, 0:nh, :], op=ALU.add),
                     r=[ra, rb_], w=[qb, kb])

            import os
            pending_st = [None]
            for bi, (t0, nt, cond) in enumerate(BLOCKS[:int(os.environ.get('DBG_P1_BLOCKS', NB))]):
                hT = hTr.next()
                self.norm_block(es, bi, xsrc, xr, hT, scr)
                if pending_st[0] is not None:
                    pending_st[0]()
                    pending_st[0] = None
                ntile = nt // 128
                vb4 = vblk.next()
                qT = qTb.next()
                kT = kTb.next()
                for j in range(ntile):
                    tt0 = t0 + j * 128
                    kt = tt0 // 128
                    tab = rtab.next()
                    k.dma("sp", tab[:, :, :], rope_d.t[tt0:tt0 + 128, :, :], w=[tab])
                    hTj = lambda kk: hT[:, kk, j * 128:(j + 1) * 128]
                    if kind == "mla":
                        psA = self.psg.next()
                        psB = self.psg.next()
                        for kk in range(8):
                            k.op("pe", lambda pe, kk=kk: pe.matmul(psA[:, 0:384], lhsT=hTj(kk), rhs=win[:, kk, 0:384],
                                                                  start=(kk == 0), stop=(kk == 7)), r=[hT, win], w=[psA])
                        for kk in range(8):
                            k.op("pe", lambda pe, kk=kk: pe.matmul(psB[:, 0:288], lhsT=hTj(kk), rhs=win[:, kk, 384:672],
                                                                  start=(kk == 0), stop=(kk == 7)), r=[hT, win], w=[psB])
                        for (ps_, n_, gb_, o_) in ((psA, 384, qnbc, 0), (psB, 256, kvnbc, 384)):
                            k.op("act", lambda a, ps_=ps_, n_=n_: a.activation(out=sq[:, 0:n_], in_=ps_[:, 0:n_], func=AF.Square,
                                                                              accum_out=ss2[:, 0:1]), r=[ps_], w=[sq, ss2])
                            k.op("act", lambda a, n_=n_: a.activation(out=rs2[:, 0:1], in_=ss2[:, 0:1], func=AF.Sqrt,
                                                                     scale=1.0 / n_, bias=self.epsb[:, :]),
                                 r=[ss2, self.epsb], w=[rs2])
                            k.op("dve", lambda v: v.reciprocal(out=rs2[:, 0:1], in_=rs2[:, 0:1]), r=[rs2], w=[rs2])
                            k.op("dve", lambda v, ps_=ps_, n_=n_, gb_=gb_, o_=o_: v.scalar_tensor_tensor(
                                out=cqn[:, o_:o_ + n_], in0=ps_[:, 0:n_], scalar=rs2[:, 0:1], in1=gb_[:, :],
                                op0=ALU.mult, op1=ALU.mult), r=[ps_, rs2, gb_], w=[cqn])
                        k.op("act", lambda a: a.copy(out=ksb[:, 0:32], in_=psB[:, 256:288]), r=[psB], w=[ksb])
                        pst = self.psg.next()
                        ptv = pst[:, :].bitcast(BF16).rearrange("p (c t) -> p c t", c=8)
                        for c in range(5):
                            k.op("pe", lambda pe, c=c: pe.transpose(out=ptv[:, c, :], in_=cqn[:, c * 128:(c + 1) * 128],
                                                                    identity=self.identb[:, :]), r=[cqn, self.identb], w=[pst])
                        k.op("act", lambda a: a.copy(out=cT[:, :, :], in_=ptv[:, 0:5, :]), r=[pst], w=[cT])
                        for cb in range(3):
                            ps = self.psg.next()
                            for kk in range(3):
                                k.op("pe", lambda pe, kk=kk, cb=cb, ps=ps: pe.matmul(
                                    ps[:, :], lhsT=cT[:, kk, :], rhs=wuq[:, kk, cb * 512:(cb + 1) * 512],
                                    start=(kk == 0), stop=(kk == 2)), r=[cT, wuq], w=[ps])
                            k.op("act", lambda a, cb=cb, ps=ps: a.copy(out=qsb[:, cb * 512:(cb + 1) * 512], in_=ps[:, :]),
                                 r=[ps], w=[qsb])
                        q3 = qsb[:, :].rearrange("p (h d) -> p h d", h=16)
                        k.op("pool", lambda g: g.tensor_copy(out=qb[:, :, 0:64], in_=q3[:, :, 0:64]), r=[qsb], w=[qb])
                        rope(q3[:, :, 64:96], 16, qb[:, :, 64:96], tab)
                        krv = ksb[:, 0:32].rearrange("p (h d) -> p h d", h=1)
                        rope(krv, 1, kb[:, 0:1, 64:96], tab)
                        k.op("pool", lambda g: g.tensor_copy(out=kb[:, 1:16, 64:96],
                                                             in_=kb[:, 0:1, 64:96].broadcast_to([128, 15, 32])),
                             r=[kb], w=[kb])
                        for cb in range(4):
                            ps = self.psg.next()
                            for kk in range(2):
                                k.op("pe", lambda pe, kk=kk, cb=cb, ps=ps: pe.matmul(
                                    ps[:, :], lhsT=cT[:, 3 + kk, :], rhs=wukv[:, kk, cb * 512:(cb + 1) * 512],
                                    start=(kk == 0), stop=(kk == 1)), r=[cT, wukv], w=[ps])
                            p3 = ps[:, :].rearrange("p (h d) -> p h d", h=4)
                            k.op("act", lambda a, cb=cb, p3=p3: a.copy(out=kb[:, cb * 4:(cb + 1) * 4, 0:64], in_=p3[:, :, 0:64]),
                                 r=[ps], w=[kb])
                            k.op("dve", lambda v, cb=cb, p3=p3: v.tensor_copy(out=vb4[:, j, cb * 4:(cb + 1) * 4, 0:64],
                                                                              in_=p3[:, :, 64:128]), r=[ps], w=[vb4])
                    else:
                        import os
                        for cb in range(3 if int(os.environ.get('DBG_STEP', 9)) >= 1 else 0):
                            ps = self.psg.next()
                            for kk in range(8):
                                k.op("pe", lambda pe, kk=kk, cb=cb, ps=ps: pe.matmul(
                                    ps[:, :], lhsT=hTj(kk), rhs=win[:, kk, cb * 512:(cb + 1) * 512],
                                    start=(kk == 0), stop=(kk == 7)), r=[hT, win], w=[ps])
                            SUB = os.environ.get('DBG_SUB', 'abc')
                            if cb < 2:
                                if 'a' in SUB:
                                    k.op("act", lambda a, cb=cb, ps=ps: a.copy(out=qsb[:, cb * 512:(cb + 1) * 512], in_=ps[:, :]),
                                         r=[ps], w=[qsb])
                            elif 'b' in SUB:
                                k.op("act", lambda a, ps=ps: a.copy(out=ksb[:, 0:256], in_=ps[:, 0:256]), r=[ps], w=[ksb])
                                p3 = ps[:, 256:512].rearrange("p (h d) -> p h d", h=4)
                                if 'c' in SUB:
                                    for hh in range(4):
                                        k.op("act", lambda a, hh=hh: a.copy(out=vb4[:, j, hh, 0:64], in_=ps[:, 256 + hh * 64:256 + (hh + 1) * 64]),
                                             r=[ps], w=[vb4])
                        import os
                        DS = int(os.environ.get('DBG_STEP', 9))
                        for (src_, nh, gb_, dstb) in ((qsb, 16, qnbc, qb), (ksb, 4, knbc, kb)) if DS >= 2 else ():
                            s3 = src_[:, 0:nh * 64].rearrange("p (h d) -> p h d", h=nh)
                            sq3 = sq[:, 0:nh * 64].rearrange("p (h d) -> p h d", h=nh)
                            k.op("dve", lambda v, s3=s3, sq3=sq3: v.tensor_tensor(out=sq3, in0=s3, in1=s3, op=ALU.mult),
                                 r=[src_], w=[sq])
                            k.op("dve", lambda v, sq3=sq3, nh=nh: v.tensor_reduce(out=ss2[:, 0:nh], in_=sq3, axis=AX.X, op=ALU.add),
                                 r=[sq], w=[ss2])
                            k.op("act", lambda a, nh=nh: a.activation(out=rs2[:, 0:nh], in_=ss2[:, 0:nh], func=AF.Sqrt,
                                                                     scale=1.0 / 64, bias=self.epsb[:, :]),
                                 r=[ss2, self.epsb], w=[rs2])
                            k.op("dve", lambda v, nh=nh: v.reciprocal(out=rs2[:, 0:nh], in_=rs2[:, 0:nh]), r=[rs2], w=[rs2])
                            k.op("dve", lambda v, s3=s3, nh=nh: v.tensor_tensor(
                                out=s3, in0=s3, in1=rs2[:, 0:nh].unsqueeze(2).broadcast_to([128, nh, 64]), op=ALU.mult),
                                r=[src_, rs2], w=[src_])
                            k.op("dve", lambda v, s3=s3, nh=nh, gb_=gb_: v.tensor_tensor(
                                out=s3, in0=s3, in1=gb_[:, :].unsqueeze(1).broadcast_to([128, nh, 64]), op=ALU.mult),
                                r=[src_, gb_], w=[src_])
                            if DS >= 3:
                                rope(s3, nh, dstb[:, :, :], tab)
                    import os
                    for (srcb, nh, dstT) in ((qb, H, qT), (kb, HK, kT)) if int(os.environ.get('DBG_STEP', 9)) >= 4 else ():
                        for h0 in range(0, nh, 8):
                            hn = min(8, nh - h0)
                            ps = self.psg.next()
                            ptv = ps[:, :].bitcast(BF16).rearrange("p (c t) -> p c t", c=8)
                            for hh in range(hn):
                                k.op("pe", lambda pe, hh=hh, h0=h0, ptv=ptv, srcb=srcb: pe.transpose(
                                    out=ptv[0:DQ, hh, :], in_=srcb[:, h0 + hh, :], identity=self.identb[:, :]),
                                    r=[srcb, self.identb], w=[ps])
                            k.op("act", lambda a, h0=h0, hn=hn, ptv=ptv, dstT=dstT: a.copy(
                                out=dstT[:, h0:h0 + hn, j * 128:(j + 1) * 128], in_=ptv[0:DQ, 0:hn, :]), r=[ps], w=[dstT])
                for hp in range(8):
                    ps = self.psg.next()
                    for kk in range(8):
                        k.op("pe", lambda pe, kk=kk, hp=hp, ps=ps: pe.matmul(
                            ps[:, 0:nt], lhsT=win[:, kk, GOFF + hp * 128:GOFF + (hp + 1) * 128], rhs=hT[:, kk, 0:nt],
                            start=(kk == 0), stop=(kk == 7)), r=[hT, win], w=[ps])
                    gs = gsr.next()
                    k.op("act", lambda a, ps=ps, gs=gs: a.activation(out=gs[:, 0:nt], in_=ps[:, 0:nt], func=AF.Silu),
                         r=[ps], w=[gs])
                    k.dma("pool", GS.t[hp, :, t0:t0 + nt], gs[:, 0:nt], r=[gs], w=[GS])
                def make_stores(qT, kT, vb4, t0, nt, ntile):
                    def st():
                        for h0 in range(0, H, 4):
                            k.dma("sp", QT.t[h0:h0 + 4, :, t0:t0 + nt].rearrange("h d t -> d h t"), qT[:, h0:h0 + 4, 0:nt],
                                  r=[qT], w=[QT])
                        for h0 in range(0, HK, 4):
                            k.dma("sp", KT.t[h0:h0 + 4, :, t0:t0 + nt].rearrange("h d t -> d h t"), kT[:, h0:h0 + 4, 0:nt],
                                  r=[kT], w=[KT])
                        kt0 = t0 // 128
                        for j in range(ntile):
                            for h0 in range(0, HK, 4):
                                k.dma("sp", VV.t[h0:h0 + 4, :, kt0 + j, :].rearrange("h p e -> p h e"), vb4[:, j, h0:h0 + 4, :],
                                      r=[vb4], w=[VV], allow_slow_non_contiguous=True)
                    return st
                pending_st[0] = make_stores(qT, kT, vb4, t0, nt, ntile)
            if pending_st[0] is not None:
                pending_st[0]()
                pending_st[0] = None
            k.barrier()

        if self.stop == "p1":
            return
        with ExitStack() as es:
            DQP = 128 if DQ == 64 else DQ
            KTs = Ring([k.sb(es, [DQP, T], BF16) for _ in range(2)])
            Vs = Ring([k.sb(es, [128, NT, 65], BF16) for _ in range(2)])
            Qs = Ring([k.sb(es, [DQP, T], BF16) for _ in range(2)])
            if DQP != DQ:
                for b_ in KTs.items + Qs.items:
                    k.op("pool", lambda g, b_=b_: g.memset(b_[DQ:DQP, :], 0.0), w=[b_])
            Gs = Ring([k.sb(es, [64, T], BF16) for _ in range(2)])
            Ps = Ring([k.sb(es, [128, 512], BF16) for _ in range(6)])
            rr = k.sb(es, [65, 512], F32)
            tmp = k.sb(es, [64, 512], F32)
            ogr = Ring([k.sb(es, [64, 512], BF16) for _ in range(2)])
            pss = Ring(self.psb[0:5])
            psm = Ring(self.psb[5:6])
            import os
            pending_epi = [None]
            for hk in range(int(os.environ.get('DBG_P2_HEADS', HK))):
                Kt = KTs.next()
                Vt = Vs.next()
                k.dma("sp", Kt[0:DQ, :], KT.t[hk], r=[KT], w=[Kt])
                k.dma("sp", Vt[:, :, :], VV.t[hk], r=[VV], w=[Vt])
                for hr in range(REP):
                    h = hk * REP + hr
                    Qt = Qs.next()
                    Gt = Gs.next()
                    k.dma("sp", Qt[0:DQ, :], QT.t[h], r=[QT], w=[Qt])
                    k.dma("sp", Gt[:, :], GS.t[h // 2, (h % 2) * 64:(h % 2) * 64 + 64, :], r=[GS], w=[Gt])
                    for bi, (t0, nt, cond) in enumerate(BLOCKS):
                        nkt = 2 if cond == 0 else NT
                        po = self.pso.next()
                        pend = []

                        def s_mm(kt):
                            ps = pss.next()
                            k.op("pe", lambda pe: pe.matmul(ps[:, 0:nt], lhsT=Kt[:, kt * 128:(kt + 1) * 128], rhs=Qt[:, t0:t0 + nt],
                                                            start=True, stop=True), r=[Kt, Qt], w=[ps])
                            pt = Ps.next()
                            k.op("act", lambda a: a.activation(out=pt[:, 0:nt], in_=ps[:, 0:nt], func=AF.Exp, scale=scale),
                                 r=[ps], w=[pt])
                            return pt

                        def pv_mm(kt, pt):
                            k.op("pe", lambda pe: pe.matmul(po[0:65, 0:nt], lhsT=Vt[:, kt, :], rhs=pt[:, 0:nt],
                                                            start=(kt == 0), stop=(kt == nkt - 1)), r=[Vt, pt], w=[po])

                        SK = 3
                        for kt in range(nkt + SK):
                            if kt < nkt:
                                pend.append((kt, s_mm(kt)))
                            if kt >= SK:
                                a_, b_ = pend.pop(0)
                                pv_mm(a_, b_)
                            if kt == 10 and pending_epi[0] is not None:
                                pending_epi[0]()
                                pending_epi[0] = None
                        if pending_epi[0] is not None:
                            pending_epi[0]()
                            pending_epi[0] = None

                        def make_epi(po, Gt, t0, nt, bi, h):
                            def epi():
                                k.op("dve", lambda v: v.reciprocal(out=rr[64:65, 0:nt], in_=po[64:65, 0:nt]), r=[po], w=[rr])
                                pm = psm.next()
                                k.op("pe", lambda pe: pe.matmul(pm[0:64, 0:nt], lhsT=self.ones_f[64:65, 0:64], rhs=rr[64:65, 0:nt],
                                                                start=True, stop=True), r=[rr, self.ones_f], w=[pm])
                                k.op("dve", lambda v: v.tensor_tensor(out=tmp[:, 0:nt], in0=pm[0:64, 0:nt], in1=Gt[:, t0:t0 + nt],
                                                                      op=ALU.mult), r=[pm, Gt], w=[tmp])
                                og = ogr.next()
                                k.op("dve", lambda v: v.tensor_tensor(out=og[:, 0:nt], in0=po[0:64, 0:nt], in1=tmp[:, 0:nt], op=ALU.mult),
                                     r=[po, tmp], w=[og])
                                k.dma("pool", OG.t[bi, :, h, 0:nt], og[:, 0:nt], r=[og], w=[OG])
                            return epi
                        pending_epi[0] = make_epi(po, Gt, t0, nt, bi, h)
            if pending_epi[0] is not None:
                pending_epi[0]()
                pending_epi[0] = None
            k.barrier()

        if self.stop == "p2":
            return
        with ExitStack() as es:
            stg = Ring([k.sb(es, [128, 1024], F32) for _ in range(2)])
            wo = k.sb(es, [128, 8, D], BF16)
            self.load_w(es, wo, wo, w_out, 8, 128, D, stg)
            ogb = Ring([k.sb(es, [128, 8, 512], BF16) for _ in range(2)])
            xr = Ring([k.sb(es, [128, D], F32) for _ in range(3)])
            tm = k.sb(es, [128, 512], F32)
            self.outproj(es, lambda bi, nt: (OG.t[bi, :, :, 0:nt], OG), ogb, 128, 8, wo, xsrc, xr, tm)
            k.barrier()


    def layer_gla(self, L, xsrc):
        k = self.k
        w_in = self.W["gla_w_in"].t[0]
        w_out = self.W["gla_w_out"].t[0]
        REC = k.dram([NT, 128, 3584], BF16, f"GREC{L}")
        GG = k.dram([NT, 128, 1024], F32, f"GGG{L}")
        GOF = k.dram([NT, 128, 1024], F32, f"GOF{L}")
        recT = [TT(REC.t) for _ in range(NT)]
        ggT = [TT(GG.t) for _ in range(NT)]
        ofT = [TT(GOF.t) for _ in range(NT)]
        ps8 = self.ps8
        with ExitStack() as es:
            self.epsb = k.sb(es, [128, 1], F32)
            k.op("pool", lambda g: g.memset(self.epsb[:, :], EPS), w=[self.epsb])
            onec = k.sb(es, [128, 1], F32)
            k.op("pool", lambda g: g.memset(onec[:, :], 1.0), w=[onec])
            stg = Ring([k.sb(es, [128, 1024], F32) for _ in range(2)])
            win = k.sb(es, [128, 8, 3104], BF16)
            self.load_w(es, win, win, w_in, 8, 128, 3104, stg)
            wg = k.sb(es, [16, 2, 512], F32)
            k.dma("sp", wg[:, 0, :], self.W["gla_w_gf"].t[0], w=[wg])
            k.dma("sp", wg[:, 1, :], self.W["gla_w_gb"].t[0], w=[wg])
            bg = k.sb(es, [128, 2, 512], F32)
            k.dma("sp", bg[:, 0, :], self.W["gla_b_gf"].t[0, :].partition_broadcast(128), w=[bg])
            k.dma("sp", bg[:, 1, :], self.W["gla_b_gb"].t[0, :].partition_broadcast(128), w=[bg])
            xr = Ring([k.sb(es, [128, D], F32) for _ in range(4)])
            hTr = Ring([k.sb(es, [128, 8, 512], BF16) for _ in range(2)])
            scr = (k.sb(es, [128, D], BF16), k.sb(es, [128, 1], F32), k.sb(es, [128, 1], F32),
                   k.sb(es, [128, D], F32), k.sb(es, [128, D], BF16))
            recr = Ring([k.sb(es, [128, 3584], BF16) for _ in range(2)])
            ggr = Ring([k.sb(es, [128, 1024], F32) for _ in range(2)])
            rT = k.sb(es, [16, 2, 128], F32)
            zt = k.sb(es, [128, 512], F32)
            for bi, (t0, nt, cond) in enumerate(BLOCKS):
                hT = hTr.next()
                self.norm_block(es, bi, xsrc, xr, hT, scr)
                for j in range(nt // 128):
                    t = (t0 + j * 128) // 128
                    rec = recr.next()
                    gg = ggr.next()
                    hTj = lambda kk: hT[:, kk, j * 128:(j + 1) * 128]

                    def tokmm(c0, n):
                        ps = ps8.next()
                        for kk in range(8):
                            k.op("pe", lambda pe: pe.matmul(ps[:, 0:n], lhsT=hTj(kk), rhs=win[:, kk, c0:c0 + n],
                                                            start=(kk == 0), stop=(kk == 7)), r=[hT, win], w=[ps])
                        return ps

                    ps = tokmm(512, 512)
                    k.op("act", lambda a: a.copy(out=rec[:, 1024:1536], in_=ps[:, :]), r=[ps], w=[rec])
                    for cb in range(2):
                        ps = tokmm(1024 + cb * 512, 512)
                        k.op("dve", lambda v: v.tensor_copy(out=rec[:, 1536 + cb * 512:2048 + cb * 512], in_=ps[:, :]),
                             r=[ps], w=[rec])
                    for cb in range(2):
                        ps = tokmm(2048 + cb * 512, 512)
                        k.op("act", lambda a: a.activation(out=rec[:, 2560 + cb * 512:3072 + cb * 512], in_=ps[:, :],
                                                           func=AF.Silu), r=[ps], w=[rec])
                    for qk in range(2):
                        ps = ps8.next()
                        for h in range(4):
                            for kk in range(8):
                                k.op("pe", lambda pe: pe.matmul(
                                    ps[:, h * 128:(h + 1) * 128], lhsT=win[:, kk, qk * 512 + h * 128:qk * 512 + (h + 1) * 128],
                                    rhs=hTj(kk), start=(kk == 0), stop=(kk == 7)), r=[hT, win], w=[ps])
                        if qk == 0:
                            k.op("act", lambda a: a.mul(out=rec[:, 0:512], in_=ps[:, :], mul=128 ** -0.5), r=[ps], w=[rec])
                        else:
                            k.op("dve", lambda v: v.tensor_copy(out=rec[:, 512:1024], in_=ps[:, :]), r=[ps], w=[rec])
                    ps = ps8.next()
                    for d in range(2):
                        for kk in range(8):
                            k.op("pe", lambda pe: pe.matmul(
                                ps[0:16, d * 128:(d + 1) * 128], lhsT=win[:, kk, 3072 + 16 * d:3088 + 16 * d],
                                rhs=hTj(kk), start=(kk == 0), stop=(kk == 7)), r=[hT, win], w=[ps])
                    k.op("act", lambda a: a.copy(out=rT[:, :, :], in_=ps[0:16, 0:256].rearrange("p (d t) -> p d t", d=2)),
                         r=[ps], w=[rT])
                    for d in range(2):
                        ps = ps8.next()
                        k.op("pe", lambda pe: pe.matmul(ps[:, :], lhsT=rT[:, d, :], rhs=wg[:, d, :], start=True, stop=True),
                             r=[rT, wg], w=[ps])
                        k.op("dve", lambda v: v.tensor_tensor(out=zt[:, :], in0=ps[:, :], in1=bg[:, d, :], op=ALU.add),
                             r=[ps, bg], w=[zt])
                        k.op("act", lambda a: a.activation(out=zt[:, :], in_=zt[:, :], func=AF.Exp, scale=-1.0), r=[zt], w=[zt])
                        k.op("act", lambda a: a.activation(out=zt[:, :], in_=zt[:, :], func=AF.Ln, bias=onec[:, :]),
                             r=[zt, onec], w=[zt])
                        k.op("dve", lambda v: v.tensor_scalar(out=gg[:, d * 512:(d + 1) * 512], in0=zt[:, :],
                                                              scalar1=-1.0 / 16.0, scalar2=None, op0=ALU.mult),
                             r=[zt], w=[gg])
                    k.dma("pool", REC.t[t], rec[:, :], r=[rec], w=[recT[t]])
                    k.dma("pool", GG.t[t], gg[:, :], r=[gg], w=[ggT[t]])
            k.barrier()

        with ExitStack() as es:
            self.epsb = k.sb(es, [128, 1], F32)
            k.op("pool", lambda g: g.memset(self.epsb[:, :], EPS), w=[self.epsb])
            tri = k.sb(es, [128, 4, 128], F32)
            k.dma("sp", tri[:, :, :], self.tri_d.t.rearrange("m j i -> j m i"), w=[tri])
            S = [k.sb(es, [128, 256], F32) for _ in range(4)]
            Sb = [k.sb(es, [128, 256], BF16) for _ in range(4)]
            recr = Ring([k.sb(es, [128, 3584], BF16) for _ in range(3)])
            ggr = Ring([k.sb(es, [128, 1024], F32) for _ in range(3)])
            E1r = Ring([k.sb(es, [128, 512], F32) for _ in range(2)])
            E2r = Ring([k.sb(es, [128, 512], F32) for _ in range(2)])
            E3r = Ring([k.sb(es, [128, 512], F32) for _ in range(2)])
            qtr = Ring([k.sb(es, [128, 512], BF16) for _ in range(2)])
            ktr = Ring([k.sb(es, [128, 512], BF16) for _ in range(2)])
            khr = Ring([k.sb(es, [128, 512], BF16) for _ in range(2)])
            Amr = Ring([k.sb(es, [128, 512], BF16) for _ in range(2)])
            osb = Ring([k.sb(es, [128, 1024], F32) for _ in range(2)])
            stg = Ring([k.sb(es, [128, 1024], F32) for _ in range(2)])
            wo = k.sb(es, [128, 8, D], BF16)
            self.load_w(es, wo, wo, w_out, 8, 128, D, stg)
            onbc = k.sb(es, [128, 256], F32)
            k.dma("sp", onbc[:, :], self.W["gla_o_norm"].t[0, :].partition_broadcast(128), w=[onbc])
            ofr = Ring([k.sb(es, [128, 1024], F32) for _ in range(2)])
            xr = Ring([k.sb(es, [128, D], F32) for _ in range(2)])
            ogf = k.sb(es, [128, 1024], F32)
            ogb = k.sb(es, [128, 1024], BF16)
            ogT = k.sb(es, [128, 8, 128], BF16)
            junk = k.sb(es, [128, 256], BF16)
            ss4 = k.sb(es, [128, 4], F32)
            rs4 = k.sb(es, [128, 4], F32)
            tm = k.sb(es, [128, 512], F32)

            def stageA(d, rec, gg):
                cm = tri[:, 0 if d == 0 else 2, :]
                sm = tri[:, 1 if d == 0 else 3, :]
                g = lambda a_, b_: gg[:, d * 512 + a_:d * 512 + b_]
                E1, E2, E3 = E1r.next(), E2r.next(), E3r.next()
                qt, kt_, kh, Am = qtr.next(), ktr.next(), khr.next(), Amr.next()
                psA = ps8.next()
                for h in range(4):
                    k.op("pe", lambda pe: pe.matmul(psA[:, h * 128:(h + 1) * 128], lhsT=g(h * 128, (h + 1) * 128), rhs=cm,
                                                    start=True, stop=True), r=[gg, tri], w=[psA])
                psB = ps8.next()
                k.op("pe", lambda pe: pe.matmul(psB[:, :], lhsT=sm, rhs=g(0, 512), start=True, stop=True), r=[gg, tri], w=[psB])
                k.op("act", lambda a: a.activation(out=E1[:, :], in_=psA[:, :], func=AF.Exp), r=[psA], w=[E1])
                k.op("act", lambda a: a.activation(out=E2[:, :], in_=psA[:, :], func=AF.Exp, scale=-1.0), r=[psA], w=[E2])
                k.op("act", lambda a: a.activation(out=E3[:, :], in_=psB[:, :], func=AF.Exp), r=[psB], w=[E3])
                k.op("dve", lambda v: v.tensor_tensor(out=qt[:, :], in0=rec[:, 0:512], in1=E1[:, :], op=ALU.mult), r=[rec, E1], w=[qt])
                k.op("pool", lambda v: v.tensor_tensor(out=kt_[:, :], in0=rec[:, 512:1024], in1=E2[:, :], op=ALU.mult), r=[rec, E2], w=[kt_])
                k.op("pool", lambda v: v.tensor_tensor(out=kh[:, :], in0=rec[:, 1024:1536], in1=E3[:, :], op=ALU.mult), r=[rec, E3], w=[kh])
                return dict(d=d, rec=rec, E1=E1, qt=qt, kh=kh, Am=Am, kt_=kt_, cm=cm)

            def stageA2(c):
                qt, kt_, Am, cm = c["qt"], c["kt_"], c["Am"], c["cm"]
                psD = ps8.next()
                for h in range(4):
                    hs = slice(h * 128, (h + 1) * 128)
                    k.op("pe", lambda pe: pe.matmul(psD[:, hs], lhsT=kt_[:, hs], rhs=qt[:, hs], start=True, stop=True),
                         r=[kt_, qt], w=[psD])
                k.op("dve", lambda v: v.tensor_tensor(out=Am[:, :].rearrange("p (h i) -> p h i", h=4),
                                                      in0=psD[:, :].rearrange("p (h i) -> p h i", h=4),
                                                      in1=cm.unsqueeze(1).broadcast_to([128, 4, 128]), op=ALU.mult),
                     r=[psD, tri], w=[Am])

            def stageB(c):
                d, rec, E1, qt, kh, Am = (c[n] for n in ("d", "rec", "E1", "qt", "kh", "Am"))
                ecol = 127 if d == 0 else 0
                po = [ps8.next(), ps8.next()]
                for h in range(4):
                    hs = slice(h * 128, (h + 1) * 128)
                    bank = po[h // 2]
                    cs = slice((h % 2) * 256, (h % 2) * 256 + 256)
                    vs = slice(1536 + h * 256, 1536 + (h + 1) * 256)
                    k.op("pe", lambda pe: pe.matmul(bank[:, cs], lhsT=qt[:, hs], rhs=Sb[h][:, :], start=True, stop=False),
                         r=[qt, Sb[h]], w=[bank])
                    k.op("pe", lambda pe: pe.matmul(bank[:, cs], lhsT=Am[:, hs], rhs=rec[:, vs], start=False, stop=True),
                         r=[Am, rec], w=[bank])
                pss = [ps8.next(), ps8.next()]
                for h in range(4):
                    hs = slice(h * 128, (h + 1) * 128)
                    bank = pss[h // 2]
                    cs = slice((h % 2) * 256, (h % 2) * 256 + 256)
                    vs = slice(1536 + h * 256, 1536 + (h + 1) * 256)
                    k.op("pe", lambda pe: pe.matmul(bank[:, cs], lhsT=kh[:, hs], rhs=rec[:, vs], start=True, stop=True),
                         r=[kh, rec], w=[bank])
                    k.op("dve", lambda v: v.scalar_tensor_tensor(out=S[h][:, :], in0=S[h][:, :],
                                                                 scalar=E1[:, h * 128 + ecol:h * 128 + ecol + 1],
                                                                 in1=bank[:, cs], op0=ALU.mult, op1=ALU.add),
                         r=[S[h], E1, bank], w=[S[h]])
                    k.op("act", lambda gp: gp.copy(out=Sb[h][:, :], in_=S[h][:, :]), r=[S[h]], w=[Sb[h]])
                return po

            def reset_state():
                for h in range(4):
                    k.op("pool", lambda gp: gp.memset(S[h][:, :], 0.0), w=[S[h]])
                    k.op("pool", lambda gp: gp.memset(Sb[h][:, :], 0.0), w=[Sb[h]])

            def gl_loads(t):
                rec = recr.next()
                gg = ggr.next()
                k.dma("sp", rec[:, :], REC.t[t], r=[recT[t]], w=[rec])
                k.dma("sp", gg[:, :], GG.t[t], r=[ggT[t]], w=[gg])
                return rec, gg

            reset_state()
            cnext = stageA(0, *gl_loads(0))
            stageA2(cnext)
            for t in range(NT):
                c = cnext
                if t + 1 < NT:
                    cnext = stageA(0, *gl_loads(t + 1))
                po = stageB(c)
                if t + 1 < NT:
                    stageA2(cnext)
                ob = osb.next()
                for c in range(2):
                    k.op("act", lambda a: a.copy(out=ob[:, c * 512:(c + 1) * 512], in_=po[c][:, :]), r=[po[c]], w=[ob])
                k.dma("pool", GOF.t[t], ob[:, :], r=[ob], w=[ofT[t]])
            reset_state()
            order = [1, 0] + list(range(NT - 1, 1, -1))
            cnext = stageA(1, *gl_loads(order[0]))
            stageA2(cnext)
            osr = Ring([k.sb(es, [128, 1024], F32) for _ in range(2)])

            def gl_out(t, rec, osum):
                cond = 0 if t < 2 else 1
                bi = 0 if t < 2 else 1 + (t - 2) // 4
                xt = xr.next()
                sap, rr = self.xsrc_ap(xsrc, t * 128, 128)
                k.dma("sp", xt[:, :], sap, r=(rr or [self.xblk[bi]]), w=[xt])
                for h in range(4):
                    k.op("act", lambda a: a.activation(out=junk[:, :], in_=osum[:, h * 256:(h + 1) * 256], func=AF.Square,
                                                       accum_out=ss4[:, h:h + 1]), r=[osum], w=[junk, ss4])
                k.op("act", lambda a: a.activation(out=rs4[:, :], in_=ss4[:, :], func=AF.Sqrt, scale=1.0 / 256, bias=self.epsb[:, :]),
                     r=[ss4, self.epsb], w=[rs4])
                k.op("dve", lambda v: v.reciprocal(out=rs4[:, :], in_=rs4[:, :]), r=[rs4], w=[rs4])
                for h in range(4):
                    k.op("dve", lambda v: v.scalar_tensor_tensor(out=ogf[:, h * 256:(h + 1) * 256], in0=osum[:, h * 256:(h + 1) * 256],
                                                                 scalar=rs4[:, h:h + 1], in1=onbc[:, :], op0=ALU.mult, op1=ALU.mult),
                         r=[osum, rs4, onbc], w=[ogf])
                k.op("dve", lambda v: v.tensor_tensor(out=ogb[:, :], in0=ogf[:, :], in1=rec[:, 2560:3584], op=ALU.mult),
                     r=[ogf, rec], w=[ogb])
                self.tok_outproj(ogb, 8, ogT, wo, xt, tm, cond)
                k.dma("pool", self.xres.t[t * 128:(t + 1) * 128, :], xt[:, :], r=[xt], w=[self.xblk[bi]])

            pending = None
            for oi, t in enumerate(order):
                c = cnext
                rec = c["rec"]
                if oi + 1 < NT:
                    cnext = stageA(1, *gl_loads(order[oi + 1]))
                of = ofr.next()
                k.dma("sp", of[:, :], GOF.t[t], r=[ofT[t]], w=[of])
                po = stageB(c)
                if oi + 1 < NT:
                    stageA2(cnext)
                osum = osr.next()
                for cc in range(2):
                    k.op("dve", lambda v: v.tensor_tensor(out=osum[:, cc * 512:(cc + 1) * 512], in0=po[cc][:, :],
                                                          in1=of[:, cc * 512:(cc + 1) * 512], op=ALU.add),
                         r=[po[cc], of], w=[osum])
                if pending is not None:
                    gl_out(*pending)
                pending = (t, rec, osum)
            gl_out(*pending)
            k.barrier()


    def norm_rows(self, src, rtrk, n, cond, dst, scr, xt):
        k = self.k
        junk, ss, rs, hf, hb = scr
        k.dma("sp", xt[0:n, :], src, r=rtrk, w=[xt])
        k.op("act", lambda a: a.activation(out=junk[0:n, :], in_=xt[0:n, :], func=AF.Square, accum_out=ss[0:n, :]),
             r=[xt], w=[junk, ss])
        k.op("act", lambda a: a.activation(out=rs[0:n, :], in_=ss[0:n, :], func=AF.Sqrt, scale=1.0 / D, bias=self.epsb[0:n, :]),
             r=[ss, self.epsb], w=[rs])
        k.op("dve", lambda v: v.reciprocal(out=rs[0:n, :], in_=rs[0:n, :]), r=[rs], w=[rs])
        k.op("dve", lambda v: v.scalar_tensor_tensor(out=hf[0:n, :], in0=xt[0:n, :], scalar=rs[0:n, 0:1],
                                                     in1=self.gmod[cond][0:n, :], op0=ALU.mult, op1=ALU.mult),
             r=[xt, rs, self.gmod[cond]], w=[hf])
        k.op("dve", lambda v: v.tensor_tensor(out=hb[0:n, :], in0=hf[0:n, :], in1=self.bcm[cond][0:n, 0:D], op=ALU.add),
             r=[hf, self.bcm[cond]], w=[hb])
        ps = self.ps8.next()
        pv = ps[:, :].bitcast(BF16).rearrange("p (c t) -> p c t", c=8)
        for c in range(8):
            k.op("pe", lambda pe: pe.transpose(out=pv[:, c, 0:n], in_=hb[0:n, c * 128:(c + 1) * 128],
                                               identity=self.identb[0:n, 0:n]), r=[hb, self.identb], w=[ps])
        return ps, pv

    def layer_ssd(self, L, xsrc):
        k = self.k
        ps8 = self.ps8
        w_in = self.W["ssd_w_in"].t[0]
        w_out = self.W["ssd_w_out"].t[0]
        RX = k.dram([NT, 128, 2560], BF16, f"SRX{L}")
        RZ = k.dram([NT, 128, 2048], BF16, f"SRZ{L}")
        RBC = k.dram([NT, 128, 1024], BF16, f"SRBC{L}")
        DD = k.dram([NT, 128, 128], F32, f"SDD{L}")
        YF = k.dram([NT, 128, 2048], F32, f"SYF{L}")
        rxT = [TT(RX.t) for _ in range(NT)]
        rzT = [TT(RZ.t) for _ in range(NT)]
        rbcT = [TT(RBC.t) for _ in range(NB)]
        ddT = [TT(DD.t) for _ in range(NT)]
        yfT = [TT(YF.t) for _ in range(NT)]
        with ExitStack() as es:
            self.epsb = k.sb(es, [128, 1], F32)
            k.op("pool", lambda g: g.memset(self.epsb[:, :], EPS), w=[self.epsb])
            onec = k.sb(es, [128, 1], F32)
            k.op("pool", lambda g: g.memset(onec[:, :], 1.0), w=[onec])
            stg = Ring([k.sb(es, [128, 1024], F32) for _ in range(2)])
            wz = k.sb(es, [128, 8, 2048], BF16)
            self.load_w(es, wz, wz, w_in[:, 0:2048], 8, 128, 2048, stg)
            wx = k.sb(es, [128, 8, 3072], BF16)
            self.load_w(es, wx, wx, w_in[:, 2048:5120], 8, 128, 3072, stg)
            wdt = k.sb(es, [128, 8, 64], BF16)
            self.load_w(es, wdt, wdt, w_in[:, 5120:5184], 8, 128, 64, stg)
            cw = k.sb(es, [128, 24, 5], F32)
            for kk in range(5):
                k.dma("sp", cw[:, :, kk], self.W["ssd_conv_w"].t[0, kk, :].rearrange("(c p) -> p c", p=128), w=[cw],
                      allow_slow_non_contiguous=True)
            cbias = k.sb(es, [128, 24], F32)
            k.dma("sp", cbias[:, :], self.W["ssd_conv_b"].t[0, :].rearrange("(c p) -> p c", p=128), w=[cbias],
                  allow_slow_non_contiguous=True)
            dtb = k.sb(es, [128, 64], F32)
            k.dma("sp", dtb[:, 0:32], self.W["ssd_dt_bias_f"].t[0, :].partition_broadcast(128), w=[dtb])
            k.dma("sp", dtb[:, 32:64], self.W["ssd_dt_bias_b"].t[0, :].partition_broadcast(128), w=[dtb])
            abc = k.sb(es, [128, 64], F32)
            k.dma("sp", abc[:, 0:32], self.W["ssd_a_log_f"].t[0, :].partition_broadcast(128), w=[abc])
            k.dma("sp", abc[:, 32:64], self.W["ssd_a_log_b"].t[0, :].partition_broadcast(128), w=[abc])
            k.op("act", lambda a: a.activation(out=abc[:, :], in_=abc[:, :], func=AF.Exp), r=[abc], w=[abc])
            k.op("dve", lambda v: v.tensor_scalar(out=abc[:, :], in0=abc[:, :], scalar1=-1.0, scalar2=None, op0=ALU.mult),
                 r=[abc], w=[abc])
            xr = Ring([k.sb(es, [128, D], F32) for _ in range(2)])
            hT = k.sb(es, [128, 8, 516], BF16)
            scr = (k.sb(es, [128, D], BF16), k.sb(es, [128, 1], F32), k.sb(es, [128, 1], F32),
                   k.sb(es, [128, D], F32), k.sb(es, [128, D], BF16))
            prer = Ring([k.sb(es, [128, 516], BF16) for _ in range(3)])
            accr = Ring([k.sb(es, [128, 512], F32) for _ in range(2)])
            xc = k.sb(es, [128, 24, 512], BF16)
            rxr = Ring([k.sb(es, [128, 2560], BF16) for _ in range(2)])
            rzr = Ring([k.sb(es, [128, 2048], BF16) for _ in range(2)])
            ddr = Ring([k.sb(es, [128, 128], F32) for _ in range(2)])
            dtt = k.sb(es, [128, 64], F32)
            for bi, (t0, nt, cond) in enumerate(BLOCKS):
                ntile = nt // 128
                for j in range(ntile):
                    src, rr = self.xsrc_ap(xsrc, t0 + j * 128, 128)
                    ps, pv = self.norm_rows(src, rr or [self.xblk[bi]], 128, cond, None, scr, xr.next())
                    k.op("act", lambda a: a.copy(out=hT[:, :, j * 128:(j + 1) * 128], in_=pv), r=[ps], w=[hT])
                for side in range(2):
                    col = 512 + 2 * side
                    has = (bi >= 2) if side == 0 else (1 <= bi < NB - 1)
                    if not has:
                        k.op("pool", lambda g: g.memset(hT[:, :, col:col + 2], 0.0), w=[hT])
                    else:
                        r0 = t0 - 2 if side == 0 else t0 + nt
                        nb_ = bi - 1 if side == 0 else bi + 1
                        src, rr = self.xsrc_ap(xsrc, r0, 2)
                        ps, pv = self.norm_rows(src, rr or [self.xblk[nb_]], 2, cond, None, scr, xr.next())
                        k.op("act", lambda a: a.copy(out=hT[:, :, col:col + 2], in_=pv[:, :, 0:2]), r=[ps], w=[hT])
                for c in range(24):
                    ps = ps8.next()
                    for kk in range(8):
                        k.op("pe", lambda pe: pe.matmul(ps[:, 0:nt], lhsT=wx[:, kk, c * 128:(c + 1) * 128], rhs=hT[:, kk, 0:nt],
                                                        start=(kk == 0), stop=(kk == 7)), r=[wx, hT], w=[ps])
                    ps2 = ps8.next()
                    for kk in range(8):
                        k.op("pe", lambda pe: pe.matmul(ps2[:, 0:4], lhsT=wx[:, kk, c * 128:(c + 1) * 128], rhs=hT[:, kk, 512:516],
                                                        start=(kk == 0), stop=(kk == 7)), r=[wx, hT], w=[ps2])
                    pre = prer.next()
                    k.op("act", lambda a: a.copy(out=pre[:, 2:2 + nt], in_=ps[:, 0:nt]), r=[ps], w=[pre])
                    k.op("act", lambda a: a.copy(out=pre[:, 0:2], in_=ps2[:, 0:2]), r=[ps2], w=[pre])
                    k.op("act", lambda a: a.copy(out=pre[:, 2 + nt:4 + nt], in_=ps2[:, 2:4]), r=[ps2], w=[pre])
                    acc = accr.next()
                    k.op("dve", lambda v: v.tensor_scalar(out=acc[:, 0:nt], in0=pre[:, 0:nt], scalar1=cw[:, c, 0:1], scalar2=None,
                                                          op0=ALU.mult), r=[pre, cw], w=[acc])
                    for kk in range(1, 5):
                        k.op("dve", lambda v: v.scalar_tensor_tensor(out=acc[:, 0:nt], in0=pre[:, kk:kk + nt], scalar=cw[:, c, kk:kk + 1],
                                                                     in1=acc[:, 0:nt], op0=ALU.mult, op1=ALU.add),
                             r=[pre, cw, acc], w=[acc])
                    k.op("act", lambda a: a.activation(out=xc[:, c, 0:nt], in_=acc[:, 0:nt], func=AF.Silu, bias=cbias[:, c:c + 1]),
                         r=[acc, cbias], w=[xc])
                kt0 = t0 // 128
                for j in range(ntile):
                    for s_ in range(2):
                        k.dma("act", RBC.t[kt0 + j, :, s_ * 512:(s_ + 1) * 512].rearrange("p (g t) -> p g t", g=4),
                              xc[:, 16 + 4 * s_:20 + 4 * s_, j * 128:(j + 1) * 128], r=[xc], w=[rbcT[bi]])
                for j in range(ntile):
                    t = kt0 + j
                    rx = rxr.next()
                    for c0 in (0, 8, 16):
                        cn = 8 if c0 < 16 else 4
                        ps = ps8.next()
                        pv = ps[:, :].bitcast(BF16).rearrange("p (c t) -> p c t", c=8)
                        for c in range(cn):
                            k.op("pe", lambda pe: pe.transpose(out=pv[:, c, :], in_=xc[:, c0 + c, j * 128:(j + 1) * 128],
                                                               identity=self.identb[:, :]), r=[xc, self.identb], w=[ps])
                        k.op("act" if c0 != 8 else "dve",
                             (lambda a: a.copy(out=rx[:, c0 * 128:(c0 + cn) * 128].rearrange("p (c t) -> p c t", c=cn), in_=pv[:, 0:cn, :]))
                             if c0 != 8 else
                             (lambda v: v.tensor_copy(out=rx[:, c0 * 128:(c0 + cn) * 128].rearrange("p (c t) -> p c t", c=cn), in_=pv[:, 0:cn, :])),
                             r=[ps], w=[rx])
                    k.dma("pool", RX.t[t], rx[:, :], r=[rx], w=[rxT[t]])
                    rz = rzr.next()
                    for cb in range(4):
                        ps = ps8.next()
                        for kk in range(8):
                            k.op("pe", lambda pe: pe.matmul(ps[:, :], lhsT=hT[:, kk, j * 128:(j + 1) * 128],
                                                            rhs=wz[:, kk, cb * 512:(cb + 1) * 512],
                                                            start=(kk == 0), stop=(kk == 7)), r=[hT, wz], w=[ps])
                        k.op("act", lambda a: a.activation(out=rz[:, cb * 512:(cb + 1) * 512], in_=ps[:, :], func=AF.Silu),
                             r=[ps], w=[rz])
                    k.dma("pool", RZ.t[t], rz[:, :], r=[rz], w=[rzT[t]])
                    dd = ddr.next()
                    ps = ps8.next()
                    for kk in range(8):
                        k.op("pe", lambda pe: pe.matmul(ps[:, 0:64], lhsT=hT[:, kk, j * 128:(j + 1) * 128], rhs=wdt[:, kk, :],
                                                        start=(kk == 0), stop=(kk == 7)), r=[hT, wdt], w=[ps])
                    k.op("dve", lambda v: v.tensor_tensor(out=dtt[:, :], in0=ps[:, 0:64], in1=dtb[:, :], op=ALU.add),
                         r=[ps, dtb], w=[dtt])
                    k.op("act", lambda a: a.activation(out=dtt[:, :], in_=dtt[:, :], func=AF.Exp), r=[dtt], w=[dtt])
                    k.op("act", lambda a: a.activation(out=dd[:, 0:64], in_=dtt[:, :], func=AF.Ln, bias=onec[:, :]),
                         r=[dtt, onec], w=[dd])
                    k.op("dve", lambda v: v.tensor_tensor(out=dd[:, 64:128], in0=dd[:, 0:64], in1=abc[:, :], op=ALU.mult),
                         r=[dd, abc], w=[dd])
                    k.dma("pool", DD.t[t], dd[:, :], r=[dd], w=[ddT[t]])
            k.barrier()

        with ExitStack() as es:
            self.epsb = k.sb(es, [128, 1], F32)
            k.op("pool", lambda g: g.memset(self.epsb[:, :], EPS), w=[self.epsb])
            tri = k.sb(es, [128, 4, 128], F32)
            k.dma("sp", tri[:, :, :], self.tri_d.t.rearrange("m j i -> j m i"), w=[tri])
            negm = k.sb(es, [128, 2, 128], F32)
            k.dma("sp", negm[:, :, :], self.negm_d.t.rearrange("m j i -> j m i"), w=[negm])
            onehot = k.sb(es, [96, 32, 128], BF16)
            k.dma("sp", onehot[:, :, :], self.onehot_d.t, w=[onehot])
            negmb = k.sb(es, [128, 2, 128], BF16)
            k.dma("sp", negmb[:, :, :], self.negmb_d.t.rearrange("m j i -> j m i"), w=[negmb])
            identf = k.sb(es, [128, 128], F32)
            k.dma("sp", identf[:, :], self.identf_d.t, w=[identf])
            Hs = [k.sb(es, [128, 512], F32) for _ in range(4)]
            Hb = [k.sb(es, [128, 512], BF16) for _ in range(4)]
            rxr = Ring([k.sb(es, [128, 2560], BF16) for _ in range(3)])
            rbcr = Ring([k.sb(es, [128, 1024], BF16) for _ in range(2)])
            ddr = Ring([k.sb(es, [128, 128], F32) for _ in range(3)])
            bankY = Ring(self.psb[0:2])
            bankLb = Ring(self.psb[2:5])
            bankM = Ring(self.psb[5:8])
            def ring2(shape, dt):
                return Ring([k.sb(es, shape, dt) for _ in range(2)])
            ela_r = ring2([128, 32], F32)
            edl_r = ring2([128, 32], F32)
            etot_r = ring2([128, 32], F32)
            laT_r = ring2([96, 128], BF16)
            larep_r = ring2([128, 96], F32)
            Abf_r = ring2([96, 128], BF16)
            Bbf_r = ring2([96, 128], BF16)
            R1_r = ring2([96, 128], F32)
            nla_r = ring2([128, 32], F32)
            xdt_r = ring2([128, 2048], BF16)
            xdl_r = ring2([128, 2048], BF16)
            CBm_r = ring2([128, 4, 128], BF16)
            Eh_r = Ring([k.sb(es, [128, 4, 128], BF16) for _ in range(3)])
            Mh = Ring([k.sb(es, [128, 4, 128], BF16) for _ in range(3)])
            ysr = Ring([k.sb(es, [128, 2048], F32) for _ in range(3)])
            wo = k.sb(es, [128, 16, D], BF16)
            with ExitStack() as es2:
                stg = Ring([k.sb(es2, [128, 1024], F32) for _ in range(2)])
                self.load_w(es2, wo, wo, w_out, 16, 128, D, stg)
                k.barrier()
            gnbc = k.sb(es, [128, 2048], F32)
            k.dma("sp", gnbc[:, :], self.W["ssd_norm"].t[0, :].partition_broadcast(128), w=[gnbc])
            dsk = k.sb(es, [128, 32], F32)
            k.dma("sp", dsk[:, :], self.W["ssd_d"].t[0, :].partition_broadcast(128), w=[dsk])
            rzr = Ring([k.sb(es, [128, 2048], BF16) for _ in range(1)])
            yfr = Ring([k.sb(es, [128, 2048], F32) for _ in range(1)])
            xr = Ring([k.sb(es, [128, D], F32) for _ in range(2)])
            ynb = k.sb(es, [128, 2048], BF16)
            ogT = k.sb(es, [128, 16, 128], BF16)
            junk = k.sb(es, [128, 512], BF16)
            ss4 = k.sb(es, [128, 4], F32)
            rs4 = k.sb(es, [128, 4], F32)
            tm = k.sb(es, [128, 512], F32)

            def stageA(d, rx, rbc, dd):
                c = dict(d=d, rx=rx, rbc=rbc, dd=dd)
                cm = tri[:, 0 if d == 0 else 2, :]
                sm = tri[:, 1 if d == 0 else 3, :]
                dA = dd[:, 64 + 32 * d:96 + 32 * d]
                dt = dd[:, 32 * d:32 * d + 32]
                larep = larep_r.next()
                la_sb = larep
                ela, edl, etot, laT = ela_r.next(), edl_r.next(), etot_r.next(), laT_r.next()
                Abf, Bbf, R1 = Abf_r.next(), Bbf_r.next(), R1_r.next()
                xdt, xdl, CBm = xdt_r.next(), xdl_r.next(), CBm_r.next()
                nla = nla_r.next()
                c.update(la_sb=la_sb, ela=ela, etot=etot, laT=laT, xdt=xdt, xdl=xdl, CBm=CBm, nla=nla)
                psL = bankM.next()
                k.op("pe", lambda pe: pe.matmul(psL[:, 0:32], lhsT=cm, rhs=dA, start=True, stop=True), r=[tri, dd], w=[psL])
                k.op("pe", lambda pe: pe.matmul(psL[:, 32:64], lhsT=sm, rhs=dA, start=True, stop=True), r=[tri, dd], w=[psL])
                k.op("pe", lambda pe: pe.matmul(psL[:, 64:96], lhsT=self.ones_f[:, :], rhs=dA, start=True, stop=True),
                     r=[self.ones_f, dd], w=[psL])
                for rep in range(3):
                    k.op("act", lambda a: a.copy(out=larep[:, rep * 32:(rep + 1) * 32], in_=psL[:, 0:32]), r=[psL], w=[larep])
                k.op("act", lambda a: a.mul(out=nla[:, :], in_=psL[:, 0:32], mul=-1.0), r=[psL], w=[nla])
                k.op("act", lambda a: a.activation(out=ela[:, :], in_=psL[:, 0:32], func=AF.Exp), r=[psL], w=[ela])
                k.op("act", lambda a: a.activation(out=edl[:, :], in_=psL[:, 32:64], func=AF.Exp), r=[psL], w=[edl])
                k.op("act", lambda a: a.activation(out=etot[:, :], in_=psL[:, 64:96], func=AF.Exp), r=[psL], w=[etot])
                xs3 = rx[:, 0:2048].rearrange("p (h e) -> p h e", h=32)
                k.op("pool", lambda g: g.tensor_tensor(out=xdt[:, :].rearrange("p (h e) -> p h e", h=32), in0=xs3,
                                                       in1=dt.unsqueeze(2).broadcast_to([128, 32, 64]), op=ALU.mult),
                     r=[rx, dd], w=[xdt])
                k.op("pool", lambda g: g.tensor_tensor(out=xdl[:, :].rearrange("p (h e) -> p h e", h=32),
                                                       in0=xdt[:, :].rearrange("p (h e) -> p h e", h=32),
                                                       in1=edl[:, :].unsqueeze(2).broadcast_to([128, 32, 64]), op=ALU.mult),
                     r=[xdt, edl], w=[xdl])
                psT = bankM.next()
                k.op("pe", lambda pe: pe.transpose(out=psT[0:96, 0:128], in_=larep[:, 0:96], identity=identf[:, :]),
                     r=[larep, identf], w=[psT])
                k.op("act", lambda a: a.copy(out=Abf[:, :], in_=psT[0:96, 0:128]), r=[psT], w=[Abf])
                k.op("dve", lambda v: v.tensor_tensor(out=R1[:, :], in0=psT[0:96, 0:128], in1=Abf[:, :], op=ALU.subtract),
                     r=[psT, Abf], w=[R1])
                k.op("act", lambda a: a.copy(out=Bbf[:, :], in_=R1[:, :]), r=[R1], w=[Bbf])
                k.op("dve", lambda v: v.tensor_tensor(out=R1[:, :], in0=R1[:, :], in1=Bbf[:, :], op=ALU.subtract),
                     r=[R1, Bbf], w=[R1])
                k.op("pool", lambda g: g.tensor_copy(out=laT[0:32, :], in_=Abf[0:32, :]), r=[Abf], w=[laT])
                k.op("pool", lambda g: g.tensor_copy(out=laT[32:64, :], in_=Bbf[32:64, :]), r=[Bbf], w=[laT])
                k.op("pool", lambda g: g.tensor_copy(out=laT[64:96, :], in_=R1[64:96, :]), r=[R1], w=[laT])
                psCB = bankM.next()
                for g in range(4):
                    k.op("pe", lambda pe: pe.matmul(psCB[:, g * 128:(g + 1) * 128], lhsT=rbc[:, g * 128:(g + 1) * 128],
                                                    rhs=rbc[:, 512 + g * 128:512 + (g + 1) * 128], start=True, stop=True),
                         r=[rbc], w=[psCB])
                k.op("dve", lambda v: v.tensor_tensor(out=CBm[:, :, :], in0=psCB[:, :].rearrange("p (g i) -> p g i", g=4),
                                                      in1=cm.unsqueeze(1).broadcast_to([128, 4, 128]), op=ALU.mult),
                     r=[psCB, tri], w=[CBm])
                return c

            def stageB(c, ys):
                d, rx, rbc = c["d"], c["rx"], c["rbc"]
                la_sb, ela, etot, laT, xdt, xdl, CBm = (c[n] for n in ("la_sb", "ela", "etot", "laT", "xdt", "xdl", "CBm"))
                ng = negm[:, d, :]
                nla = c["nla"]

                def lb_mm(i):
                    h0 = i * 4
                    psLb = bankLb.next()
                    for hh in range(4):
                        k.op("pe", lambda pe: pe.matmul(psLb[:, hh * 128:(hh + 1) * 128], lhsT=onehot[:, h0 + hh, :],
                                                        rhs=laT[:, :], start=True, stop=False), r=[onehot, laT], w=[psLb])
                        k.op("pe", lambda pe: pe.matmul(psLb[:, hh * 128:(hh + 1) * 128], lhsT=self.identb[:, :],
                                                        rhs=negmb[:, d, :], start=False, stop=True), r=[self.identb, negmb], w=[psLb])
                    return psLb

                pend = [lb_mm(0), lb_mm(1)]

                def heads(g):
                    psY = bankY.next()
                    for hq in range(2):
                        i = g * 2 + hq
                        h0 = i * 4
                        psLb = pend.pop(0)
                        if i + 2 < 8:
                            pend.append(lb_mm(i + 2))
                        Eh = Eh_r.next()
                        for hh in range(4):
                            k.op("act", lambda a: a.activation(out=Eh[:, hh, :], in_=psLb[:, hh * 128:(hh + 1) * 128], func=AF.Exp,
                                                               bias=nla[:, h0 + hh:h0 + hh + 1]), r=[psLb, nla], w=[Eh])
                        mh = Mh.next()
                        k.op("dve", lambda v: v.tensor_tensor(out=mh[:, :, :], in0=Eh[:, :, :],
                                                              in1=CBm[:, g:g + 1, :].broadcast_to([128, 4, 128]), op=ALU.mult),
                             r=[Eh, CBm], w=[mh])
                        for hh in range(4):
                            h = h0 + hh
                            k.op("pe", lambda pe: pe.matmul(psY[:, (h % 8) * 64:(h % 8 + 1) * 64], lhsT=mh[:, hh, :],
                                                            rhs=xdt[:, h * 64:(h + 1) * 64], start=True, stop=True),
                                 r=[mh, xdt], w=[psY])
                    return psY

                def tail(g, psY):
                    psYi = bankM.next()
                    k.op("pe", lambda pe: pe.matmul(psYi[:, :], lhsT=rbc[:, 512 + g * 128:512 + (g + 1) * 128], rhs=Hb[g][:, :],
                                                    start=True, stop=True), r=[rbc, Hb[g]], w=[psYi])
                    yv = ys[:, g * 512:(g + 1) * 512]
                    k.op("dve", lambda v: v.tensor_tensor(out=yv.rearrange("p (h e) -> p h e", h=8),
                                                          in0=psYi[:, :].rearrange("p (h e) -> p h e", h=8),
                                                          in1=ela[:, g * 8:(g + 1) * 8].unsqueeze(2).broadcast_to([128, 8, 64]),
                                                          op=ALU.mult), r=[psYi, ela], w=[ys])
                    k.op("dve", lambda v: v.tensor_tensor(out=yv, in0=yv, in1=psY[:, :], op=ALU.add), r=[ys, psY], w=[ys])
                    psH = bankM.next()
                    k.op("pe", lambda pe: pe.matmul(psH[:, :], lhsT=rx[:, 2048 + g * 128:2048 + (g + 1) * 128],
                                                    rhs=xdl[:, g * 512:(g + 1) * 512], start=True, stop=True), r=[rx, xdl], w=[psH])
                    k.op("pool", lambda v: v.tensor_tensor(out=Hs[g][:, :].rearrange("p (h e) -> p h e", h=8),
                                                           in0=Hs[g][:, :].rearrange("p (h e) -> p h e", h=8),
                                                           in1=etot[:, g * 8:(g + 1) * 8].unsqueeze(2).broadcast_to([128, 8, 64]),
                                                           op=ALU.mult), r=[Hs[g], etot], w=[Hs[g]])
                    k.op("dve", lambda v: v.tensor_tensor(out=Hs[g][:, :], in0=Hs[g][:, :], in1=psH[:, :], op=ALU.add),
                         r=[Hs[g], psH], w=[Hs[g]])
                    k.op("pool", lambda gp: gp.tensor_copy(out=Hb[g][:, :], in_=Hs[g][:, :]), r=[Hs[g]], w=[Hb[g]])

                prev = None
                for g in range(4):
                    py = heads(g)
                    if prev is not None:
                        tail(*prev)
                    prev = (g, py)
                tail(*prev)

            def reset_state():
                for g in range(4):
                    k.op("pool", lambda gp: gp.memset(Hs[g][:, :], 0.0), w=[Hs[g]])
                    k.op("pool", lambda gp: gp.memset(Hb[g][:, :], 0.0), w=[Hb[g]])

            def loads(t):
                bi = 0 if t < 2 else 1 + (t - 2) // 4
                rx = rxr.next()
                rbc = rbcr.next()
                dd = ddr.next()
                k.dma("sp", rx[:, :], RX.t[t], r=[rxT[t]], w=[rx])
                k.dma("sp", rbc[:, :], RBC.t[t], r=[rbcT[bi]], w=[rbc])
                k.dma("sp", dd[:, :], DD.t[t], r=[ddT[t]], w=[dd])
                return rx, rbc, dd

            reset_state()
            cnext = stageA(0, *loads(0))
            for t in range(NT):
                c = cnext
                if t + 1 < NT:
                    cnext = stageA(0, *loads(t + 1))
                ys = ysr.next()
                stageB(c, ys)
                k.dma("pool", YF.t[t], ys[:, :], r=[ys], w=[yfT[t]])
            reset_state()
            order = [1, 0] + list(range(NT - 1, 1, -1))
            cnext = stageA(1, *loads(order[0]))

            def out_stage(t, rx, ys):
                cond = 0 if t < 2 else 1
                bi = 0 if t < 2 else 1 + (t - 2) // 4
                rz = rzr.next()
                yf = yfr.next()
                xt = xr.next()
                k.dma("sp", rz[:, :], RZ.t[t], r=[rzT[t]], w=[rz])
                k.dma("sp", yf[:, :], YF.t[t], r=[yfT[t]], w=[yf])
                sap, rr = self.xsrc_ap(xsrc, t * 128, 128)
                k.dma("sp", xt[:, :], sap, r=(rr or [self.xblk[bi]]), w=[xt])
                k.op("dve", lambda v: v.tensor_tensor(out=ys[:, :], in0=ys[:, :], in1=yf[:, :], op=ALU.add), r=[ys, yf], w=[ys])
                ytmp = yf
                k.op("pool", lambda g: g.tensor_tensor(out=ytmp[:, :].rearrange("p (h e) -> p h e", h=32),
                                                       in0=rx[:, 0:2048].rearrange("p (h e) -> p h e", h=32),
                                                       in1=dsk[:, :].unsqueeze(2).broadcast_to([128, 32, 64]), op=ALU.mult),
                     r=[rx, dsk, yf], w=[ytmp])
                k.op("dve", lambda v: v.tensor_tensor(out=ys[:, :], in0=ys[:, :], in1=ytmp[:, :], op=ALU.add), r=[ys, ytmp], w=[ys])
                k.op("dve", lambda v: v.tensor_tensor(out=ys[:, :], in0=ys[:, :], in1=rz[:, :], op=ALU.mult), r=[ys, rz], w=[ys])
                for g in range(4):
                    k.op("act", lambda a: a.activation(out=junk[:, :], in_=ys[:, g * 512:(g + 1) * 512], func=AF.Square,
                                                       accum_out=ss4[:, g:g + 1]), r=[ys], w=[junk, ss4])
                k.op("act", lambda a: a.activation(out=rs4[:, :], in_=ss4[:, :], func=AF.Sqrt, scale=1.0 / 512, bias=self.epsb[:, :]),
                     r=[ss4, self.epsb], w=[rs4])
                k.op("dve", lambda v: v.reciprocal(out=rs4[:, :], in_=rs4[:, :]), r=[rs4], w=[rs4])
                for g in range(4):
                    k.op("dve", lambda v: v.scalar_tensor_tensor(out=ynb[:, g * 512:(g + 1) * 512], in0=ys[:, g * 512:(g + 1) * 512],
                                                                 scalar=rs4[:, g:g + 1], in1=gnbc[:, g * 512:(g + 1) * 512],
                                                                 op0=ALU.mult, op1=ALU.mult), r=[ys, rs4, gnbc], w=[ynb])
                self.tok_outproj(ynb, 16, ogT, wo, xt, tm, cond)
                k.dma("pool", self.xres.t[t * 128:(t + 1) * 128, :], xt[:, :], r=[xt], w=[self.xblk[bi]])

            pending_out = None
            for oi, t in enumerate(order):
                c = cnext
                if oi + 1 < NT:
                    cnext = stageA(1, *loads(order[oi + 1]))
                ys = ysr.next()
                stageB(c, ys)
                if pending_out is not None:
                    out_stage(*pending_out)
                pending_out = (t, c["rx"], ys)
            out_stage(*pending_out)
            k.barrier()

    def tok_outproj(self, ogb, kc, ogT, wo, xt, tm, cond):
        k = self.k
        for c0 in range(0, kc, 8):
            ps = self.ps8.next()
            pv = ps[:, :].bitcast(BF16).rearrange("p (c t) -> p c t", c=8)
            for c in range(8):
                k.op("pe", lambda pe: pe.transpose(out=pv[:, c, :], in_=ogb[:, (c0 + c) * 128:(c0 + c + 1) * 128],
                                                   identity=self.identb[:, :]), r=[ogb, self.identb], w=[ps])
            k.op("act", lambda a: a.copy(out=ogT[:, c0:c0 + 8, :], in_=pv), r=[ps], w=[ogT])
        for cb in range(2):
            ps = self.ps8.next()
            for kk in range(kc):
                k.op("pe", lambda pe: pe.matmul(ps[:, :], lhsT=ogT[:, kk, :], rhs=wo[:, kk, cb * 512:(cb + 1) * 512],
                                                start=(kk == 0), stop=(kk == kc - 1)), r=[ogT, wo], w=[ps])
            k.op("dve", lambda v: v.tensor_tensor(out=tm[:, :], in0=ps[:, :],
                                                  in1=self.bcm[cond][:, 2 * D + cb * 512:2 * D + (cb + 1) * 512], op=ALU.mult),
                 r=[ps, self.bcm[cond]], w=[tm])
            k.op("dve", lambda v: v.tensor_tensor(out=xt[:, cb * 512:(cb + 1) * 512], in0=tm[:, :],
                                                  in1=xt[:, cb * 512:(cb + 1) * 512], op=ALU.add), r=[tm, xt], w=[xt])

    def outproj(self, es, og_src, ogb, kp, kc, wo, xsrc, xr, tm):
        k = self.k
        for bi, (t0, nt, cond) in enumerate(BLOCKS):
            ob = ogb.next()
            src, trk = og_src(bi, nt)
            if kp == 128:
                s4 = src.rearrange("d (k two) t -> d two k t", two=2)
                for hp in range(2):
                    k.dma("sp", ob[hp * 64:(hp + 1) * 64, :, 0:nt], s4[:, hp, :, :], r=[trk], w=[ob])
            else:
                k.dma("sp", ob[:, :, 0:nt], src, r=[trk], w=[ob])
            for j in range(nt // 128):
                xt = xr.next()
                sap, rr = self.xsrc_ap(xsrc, t0 + j * 128, 128)
                k.dma("sp", xt[:, :], sap, r=(rr or [self.xblk[bi]]), w=[xt])
                import os
                for cb in range(2 if not os.environ.get("DBG_SKIPMM") else 0):
                    ps = self.psg.next()
                    for kk in range(kc):
                        k.op("pe", lambda pe, kk=kk, cb=cb, ps=ps: pe.matmul(
                            ps[:, :], lhsT=ob[0:kp, kk, j * 128:(j + 1) * 128], rhs=wo[0:kp, kk, cb * 512:(cb + 1) * 512],
                            start=(kk == 0), stop=(kk == kc - 1)), r=[ob, wo], w=[ps])
                    k.op("dve", lambda v, cb=cb, ps=ps: v.tensor_tensor(
                        out=tm[:, :], in0=ps[:, :], in1=self.bcm[cond][:, 2 * D + cb * 512:2 * D + (cb + 1) * 512], op=ALU.mult),
                        r=[ps, self.bcm[cond]], w=[tm])
                    k.op("dve", lambda v, cb=cb, xt=xt: v.tensor_tensor(
                        out=xt[:, cb * 512:(cb + 1) * 512], in0=tm[:, :], in1=xt[:, cb * 512:(cb + 1) * 512], op=ALU.add),
                        r=[tm, xt], w=[xt])
                k.dma("pool", self.xres.t[t0 + j * 128:t0 + (j + 1) * 128, :], xt[:, :], r=[xt], w=[self.xblk[bi]])

    def final(self, xsrc):
        k = self.k
        with ExitStack() as es:
            xr = Ring([k.sb(es, [128, D], F32) for _ in range(3)])
            junk = k.sb(es, [128, D], BF16)
            ss = k.sb(es, [128, 1], F32)
            rs = k.sb(es, [128, 1], F32)
            epsb = k.sb(es, [128, 1], F32)
            k.op("pool", lambda g: g.memset(epsb[:, :], EPS), w=[epsb])
            fg = k.sb(es, [128, D], F32)
            k.dma("sp", fg[:, :], self.final_g.t.partition_broadcast(128), w=[fg])
            for bi, (t0, nt, cond) in enumerate(BLOCKS):
                if cond == 0 and not self.debug_x:
                    continue
                for j in range(nt // 128):
                    xt = xr.next()
                    tt0 = t0 + j * 128
                    sap, rr = self.xsrc_ap(xsrc, tt0, 128)
                    k.dma("sp", xt[:, :], sap, r=(rr or [self.xblk[bi]]), w=[xt])
                    if self.debug_x:
                        k.dma("pool", self.out.t[tt0:tt0 + 128, :], xt[:, :], r=[xt], w=[self.out])
                        continue
                    k.op("act", lambda a, xt=xt: a.activation(out=junk[:, :], in_=xt[:, :], func=AF.Square, accum_out=ss[:, :]),
                         r=[xt], w=[junk, ss])
                    k.op("act", lambda a: a.activation(out=rs[:, :], in_=ss[:, :], func=AF.Sqrt, scale=1.0 / D, bias=epsb[:, :]),
                         r=[ss, epsb], w=[rs])
                    k.op("dve", lambda v: v.reciprocal(out=rs[:, :], in_=rs[:, :]), r=[rs], w=[rs])
                    k.op("dve", lambda v, xt=xt: v.scalar_tensor_tensor(out=xt[:, :], in0=xt[:, :], scalar=rs[:, 0:1],
                                                                       in1=fg[:, :], op0=ALU.mult, op1=ALU.mult),
                         r=[xt, rs, fg], w=[xt])
                    k.dma("pool", self.out.t[tt0 - CTX:tt0 - CTX + 128, :], xt[:, :], r=[xt], w=[self.out])


WSHAPES = {
    "mla_w_in": [1, 1024, 1696], "mla_q_norm": [1, 384], "mla_w_uq": [1, 384, 1536], "mla_kv_norm": [1, 256],
    "mla_w_ukv": [1, 256, 2048], "mla_w_out": [1, 1024, 1024],
    "gla_w_in": [1, 1024, 3104], "gla_w_gf": [1, 16, 512], "gla_b_gf": [1, 512], "gla_w_gb": [1, 16, 512],
    "gla_b_gb": [1, 512], "gla_o_norm": [1, 256], "gla_w_out": [1, 1024, 1024],
    "gqa_w_in": [1, 1024, 2560], "gqa_q_norm": [1, 64], "gqa_k_norm": [1, 64], "gqa_w_out": [1, 1024, 1024],
    "ssd_w_in": [1, 1024, 5184], "ssd_conv_w": [1, 5, 3072], "ssd_conv_b": [1, 3072], "ssd_dt_bias_f": [1, 32],
    "ssd_dt_bias_b": [1, 32], "ssd_a_log_f": [1, 32], "ssd_a_log_b": [1, 32], "ssd_d": [1, 32], "ssd_norm": [1, 2048],
    "ssd_w_out": [1, 2048, 1024],
}


def tri_consts():
    j = np.arange(128)[:, None]
    i = np.arange(128)[None, :]
    return np.stack([(j <= i), (j > i), (j >= i), (j < i)]).astype(np.float32)


def rope_tables(rd):
    hf = rd // 4
    inv = 10000.0 ** (-np.arange(hf, dtype=np.float64) / hf)
    p = np.arange(SEQ)
    row = (p // 64).astype(np.float64)[:, None] * inv[None, :]
    col = (p % 64).astype(np.float64)[:, None] * inv[None, :]
    cos = np.concatenate([np.cos(row), np.cos(row), np.cos(col), np.cos(col)], axis=1)
    sin = np.concatenate([-np.sin(row), np.sin(row), -np.sin(col), np.sin(col)], axis=1)
    tab = np.zeros((T, 2, rd), np.float32)
    tab[:CTX, 0, :] = 1.0
    tab[CTX:, 0, :] = cos
    tab[CTX:, 1, :] = sin
    return tab


def run(inputs, layers=(0, 1, 2, 3), debug_x=False, cores=(0, 1), stop=None):
    nc = bass.Bass("TRN2", target_bir_lowering=False)
    Prog(nc, layers=layers, debug_x=debug_x, stop=stop).build()
    f = lambda a: np.ascontiguousarray(np.asarray(a, dtype=np.float32))
    common = {nm: f(inputs[nm]) for nm in WSHAPES}
    for nm in ("ada_w", "ada_b", "norm_g", "final_g"):
        common[nm] = f(inputs[nm])
    common["ident_bf"] = np.eye(128, dtype=np.float32).astype(ml_dtypes.bfloat16)
    common["rope_mla"] = rope_tables(32)
    common["rope_gqa"] = rope_tables(64)
    common["tri"] = tri_consts()
    common["negm"] = ((1.0 - tri_consts()[[0, 2]]) * -1e30).astype(np.float32)
    oh = np.zeros((3, 32, 32, 128), np.float32)
    oh[:, np.arange(32), np.arange(32), :] = 1.0
    common["onehot3"] = oh.reshape(96, 32, 128).astype(ml_dtypes.bfloat16)
    common["negmb"] = common["negm"].astype(ml_dtypes.bfloat16)
    common["ident_f"] = np.eye(128, dtype=np.float32)
    in_maps = []
    for b in cores:
        m = dict(common)
        m["xin"] = np.ascontiguousarray(np.concatenate([f(inputs["ctx"])[b], f(inputs["x"])[b]], axis=0))
        m["c2"] = np.ascontiguousarray(np.stack([f(inputs["c_ctx"]), f(inputs["c"])[b]], axis=0))
        in_maps.append(m)
    res = run_bass_kernel_spmd(nc, in_maps, core_ids=list(range(len(cores))))
    return [r["y"] for r in res.results]


FUSED = True


def kernel(**inputs):
    if FUSED:
        outs = run(inputs)
        return np.stack(outs, axis=0).astype(np.float32)
    cur = dict(inputs)
    for L in (0, 1, 2):
        outs = run(cur, layers=(L,), debug_x=True)
        st = np.stack(outs, axis=0)
        cur["ctx"] = np.ascontiguousarray(st[:, :CTX])
        cur["x"] = np.ascontiguousarray(st[:, CTX:])
    outs = run(cur, layers=(3,), debug_x=False)
    return np.stack(outs, axis=0).astype(np.float32)
```

```python
import math
from contextlib import ExitStack

import numpy as np
import ml_dtypes
import concourse.bass as bass
import concourse.mybir as mybir
from concourse.bass_utils import run_bass_kernel_spmd

F32 = mybir.dt.float32
BF16 = mybir.dt.bfloat16
AF = mybir.ActivationFunctionType
ALU = mybir.AluOpType
AX = mybir.AxisListType

D = 1024
SEQ = 8192
CTX = 256
T = SEQ + CTX
NT = T // 128
EPS = 1e-6
EPOCH = 30000

BLOCKS = [(0, 256, 0)] + [(256 + 512 * i, 512, 1) for i in range(16)]
NB = len(BLOCKS)


class Buf:
    __slots__ = ("w", "r")

    def __init__(self):
        self.w = None
        self.r = {}


class TT:
    def __init__(self, t):
        self.t = t
        self.b = Buf()

    def __getitem__(self, idx):
        return self.t[idx]


class Ring:
    def __init__(self, items):
        self.items = items
        self.i = 0

    def next(self):
        it = self.items[self.i % len(self.items)]
        self.i += 1
        return it


class KB:
    def __init__(self, nc, es):
        self.nc = nc
        self.es = es
        self.eng = {"pe": nc.tensor, "act": nc.scalar, "dve": nc.vector, "pool": nc.gpsimd, "sp": nc.sync}
        self.sems = {e: [] for e in self.eng}
        self.cnt = {e: 0 for e in self.eng}
        self.seen = {e: {} for e in self.eng}
        self.last = {e: None for e in self.eng}
        self.slots = {}
        self.slot_i = {}
        for q in ("sp", "pool", "act"):
            self.slots[q] = [[es.enter_context(nc.semaphore(f"d_{q}_{i}")), 0, f"d_{q}_{i}"] for i in range(12)]
            self.slot_i[q] = 0
        self.nsb = 0

    def sb(self, es, shape, dt, name=None):
        self.nsb += 1
        return TT(es.enter_context(self.nc.sbuf_tensor(name or f"sb{self.nsb}", list(shape), dt)))

    def dram(self, shape, dt, name):
        h = self.nc.dram_tensor(name, list(shape), dt, kind="Internal")
        return TT(h.ap())

    def _wait(self, e, deps):
        seen = self.seen[e]
        for ev in deps:
            key, sem, val, src = ev
            if src == "pe" and e == "pe":
                continue
            if seen.get(key, 0) >= val:
                continue
            self.eng[e].wait_ge(sem, val)
            seen[key] = val

    def _deps(self, reads, writes):
        deps = []
        for t in reads:
            if t.b.w is not None:
                deps.append(t.b.w)
        for t in writes:
            if t.b.w is not None:
                deps.append(t.b.w)
            deps.extend(t.b.r.values())
        return deps

    def _mark(self, ev, reads, writes):
        for t in reads:
            t.b.r[ev[0]] = ev
        for t in writes:
            t.b.w = ev
            t.b.r = {}

    def op(self, e, fn, r=(), w=()):
        self._wait(e, self._deps(r, w))
        ins = fn(self.eng[e])
        epoch = self.cnt[e] // EPOCH
        while len(self.sems[e]) <= epoch:
            self.sems[e].append(self.es.enter_context(self.nc.semaphore(f"s_{e}_{len(self.sems[e])}")))
        sem = self.sems[e][epoch]
        val = self.cnt[e] % EPOCH + 1
        ins.then_inc(sem, 1)
        self.cnt[e] += 1
        ev = ((e, epoch), sem, val, e)
        self.last[e] = ev
        self._mark(ev, r, w)
        return ev

    def dma(self, q, out, in_, r=(), w=(), **kw):
        deps = self._deps(r, w)
        slots = self.slots[q]
        si = self.slot_i[q] % len(slots)
        self.slot_i[q] += 1
        slot = slots[si]
        if slot[1] > 0:
            deps.append((slot[2], slot[0], 16 * slot[1], "dma"))
        self._wait(q, deps)
        ins = self.eng[q].dma_start(out=out, in_=in_, **kw)
        ins.then_inc(slot[0], 16)
        slot[1] += 1
        ev = (slot[2], slot[0], 16 * slot[1], "dma")
        self._mark(ev, r, w)
        return ev

    def barrier(self):
        evs = [self.last[e] for e in self.eng if self.last[e] is not None]
        for q in self.slots:
            for slot in self.slots[q]:
                if slot[1] > 0:
                    evs.append((slot[2], slot[0], 16 * slot[1], "dma"))
        for e in self.eng:
            seen = self.seen[e]
            for ev in evs:
                key, sem, val, src = ev
                if src == e:
                    continue
                if seen.get(key, 0) >= val:
                    continue
                self.eng[e].wait_ge(sem, val)
                seen[key] = val


class Prog:
    def __init__(self, nc, layers=(0, 1, 2, 3), debug_x=False, stop=None):
        self.nc = nc
        self.stop = stop
        self.layers = layers
        self.debug_x = debug_x

    def din(self, name, shape, dt=F32):
        return TT(self.nc.dram_tensor(name, list(shape), dt, kind="ExternalInput").ap())

    def build(self):
        nc = self.nc
        with ExitStack() as es:
            self.k = k = KB(nc, es)
            self.es = es
            self.xin = self.din("xin", [T, D])
            self.c2 = self.din("c2", [2, D])
            self.ada_w = self.din("ada_w", [4, D, 3 * D])
            self.ada_b = self.din("ada_b", [4, 3 * D])
            self.norm_g = self.din("norm_g", [4, D])
            self.final_g = self.din("final_g", [D])
            self.W = {}
            for nm, shp in WSHAPES.items():
                self.W[nm] = self.din(nm, shp)
            self.identb_d = self.din("ident_bf", [128, 128], BF16)
            self.rope_mla = self.din("rope_mla", [T, 2, 32])
            self.rope_gqa = self.din("rope_gqa", [T, 2, 64])
            self.tri_d = self.din("tri", [4, 128, 128])
            self.negm_d = self.din("negm", [2, 128, 128])
            self.onehot_d = self.din("onehot3", [96, 32, 128], BF16)
            self.negmb_d = self.din("negmb", [2, 128, 128], BF16)
            self.identf_d = self.din("ident_f", [128, 128])
            if self.debug_x:
                self.out = TT(nc.dram_tensor("y", [T, D], F32, kind="ExternalOutput").ap())
            else:
                self.out = TT(nc.dram_tensor("y", [SEQ, D], F32, kind="ExternalOutput").ap())
            self.xres = k.dram([T, D], F32, "xres")
            self.xblk = [TT(self.xres.t) for _ in range(NB)]
            self.modd = k.dram([4, 2, 3 * D], F32, "modd")
            self.identb = k.sb(es, [128, 128], BF16, "identb")
            k.dma("sp", self.identb[:, :], self.identb_d.t[:, :], w=[self.identb])
            self.ones_f = k.sb(es, [128, 128], F32, "ones_f")
            k.op("pool", lambda g: g.memset(self.ones_f[:, :], 1.0), w=[self.ones_f])
            self.psb = [TT(es.enter_context(nc.psum_tensor(f"ps{i}", [128, 512], F32))) for i in range(8)]
            self.psg = Ring(self.psb[0:6])
            self.pso = Ring(self.psb[6:8])
            self.ps8 = Ring(self.psb)
            self.bcm = [k.sb(es, [128, 3 * D], F32, f"bcm{c}") for c in range(2)]
            self.gmod = [k.sb(es, [128, D], F32, f"gmod{c}") for c in range(2)]
            self.ngbc = k.sb(es, [128, D], F32, "ngbc")

            first = True
            for L in self.layers:
                self.modulation(L)
                if self.stop == "mod":
                    break
                xsrc = self.xin if first else None
                if L == 0:
                    self.layer_attn(L, "mla", xsrc)
                elif L == 2:
                    self.layer_attn(L, "gqa", xsrc)
                elif L == 1:
                    self.layer_gla(L, xsrc)
                elif L == 3:
                    self.layer_ssd(L, xsrc)
                first = False
                k.barrier()
            self.final(self.xin if first else None)
            k.barrier()
        return nc

    def xsrc_ap(self, xsrc, t0, n):
        if xsrc is not None:
            return xsrc.t[t0:t0 + n, :], [xsrc]
        return self.xres.t[t0:t0 + n, :], None

    def modulation(self, L):
        k = self.k
        with ExitStack() as es:
            cT = k.sb(es, [128, 8, 2], F32)
            sT = k.sb(es, [128, 8, 2], F32)
            for kk in range(8):
                k.dma("sp", cT[:, kk, :], self.c2.t[:, kk * 128:(kk + 1) * 128].rearrange("c p -> p c"), w=[cT],
                      allow_slow_non_contiguous=True)
            k.op("act", lambda a: a.activation(out=sT[:, :, :], in_=cT[:, :, :], func=AF.Silu), r=[cT], w=[sT])
            msb = k.sb(es, [2, 3 * D], F32)
            bb = k.sb(es, [2, 3 * D], F32)
            k.dma("sp", bb[:, :], self.ada_b.t[L, :].partition_broadcast(2), w=[bb])
            wr = Ring([k.sb(es, [128, 8, 512], F32) for _ in range(2)])
            for cb in range(6):
                wt = wr.next()
                k.dma("sp", wt[:, :, :],
                      self.ada_w.t[L, :, cb * 512:(cb + 1) * 512].rearrange("(k p) n -> p k n", p=128), w=[wt])
                ps = self.psg.next()
                for kk in range(8):
                    k.op("pe", lambda pe, kk=kk: pe.matmul(ps[0:2, :], lhsT=sT[:, kk, :], rhs=wt[:, kk, :],
                                                          start=(kk == 0), stop=(kk == 7)), r=[sT, wt], w=[ps])
                k.op("dve", lambda v: v.tensor_tensor(out=msb[:, cb * 512:(cb + 1) * 512], in0=ps[0:2, :],
                                                      in1=bb[:, cb * 512:(cb + 1) * 512], op=ALU.add),
                     r=[ps, bb], w=[msb])
            md = TT(self.modd.t)
            k.dma("sp", self.modd.t[L, :, :], msb[:, :], r=[msb], w=[md])
            for c in range(2):
                k.dma("sp", self.bcm[c][:, :], self.modd.t[L, c, :].partition_broadcast(128), r=[md], w=[self.bcm[c]])
            k.dma("sp", self.ngbc[:, :], self.norm_g.t[L, :].partition_broadcast(128), w=[self.ngbc])
            for c in range(2):
                k.op("dve", lambda v, c=c: v.scalar_tensor_tensor(out=self.gmod[c][:, :], in0=self.bcm[c][:, D:2 * D],
                                                                 scalar=1.0, in1=self.ngbc[:, :], op0=ALU.add,
                                                                 op1=ALU.mult),
                     r=[self.bcm[c], self.ngbc], w=[self.gmod[c]])
            k.barrier()

    def load_w(self, es_stage, dst, dview, src_ap, kc, kp, n, stg):
        k = self.k
        CH = 1024
        for kk in range(kc):
            for c0 in range(0, n, CH):
                cn = min(CH, n - c0)
                st = stg.next()
                k.dma("sp", st[0:kp, 0:cn], src_ap[kk * kp:(kk + 1) * kp, c0:c0 + cn], w=[st])
                k.op("pool", lambda g, st=st, kk=kk, c0=c0, cn=cn: g.tensor_copy(out=dview[0:kp, kk, c0:c0 + cn],
                                                                                in_=st[0:kp, 0:cn]),
                     r=[st], w=[dst])

    def norm_block(self, es, bi, xsrc, xr, hT, scr):
        k = self.k
        t0, nt, cond = BLOCKS[bi]
        junk, _ss, _rs, hf0, hb0 = scr
        key = id(es)
        if getattr(self, "_nb_key", None) != key:
            self._nb_key = key
            self._nb = dict(ss=k.sb(es, [128, 4], F32), rs=k.sb(es, [128, 4], F32),
                            hf=Ring([hf0, k.sb(es, [128, D], F32)]),
                            hb=Ring([hb0] + [k.sb(es, [128, D], BF16) for _ in range(3)]))
        nb = self._nb
        ss, rs = nb["ss"], nb["rs"]
        ntile = nt // 128
        xts = []
        for j in range(ntile):
            xt = xr.next()
            xts.append(xt)
            src, rr = self.xsrc_ap(xsrc, t0 + j * 128, 128)
            k.dma("sp", xt[:, :], src, r=(rr or [self.xblk[bi]]), w=[xt])
            k.op("act", lambda a: a.activation(out=junk[:, :], in_=xt[:, :], func=AF.Square, accum_out=ss[:, j:j + 1]),
                 r=[xt], w=[junk, ss])
            if len(xr.items) < ntile and j % len(xr.items) == len(xr.items) - 1:
                pass
        k.op("act", lambda a: a.activation(out=rs[:, 0:ntile], in_=ss[:, 0:ntile], func=AF.Sqrt, scale=1.0 / D, bias=self.epsb[:, :]),
             r=[ss, self.epsb], w=[rs])
        k.op("dve", lambda v: v.reciprocal(out=rs[:, 0:ntile], in_=rs[:, 0:ntile]), r=[rs], w=[rs])
        hbs = []
        for j in range(ntile):
            xt = xts[j]
            hf = nb["hf"].next()
            hb = nb["hb"].next()
            hbs.append(hb)
            k.op("dve", lambda v: v.scalar_tensor_tensor(out=hf[:, :], in0=xt[:, :], scalar=rs[:, j:j + 1],
                                                         in1=self.gmod[cond][:, :], op0=ALU.mult, op1=ALU.mult),
                 r=[xt, rs, self.gmod[cond]], w=[hf])
            k.op("dve", lambda v: v.tensor_tensor(out=hb[:, :], in0=hf[:, :], in1=self.bcm[cond][:, 0:D], op=ALU.add),
                 r=[hf, self.bcm[cond]], w=[hb])
        for j in range(ntile):
            hb = hbs[j]
            ps = self.psg.next()
            pv = ps[:, :].bitcast(BF16).rearrange("p (c t) -> p c t", c=8)
            for c in range(8):
                k.op("pe", lambda pe: pe.transpose(out=pv[:, c, :], in_=hb[:, c * 128:(c + 1) * 128],
                                                   identity=self.identb[:, :]), r=[hb, self.identb], w=[ps])
            k.op("act", lambda a: a.copy(out=hT[:, :, j * 128:(j + 1) * 128], in_=pv), r=[ps], w=[hT])

    def layer_attn(self, L, kind, xsrc):
        k = self.k
        nc = self.nc
        if kind == "mla":
            H, HK, DQ = 16, 16, 96
            w_in = self.W["mla_w_in"].t[0]
            w_out = self.W["mla_w_out"].t[0]
            GOFF = 672
            scale = 96 ** -0.5
        else:
            H, HK, DQ = 16, 4, 64
            w_in = self.W["gqa_w_in"].t[0]
            w_out = self.W["gqa_w_out"].t[0]
            GOFF = 1536
            scale = 64 ** -0.5
        REP = H // HK
        QT = k.dram([H, DQ, T], BF16, f"QT{L}")
        KT = k.dram([HK, DQ, T], BF16, f"KT{L}")
        VV = k.dram([HK, 128, NT, 65], BF16, f"VV{L}")
        GS = k.dram([8, 128, T], BF16, f"GS{L}")
        OG = k.dram([NB, 64, 16, 512], BF16, f"OG{L}")

        with ExitStack() as es:
            self.epsb = k.sb(es, [128, 1], F32)
            k.op("pool", lambda g: g.memset(self.epsb[:, :], EPS), w=[self.epsb])
            NIN = 1696 if kind == "mla" else 2560
            win = k.sb(es, [128, 8, NIN], BF16)
            if kind == "mla":
                wuq = k.sb(es, [128, 3, 1536], BF16)
                wukv = k.sb(es, [128, 2, 2048], BF16)
            with ExitStack() as es2:
                stg = Ring([k.sb(es2, [128, 1024], F32) for _ in range(2)])
                self.load_w(es2, win, win, w_in, 8, 128, NIN, stg)
                if kind == "mla":
                    self.load_w(es2, wuq, wuq, self.W["mla_w_uq"].t[0], 3, 128, 1536, stg)
                    self.load_w(es2, wukv, wukv, self.W["mla_w_ukv"].t[0], 2, 128, 2048, stg)
                k.barrier()
            if kind == "mla":
                qnbc = k.sb(es, [128, 384], F32)
                k.dma("sp", qnbc[:, :], self.W["mla_q_norm"].t[0, :].partition_broadcast(128), w=[qnbc])
                kvnbc = k.sb(es, [128, 256], F32)
                k.dma("sp", kvnbc[:, :], self.W["mla_kv_norm"].t[0, :].partition_broadcast(128), w=[kvnbc])
                RD, HF = 32, 8
                rope_d = self.rope_mla
            else:
                qnbc = k.sb(es, [128, 64], F32)
                k.dma("sp", qnbc[:, :], self.W["gqa_q_norm"].t[0, :].partition_broadcast(128), w=[qnbc])
                knbc = k.sb(es, [128, 64], F32)
                k.dma("sp", knbc[:, :], self.W["gqa_k_norm"].t[0, :].partition_broadcast(128), w=[knbc])
                RD, HF = 64, 16
                rope_d = self.rope_gqa
            xr = Ring([k.sb(es, [128, D], F32) for _ in range(4)])
            hTr = Ring([k.sb(es, [128, 8, 512], BF16) for _ in range(2)])
            scr = (k.sb(es, [128, D], BF16), k.sb(es, [128, 1], F32), k.sb(es, [128, 1], F32),
                   k.sb(es, [128, D], F32), k.sb(es, [128, D], BF16))
            qsb = k.sb(es, [128, H * DQ], F32)
            qb = k.sb(es, [128, H, DQ], BF16)
            kb = k.sb(es, [128, HK, DQ], BF16)
            ksb = k.sb(es, [128, HK * DQ if kind == "gqa" else 32], F32)
            vblk = Ring([k.sb(es, [128, 4, HK, 65], BF16) for _ in range(1)])
            for vb_ in vblk.items:
                k.op("pool", lambda g, vb_=vb_: g.memset(vb_[:, :, :, :], 1.0), w=[vb_])
            qTb = Ring([k.sb(es, [DQ, H, 512], BF16) for _ in range(1)])
            kTb = Ring([k.sb(es, [DQ, HK, 512], BF16) for _ in range(1)])
            rtab = Ring([k.sb(es, [128, 2, RD], F32) for _ in range(2)])
            ra = k.sb(es, [128, H, RD], F32)
            rb_ = k.sb(es, [128, H, RD], F32)
            ss2 = k.sb(es, [128, 32], F32)
            rs2 = k.sb(es, [128, 32], F32)
            sq = k.sb(es, [128, H * DQ], F32)
            if kind == "mla":
                cqn = k.sb(es, [128, 640], BF16)
                cT = k.sb(es, [128, 5, 128], BF16)
            gsr = Ring([k.sb(es, [128, 512], BF16) for _ in range(2)])

            def rope(xv, nh, dst, tab):
                cosb = tab[:, 0:1, :].broadcast_to([128, nh, RD])
                k.op("dve", lambda v: v.tensor_tensor(out=ra[:, 0:nh, :], in0=xv, in1=cosb, op=ALU.mult),
                     r=[tab, qsb, ksb], w=[ra])
                x5 = xv.rearrange("p h (g s f) -> p h g s f", g=2, s=2)
                b5 = rb_[:, 0:nh, :].rearrange("p h (g s f) -> p h g s f", g=2, s=2)
                s5 = tab[:, 1, :].rearrange("p (g s f) -> p g s f", g=2, s=2)
                for g in range(2):
                    for s in range(2):
                        sinb = s5[:, g:g + 1, s, :].broadcast_to([128, nh, HF])
                        k.op("dve", lambda v, g=g, s=s, sinb=sinb: v.tensor_tensor(
                            out=b5[:, :, g, s, :], in0=x5[:, :, g, 1 - s, :], in1=sinb, op=ALU.mult),
                            r=[tab, qsb, ksb], w=[rb_])
                k.op("dve", lambda v: v.tensor_tensor(out=dst, in0=ra[:, 0:nh, :], in1=rb_[:, 0:nh, :], op=ALU.add),
                     r=[ra, rb_], w=[qb, kb])

            import os
            pending_st = [None]
            for bi, (t0, nt, cond) in enumerate(BLOCKS[:int(os.environ.get('DBG_P1_BLOCKS', NB))]):
                hT = hTr.next()
                self.norm_block(es, bi, xsrc, xr, hT, scr)
                if pending_st[0] is not None:
                    pending_st[0]()
                    pending_st[0] = None
                ntile = nt // 128
                vb4 = vblk.next()
                qT = qTb.next()
                kT = kTb.next()
                for j in range(ntile):
                    tt0 = t0 + j * 128
                    kt = tt0 // 128
                    tab = rtab.next()
                    k.dma("sp", tab[:, :, :], rope_d.t[tt0:tt0 + 128, :, :], w=[tab])
                    hTj = lambda kk: hT[:, kk, j * 128:(j + 1) * 128]
                    if kind == "mla":
                        psA = self.psg.next()
                        psB = self.psg.next()
                        for kk in range(8):
                            k.op("pe", lambda pe, kk=kk: pe.matmul(psA[:, 0:384], lhsT=hTj(kk), rhs=win[:, kk, 0:384],
                                                                  start=(kk == 0), stop=(kk == 7)), r=[hT, win], w=[psA])
                        for kk in range(8):
                            k.op("pe", lambda pe, kk=kk: pe.matmul(psB[:, 0:288], lhsT=hTj(kk), rhs=win[:, kk, 384:672],
                                                                  start=(kk == 0), stop=(kk == 7)), r=[hT, win], w=[psB])
                        for (ps_, n_, gb_, o_) in ((psA, 384, qnbc, 0), (psB, 256, kvnbc, 384)):
                            k.op("act", lambda a, ps_=ps_, n_=n_: a.activation(out=sq[:, 0:n_], in_=ps_[:, 0:n_], func=AF.Square,
                                                                              accum_out=ss2[:, 0:1]), r=[ps_], w=[sq, ss2])
                            k.op("act", lambda a, n_=n_: a.activation(out=rs2[:, 0:1], in_=ss2[:, 0:1], func=AF.Sqrt,
                                                                     scale=1.0 / n_, bias=self.epsb[:, :]),
                                 r=[ss2, self.epsb], w=[rs2])
                            k.op("dve", lambda v: v.reciprocal(out=rs2[:, 0:1], in_=rs2[:, 0:1]), r=[rs2], w=[rs2])
                            k.op("dve", lambda v, ps_=ps_, n_=n_, gb_=gb_, o_=o_: v.scalar_tensor_tensor(
                                out=cqn[:, o_:o_ + n_], in0=ps_[:, 0:n_], scalar=rs2[:, 0:1], in1=gb_[:, :],
                                op0=ALU.mult, op1=ALU.mult), r=[ps_, rs2, gb_], w=[cqn])
                        k.op("act", lambda a: a.copy(out=ksb[:, 0:32], in_=psB[:, 256:288]), r=[psB], w=[ksb])
                        pst = self.psg.next()
                        ptv = pst[:, :].bitcast(BF16).rearrange("p (c t) -> p c t", c=8)
                        for c in range(5):
                            k.op("pe", lambda pe, c=c: pe.transpose(out=ptv[:, c, :], in_=cqn[:, c * 128:(c + 1) * 128],
                                                                    identity=self.identb[:, :]), r=[cqn, self.identb], w=[pst])
                        k.op("act", lambda a: a.copy(out=cT[:, :, :], in_=ptv[:, 0:5, :]), r=[pst], w=[cT])
                        for cb in range(3):
                            ps = self.psg.next()
                            for kk in range(3):
                                k.op("pe", lambda pe, kk=kk, cb=cb, ps=ps: pe.matmul(
                                    ps[:, :], lhsT=cT[:, kk, :], rhs=wuq[:, kk, cb * 512:(cb + 1) * 512],
                                    start=(kk == 0), stop=(kk == 2)), r=[cT, wuq], w=[ps])
                            k.op("act", lambda a, cb=cb, ps=ps: a.copy(out=qsb[:, cb * 512:(cb + 1) * 512], in_=ps[:, :]),
                                 r=[ps], w=[qsb])
                        q3 = qsb[:, :].rearrange("p (h d) -> p h d", h=16)
                        k.op("pool", lambda g: g.tensor_copy(out=qb[:, :, 0:64], in_=q3[:, :, 0:64]), r=[qsb], w=[qb])
                        rope(q3[:, :, 64:96], 16, qb[:, :, 64:96], tab)
                        krv = ksb[:, 0:32].rearrange("p (h d) -> p h d", h=1)
                        rope(krv, 1, kb[:, 0:1, 64:96], tab)
                        k.op("pool", lambda g: g.tensor_copy(out=kb[:, 1:16, 64:96],
                                                             in_=kb[:, 0:1, 64:96].broadcast_to([128, 15, 32])),
                             r=[kb], w=[kb])
                        for cb in range(4):
                            ps = self.psg.next()
                            for kk in range(2):
                                k.op("pe", lambda pe, kk=kk, cb=cb, ps=ps: pe.matmul(
                                    ps[:, :], lhsT=cT[:, 3 + kk, :], rhs=wukv[:, kk, cb * 512:(cb + 1) * 512],
                                    start=(kk == 0), stop=(kk == 1)), r=[cT, wukv], w=[ps])
                            p3 = ps[:, :].rearrange("p (h d) -> p h d", h=4)
                            k.op("act", lambda a, cb=cb, p3=p3: a.copy(out=kb[:, cb * 4:(cb + 1) * 4, 0:64], in_=p3[:, :, 0:64]),
                                 r=[ps], w=[kb])
                            k.op("dve", lambda v, cb=cb, p3=p3: v.tensor_copy(out=vb4[:, j, cb * 4:(cb + 1) * 4, 0:64],
                                                                              in_=p3[:, :, 64:128]), r=[ps], w=[vb4])
                    else:
                        import os
                        for cb in range(3 if int(os.environ.get('DBG_STEP', 9)) >= 1 else 0):
                            ps = self.psg.next()
                            for kk in range(8):
                                k.op("pe", lambda pe, kk=kk, cb=cb, ps=ps: pe.matmul(
                                    ps[:, :], lhsT=hTj(kk), rhs=win[:, kk, cb * 512:(cb + 1) * 512],
                                    start=(kk == 0), stop=(kk == 7)), r=[hT, win], w=[ps])
                            SUB = os.environ.get('DBG_SUB', 'abc')
                            if cb < 2:
                                if 'a' in SUB:
                                    k.op("act", lambda a, cb=cb, ps=ps: a.copy(out=qsb[:, cb * 512:(cb + 1) * 512], in_=ps[:, :]),
                                         r=[ps], w=[qsb])
                            elif 'b' in SUB:
                                k.op("act", lambda a, ps=ps: a.copy(out=ksb[:, 0:256], in_=ps[:, 0:256]), r=[ps], w=[ksb])
                                p3 = ps[:, 256:512].rearrange("p (h d) -> p h d", h=4)
                                if 'c' in SUB:
                                    for hh in range(4):
                                        k.op("act", lambda a, hh=hh: a.copy(out=vb4[:, j, hh, 0:64], in_=ps[:, 256 + hh * 64:256 + (hh + 1) * 64]),
                                             r=[ps], w=[vb4])
                        import os
                        DS = int(os.environ.get('DBG_STEP', 9))
                        for (src_, nh, gb_, dstb) in ((qsb, 16, qnbc, qb), (ksb, 4, knbc, kb)) if DS >= 2 else ():
                            s3 = src_[:, 0:nh * 64].rearrange("p (h d) -> p h d", h=nh)
                            sq3 = sq[:, 0:nh * 64].rearrange("p (h d) -> p h d", h=nh)
                            k.op("dve", lambda v, s3=s3, sq3=sq3: v.tensor_tensor(out=sq3, in0=s3, in1=s3, op=ALU.mult),
                                 r=[src_], w=[sq])
                            k.op("dve", lambda v, sq3=sq3, nh=nh: v.tensor_reduce(out=ss2[:, 0:nh], in_=sq3, axis=AX.X, op=ALU.add),
                                 r=[sq], w=[ss2])
                            k.op("act", lambda a, nh=nh: a.activation(out=rs2[:, 0:nh], in_=ss2[:, 0:nh], func=AF.Sqrt,
                                                                     scale=1.0 / 64, bias=self.epsb[:, :]),
                                 r=[ss2, self.epsb], w=[rs2])
                            k.op("dve", lambda v, nh=nh: v.reciprocal(out=rs2[:, 0:nh], in_=rs2[:, 0:nh]), r=[rs2], w=[rs2])
                            k.op("dve", lambda v, s3=s3, nh=nh: v.tensor_tensor(
                                out=s3, in0=s3, in1=rs2[:, 0:nh].unsqueeze(2).broadcast_to([128, nh, 64]), op=ALU.mult),
                                r=[src_, rs2], w=[src_])
                            k.op("dve", lambda v, s3=s3, nh=nh, gb_=gb_: v.tensor_tensor(
                                out=s3, in0=s3, in1=gb_[:, :].unsqueeze(1).broadcast_to([128, nh, 64]), op=ALU.mult),
                                r=[src_, gb_], w=[src_])
                            if DS >= 3:
                                rope(s3, nh, dstb[:, :, :], tab)
                    import os
                    for (srcb, nh, dstT) in ((qb, H, qT), (kb, HK, kT)) if int(os.environ.get('DBG_STEP', 9)) >= 4 else ():
                        for h0 in range(0, nh, 8):
                            hn = min(8, nh - h0)
                            ps = self.psg.next()
                            ptv = ps[:, :].bitcast(BF16).rearrange("p (c t) -> p c t", c=8)
                            for hh in range(hn):
                                k.op("pe", lambda pe, hh=hh, h0=h0, ptv=ptv, srcb=srcb: pe.transpose(
                                    out=ptv[0:DQ, hh, :], in_=srcb[:, h0 + hh, :], identity=self.identb[:, :]),
                                    r=[srcb, self.identb], w=[ps])
                            k.op("act", lambda a, h0=h0, hn=hn, ptv=ptv, dstT=dstT: a.copy(
                                out=dstT[:, h0:h0 + hn, j * 128:(j + 1) * 128], in_=ptv[0:DQ, 0:hn, :]), r=[ps], w=[dstT])
                for hp in range(8):
                    ps = self.psg.next()
                    for kk in range(8):
                        k.op("pe", lambda pe, kk=kk, hp=hp, ps=ps: pe.matmul(
                            ps[:, 0:nt], lhsT=win[:, kk, GOFF + hp * 128:GOFF + (hp + 1) * 128], rhs=hT[:, kk, 0:nt],
                            start=(kk == 0), stop=(kk == 7)), r=[hT, win], w=[ps])
                    gs = gsr.next()
                    k.op("act", lambda a, ps=ps, gs=gs: a.activation(out=gs[:, 0:nt], in_=ps[:, 0:nt], func=AF.Silu),
                         r=[ps], w=[gs])
                    k.dma("pool", GS.t[hp, :, t0:t0 + nt], gs[:, 0:nt], r=[gs], w=[GS])
                def make_stores(qT, kT, vb4, t0, nt, ntile):
                    def st():
                        for h0 in range(0, H, 4):
                            k.dma("sp", QT.t[h0:h0 + 4, :, t0:t0 + nt].rearrange("h d t -> d h t"), qT[:, h0:h0 + 4, 0:nt],
                                  r=[qT], w=[QT])
                        for h0 in range(0, HK, 4):
                            k.dma("sp", KT.t[h0:h0 + 4, :, t0:t0 + nt].rearrange("h d t -> d h t"), kT[:, h0:h0 + 4, 0:nt],
                                  r=[kT], w=[KT])
                        kt0 = t0 // 128
                        for j in range(ntile):
                            for h0 in range(0, HK, 4):
                                k.dma("sp", VV.t[h0:h0 + 4, :, kt0 + j, :].rearrange("h p e -> p h e"), vb4[:, j, h0:h0 + 4, :],
                                      r=[vb4], w=[VV], allow_slow_non_contiguous=True)
                    return st
                pending_st[0] = make_stores(qT, kT, vb4, t0, nt, ntile)
            if pending_st[0] is not None:
                pending_st[0]()
                pending_st[0] = None
            k.barrier()

        if self.stop == "p1":
            return
        with ExitStack() as es:
            DQP = 128 if DQ == 64 else DQ
            KTs = Ring([k.sb(es, [DQP, T], BF16) for _ in range(2)])
            Vs = Ring([k.sb(es, [128, NT, 65], BF16) for _ in range(2)])
            Qs = Ring([k.sb(es, [DQP, T], BF16) for _ in range(2)])
            if DQP != DQ:
                for b_ in KTs.items + Qs.items:
                    k.op("pool", lambda g, b_=b_: g.memset(b_[DQ:DQP, :], 0.0), w=[b_])
            Gs = Ring([k.sb(es, [64, T], BF16) for _ in range(2)])
            Ps = Ring([k.sb(es, [128, 512], BF16) for _ in range(6)])
            rr = k.sb(es, [65, 512], F32)
            tmp = k.sb(es, [64, 512], F32)
            ogr = Ring([k.sb(es, [64, 512], BF16) for _ in range(2)])
            pss = Ring(self.psb[0:5])
            psm = Ring(self.psb[5:6])
            import os
            pending_epi = [None]
            for hk in range(int(os.environ.get('DBG_P2_HEADS', HK))):
                Kt = KTs.next()
                Vt = Vs.next()
                k.dma("sp", Kt[0:DQ, :], KT.t[hk], r=[KT], w=[Kt])
                k.dma("sp", Vt[:, :, :], VV.t[hk], r=[VV], w=[Vt])
                for hr in range(REP):
                    h = hk * REP + hr
                    Qt = Qs.next()
                    Gt = Gs.next()
                    k.dma("sp", Qt[0:DQ, :], QT.t[h], r=[QT], w=[Qt])
                    k.dma("sp", Gt[:, :], GS.t[h // 2, (h % 2) * 64:(h % 2) * 64 + 64, :], r=[GS], w=[Gt])
                    for bi, (t0, nt, cond) in enumerate(BLOCKS):
                        nkt = 2 if cond == 0 else NT
                        po = self.pso.next()
                        pend = []

                        def s_mm(kt):
                            ps = pss.next()
                            k.op("pe", lambda pe: pe.matmul(ps[:, 0:nt], lhsT=Kt[:, kt * 128:(kt + 1) * 128], rhs=Qt[:, t0:t0 + nt],
                                                            start=True, stop=True), r=[Kt, Qt], w=[ps])
                            pt = Ps.next()
                            k.op("act", lambda a: a.activation(out=pt[:, 0:nt], in_=ps[:, 0:nt], func=AF.Exp, scale=scale),
                                 r=[ps], w=[pt])
                            return pt

                        def pv_mm(kt, pt):
                            k.op("pe", lambda pe: pe.matmul(po[0:65, 0:nt], lhsT=Vt[:, kt, :], rhs=pt[:, 0:nt],
                                                            start=(kt == 0), stop=(kt == nkt - 1)), r=[Vt, pt], w=[po])

                        SK = 3
                        for kt in range(nkt + SK):
                            if kt < nkt:
                                pend.append((kt, s_mm(kt)))
                            if kt >= SK:
                                a_, b_ = pend.pop(0)
                                pv_mm(a_, b_)
                            if kt == 10 and pending_epi[0] is not None:
                                pending_epi[0]()
                                pending_epi[0] = None
                        if pending_epi[0] is not None:
                            pending_epi[0]()
                            pending_epi[0] = None

                        def make_epi(po, Gt, t0, nt, bi, h):
                            def epi():
                                k.op("dve", lambda v: v.reciprocal(out=rr[64:65, 0:nt], in_=po[64:65, 0:nt]), r=[po], w=[rr])
                                pm = psm.next()
                                k.op("pe", lambda pe: pe.matmul(pm[0:64, 0:nt], lhsT=self.ones_f[64:65, 0:64], rhs=rr[64:65, 0:nt],
                                                                start=True, stop=True), r=[rr, self.ones_f], w=[pm])
                                k.op("dve", lambda v: v.tensor_tensor(out=tmp[:, 0:nt], in0=pm[0:64, 0:nt], in1=Gt[:, t0:t0 + nt],
                                                                      op=ALU.mult), r=[pm, Gt], w=[tmp])
                                og = ogr.next()
                                k.op("dve", lambda v: v.tensor_tensor(out=og[:, 0:nt], in0=po[0:64, 0:nt], in1=tmp[:, 0:nt], op=ALU.mult),
                                     r=[po, tmp], w=[og])
                                k.dma("pool", OG.t[bi, :, h, 0:nt], og[:, 0:nt], r=[og], w=[OG])
                            return epi
                        pending_epi[0] = make_epi(po, Gt, t0, nt, bi, h)
            if pending_epi[0] is not None:
                pending_epi[0]()
                pending_epi[0] = None
            k.barrier()

        if self.stop == "p2":
            return
        with ExitStack() as es:
            stg = Ring([k.sb(es, [128, 1024], F32) for _ in range(2)])
            wo = k.sb(es, [128, 8, D], BF16)
            self.load_w(es, wo, wo, w_out, 8, 128, D, stg)
            ogb = Ring([k.sb(es, [128, 8, 512], BF16) for _ in range(2)])
            xr = Ring([k.sb(es, [128, D], F32) for _ in range(3)])
            tm = k.sb(es, [128, 512], F32)
            self.outproj(es, lambda bi, nt: (OG.t[bi, :, :, 0:nt], OG), ogb, 128, 8, wo, xsrc, xr, tm)
            k.barrier()


    def layer_gla(self, L, xsrc):
        k = self.k
        w_in = self.W["gla_w_in"].t[0]
        w_out = self.W["gla_w_out"].t[0]
        REC = k.dram([NT, 128, 3584], BF16, f"GREC{L}")
        GG = k.dram([NT, 128, 1024], F32, f"GGG{L}")
        GOF = k.dram([NT, 128, 1024], F32, f"GOF{L}")
        recT = [TT(REC.t) for _ in range(NT)]
        ggT = [TT(GG.t) for _ in range(NT)]
        ofT = [TT(GOF.t) for _ in range(NT)]
        ps8 = self.ps8
        with ExitStack() as es:
            self.epsb = k.sb(es, [128, 1], F32)
            k.op("pool", lambda g: g.memset(self.epsb[:, :], EPS), w=[self.epsb])
            onec = k.sb(es, [128, 1], F32)
            k.op("pool", lambda g: g.memset(onec[:, :], 1.0), w=[onec])
            stg = Ring([k.sb(es, [128, 1024], F32) for _ in range(2)])
            win = k.sb(es, [128, 8, 3104], BF16)
            self.load_w(es, win, win, w_in, 8, 128, 3104, stg)
            wg = k.sb(es, [16, 2, 512], F32)
            k.dma("sp", wg[:, 0, :], self.W["gla_w_gf"].t[0], w=[wg])
            k.dma("sp", wg[:, 1, :], self.W["gla_w_gb"].t[0], w=[wg])
            bg = k.sb(es, [128, 2, 512], F32)
            k.dma("sp", bg[:, 0, :], self.W["gla_b_gf"].t[0, :].partition_broadcast(128), w=[bg])
            k.dma("sp", bg[:, 1, :], self.W["gla_b_gb"].t[0, :].partition_broadcast(128), w=[bg])
            xr = Ring([k.sb(es, [128, D], F32) for _ in range(4)])
            hTr = Ring([k.sb(es, [128, 8, 512], BF16) for _ in range(2)])
            scr = (k.sb(es, [128, D], BF16), k.sb(es, [128, 1], F32), k.sb(es, [128, 1], F32),
                   k.sb(es, [128, D], F32), k.sb(es, [128, D], BF16))
            recr = Ring([k.sb(es, [128, 3584], BF16) for _ in range(2)])
            ggr = Ring([k.sb(es, [128, 1024], F32) for _ in range(2)])
            rT = k.sb(es, [16, 2, 128], F32)
            zt = k.sb(es, [128, 512], F32)
            for bi, (t0, nt, cond) in enumerate(BLOCKS):
                hT = hTr.next()
                self.norm_block(es, bi, xsrc, xr, hT, scr)
                for j in range(nt // 128):
                    t = (t0 + j * 128) // 128
                    rec = recr.next()
                    gg = ggr.next()
                    hTj = lambda kk: hT[:, kk, j * 128:(j + 1) * 128]

                    def tokmm(c0, n):
                        ps = ps8.next()
                        for kk in range(8):
                            k.op("pe", lambda pe: pe.matmul(ps[:, 0:n], lhsT=hTj(kk), rhs=win[:, kk, c0:c0 + n],
                                                            start=(kk == 0), stop=(kk == 7)), r=[hT, win], w=[ps])
                        return ps

                    ps = tokmm(512, 512)
                    k.op("act", lambda a: a.copy(out=rec[:, 1024:1536], in_=ps[:, :]), r=[ps], w=[rec])
                    for cb in range(2):
                        ps = tokmm(1024 + cb * 512, 512)
                        k.op("dve", lambda v: v.tensor_copy(out=rec[:, 1536 + cb * 512:2048 + cb * 512], in_=ps[:, :]),
                             r=[ps], w=[rec])
                    for cb in range(2):
                        ps = tokmm(2048 + cb * 512, 512)
                        k.op("act", lambda a: a.activation(out=rec[:, 2560 + cb * 512:3072 + cb * 512], in_=ps[:, :],
                                                           func=AF.Silu), r=[ps], w=[rec])
                    for qk in range(2):
                        ps = ps8.next()
                        for h in range(4):
                            for kk in range(8):
                                k.op("pe", lambda pe: pe.matmul(
                                    ps[:, h * 128:(h + 1) * 128], lhsT=win[:, kk, qk * 512 + h * 128:qk * 512 + (h + 1) * 128],
                                    rhs=hTj(kk), start=(kk == 0), stop=(kk == 7)), r=[hT, win], w=[ps])
                        if qk == 0:
                            k.op("act", lambda a: a.mul(out=rec[:, 0:512], in_=ps[:, :], mul=128 ** -0.5), r=[ps], w=[rec])
                        else:
                            k.op("dve", lambda v: v.tensor_copy(out=rec[:, 512:1024], in_=ps[:, :]), r=[ps], w=[rec])
                    ps = ps8.next()
                    for d in range(2):
                        for kk in range(8):
                            k.op("pe", lambda pe: pe.matmul(
                                ps[0:16, d * 128:(d + 1) * 128], lhsT=win[:, kk, 3072 + 16 * d:3088 + 16 * d],
                                rhs=hTj(kk), start=(kk == 0), stop=(kk == 7)), r=[hT, win], w=[ps])
                    k.op("act", lambda a: a.copy(out=rT[:, :, :], in_=ps[0:16, 0:256].rearrange("p (d t) -> p d t", d=2)),
                         r=[ps], w=[rT])
                    for d in range(2):
                        ps = ps8.next()
                        k.op("pe", lambda pe: pe.matmul(ps[:, :], lhsT=rT[:, d, :], rhs=wg[:, d, :], start=True, stop=True),
                             r=[rT, wg], w=[ps])
                        k.op("dve", lambda v: v.tensor_tensor(out=zt[:, :], in0=ps[:, :], in1=bg[:, d, :], op=ALU.add),
                             r=[ps, bg], w=[zt])
                        k.op("act", lambda a: a.activation(out=zt[:, :], in_=zt[:, :], func=AF.Exp, scale=-1.0), r=[zt], w=[zt])
                        k.op("act", lambda a: a.activation(out=zt[:, :], in_=zt[:, :], func=AF.Ln, bias=onec[:, :]),
                             r=[zt, onec], w=[zt])
                        k.op("dve", lambda v: v.tensor_scalar(out=gg[:, d * 512:(d + 1) * 512], in0=zt[:, :],
                                                              scalar1=-1.0 / 16.0, scalar2=None, op0=ALU.mult),
                             r=[zt], w=[gg])
                    k.dma("pool", REC.t[t], rec[:, :], r=[rec], w=[recT[t]])
                    k.dma("pool", GG.t[t], gg[:, :], r=[gg], w=[ggT[t]])
            k.barrier()

        with ExitStack() as es:
            self.epsb = k.sb(es, [128, 1], F32)
            k.op("pool", lambda g: g.memset(self.epsb[:, :], EPS), w=[self.epsb])
            tri = k.sb(es, [128, 4, 128], F32)
            k.dma("sp", tri[:, :, :], self.tri_d.t.rearrange("m j i -> j m i"), w=[tri])
            S = [k.sb(es, [128, 256], F32) for _ in range(4)]
            Sb = [k.sb(es, [128, 256], BF16) for _ in range(4)]
            recr = Ring([k.sb(es, [128, 3584], BF16) for _ in range(3)])
            ggr = Ring([k.sb(es, [128, 1024], F32) for _ in range(3)])
            E1r = Ring([k.sb(es, [128, 512], F32) for _ in range(2)])
            E2r = Ring([k.sb(es, [128, 512], F32) for _ in range(2)])
            E3r = Ring([k.sb(es, [128, 512], F32) for _ in range(2)])
            qtr = Ring([k.sb(es, [128, 512], BF16) for _ in range(2)])
            ktr = Ring([k.sb(es, [128, 512], BF16) for _ in range(2)])
            khr = Ring([k.sb(es, [128, 512], BF16) for _ in range(2)])
            Amr = Ring([k.sb(es, [128, 512], BF16) for _ in range(2)])
            osb = Ring([k.sb(es, [128, 1024], F32) for _ in range(2)])
            stg = Ring([k.sb(es, [128, 1024], F32) for _ in range(2)])
            wo = k.sb(es, [128, 8, D], BF16)
            self.load_w(es, wo, wo, w_out, 8, 128, D, stg)
            onbc = k.sb(es, [128, 256], F32)
            k.dma("sp", onbc[:, :], self.W["gla_o_norm"].t[0, :].partition_broadcast(128), w=[onbc])
            ofr = Ring([k.sb(es, [128, 1024], F32) for _ in range(2)])
            xr = Ring([k.sb(es, [128, D], F32) for _ in range(2)])
            ogf = k.sb(es, [128, 1024], F32)
            ogb = k.sb(es, [128, 1024], BF16)
            ogT = k.sb(es, [128, 8, 128], BF16)
            junk = k.sb(es, [128, 256], BF16)
            ss4 = k.sb(es, [128, 4], F32)
            rs4 = k.sb(es, [128, 4], F32)
            tm = k.sb(es, [128, 512], F32)

            def stageA(d, rec, gg):
                cm = tri[:, 0 if d == 0 else 2, :]
                sm = tri[:, 1 if d == 0 else 3, :]
                g = lambda a_, b_: gg[:, d * 512 + a_:d * 512 + b_]
                E1, E2, E3 = E1r.next(), E2r.next(), E3r.next()
                qt, kt_, kh, Am = qtr.next(), ktr.next(), khr.next(), Amr.next()
                psA = ps8.next()
                for h in range(4):
                    k.op("pe", lambda pe: pe.matmul(psA[:, h * 128:(h + 1) * 128], lhsT=g(h * 128, (h + 1) * 128), rhs=cm,
                                                    start=True, stop=True), r=[gg, tri], w=[psA])
                psB = ps8.next()
                k.op("pe", lambda pe: pe.matmul(psB[:, :], lhsT=sm, rhs=g(0, 512), start=True, stop=True), r=[gg, tri], w=[psB])
                k.op("act", lambda a: a.activation(out=E1[:, :], in_=psA[:, :], func=AF.Exp), r=[psA], w=[E1])
                k.op("act", lambda a: a.activation(out=E2[:, :], in_=psA[:, :], func=AF.Exp, scale=-1.0), r=[psA], w=[E2])
                k.op("act", lambda a: a.activation(out=E3[:, :], in_=psB[:, :], func=AF.Exp), r=[psB], w=[E3])
                k.op("dve", lambda v: v.tensor_tensor(out=qt[:, :], in0=rec[:, 0:512], in1=E1[:, :], op=ALU.mult), r=[rec, E1], w=[qt])
                k.op("pool", lambda v: v.tensor_tensor(out=kt_[:, :], in0=rec[:, 512:1024], in1=E2[:, :], op=ALU.mult), r=[rec, E2], w=[kt_])
                k.op("pool", lambda v: v.tensor_tensor(out=kh[:, :], in0=rec[:, 1024:1536], in1=E3[:, :], op=ALU.mult), r=[rec, E3], w=[kh])
                return dict(d=d, rec=rec, E1=E1, qt=qt, kh=kh, Am=Am, kt_=kt_, cm=cm)

            def stageA2(c):
                qt, kt_, Am, cm = c["qt"], c["kt_"], c["Am"], c["cm"]
                psD = ps8.next()
                for h in range(4):
                    hs = slice(h * 128, (h + 1) * 128)
                    k.op("pe", lambda pe: pe.matmul(psD[:, hs], lhsT=kt_[:, hs], rhs=qt[:, hs], start=True, stop=True),
                         r=[kt_, qt], w=[psD])
                k.op("dve", lambda v: v.tensor_tensor(out=Am[:, :].rearrange("p (h i) -> p h i", h=4),
                                                      in0=psD[:, :].rearrange("p (h i) -> p h i", h=4),
                                                      in1=cm.unsqueeze(1).broadcast_to([128, 4, 128]), op=ALU.mult),
                     r=[psD, tri], w=[Am])

            def stageB(c):
                d, rec, E1, qt, kh, Am = (c[n] for n in ("d", "rec", "E1", "qt", "kh", "Am"))
                ecol = 127 if d == 0 else 0
                po = [ps8.next(), ps8.next()]
                for h in range(4):
                    hs = slice(h * 128, (h + 1) * 128)
                    bank = po[h // 2]
                    cs = slice((h % 2) * 256, (h % 2) * 256 + 256)
                    vs = slice(1536 + h * 256, 1536 + (h + 1) * 256)
                    k.op("pe", lambda pe: pe.matmul(bank[:, cs], lhsT=qt[:, hs], rhs=Sb[h][:, :], start=True, stop=False),
                         r=[qt, Sb[h]], w=[bank])
                    k.op("pe", lambda pe: pe.matmul(bank[:, cs], lhsT=Am[:, hs], rhs=rec[:, vs], start=False, stop=True),
                         r=[Am, rec], w=[bank])
                pss = [ps8.next(), ps8.next()]
                for h in range(4):
                    hs = slice(h * 128, (h + 1) * 128)
                    bank = pss[h // 2]
                    cs = slice((h % 2) * 256, (h % 2) * 256 + 256)
                    vs = slice(1536 + h * 256, 1536 + (h + 1) * 256)
                    k.op("pe", lambda pe: pe.matmul(bank[:, cs], lhsT=kh[:, hs], rhs=rec[:, vs], start=True, stop=True),
                         r=[kh, rec], w=[bank])
                    k.op("dve", lambda v: v.scalar_tensor_tensor(out=S[h][:, :], in0=S[h][:, :],
                                                                 scalar=E1[:, h * 128 + ecol:h * 128 + ecol + 1],
                                                                 in1=bank[:, cs], op0=ALU.mult, op1=ALU.add),
                         r=[S[h], E1, bank], w=[S[h]])
                    k.op("act", lambda gp: gp.copy(out=Sb[h][:, :], in_=S[h][:, :]), r=[S[h]], w=[Sb[h]])
                return po

            def reset_state():
                for h in range(4):
                    k.op("pool", lambda gp: gp.memset(S[h][:, :], 0.0), w=[S[h]])
                    k.op("pool", lambda gp: gp.memset(Sb[h][:, :], 0.0), w=[Sb[h]])

            def gl_loads(t):
                rec = recr.next()
                gg = ggr.next()
                k.dma("sp", rec[:, :], REC.t[t], r=[recT[t]], w=[rec])
                k.dma("sp", gg[:, :], GG.t[t], r=[ggT[t]], w=[gg])
                return rec, gg

            reset_state()
            cnext = stageA(0, *gl_loads(0))
            stageA2(cnext)
            for t in range(NT):
                c = cnext
                if t + 1 < NT:
                    cnext = stageA(0, *gl_loads(t + 1))
                po = stageB(c)
                if t + 1 < NT:
                    stageA2(cnext)
                ob = osb.next()
                for c in range(2):
                    k.op("act", lambda a: a.copy(out=ob[:, c * 512:(c + 1) * 512], in_=po[c][:, :]), r=[po[c]], w=[ob])
                k.dma("pool", GOF.t[t], ob[:, :], r=[ob], w=[ofT[t]])
            reset_state()
            order = [1, 0] + list(range(NT - 1, 1, -1))
            cnext = stageA(1, *gl_loads(order[0]))
            stageA2(cnext)
            osr = Ring([k.sb(es, [128, 1024], F32) for _ in range(2)])

            def gl_out(t, rec, osum):
                cond = 0 if t < 2 else 1
                bi = 0 if t < 2 else 1 + (t - 2) // 4
                xt = xr.next()
                sap, rr = self.xsrc_ap(xsrc, t * 128, 128)
                k.dma("sp", xt[:, :], sap, r=(rr or [self.xblk[bi]]), w=[xt])
                for h in range(4):
                    k.op("act", lambda a: a.activation(out=junk[:, :], in_=osum[:, h * 256:(h + 1) * 256], func=AF.Square,
                                                       accum_out=ss4[:, h:h + 1]), r=[osum], w=[junk, ss4])
                k.op("act", lambda a: a.activation(out=rs4[:, :], in_=ss4[:, :], func=AF.Sqrt, scale=1.0 / 256, bias=self.epsb[:, :]),
                     r=[ss4, self.epsb], w=[rs4])
                k.op("dve", lambda v: v.reciprocal(out=rs4[:, :], in_=rs4[:, :]), r=[rs4], w=[rs4])
                for h in range(4):
                    k.op("dve", lambda v: v.scalar_tensor_tensor(out=ogf[:, h * 256:(h + 1) * 256], in0=osum[:, h * 256:(h + 1) * 256],
                                                                 scalar=rs4[:, h:h + 1], in1=onbc[:, :], op0=ALU.mult, op1=ALU.mult),
                         r=[osum, rs4, onbc], w=[ogf])
                k.op("dve", lambda v: v.tensor_tensor(out=ogb[:, :], in0=ogf[:, :], in1=rec[:, 2560:3584], op=ALU.mult),
                     r=[ogf, rec], w=[ogb])
                self.tok_outproj(ogb, 8, ogT, wo, xt, tm, cond)
                k.dma("pool", self.xres.t[t * 128:(t + 1) * 128, :], xt[:, :], r=[xt], w=[self.xblk[bi]])

            pending = None
            for oi, t in enumerate(order):
                c = cnext
                rec = c["rec"]
                if oi + 1 < NT:
                    cnext = stageA(1, *gl_loads(order[oi + 1]))
                of = ofr.next()
                k.dma("sp", of[:, :], GOF.t[t], r=[ofT[t]], w=[of])
                po = stageB(c)
                if oi + 1 < NT:
                    stageA2(cnext)
                osum = osr.next()
                for cc in range(2):
                    k.op("dve", lambda v: v.tensor_tensor(out=osum[:, cc * 512:(cc + 1) * 512], in0=po[cc][:, :],
                                                          in1=of[:, cc * 512:(cc + 1) * 512], op=ALU.add),
                         r=[po[cc], of], w=[osum])
                if pending is not None:
                    gl_out(*pending)
                pending = (t, rec, osum)
            gl_out(*pending)
            k.barrier()


    def norm_rows(self, src, rtrk, n, cond, dst, scr, xt):
        k = self.k
        junk, ss, rs, hf, hb = scr
        k.dma("sp", xt[0:n, :], src, r=rtrk, w=[xt])
        k.op("act", lambda a: a.activation(out=junk[0:n, :], in_=xt[0:n, :], func=AF.Square, accum_out=ss[0:n, :]),
             r=[xt], w=[junk, ss])
        k.op("act", lambda a: a.activation(out=rs[0:n, :], in_=ss[0:n, :], func=AF.Sqrt, scale=1.0 / D, bias=self.epsb[0:n, :]),
             r=[ss, self.epsb], w=[rs])
        k.op("dve", lambda v: v.reciprocal(out=rs[0:n, :], in_=rs[0:n, :]), r=[rs], w=[rs])
        k.op("dve", lambda v: v.scalar_tensor_tensor(out=hf[0:n, :], in0=xt[0:n, :], scalar=rs[0:n, 0:1],
                                                     in1=self.gmod[cond][0:n, :], op0=ALU.mult, op1=ALU.mult),
             r=[xt, rs, self.gmod[cond]], w=[hf])
        k.op("dve", lambda v: v.tensor_tensor(out=hb[0:n, :], in0=hf[0:n, :], in1=self.bcm[cond][0:n, 0:D], op=ALU.add),
             r=[hf, self.bcm[cond]], w=[hb])
        ps = self.ps8.next()
        pv = ps[:, :].bitcast(BF16).rearrange("p (c t) -> p c t", c=8)
        for c in range(8):
            k.op("pe", lambda pe: pe.transpose(out=pv[:, c, 0:n], in_=hb[0:n, c * 128:(c + 1) * 128],
                                               identity=self.identb[0:n, 0:n]), r=[hb, self.identb], w=[ps])
        return ps, pv

    def layer_ssd(self, L, xsrc):
        k = self.k
        ps8 = self.ps8
        w_in = self.W["ssd_w_in"].t[0]
        w_out = self.W["ssd_w_out"].t[0]
        RX = k.dram([NT, 128, 2560], BF16, f"SRX{L}")
        RZ = k.dram([NT, 128, 2048], BF16, f"SRZ{L}")
        RBC = k.dram([NT, 128, 1024], BF16, f"SRBC{L}")
        DD = k.dram([NT, 128, 128], F32, f"SDD{L}")
        YF = k.dram([NT, 128, 2048], F32, f"SYF{L}")
        rxT = [TT(RX.t) for _ in range(NT)]
        rzT = [TT(RZ.t) for _ in range(NT)]
        rbcT = [TT(RBC.t) for _ in range(NB)]
        ddT = [TT(DD.t) for _ in range(NT)]
        yfT = [TT(YF.t) for _ in range(NT)]
        with ExitStack() as es:
            self.epsb = k.sb(es, [128, 1], F32)
            k.op("pool", lambda g: g.memset(self.epsb[:, :], EPS), w=[self.epsb])
            onec = k.sb(es, [128, 1], F32)
            k.op("pool", lambda g: g.memset(onec[:, :], 1.0), w=[onec])
            wz = k.sb(es, [128, 8, 2048], BF16)
            wx = k.sb(es, [128, 8, 3072], BF16)
            wdt = k.sb(es, [128, 8, 64], BF16)
            with ExitStack() as es2:
                stg = Ring([k.sb(es2, [128, 1024], F32) for _ in range(2)])
                self.load_w(es2, wz, wz, w_in[:, 0:2048], 8, 128, 2048, stg)
                self.load_w(es2, wx, wx, w_in[:, 2048:5120], 8, 128, 3072, stg)
                self.load_w(es2, wdt, wdt, w_in[:, 5120:5184], 8, 128, 64, stg)
                k.barrier()
            cw = k.sb(es, [128, 24, 5], F32)
            for kk in range(5):
                k.dma("sp", cw[:, :, kk], self.W["ssd_conv_w"].t[0, kk, :].rearrange("(c p) -> p c", p=128), w=[cw],
                      allow_slow_non_contiguous=True)
            cbias = k.sb(es, [128, 24], F32)
            k.dma("sp", cbias[:, :], self.W["ssd_conv_b"].t[0, :].rearrange("(c p) -> p c", p=128), w=[cbias],
                  allow_slow_non_contiguous=True)
            dtb = k.sb(es, [128, 64], F32)
            k.dma("sp", dtb[:, 0:32], self.W["ssd_dt_bias_f"].t[0, :].partition_broadcast(128), w=[dtb])
            k.dma("sp", dtb[:, 32:64], self.W["ssd_dt_bias_b"].t[0, :].partition_broadcast(128), w=[dtb])
            abc = k.sb(es, [128, 64], F32)
            k.dma("sp", abc[:, 0:32], self.W["ssd_a_log_f"].t[0, :].partition_broadcast(128), w=[abc])
            k.dma("sp", abc[:, 32:64], self.W["ssd_a_log_b"].t[0, :].partition_broadcast(128), w=[abc])
            k.op("act", lambda a: a.activation(out=abc[:, :], in_=abc[:, :], func=AF.Exp), r=[abc], w=[abc])
            k.op("dve", lambda v: v.tensor_scalar(out=abc[:, :], in0=abc[:, :], scalar1=-1.0, scalar2=None, op0=ALU.mult),
                 r=[abc], w=[abc])
            xr = Ring([k.sb(es, [128, D], F32) for _ in range(4)])
            hT = k.sb(es, [128, 8, 516], BF16)
            scr = (k.sb(es, [128, D], BF16), k.sb(es, [128, 1], F32), k.sb(es, [128, 1], F32),
                   k.sb(es, [128, D], F32), k.sb(es, [128, D], BF16))
            prer = Ring([k.sb(es, [128, 516], BF16) for _ in range(3)])
            accr = Ring([k.sb(es, [128, 512], F32) for _ in range(2)])
            xc = k.sb(es, [128, 24, 512], BF16)
            rxr = Ring([k.sb(es, [128, 2560], BF16) for _ in range(1)])
            rzr = Ring([k.sb(es, [128, 2048], BF16) for _ in range(1)])
            ddr = Ring([k.sb(es, [128, 128], F32) for _ in range(2)])
            dtt = k.sb(es, [128, 64], F32)
            for bi, (t0, nt, cond) in enumerate(BLOCKS):
                ntile = nt // 128
                self.norm_block(es, bi, xsrc, xr, hT, scr)
                for side in range(2):
                    col = 512 + 2 * side
                    has = (bi >= 2) if side == 0 else (1 <= bi < NB - 1)
                    if not has:
                        k.op("pool", lambda g: g.memset(hT[:, :, col:col + 2], 0.0), w=[hT])
                    else:
                        r0 = t0 - 2 if side == 0 else t0 + nt
                        nb_ = bi - 1 if side == 0 else bi + 1
                        src, rr = self.xsrc_ap(xsrc, r0, 2)
                        ps, pv = self.norm_rows(src, rr or [self.xblk[nb_]], 2, cond, None, scr, xr.next())
                        k.op("act", lambda a: a.copy(out=hT[:, :, col:col + 2], in_=pv[:, :, 0:2]), r=[ps], w=[hT])
                for c in range(24):
                    ps = ps8.next()
                    for kk in range(8):
                        k.op("pe", lambda pe: pe.matmul(ps[:, 0:nt], lhsT=wx[:, kk, c * 128:(c + 1) * 128], rhs=hT[:, kk, 0:nt],
                                                        start=(kk == 0), stop=(kk == 7)), r=[wx, hT], w=[ps])
                    ps2 = ps8.next()
                    for kk in range(8):
                        k.op("pe", lambda pe: pe.matmul(ps2[:, 0:4], lhsT=wx[:, kk, c * 128:(c + 1) * 128], rhs=hT[:, kk, 512:516],
                                                        start=(kk == 0), stop=(kk == 7)), r=[wx, hT], w=[ps2])
                    pre = prer.next()
                    k.op("act", lambda a: a.copy(out=pre[:, 2:2 + nt], in_=ps[:, 0:nt]), r=[ps], w=[pre])
                    k.op("act", lambda a: a.copy(out=pre[:, 0:2], in_=ps2[:, 0:2]), r=[ps2], w=[pre])
                    k.op("act", lambda a: a.copy(out=pre[:, 2 + nt:4 + nt], in_=ps2[:, 2:4]), r=[ps2], w=[pre])
                    acc = accr.next()
                    k.op("dve", lambda v: v.tensor_scalar(out=acc[:, 0:nt], in0=pre[:, 0:nt], scalar1=cw[:, c, 0:1], scalar2=None,
                                                          op0=ALU.mult), r=[pre, cw], w=[acc])
                    for kk in range(1, 5):
                        k.op("dve", lambda v: v.scalar_tensor_tensor(out=acc[:, 0:nt], in0=pre[:, kk:kk + nt], scalar=cw[:, c, kk:kk + 1],
                                                                     in1=acc[:, 0:nt], op0=ALU.mult, op1=ALU.add),
                             r=[pre, cw, acc], w=[acc])
                    k.op("act", lambda a: a.activation(out=xc[:, c, 0:nt], in_=acc[:, 0:nt], func=AF.Silu, bias=cbias[:, c:c + 1]),
                         r=[acc, cbias], w=[xc])
                kt0 = t0 // 128
                for j in range(ntile):
                    for s_ in range(2):
                        k.dma("sp", RBC.t[kt0 + j, :, s_ * 512:(s_ + 1) * 512].rearrange("p (g t) -> p g t", g=4),
                              xc[:, 16 + 4 * s_:20 + 4 * s_, j * 128:(j + 1) * 128], r=[xc], w=[rbcT[bi]])
                for j in range(ntile):
                    t = kt0 + j
                    rx = rxr.next()
                    for c0 in (0, 8, 16):
                        cn = 8 if c0 < 16 else 4
                        ps = ps8.next()
                        pv = ps[:, :].bitcast(BF16).rearrange("p (c t) -> p c t", c=8)
                        for c in range(cn):
                            k.op("pe", lambda pe: pe.transpose(out=pv[:, c, :], in_=xc[:, c0 + c, j * 128:(j + 1) * 128],
                                                               identity=self.identb[:, :]), r=[xc, self.identb], w=[ps])
                        k.op("act" if c0 != 8 else "dve",
                             (lambda a: a.copy(out=rx[:, c0 * 128:(c0 + cn) * 128].rearrange("p (c t) -> p c t", c=cn), in_=pv[:, 0:cn, :]))
                             if c0 != 8 else
                             (lambda v: v.tensor_copy(out=rx[:, c0 * 128:(c0 + cn) * 128].rearrange("p (c t) -> p c t", c=cn), in_=pv[:, 0:cn, :])),
                             r=[ps], w=[rx])
                    k.dma("pool", RX.t[t], rx[:, :], r=[rx], w=[rxT[t]])
                    rz = rzr.next()
                    for cb in range(4):
                        ps = ps8.next()
                        for kk in range(8):
                            k.op("pe", lambda pe: pe.matmul(ps[:, :], lhsT=hT[:, kk, j * 128:(j + 1) * 128],
                                                            rhs=wz[:, kk, cb * 512:(cb + 1) * 512],
                                                            start=(kk == 0), stop=(kk == 7)), r=[hT, wz], w=[ps])
                        k.op("act", lambda a: a.activation(out=rz[:, cb * 512:(cb + 1) * 512], in_=ps[:, :], func=AF.Silu),
                             r=[ps], w=[rz])
                    k.dma("pool", RZ.t[t], rz[:, :], r=[rz], w=[rzT[t]])
                    dd = ddr.next()
                    ps = ps8.next()
                    for kk in range(8):
                        k.op("pe", lambda pe: pe.matmul(ps[:, 0:64], lhsT=hT[:, kk, j * 128:(j + 1) * 128], rhs=wdt[:, kk, :],
                                                        start=(kk == 0), stop=(kk == 7)), r=[hT, wdt], w=[ps])
                    k.op("dve", lambda v: v.tensor_tensor(out=dtt[:, :], in0=ps[:, 0:64], in1=dtb[:, :], op=ALU.add),
                         r=[ps, dtb], w=[dtt])
                    k.op("act", lambda a: a.activation(out=dtt[:, :], in_=dtt[:, :], func=AF.Exp), r=[dtt], w=[dtt])
                    k.op("act", lambda a: a.activation(out=dd[:, 0:64], in_=dtt[:, :], func=AF.Ln, bias=onec[:, :]),
                         r=[dtt, onec], w=[dd])
                    k.op("dve", lambda v: v.tensor_tensor(out=dd[:, 64:128], in0=dd[:, 0:64], in1=abc[:, :], op=ALU.mult),
                         r=[dd, abc], w=[dd])
                    k.dma("pool", DD.t[t], dd[:, :], r=[dd], w=[ddT[t]])
            k.barrier()

        with ExitStack() as es:
            self.epsb = k.sb(es, [128, 1], F32)
            k.op("pool", lambda g: g.memset(self.epsb[:, :], EPS), w=[self.epsb])
            tri = k.sb(es, [128, 4, 128], F32)
            k.dma("sp", tri[:, :, :], self.tri_d.t.rearrange("m j i -> j m i"), w=[tri])
            negm = k.sb(es, [128, 2, 128], F32)
            k.dma("sp", negm[:, :, :], self.negm_d.t.rearrange("m j i -> j m i"), w=[negm])
            onehot = k.sb(es, [96, 32, 128], BF16)
            k.dma("sp", onehot[:, :, :], self.onehot_d.t, w=[onehot])
            negmb = k.sb(es, [128, 2, 128], BF16)
            k.dma("sp", negmb[:, :, :], self.negmb_d.t.rearrange("m j i -> j m i"), w=[negmb])
            identf = k.sb(es, [128, 128], F32)
            k.dma("sp", identf[:, :], self.identf_d.t, w=[identf])
            Hs = [k.sb(es, [128, 512], F32) for _ in range(4)]
            Hb = [k.sb(es, [128, 512], BF16) for _ in range(4)]
            rxr = Ring([k.sb(es, [128, 2560], BF16) for _ in range(3)])
            rbcr = Ring([k.sb(es, [128, 1024], BF16) for _ in range(2)])
            ddr = Ring([k.sb(es, [128, 128], F32) for _ in range(3)])
            bankY = Ring(self.psb[0:2])
            bankLb = Ring(self.psb[2:5])
            bankM = Ring(self.psb[5:8])
            def ring2(shape, dt):
                return Ring([k.sb(es, shape, dt) for _ in range(2)])
            ela_r = ring2([128, 32], F32)
            edl_r = ring2([128, 32], F32)
            etot_r = ring2([128, 32], F32)
            laT_r = ring2([96, 128], BF16)
            larep_r = ring2([128, 96], F32)
            Abf_r = ring2([96, 128], BF16)
            Bbf_r = ring2([96, 128], BF16)
            R1_r = ring2([96, 128], F32)
            nla_r = ring2([128, 32], F32)
            xdt_r = ring2([128, 2048], BF16)
            xdl_r = ring2([128, 2048], BF16)
            CBm_r = ring2([128, 4, 128], BF16)
            Eh_r = Ring([k.sb(es, [128, 4, 128], BF16) for _ in range(3)])
            Mh = Ring([k.sb(es, [128, 4, 128], BF16) for _ in range(3)])
            ysr = Ring([k.sb(es, [128, 2048], F32) for _ in range(3)])
            wo = k.sb(es, [128, 16, D], BF16)
            with ExitStack() as es2:
                stg = Ring([k.sb(es2, [128, 1024], F32) for _ in range(2)])
                self.load_w(es2, wo, wo, w_out, 16, 128, D, stg)
                k.barrier()
            gnbc = k.sb(es, [128, 2048], F32)
            k.dma("sp", gnbc[:, :], self.W["ssd_norm"].t[0, :].partition_broadcast(128), w=[gnbc])
            dsk = k.sb(es, [128, 32], F32)
            k.dma("sp", dsk[:, :], self.W["ssd_d"].t[0, :].partition_broadcast(128), w=[dsk])
            rzr = Ring([k.sb(es, [128, 2048], BF16) for _ in range(1)])
            yfr = Ring([k.sb(es, [128, 2048], F32) for _ in range(1)])
            xr = Ring([k.sb(es, [128, D], F32) for _ in range(2)])
            ynb = k.sb(es, [128, 2048], BF16)
            ogT = k.sb(es, [128, 16, 128], BF16)
            junk = k.sb(es, [128, 512], BF16)
            ss4 = k.sb(es, [128, 4], F32)
            rs4 = k.sb(es, [128, 4], F32)
            tm = k.sb(es, [128, 512], F32)

            def stageA(d, rx, rbc, dd):
                c = dict(d=d, rx=rx, rbc=rbc, dd=dd)
                cm = tri[:, 0 if d == 0 else 2, :]
                sm = tri[:, 1 if d == 0 else 3, :]
                dA = dd[:, 64 + 32 * d:96 + 32 * d]
                dt = dd[:, 32 * d:32 * d + 32]
                larep = larep_r.next()
                la_sb = larep
                ela, edl, etot, laT = ela_r.next(), edl_r.next(), etot_r.next(), laT_r.next()
                Abf, Bbf, R1 = Abf_r.next(), Bbf_r.next(), R1_r.next()
                xdt, xdl, CBm = xdt_r.next(), xdl_r.next(), CBm_r.next()
                nla = nla_r.next()
                c.update(la_sb=la_sb, ela=ela, etot=etot, laT=laT, xdt=xdt, xdl=xdl, CBm=CBm, nla=nla)
                psL = bankM.next()
                k.op("pe", lambda pe: pe.matmul(psL[:, 0:32], lhsT=cm, rhs=dA, start=True, stop=True), r=[tri, dd], w=[psL])
                k.op("pe", lambda pe: pe.matmul(psL[:, 32:64], lhsT=sm, rhs=dA, start=True, stop=True), r=[tri, dd], w=[psL])
                k.op("pe", lambda pe: pe.matmul(psL[:, 64:96], lhsT=self.ones_f[:, :], rhs=dA, start=True, stop=True),
                     r=[self.ones_f, dd], w=[psL])
                for rep in range(3):
                    k.op("act", lambda a: a.copy(out=larep[:, rep * 32:(rep + 1) * 32], in_=psL[:, 0:32]), r=[psL], w=[larep])
                k.op("act", lambda a: a.mul(out=nla[:, :], in_=psL[:, 0:32], mul=-1.0), r=[psL], w=[nla])
                k.op("act", lambda a: a.activation(out=ela[:, :], in_=psL[:, 0:32], func=AF.Exp), r=[psL], w=[ela])
                k.op("act", lambda a: a.activation(out=edl[:, :], in_=psL[:, 32:64], func=AF.Exp), r=[psL], w=[edl])
                k.op("act", lambda a: a.activation(out=etot[:, :], in_=psL[:, 64:96], func=AF.Exp), r=[psL], w=[etot])
                xs3 = rx[:, 0:2048].rearrange("p (h e) -> p h e", h=32)
                k.op("pool", lambda g: g.tensor_tensor(out=xdt[:, :].rearrange("p (h e) -> p h e", h=32), in0=xs3,
                                                       in1=dt.unsqueeze(2).broadcast_to([128, 32, 64]), op=ALU.mult),
                     r=[rx, dd], w=[xdt])
                k.op("pool", lambda g: g.tensor_tensor(out=xdl[:, :].rearrange("p (h e) -> p h e", h=32),
                                                       in0=xdt[:, :].rearrange("p (h e) -> p h e", h=32),
                                                       in1=edl[:, :].unsqueeze(2).broadcast_to([128, 32, 64]), op=ALU.mult),
                     r=[xdt, edl], w=[xdl])
                psT = bankM.next()
                k.op("pe", lambda pe: pe.transpose(out=psT[0:96, 0:128], in_=larep[:, 0:96], identity=identf[:, :]),
                     r=[larep, identf], w=[psT])
                k.op("act", lambda a: a.copy(out=Abf[:, :], in_=psT[0:96, 0:128]), r=[psT], w=[Abf])
                k.op("dve", lambda v: v.tensor_tensor(out=R1[:, :], in0=psT[0:96, 0:128], in1=Abf[:, :], op=ALU.subtract),
                     r=[psT, Abf], w=[R1])
                k.op("act", lambda a: a.copy(out=Bbf[:, :], in_=R1[:, :]), r=[R1], w=[Bbf])
                k.op("dve", lambda v: v.tensor_tensor(out=R1[:, :], in0=R1[:, :], in1=Bbf[:, :], op=ALU.subtract),
                     r=[R1, Bbf], w=[R1])
                k.op("pool", lambda g: g.tensor_copy(out=laT[0:32, :], in_=Abf[0:32, :]), r=[Abf], w=[laT])
                k.op("pool", lambda g: g.tensor_copy(out=laT[32:64, :], in_=Bbf[32:64, :]), r=[Bbf], w=[laT])
                k.op("pool", lambda g: g.tensor_copy(out=laT[64:96, :], in_=R1[64:96, :]), r=[R1], w=[laT])
                psCB = bankM.next()
                for g in range(4):
                    k.op("pe", lambda pe: pe.matmul(psCB[:, g * 128:(g + 1) * 128], lhsT=rbc[:, g * 128:(g + 1) * 128],
                                                    rhs=rbc[:, 512 + g * 128:512 + (g + 1) * 128], start=True, stop=True),
                         r=[rbc], w=[psCB])
                k.op("dve", lambda v: v.tensor_tensor(out=CBm[:, :, :], in0=psCB[:, :].rearrange("p (g i) -> p g i", g=4),
                                                      in1=cm.unsqueeze(1).broadcast_to([128, 4, 128]), op=ALU.mult),
                     r=[psCB, tri], w=[CBm])
                return c

            def stageB(c, ys):
                d, rx, rbc = c["d"], c["rx"], c["rbc"]
                la_sb, ela, etot, laT, xdt, xdl, CBm = (c[n] for n in ("la_sb", "ela", "etot", "laT", "xdt", "xdl", "CBm"))
                ng = negm[:, d, :]
                nla = c["nla"]

                def lb_mm(i):
                    h0 = i * 4
                    psLb = bankLb.next()
                    for hh in range(4):
                        k.op("pe", lambda pe: pe.matmul(psLb[:, hh * 128:(hh + 1) * 128], lhsT=onehot[:, h0 + hh, :],
                                                        rhs=laT[:, :], start=True, stop=False), r=[onehot, laT], w=[psLb])
                        k.op("pe", lambda pe: pe.matmul(psLb[:, hh * 128:(hh + 1) * 128], lhsT=self.identb[:, :],
                                                        rhs=negmb[:, d, :], start=False, stop=True), r=[self.identb, negmb], w=[psLb])
                    return psLb

                pend = [lb_mm(0), lb_mm(1)]

                def heads(g):
                    psY = bankY.next()
                    for hq in range(2):
                        i = g * 2 + hq
                        h0 = i * 4
                        psLb = pend.pop(0)
                        if i + 2 < 8:
                            pend.append(lb_mm(i + 2))
                        Eh = Eh_r.next()
                        for hh in range(4):
                            k.op("act", lambda a: a.activation(out=Eh[:, hh, :], in_=psLb[:, hh * 128:(hh + 1) * 128], func=AF.Exp,
                                                               bias=nla[:, h0 + hh:h0 + hh + 1]), r=[psLb, nla], w=[Eh])
                        mh = Mh.next()
                        k.op("dve", lambda v: v.tensor_tensor(out=mh[:, :, :], in0=Eh[:, :, :],
                                                              in1=CBm[:, g:g + 1, :].broadcast_to([128, 4, 128]), op=ALU.mult),
                             r=[Eh, CBm], w=[mh])
                        for hh in range(4):
                            h = h0 + hh
                            k.op("pe", lambda pe: pe.matmul(psY[:, (h % 8) * 64:(h % 8 + 1) * 64], lhsT=mh[:, hh, :],
                                                            rhs=xdt[:, h * 64:(h + 1) * 64], start=True, stop=True),
                                 r=[mh, xdt], w=[psY])
                    return psY

                def tail(g, psY):
                    psYi = bankM.next()
                    k.op("pe", lambda pe: pe.matmul(psYi[:, :], lhsT=rbc[:, 512 + g * 128:512 + (g + 1) * 128], rhs=Hb[g][:, :],
                                                    start=True, stop=True), r=[rbc, Hb[g]], w=[psYi])
                    yv = ys[:, g * 512:(g + 1) * 512]
                    k.op("dve", lambda v: v.tensor_tensor(out=yv.rearrange("p (h e) -> p h e", h=8),
                                                          in0=psYi[:, :].rearrange("p (h e) -> p h e", h=8),
                                                          in1=ela[:, g * 8:(g + 1) * 8].unsqueeze(2).broadcast_to([128, 8, 64]),
                                                          op=ALU.mult), r=[psYi, ela], w=[ys])
                    k.op("dve", lambda v: v.tensor_tensor(out=yv, in0=yv, in1=psY[:, :], op=ALU.add), r=[ys, psY], w=[ys])
                    psH = bankM.next()
                    k.op("pe", lambda pe: pe.matmul(psH[:, :], lhsT=rx[:, 2048 + g * 128:2048 + (g + 1) * 128],
                                                    rhs=xdl[:, g * 512:(g + 1) * 512], start=True, stop=True), r=[rx, xdl], w=[psH])
                    k.op("pool", lambda v: v.tensor_tensor(out=Hs[g][:, :].rearrange("p (h e) -> p h e", h=8),
                                                           in0=Hs[g][:, :].rearrange("p (h e) -> p h e", h=8),
                                                           in1=etot[:, g * 8:(g + 1) * 8].unsqueeze(2).broadcast_to([128, 8, 64]),
                                                           op=ALU.mult), r=[Hs[g], etot], w=[Hs[g]])
                    k.op("dve", lambda v: v.tensor_tensor(out=Hs[g][:, :], in0=Hs[g][:, :], in1=psH[:, :], op=ALU.add),
                         r=[Hs[g], psH], w=[Hs[g]])
                    k.op("pool", lambda gp: gp.tensor_copy(out=Hb[g][:, :], in_=Hs[g][:, :]), r=[Hs[g]], w=[Hb[g]])

                prev = None
                for g in range(4):
                    py = heads(g)
                    if prev is not None:
                        tail(*prev)
                    prev = (g, py)
                tail(*prev)

            def reset_state():
                for g in range(4):
                    k.op("pool", lambda gp: gp.memset(Hs[g][:, :], 0.0), w=[Hs[g]])
                    k.op("pool", lambda gp: gp.memset(Hb[g][:, :], 0.0), w=[Hb[g]])

            def loads(t):
                bi = 0 if t < 2 else 1 + (t - 2) // 4
                rx = rxr.next()
                rbc = rbcr.next()
                dd = ddr.next()
                k.dma("sp", rx[:, :], RX.t[t], r=[rxT[t]], w=[rx])
                k.dma("sp", rbc[:, :], RBC.t[t], r=[rbcT[bi]], w=[rbc])
                k.dma("sp", dd[:, :], DD.t[t], r=[ddT[t]], w=[dd])
                return rx, rbc, dd

            reset_state()
            cnext = stageA(0, *loads(0))
            for t in range(NT):
                c = cnext
                if t + 1 < NT:
                    cnext = stageA(0, *loads(t + 1))
                ys = ysr.next()
                stageB(c, ys)
                k.dma("pool", YF.t[t], ys[:, :], r=[ys], w=[yfT[t]])
            reset_state()
            order = [1, 0] + list(range(NT - 1, 1, -1))
            cnext = stageA(1, *loads(order[0]))

            def out_stage(t, rx, ys):
                cond = 0 if t < 2 else 1
                bi = 0 if t < 2 else 1 + (t - 2) // 4
                rz = rzr.next()
                yf = yfr.next()
                xt = xr.next()
                k.dma("sp", rz[:, :], RZ.t[t], r=[rzT[t]], w=[rz])
                k.dma("sp", yf[:, :], YF.t[t], r=[yfT[t]], w=[yf])
                sap, rr = self.xsrc_ap(xsrc, t * 128, 128)
                k.dma("sp", xt[:, :], sap, r=(rr or [self.xblk[bi]]), w=[xt])
                k.op("dve", lambda v: v.tensor_tensor(out=ys[:, :], in0=ys[:, :], in1=yf[:, :], op=ALU.add), r=[ys, yf], w=[ys])
                ytmp = yf
                k.op("pool", lambda g: g.tensor_tensor(out=ytmp[:, :].rearrange("p (h e) -> p h e", h=32),
                                                       in0=rx[:, 0:2048].rearrange("p (h e) -> p h e", h=32),
                                                       in1=dsk[:, :].unsqueeze(2).broadcast_to([128, 32, 64]), op=ALU.mult),
                     r=[rx, dsk, yf], w=[ytmp])
                k.op("dve", lambda v: v.tensor_tensor(out=ys[:, :], in0=ys[:, :], in1=ytmp[:, :], op=ALU.add), r=[ys, ytmp], w=[ys])
                k.op("dve", lambda v: v.tensor_tensor(out=ys[:, :], in0=ys[:, :], in1=rz[:, :], op=ALU.mult), r=[ys, rz], w=[ys])
                for g in range(4):
                    k.op("act", lambda a: a.activation(out=junk[:, :], in_=ys[:, g * 512:(g + 1) * 512], func=AF.Square,
                                                       accum_out=ss4[:, g:g + 1]), r=[ys], w=[junk, ss4])
                k.op("act", lambda a: a.activation(out=rs4[:, :], in_=ss4[:, :], func=AF.Sqrt, scale=1.0 / 512, bias=self.epsb[:, :]),
                     r=[ss4, self.epsb], w=[rs4])
                k.op("dve", lambda v: v.reciprocal(out=rs4[:, :], in_=rs4[:, :]), r=[rs4], w=[rs4])
                for g in range(4):
                    k.op("dve", lambda v: v.scalar_tensor_tensor(out=ynb[:, g * 512:(g + 1) * 512], in0=ys[:, g * 512:(g + 1) * 512],
                                                                 scalar=rs4[:, g:g + 1], in1=gnbc[:, g * 512:(g + 1) * 512],
                                                                 op0=ALU.mult, op1=ALU.mult), r=[ys, rs4, gnbc], w=[ynb])
                self.tok_outproj(ynb, 16, ogT, wo, xt, tm, cond)
                k.dma("pool", self.xres.t[t * 128:(t + 1) * 128, :], xt[:, :], r=[xt], w=[self.xblk[bi]])

            pending_out = None
            for oi, t in enumerate(order):
                c = cnext
                if oi + 1 < NT:
                    cnext = stageA(1, *loads(order[oi + 1]))
                ys = ysr.next()
                stageB(c, ys)
                if pending_out is not None:
                    out_stage(*pending_out)
                pending_out = (t, c["rx"], ys)
            out_stage(*pending_out)
            k.barrier()

    def tok_outproj(self, ogb, kc, ogT, wo, xt, tm, cond):
        k = self.k
        for c0 in range(0, kc, 8):
            ps = self.ps8.next()
            pv = ps[:, :].bitcast(BF16).rearrange("p (c t) -> p c t", c=8)
            for c in range(8):
                k.op("pe", lambda pe: pe.transpose(out=pv[:, c, :], in_=ogb[:, (c0 + c) * 128:(c0 + c + 1) * 128],
                                                   identity=self.identb[:, :]), r=[ogb, self.identb], w=[ps])
            k.op("act", lambda a: a.copy(out=ogT[:, c0:c0 + 8, :], in_=pv), r=[ps], w=[ogT])
        for cb in range(2):
            ps = self.ps8.next()
            for kk in range(kc):
                k.op("pe", lambda pe: pe.matmul(ps[:, :], lhsT=ogT[:, kk, :], rhs=wo[:, kk, cb * 512:(cb + 1) * 512],
                                                start=(kk == 0), stop=(kk == kc - 1)), r=[ogT, wo], w=[ps])
            k.op("dve", lambda v: v.tensor_tensor(out=tm[:, :], in0=ps[:, :],
                                                  in1=self.bcm[cond][:, 2 * D + cb * 512:2 * D + (cb + 1) * 512], op=ALU.mult),
                 r=[ps, self.bcm[cond]], w=[tm])
            k.op("dve", lambda v: v.tensor_tensor(out=xt[:, cb * 512:(cb + 1) * 512], in0=tm[:, :],
                                                  in1=xt[:, cb * 512:(cb + 1) * 512], op=ALU.add), r=[tm, xt], w=[xt])

    def outproj(self, es, og_src, ogb, kp, kc, wo, xsrc, xr, tm):
        k = self.k
        for bi, (t0, nt, cond) in enumerate(BLOCKS):
            ob = ogb.next()
            src, trk = og_src(bi, nt)
            if kp == 128:
                s4 = src.rearrange("d (k two) t -> d two k t", two=2)
                for hp in range(2):
                    k.dma("sp", ob[hp * 64:(hp + 1) * 64, :, 0:nt], s4[:, hp, :, :], r=[trk], w=[ob])
            else:
                k.dma("sp", ob[:, :, 0:nt], src, r=[trk], w=[ob])
            for j in range(nt // 128):
                xt = xr.next()
                sap, rr = self.xsrc_ap(xsrc, t0 + j * 128, 128)
                k.dma("sp", xt[:, :], sap, r=(rr or [self.xblk[bi]]), w=[xt])
                import os
                for cb in range(2 if not os.environ.get("DBG_SKIPMM") else 0):
                    ps = self.psg.next()
                    for kk in range(kc):
                        k.op("pe", lambda pe, kk=kk, cb=cb, ps=ps: pe.matmul(
                            ps[:, :], lhsT=ob[0:kp, kk, j * 128:(j + 1) * 128], rhs=wo[0:kp, kk, cb * 512:(cb + 1) * 512],
                            start=(kk == 0), stop=(kk == kc - 1)), r=[ob, wo], w=[ps])
                    k.op("dve", lambda v, cb=cb, ps=ps: v.tensor_tensor(
                        out=tm[:, :], in0=ps[:, :], in1=self.bcm[cond][:, 2 * D + cb * 512:2 * D + (cb + 1) * 512], op=ALU.mult),
                        r=[ps, self.bcm[cond]], w=[tm])
                    k.op("dve", lambda v, cb=cb, xt=xt: v.tensor_tensor(
                        out=xt[:, cb * 512:(cb + 1) * 512], in0=tm[:, :], in1=xt[:, cb * 512:(cb + 1) * 512], op=ALU.add),
                        r=[tm, xt], w=[xt])
                k.dma("pool", self.xres.t[t0 + j * 128:t0 + (j + 1) * 128, :], xt[:, :], r=[xt], w=[self.xblk[bi]])

    def final(self, xsrc):
        k = self.k
        with ExitStack() as es:
            xr = Ring([k.sb(es, [128, D], F32) for _ in range(3)])
            junk = k.sb(es, [128, D], BF16)
            ss = k.sb(es, [128, 1], F32)
            rs = k.sb(es, [128, 1], F32)
            epsb = k.sb(es, [128, 1], F32)
            k.op("pool", lambda g: g.memset(epsb[:, :], EPS), w=[epsb])
            fg = k.sb(es, [128, D], F32)
            k.dma("sp", fg[:, :], self.final_g.t.partition_broadcast(128), w=[fg])
            for bi, (t0, nt, cond) in enumerate(BLOCKS):
                if cond == 0 and not self.debug_x:
                    continue
                for j in range(nt // 128):
                    xt = xr.next()
                    tt0 = t0 + j * 128
                    sap, rr = self.xsrc_ap(xsrc, tt0, 128)
                    k.dma("sp", xt[:, :], sap, r=(rr or [self.xblk[bi]]), w=[xt])
                    if self.debug_x:
                        k.dma("pool", self.out.t[tt0:tt0 + 128, :], xt[:, :], r=[xt], w=[self.out])
                        continue
                    k.op("act", lambda a, xt=xt: a.activation(out=junk[:, :], in_=xt[:, :], func=AF.Square, accum_out=ss[:, :]),
                         r=[xt], w=[junk, ss])
                    k.op("act", lambda a: a.activation(out=rs[:, :], in_=ss[:, :], func=AF.Sqrt, scale=1.0 / D, bias=epsb[:, :]),
                         r=[ss, epsb], w=[rs])
                    k.op("dve", lambda v: v.reciprocal(out=rs[:, :], in_=rs[:, :]), r=[rs], w=[rs])
                    k.op("dve", lambda v, xt=xt: v.scalar_tensor_tensor(out=xt[:, :], in0=xt[:, :], scalar=rs[:, 0:1],
                                                                       in1=fg[:, :], op0=ALU.mult, op1=ALU.mult),
                         r=[xt, rs, fg], w=[xt])
                    k.dma("pool", self.out.t[tt0 - CTX:tt0 - CTX + 128, :], xt[:, :], r=[xt], w=[self.out])


WSHAPES = {
    "mla_w_in": [1, 1024, 1696], "mla_q_norm": [1, 384], "mla_w_uq": [1, 384, 1536], "mla_kv_norm": [1, 256],
    "mla_w_ukv": [1, 256, 2048], "mla_w_out": [1, 1024, 1024],
    "gla_w_in": [1, 1024, 3104], "gla_w_gf": [1, 16, 512], "gla_b_gf": [1, 512], "gla_w_gb": [1, 16, 512],
    "gla_b_gb": [1, 512], "gla_o_norm": [1, 256], "gla_w_out": [1, 1024, 1024],
    "gqa_w_in": [1, 1024, 2560], "gqa_q_norm": [1, 64], "gqa_k_norm": [1, 64], "gqa_w_out": [1, 1024, 1024],
    "ssd_w_in": [1, 1024, 5184], "ssd_conv_w": [1, 5, 3072], "ssd_conv_b": [1, 3072], "ssd_dt_bias_f": [1, 32],
    "ssd_dt_bias_b": [1, 32], "ssd_a_log_f": [1, 32], "ssd_a_log_b": [1, 32], "ssd_d": [1, 32], "ssd_norm": [1, 2048],
    "ssd_w_out": [1, 2048, 1024],
}


def tri_consts():
    j = np.arange(128)[:, None]
    i = np.arange(128)[None, :]
    return np.stack([(j <= i), (j > i), (j >= i), (j < i)]).astype(np.float32)


def rope_tables(rd):
    hf = rd // 4
    inv = 10000.0 ** (-np.arange(hf, dtype=np.float64) / hf)
    p = np.arange(SEQ)
    row = (p // 64).astype(np.float64)[:, None] * inv[None, :]
    col = (p % 64).astype(np.float64)[:, None] * inv[None, :]
    cos = np.concatenate([np.cos(row), np.cos(row), np.cos(col), np.cos(col)], axis=1)
    sin = np.concatenate([-np.sin(row), np.sin(row), -np.sin(col), np.sin(col)], axis=1)
    tab = np.zeros((T, 2, rd), np.float32)
    tab[:CTX, 0, :] = 1.0
    tab[CTX:, 0, :] = cos
    tab[CTX:, 1, :] = sin
    return tab


def run(inputs, layers=(0, 1, 2, 3), debug_x=False, cores=(0, 1), stop=None):
    nc = bass.Bass("TRN2", target_bir_lowering=False)
    Prog(nc, layers=layers, debug_x=debug_x, stop=stop).build()
    f = lambda a: np.ascontiguousarray(np.asarray(a, dtype=np.float32))
    common = {nm: f(inputs[nm]) for nm in WSHAPES}
    for nm in ("ada_w", "ada_b", "norm_g", "final_g"):
        common[nm] = f(inputs[nm])
    common["ident_bf"] = np.eye(128, dtype=np.float32).astype(ml_dtypes.bfloat16)
    common["rope_mla"] = rope_tables(32)
    common["rope_gqa"] = rope_tables(64)
    common["tri"] = tri_consts()
    common["negm"] = ((1.0 - tri_consts()[[0, 2]]) * -1e30).astype(np.float32)
    oh = np.zeros((3, 32, 32, 128), np.float32)
    oh[:, np.arange(32), np.arange(32), :] = 1.0
    common["onehot3"] = oh.reshape(96, 32, 128).astype(ml_dtypes.bfloat16)
    common["negmb"] = common["negm"].astype(ml_dtypes.bfloat16)
    common["ident_f"] = np.eye(128, dtype=np.float32)
    in_maps = []
    for b in cores:
        m = dict(common)
        m["xin"] = np.ascontiguousarray(np.concatenate([f(inputs["ctx"])[b], f(inputs["x"])[b]], axis=0))
        m["c2"] = np.ascontiguousarray(np.stack([f(inputs["c_ctx"]), f(inputs["c"])[b]], axis=0))
        in_maps.append(m)
    res = run_bass_kernel_spmd(nc, in_maps, core_ids=list(range(len(cores))))
    return [r["y"] for r in res.results]


FUSED = True


def kernel(**inputs):
    if FUSED:
        outs = run(inputs)
        return np.stack(outs, axis=0).astype(np.float32)
    cur = dict(inputs)
    for L in (0, 1, 2):
        outs = run(cur, layers=(L,), debug_x=True)
        st = np.stack(outs, axis=0)
        cur["ctx"] = np.ascontiguousarray(st[:, :CTX])
        cur["x"] = np.ascontiguousarray(st[:, CTX:])
    outs = run(cur, layers=(3,), debug_x=False)
    return np.stack(outs, axis=0).astype(np.float32)
```

```python
import math
from contextlib import ExitStack

import numpy as np
import ml_dtypes
import concourse.bass as bass
import concourse.mybir as mybir
from concourse.bass_utils import run_bass_kernel_spmd

F32 = mybir.dt.float32
BF16 = mybir.dt.bfloat16
AF = mybir.ActivationFunctionType
ALU = mybir.AluOpType
AX = mybir.AxisListType

D = 1024
SEQ = 8192
CTX = 256
T = SEQ + CTX
NT = T // 128
EPS = 1e-6
EPOCH = 30000

BLOCKS = [(0, 256, 0)] + [(256 + 512 * i, 512, 1) for i in range(16)]
NB = len(BLOCKS)


class Buf:
    __slots__ = ("w", "r")

    def __init__(self):
        self.w = None
        self.r = {}


class TT:
    def __init__(self, t):
        self.t = t
        self.b = Buf()

    def __getitem__(self, idx):
        return self.t[idx]


class Ring:
    def __init__(self, items):
        self.items = items
        self.i = 0

    def next(self):
        it = self.items[self.i % len(self.items)]
        self.i += 1
        return it


class KB:
    def __init__(self, nc, es):
        self.nc = nc
        self.es = es
        self.eng = {"pe": nc.tensor, "act": nc.scalar, "dve": nc.vector, "pool": nc.gpsimd, "sp": nc.sync}
        self.sems = {e: [] for e in self.eng}
        self.cnt = {e: 0 for e in self.eng}
        self.seen = {e: {} for e in self.eng}
        self.last = {e: None for e in self.eng}
        self.slots = {}
        self.slot_i = {}
        for q in ("sp", "pool", "act"):
            self.slots[q] = [[es.enter_context(nc.semaphore(f"d_{q}_{i}")), 0, f"d_{q}_{i}"] for i in range(12)]
            self.slot_i[q] = 0
        self.nsb = 0

    def sb(self, es, shape, dt, name=None):
        self.nsb += 1
        return TT(es.enter_context(self.nc.sbuf_tensor(name or f"sb{self.nsb}", list(shape), dt)))

    def dram(self, shape, dt, name):
        h = self.nc.dram_tensor(name, list(shape), dt, kind="Internal")
        return TT(h.ap())

    def _wait(self, e, deps):
        seen = self.seen[e]
        for ev in deps:
            key, sem, val, src = ev
            if src == "pe" and e == "pe":
                continue
            if seen.get(key, 0) >= val:
                continue
            self.eng[e].wait_ge(sem, val)
            seen[key] = val

    def _deps(self, reads, writes):
        deps = []
        for t in reads:
            if t.b.w is not None:
                deps.append(t.b.w)
        for t in writes:
            if t.b.w is not None:
                deps.append(t.b.w)
            deps.extend(t.b.r.values())
        return deps

    def _mark(self, ev, reads, writes):
        for t in reads:
            t.b.r[ev[0]] = ev
        for t in writes:
            t.b.w = ev
            t.b.r = {}

    def op(self, e, fn, r=(), w=()):
        self._wait(e, self._deps(r, w))
        ins = fn(self.eng[e])
        epoch = self.cnt[e] // EPOCH
        while len(self.sems[e]) <= epoch:
            self.sems[e].append(self.es.enter_context(self.nc.semaphore(f"s_{e}_{len(self.sems[e])}")))
        sem = self.sems[e][epoch]
        val = self.cnt[e] % EPOCH + 1
        ins.then_inc(sem, 1)
        self.cnt[e] += 1
        ev = ((e, epoch), sem, val, e)
        self.last[e] = ev
        self._mark(ev, r, w)
        return ev

    def dma(self, q, out, in_, r=(), w=(), **kw):
        deps = self._deps(r, w)
        slots = self.slots[q]
        si = self.slot_i[q] % len(slots)
        self.slot_i[q] += 1
        slot = slots[si]
        if slot[1] > 0:
            deps.append((slot[2], slot[0], 16 * slot[1], "dma"))
        self._wait(q, deps)
        ins = self.eng[q].dma_start(out=out, in_=in_, **kw)
        ins.then_inc(slot[0], 16)
        slot[1] += 1
        ev = (slot[2], slot[0], 16 * slot[1], "dma")
        self._mark(ev, r, w)
        return ev

    def barrier(self):
        evs = [self.last[e] for e in self.eng if self.last[e] is not None]
        for q in self.slots:
            for slot in self.slots[q]:
                if slot[1] > 0:
                    evs.append((slot[2], slot[0], 16 * slot[1], "dma"))
        for e in self.eng:
            seen = self.seen[e]
            for ev in evs:
                key, sem, val, src = ev
                if src == e:
                    continue
                if seen.get(key, 0) >= val:
                    continue
                self.eng[e].wait_ge(sem, val)
                seen[key] = val


class Prog:
    def __init__(self, nc, layers=(0, 1, 2, 3), debug_x=False, stop=None):
        self.nc = nc
        self.stop = stop
        self.layers = layers
        self.debug_x = debug_x

    def din(self, name, shape, dt=F32):
        return TT(self.nc.dram_tensor(name, list(shape), dt, kind="ExternalInput").ap())

    def build(self):
        nc = self.nc
        with ExitStack() as es:
            self.k = k = KB(nc, es)
            self.es = es
            self.xin = self.din("xin", [T, D])
            self.c2 = self.din("c2", [2, D])
            self.ada_w = self.din("ada_w", [4, D, 3 * D])
            self.ada_b = self.din("ada_b", [4, 3 * D])
            self.norm_g = self.din("norm_g", [4, D])
            self.final_g = self.din("final_g", [D])
            self.W = {}
            for nm, shp in WSHAPES.items():
                self.W[nm] = self.din(nm, shp)
            self.identb_d = self.din("ident_bf", [128, 128], BF16)
            self.rope_mla = self.din("rope_mla", [T, 2, 32])
            self.rope_gqa = self.din("rope_gqa", [T, 2, 64])
            self.tri_d = self.din("tri", [4, 128, 128])
            self.negm_d = self.din("negm", [2, 128, 128])
            self.onehot_d = self.din("onehot3", [96, 32, 128], BF16)
            self.negmb_d = self.din("negmb", [2, 128, 128], BF16)
            self.identf_d = self.din("ident_f", [128, 128])
            if self.debug_x:
                self.out = TT(nc.dram_tensor("y", [T, D], F32, kind="ExternalOutput").ap())
            else:
                self.out = TT(nc.dram_tensor("y", [SEQ, D], F32, kind="ExternalOutput").ap())
            self.xres = k.dram([T, D], F32, "xres")
            self.xblk = [TT(self.xres.t) for _ in range(NB)]
            self.modd = k.dram([4, 2, 3 * D], F32, "modd")
            self.identb = k.sb(es, [128, 128], BF16, "identb")
            k.dma("sp", self.identb[:, :], self.identb_d.t[:, :], w=[self.identb])
            self.ones_f = k.sb(es, [128, 128], F32, "ones_f")
            k.op("pool", lambda g: g.memset(self.ones_f[:, :], 1.0), w=[self.ones_f])
            self.psb = [TT(es.enter_context(nc.psum_tensor(f"ps{i}", [128, 512], F32))) for i in range(8)]
            self.psg = Ring(self.psb[0:6])
            self.pso = Ring(self.psb[6:8])
            self.ps8 = Ring(self.psb)
            self.bcm = [k.sb(es, [128, 3 * D], F32, f"bcm{c}") for c in range(2)]
            self.gmod = [k.sb(es, [128, D], F32, f"gmod{c}") for c in range(2)]
            self.ngbc = k.sb(es, [128, D], F32, "ngbc")

            first = True
            for L in self.layers:
                self.modulation(L)
                if self.stop == "mod":
                    break
                xsrc = self.xin if first else None
                if L == 0:
                    self.layer_attn(L, "mla", xsrc)
                elif L == 2:
                    self.layer_attn(L, "gqa", xsrc)
                elif L == 1:
                    self.layer_gla(L, xsrc)
                elif L == 3:
                    self.layer_ssd(L, xsrc)
                first = False
                k.barrier()
            self.final(self.xin if first else None)
            k.barrier()
        return nc

    def xsrc_ap(self, xsrc, t0, n):
        if xsrc is not None:
            return xsrc.t[t0:t0 + n, :], [xsrc]
        return self.xres.t[t0:t0 + n, :], None

    def modulation(self, L):
        k = self.k
        with ExitStack() as es:
            cT = k.sb(es, [128, 8, 2], F32)
            sT = k.sb(es, [128, 8, 2], F32)
            for kk in range(8):
                k.dma("sp", cT[:, kk, :], self.c2.t[:, kk * 128:(kk + 1) * 128].rearrange("c p -> p c"), w=[cT],
                      allow_slow_non_contiguous=True)
            k.op("act", lambda a: a.activation(out=sT[:, :, :], in_=cT[:, :, :], func=AF.Silu), r=[cT], w=[sT])
            msb = k.sb(es, [2, 3 * D], F32)
            bb = k.sb(es, [2, 3 * D], F32)
            k.dma("sp", bb[:, :], self.ada_b.t[L, :].partition_broadcast(2), w=[bb])
            wr = Ring([k.sb(es, [128, 8, 512], F32) for _ in range(2)])
            for cb in range(6):
                wt = wr.next()
                k.dma("sp", wt[:, :, :],
                      self.ada_w.t[L, :, cb * 512:(cb + 1) * 512].rearrange("(k p) n -> p k n", p=128), w=[wt])
                ps = self.psg.next()
                for kk in range(8):
                    k.op("pe", lambda pe, kk=kk: pe.matmul(ps[0:2, :], lhsT=sT[:, kk, :], rhs=wt[:, kk, :],
                                                          start=(kk == 0), stop=(kk == 7)), r=[sT, wt], w=[ps])
                k.op("dve", lambda v: v.tensor_tensor(out=msb[:, cb * 512:(cb + 1) * 512], in0=ps[0:2, :],
                                                      in1=bb[:, cb * 512:(cb + 1) * 512], op=ALU.add),
                     r=[ps, bb], w=[msb])
            md = TT(self.modd.t)
            k.dma("sp", self.modd.t[L, :, :], msb[:, :], r=[msb], w=[md])
            for c in range(2):
                k.dma("sp", self.bcm[c][:, :], self.modd.t[L, c, :].partition_broadcast(128), r=[md], w=[self.bcm[c]])
            k.dma("sp", self.ngbc[:, :], self.norm_g.t[L, :].partition_broadcast(128), w=[self.ngbc])
            for c in range(2):
                k.op("dve", lambda v, c=c: v.scalar_tensor_tensor(out=self.gmod[c][:, :], in0=self.bcm[c][:, D:2 * D],
                                                                 scalar=1.0, in1=self.ngbc[:, :], op0=ALU.add,
                                                                 op1=ALU.mult),
                     r=[self.bcm[c], self.ngbc], w=[self.gmod[c]])
            k.barrier()

    def load_w(self, es_stage, dst, dview, src_ap, kc, kp, n, stg):
        k = self.k
        CH = 1024
        idx = 0
        for kk in range(kc):
            for c0 in range(0, n, CH):
                cn = min(CH, n - c0)
                st = stg.next()
                k.dma("sp", st[0:kp, 0:cn], src_ap[kk * kp:(kk + 1) * kp, c0:c0 + cn], w=[st])
                k.op("pool" if idx % 2 == 0 else "dve",
                     lambda g, st=st, kk=kk, c0=c0, cn=cn: g.tensor_copy(out=dview[0:kp, kk, c0:c0 + cn], in_=st[0:kp, 0:cn]),
                     r=[st], w=[dst])
                idx += 1

    def norm_block(self, es, bi, xsrc, xr, hT, scr):
        k = self.k
        t0, nt, cond = BLOCKS[bi]
        junk, _ss, _rs, hf0, hb0 = scr
        key = id(es)
        if getattr(self, "_nb_key", None) != key:
            self._nb_key = key
            self._nb = dict(ss=k.sb(es, [128, 4], F32), rs=k.sb(es, [128, 4], F32),
                            hf=Ring([hf0, k.sb(es, [128, D], F32)]),
                            hb=Ring([hb0] + [k.sb(es, [128, D], BF16) for _ in range(3)]))
        nb = self._nb
        ss, rs = nb["ss"], nb["rs"]
        ntile = nt // 128
        xts = []
        for j in range(ntile):
            xt = xr.next()
            xts.append(xt)
            src, rr = self.xsrc_ap(xsrc, t0 + j * 128, 128)
            k.dma("sp", xt[:, :], src, r=(rr or [self.xblk[bi]]), w=[xt])
            k.op("act", lambda a: a.activation(out=junk[:, :], in_=xt[:, :], func=AF.Square, accum_out=ss[:, j:j + 1]),
                 r=[xt], w=[junk, ss])
            if len(xr.items) < ntile and j % len(xr.items) == len(xr.items) - 1:
                pass
        k.op("act", lambda a: a.activation(out=rs[:, 0:ntile], in_=ss[:, 0:ntile], func=AF.Sqrt, scale=1.0 / D, bias=self.epsb[:, :]),
             r=[ss, self.epsb], w=[rs])
        k.op("dve", lambda v: v.reciprocal(out=rs[:, 0:ntile], in_=rs[:, 0:ntile]), r=[rs], w=[rs])
        hbs = []
        for j in range(ntile):
            xt = xts[j]
            hf = nb["hf"].next()
            hb = nb["hb"].next()
            hbs.append(hb)
            k.op("dve", lambda v: v.scalar_tensor_tensor(out=hf[:, :], in0=xt[:, :], scalar=rs[:, j:j + 1],
                                                         in1=self.gmod[cond][:, :], op0=ALU.mult, op1=ALU.mult),
                 r=[xt, rs, self.gmod[cond]], w=[hf])
            k.op("dve", lambda v: v.tensor_tensor(out=hb[:, :], in0=hf[:, :], in1=self.bcm[cond][:, 0:D], op=ALU.add),
                 r=[hf, self.bcm[cond]], w=[hb])
        for j in range(ntile):
            hb = hbs[j]
            ps = self.psg.next()
            pv = ps[:, :].bitcast(BF16).rearrange("p (c t) -> p c t", c=8)
            for c in range(8):
                k.op("pe", lambda pe: pe.transpose(out=pv[:, c, :], in_=hb[:, c * 128:(c + 1) * 128],
                                                   identity=self.identb[:, :]), r=[hb, self.identb], w=[ps])
            k.op("act", lambda a: a.copy(out=hT[:, :, j * 128:(j + 1) * 128], in_=pv), r=[ps], w=[hT])

    def layer_attn(self, L, kind, xsrc):
        k = self.k
        nc = self.nc
        if kind == "mla":
            H, HK, DQ = 16, 16, 96
            w_in = self.W["mla_w_in"].t[0]
            w_out = self.W["mla_w_out"].t[0]
            GOFF = 672
            scale = 96 ** -0.5
        else:
            H, HK, DQ = 16, 4, 64
            w_in = self.W["gqa_w_in"].t[0]
            w_out = self.W["gqa_w_out"].t[0]
            GOFF = 1536
            scale = 64 ** -0.5
        REP = H // HK
        QT = k.dram([H, DQ, T], BF16, f"QT{L}")
        KT = k.dram([HK, DQ, T], BF16, f"KT{L}")
        VV = k.dram([HK, 128, NT, 65], BF16, f"VV{L}")
        GS = k.dram([8, 128, T], BF16, f"GS{L}")
        OG = k.dram([NB, 64, 16, 512], BF16, f"OG{L}")

        with ExitStack() as es:
            self.epsb = k.sb(es, [128, 1], F32)
            k.op("pool", lambda g: g.memset(self.epsb[:, :], EPS), w=[self.epsb])
            NIN = 1696 if kind == "mla" else 2560
            win = k.sb(es, [128, 8, NIN], BF16)
            if kind == "mla":
                wuq = k.sb(es, [128, 3, 1536], BF16)
                wukv = k.sb(es, [128, 2, 2048], BF16)
            with ExitStack() as es2:
                stg = Ring([k.sb(es2, [128, 1024], F32) for _ in range(2)])
                self.load_w(es2, win, win, w_in, 8, 128, NIN, stg)
                if kind == "mla":
                    self.load_w(es2, wuq, wuq, self.W["mla_w_uq"].t[0], 3, 128, 1536, stg)
                    self.load_w(es2, wukv, wukv, self.W["mla_w_ukv"].t[0], 2, 128, 2048, stg)
                k.barrier()
            if kind == "mla":
                qnbc = k.sb(es, [128, 384], F32)
                k.dma("sp", qnbc[:, :], self.W["mla_q_norm"].t[0, :].partition_broadcast(128), w=[qnbc])
                kvnbc = k.sb(es, [128, 256], F32)
                k.dma("sp", kvnbc[:, :], self.W["mla_kv_norm"].t[0, :].partition_broadcast(128), w=[kvnbc])
                RD, HF = 32, 8
                rope_d = self.rope_mla
            else:
                qnbc = k.sb(es, [128, 64], F32)
                k.dma("sp", qnbc[:, :], self.W["gqa_q_norm"].t[0, :].partition_broadcast(128), w=[qnbc])
                knbc = k.sb(es, [128, 64], F32)
                k.dma("sp", knbc[:, :], self.W["gqa_k_norm"].t[0, :].partition_broadcast(128), w=[knbc])
                RD, HF = 64, 16
                rope_d = self.rope_gqa
            xr = Ring([k.sb(es, [128, D], F32) for _ in range(4)])
            hTr = Ring([k.sb(es, [128, 8, 512], BF16) for _ in range(2)])
            scr = (k.sb(es, [128, D], BF16), k.sb(es, [128, 1], F32), k.sb(es, [128, 1], F32),
                   k.sb(es, [128, D], F32), k.sb(es, [128, D], BF16))
            qsb = k.sb(es, [128, H * DQ], F32)
            qb = k.sb(es, [128, H, DQ], BF16)
            kb = k.sb(es, [128, HK, DQ], BF16)
            ksb = k.sb(es, [128, HK * DQ if kind == "gqa" else 32], F32)
            vblk = Ring([k.sb(es, [128, 4, HK, 65], BF16) for _ in range(1)])
            for vb_ in vblk.items:
                k.op("pool", lambda g, vb_=vb_: g.memset(vb_[:, :, :, :], 1.0), w=[vb_])
            qTb = Ring([k.sb(es, [DQ, H, 512], BF16) for _ in range(1)])
            kTb = Ring([k.sb(es, [DQ, HK, 512], BF16) for _ in range(1)])
            rtab = Ring([k.sb(es, [128, 2, RD], F32) for _ in range(2)])
            ra = k.sb(es, [128, H, RD], F32)
            rb_ = k.sb(es, [128, H, RD], F32)
            ss2 = k.sb(es, [128, 32], F32)
            rs2 = k.sb(es, [128, 32], F32)
            sq = k.sb(es, [128, H * DQ], F32)
            if kind == "mla":
                cqn = k.sb(es, [128, 640], BF16)
                cT = k.sb(es, [128, 5, 128], BF16)
            gsr = Ring([k.sb(es, [128, 512], BF16) for _ in range(2)])

            def rope(xv, nh, dst, tab):
                cosb = tab[:, 0:1, :].broadcast_to([128, nh, RD])
                k.op("dve", lambda v: v.tensor_tensor(out=ra[:, 0:nh, :], in0=xv, in1=cosb, op=ALU.mult),
                     r=[tab, qsb, ksb], w=[ra])
                x5 = xv.rearrange("p h (g s f) -> p h g s f", g=2, s=2)
                b5 = rb_[:, 0:nh, :].rearrange("p h (g s f) -> p h g s f", g=2, s=2)
                s5 = tab[:, 1, :].rearrange("p (g s f) -> p g s f", g=2, s=2)
                for g in range(2):
                    for s in range(2):
                        sinb = s5[:, g:g + 1, s, :].broadcast_to([128, nh, HF])
                        k.op("dve", lambda v, g=g, s=s, sinb=sinb: v.tensor_tensor(
                            out=b5[:, :, g, s, :], in0=x5[:, :, g, 1 - s, :], in1=sinb, op=ALU.mult),
                            r=[tab, qsb, ksb], w=[rb_])
                k.op("dve", lambda v: v.tensor_tensor(out=dst, in0=ra[:, 0:nh, :], in1=rb_[:, 0:nh, :], op=ALU.add),
                     r=[ra, rb_], w=[qb, kb])

            import os
            pending_st = [None]
            for bi, (t0, nt, cond) in enumerate(BLOCKS[:int(os.environ.get('DBG_P1_BLOCKS', NB))]):
                hT = hTr.next()
                self.norm_block(es, bi, xsrc, xr, hT, scr)
                if pending_st[0] is not None:
                    pending_st[0]()
                    pending_st[0] = None
                ntile = nt // 128
                vb4 = vblk.next()
                qT = qTb.next()
                kT = kTb.next()
                for j in range(ntile):
                    tt0 = t0 + j * 128
                    kt = tt0 // 128
                    tab = rtab.next()
                    k.dma("sp", tab[:, :, :], rope_d.t[tt0:tt0 + 128, :, :], w=[tab])
                    hTj = lambda kk: hT[:, kk, j * 128:(j + 1) * 128]
                    if kind == "mla":
                        psA = self.psg.next()
                        psB = self.psg.next()
                        for kk in range(8):
                            k.op("pe", lambda pe, kk=kk: pe.matmul(psA[:, 0:384], lhsT=hTj(kk), rhs=win[:, kk, 0:384],
                                                                  start=(kk == 0), stop=(kk == 7)), r=[hT, win], w=[psA])
                        for kk in range(8):
                            k.op("pe", lambda pe, kk=kk: pe.matmul(psB[:, 0:288], lhsT=hTj(kk), rhs=win[:, kk, 384:672],
                                                                  start=(kk == 0), stop=(kk == 7)), r=[hT, win], w=[psB])
                        for (ps_, n_, gb_, o_) in ((psA, 384, qnbc, 0), (psB, 256, kvnbc, 384)):
                            k.op("act", lambda a, ps_=ps_, n_=n_: a.activation(out=sq[:, 0:n_], in_=ps_[:, 0:n_], func=AF.Square,
                                                                              accum_out=ss2[:, 0:1]), r=[ps_], w=[sq, ss2])
                            k.op("act", lambda a, n_=n_: a.activation(out=rs2[:, 0:1], in_=ss2[:, 0:1], func=AF.Sqrt,
                                                                     scale=1.0 / n_, bias=self.epsb[:, :]),
                                 r=[ss2, self.epsb], w=[rs2])
                            k.op("dve", lambda v: v.reciprocal(out=rs2[:, 0:1], in_=rs2[:, 0:1]), r=[rs2], w=[rs2])
                            k.op("dve", lambda v, ps_=ps_, n_=n_, gb_=gb_, o_=o_: v.scalar_tensor_tensor(
                                out=cqn[:, o_:o_ + n_], in0=ps_[:, 0:n_], scalar=rs2[:, 0:1], in1=gb_[:, :],
                                op0=ALU.mult, op1=ALU.mult), r=[ps_, rs2, gb_], w=[cqn])
                        k.op("act", lambda a: a.copy(out=ksb[:, 0:32], in_=psB[:, 256:288]), r=[psB], w=[ksb])
                        pst = self.psg.next()
                        ptv = pst[:, :].bitcast(BF16).rearrange("p (c t) -> p c t", c=8)
                        for c in range(5):
                            k.op("pe", lambda pe, c=c: pe.transpose(out=ptv[:, c, :], in_=cqn[:, c * 128:(c + 1) * 128],
                                                                    identity=self.identb[:, :]), r=[cqn, self.identb], w=[pst])
                        k.op("act", lambda a: a.copy(out=cT[:, :, :], in_=ptv[:, 0:5, :]), r=[pst], w=[cT])
                        for cb in range(3):
                            ps = self.psg.next()
                            for kk in range(3):
                                k.op("pe", lambda pe, kk=kk, cb=cb, ps=ps: pe.matmul(
                                    ps[:, :], lhsT=cT[:, kk, :], rhs=wuq[:, kk, cb * 512:(cb + 1) * 512],
                                    start=(kk == 0), stop=(kk == 2)), r=[cT, wuq], w=[ps])
                            k.op("act", lambda a, cb=cb, ps=ps: a.copy(out=qsb[:, cb * 512:(cb + 1) * 512], in_=ps[:, :]),
                                 r=[ps], w=[qsb])
                        q3 = qsb[:, :].rearrange("p (h d) -> p h d", h=16)
                        k.op("pool", lambda g: g.tensor_copy(out=qb[:, :, 0:64], in_=q3[:, :, 0:64]), r=[qsb], w=[qb])
                        rope(q3[:, :, 64:96], 16, qb[:, :, 64:96], tab)
                        krv = ksb[:, 0:32].rearrange("p (h d) -> p h d", h=1)
                        rope(krv, 1, kb[:, 0:1, 64:96], tab)
                        k.op("pool", lambda g: g.tensor_copy(out=kb[:, 1:16, 64:96],
                                                             in_=kb[:, 0:1, 64:96].broadcast_to([128, 15, 32])),
                             r=[kb], w=[kb])
                        for cb in range(4):
                            ps = self.psg.next()
                            for kk in range(2):
                                k.op("pe", lambda pe, kk=kk, cb=cb, ps=ps: pe.matmul(
                                    ps[:, :], lhsT=cT[:, 3 + kk, :], rhs=wukv[:, kk, cb * 512:(cb + 1) * 512],
                                    start=(kk == 0), stop=(kk == 1)), r=[cT, wukv], w=[ps])
                            p3 = ps[:, :].rearrange("p (h d) -> p h d", h=4)
                            k.op("act", lambda a, cb=cb, p3=p3: a.copy(out=kb[:, cb * 4:(cb + 1) * 4, 0:64], in_=p3[:, :, 0:64]),
                                 r=[ps], w=[kb])
                            k.op("dve", lambda v, cb=cb, p3=p3: v.tensor_copy(out=vb4[:, j, cb * 4:(cb + 1) * 4, 0:64],
                                                                              in_=p3[:, :, 64:128]), r=[ps], w=[vb4])
                    else:
                        import os
                        for cb in range(3 if int(os.environ.get('DBG_STEP', 9)) >= 1 else 0):
                            ps = self.psg.next()
                            for kk in range(8):
                                k.op("pe", lambda pe, kk=kk, cb=cb, ps=ps: pe.matmul(
                                    ps[:, :], lhsT=hTj(kk), rhs=win[:, kk, cb * 512:(cb + 1) * 512],
                                    start=(kk == 0), stop=(kk == 7)), r=[hT, win], w=[ps])
                            SUB = os.environ.get('DBG_SUB', 'abc')
                            if cb < 2:
                                if 'a' in SUB:
                                    k.op("act", lambda a, cb=cb, ps=ps: a.copy(out=qsb[:, cb * 512:(cb + 1) * 512], in_=ps[:, :]),
                                         r=[ps], w=[qsb])
                            elif 'b' in SUB:
                                k.op("act", lambda a, ps=ps: a.copy(out=ksb[:, 0:256], in_=ps[:, 0:256]), r=[ps], w=[ksb])
                                p3 = ps[:, 256:512].rearrange("p (h d) -> p h d", h=4)
                                if 'c' in SUB:
                                    for hh in range(4):
                                        k.op("act", lambda a, hh=hh: a.copy(out=vb4[:, j, hh, 0:64], in_=ps[:, 256 + hh * 64:256 + (hh + 1) * 64]),
                                             r=[ps], w=[vb4])
                        import os
                        DS = int(os.environ.get('DBG_STEP', 9))
                        for (src_, nh, gb_, dstb) in ((qsb, 16, qnbc, qb), (ksb, 4, knbc, kb)) if DS >= 2 else ():
                            s3 = src_[:, 0:nh * 64].rearrange("p (h d) -> p h d", h=nh)
                            sq3 = sq[:, 0:nh * 64].rearrange("p (h d) -> p h d", h=nh)
                            k.op("dve", lambda v, s3=s3, sq3=sq3: v.tensor_tensor(out=sq3, in0=s3, in1=s3, op=ALU.mult),
                                 r=[src_], w=[sq])
                            k.op("dve", lambda v, sq3=sq3, nh=nh: v.tensor_reduce(out=ss2[:, 0:nh], in_=sq3, axis=AX.X, op=ALU.add),
                                 r=[sq], w=[ss2])
                            k.op("act", lambda a, nh=nh: a.activation(out=rs2[:, 0:nh], in_=ss2[:, 0:nh], func=AF.Sqrt,
                                                                     scale=1.0 / 64, bias=self.epsb[:, :]),
                                 r=[ss2, self.epsb], w=[rs2])
                            k.op("dve", lambda v, nh=nh: v.reciprocal(out=rs2[:, 0:nh], in_=rs2[:, 0:nh]), r=[rs2], w=[rs2])
                            k.op("dve", lambda v, s3=s3, nh=nh: v.tensor_tensor(
                                out=s3, in0=s3, in1=rs2[:, 0:nh].unsqueeze(2).broadcast_to([128, nh, 64]), op=ALU.mult),
                                r=[src_, rs2], w=[src_])
                            k.op("dve", lambda v, s3=s3, nh=nh, gb_=gb_: v.tensor_tensor(
                                out=s3, in0=s3, in1=gb_[:, :].unsqueeze(1).broadcast_to([128, nh, 64]), op=ALU.mult),
                                r=[src_, gb_], w=[src_])
                            if DS >= 3:
                                rope(s3, nh, dstb[:, :, :], tab)
                    import os
                    for (srcb, nh, dstT) in ((qb, H, qT), (kb, HK, kT)) if int(os.environ.get('DBG_STEP', 9)) >= 4 else ():
                        for h0 in range(0, nh, 8):
                            hn = min(8, nh - h0)
                            ps = self.psg.next()
                            ptv = ps[:, :].bitcast(BF16).rearrange("p (c t) -> p c t", c=8)
                            for hh in range(hn):
                                k.op("pe", lambda pe, hh=hh, h0=h0, ptv=ptv, srcb=srcb: pe.transpose(
                                    out=ptv[0:DQ, hh, :], in_=srcb[:, h0 + hh, :], identity=self.identb[:, :]),
                                    r=[srcb, self.identb], w=[ps])
                            k.op("act", lambda a, h0=h0, hn=hn, ptv=ptv, dstT=dstT: a.copy(
                                out=dstT[:, h0:h0 + hn, j * 128:(j + 1) * 128], in_=ptv[0:DQ, 0:hn, :]), r=[ps], w=[dstT])
                for hp in range(8):
                    ps = self.psg.next()
                    for kk in range(8):
                        k.op("pe", lambda pe, kk=kk, hp=hp, ps=ps: pe.matmul(
                            ps[:, 0:nt], lhsT=win[:, kk, GOFF + hp * 128:GOFF + (hp + 1) * 128], rhs=hT[:, kk, 0:nt],
                            start=(kk == 0), stop=(kk == 7)), r=[hT, win], w=[ps])
                    gs = gsr.next()
                    k.op("act", lambda a, ps=ps, gs=gs: a.activation(out=gs[:, 0:nt], in_=ps[:, 0:nt], func=AF.Silu),
                         r=[ps], w=[gs])
                    k.dma("pool", GS.t[hp, :, t0:t0 + nt], gs[:, 0:nt], r=[gs], w=[GS])
                def make_stores(qT, kT, vb4, t0, nt, ntile):
                    def st():
                        for h0 in range(0, H, 4):
                            k.dma("sp", QT.t[h0:h0 + 4, :, t0:t0 + nt].rearrange("h d t -> d h t"), qT[:, h0:h0 + 4, 0:nt],
                                  r=[qT], w=[QT])
                        for h0 in range(0, HK, 4):
                            k.dma("sp", KT.t[h0:h0 + 4, :, t0:t0 + nt].rearrange("h d t -> d h t"), kT[:, h0:h0 + 4, 0:nt],
                                  r=[kT], w=[KT])
                        kt0 = t0 // 128
                        for j in range(ntile):
                            for h0 in range(0, HK, 4):
                                k.dma("sp", VV.t[h0:h0 + 4, :, kt0 + j, :].rearrange("h p e -> p h e"), vb4[:, j, h0:h0 + 4, :],
                                      r=[vb4], w=[VV], allow_slow_non_contiguous=True)
                    return st
                pending_st[0] = make_stores(qT, kT, vb4, t0, nt, ntile)
            if pending_st[0] is not None:
                pending_st[0]()
                pending_st[0] = None
            k.barrier()

        if self.stop == "p1":
            return
        with ExitStack() as es:
            DQP = 128 if DQ == 64 else DQ
            KTs = Ring([k.sb(es, [DQP, T], BF16) for _ in range(2)])
            Vs = Ring([k.sb(es, [128, NT, 65], BF16) for _ in range(2)])
            Qs = Ring([k.sb(es, [DQP, T], BF16) for _ in range(2)])
            if DQP != DQ:
                for b_ in KTs.items + Qs.items:
                    k.op("pool", lambda g, b_=b_: g.memset(b_[DQ:DQP, :], 0.0), w=[b_])
            Gs = Ring([k.sb(es, [64, T], BF16) for _ in range(2)])
            Ps = Ring([k.sb(es, [128, 512], BF16) for _ in range(6)])
            rr = k.sb(es, [65, 512], F32)
            tmp = k.sb(es, [64, 512], F32)
            ogr = Ring([k.sb(es, [64, 512], BF16) for _ in range(2)])
            pss = Ring(self.psb[0:5])
            psm = Ring(self.psb[5:6])
            import os
            pending_epi = [None]
            for hk in range(int(os.environ.get('DBG_P2_HEADS', HK))):
                Kt = KTs.next()
                Vt = Vs.next()
                k.dma("sp", Kt[0:DQ, :], KT.t[hk], r=[KT], w=[Kt])
                k.dma("sp", Vt[:, :, :], VV.t[hk], r=[VV], w=[Vt])
                for hr in range(REP):
                    h = hk * REP + hr
                    Qt = Qs.next()
                    Gt = Gs.next()
                    k.dma("sp", Qt[0:DQ, :], QT.t[h], r=[QT], w=[Qt])
                    k.dma("sp", Gt[:, :], GS.t[h // 2, (h % 2) * 64:(h % 2) * 64 + 64, :], r=[GS], w=[Gt])
                    for bi, (t0, nt, cond) in enumerate(BLOCKS):
                        nkt = 2 if cond == 0 else NT
                        po = self.pso.next()
                        pend = []

                        def s_mm(kt):
                            ps = pss.next()
                            k.op("pe", lambda pe: pe.matmul(ps[:, 0:nt], lhsT=Kt[:, kt * 128:(kt + 1) * 128], rhs=Qt[:, t0:t0 + nt],
                                                            start=True, stop=True), r=[Kt, Qt], w=[ps])
                            pt = Ps.next()
                            k.op("act", lambda a: a.activation(out=pt[:, 0:nt], in_=ps[:, 0:nt], func=AF.Exp, scale=scale),
                                 r=[ps], w=[pt])
                            return pt

                        def pv_mm(kt, pt):
                            k.op("pe", lambda pe: pe.matmul(po[0:65, 0:nt], lhsT=Vt[:, kt, :], rhs=pt[:, 0:nt],
                                                            start=(kt == 0), stop=(kt == nkt - 1)), r=[Vt, pt], w=[po])

                        SK = 3
                        for kt in range(nkt + SK):
                            if kt < nkt:
                                pend.append((kt, s_mm(kt)))
                            if kt >= SK:
                                a_, b_ = pend.pop(0)
                                pv_mm(a_, b_)
                            if kt == 10 and pending_epi[0] is not None:
                                pending_epi[0]()
                                pending_epi[0] = None
                        if pending_epi[0] is not None:
                            pending_epi[0]()
                            pending_epi[0] = None

                        def make_epi(po, Gt, t0, nt, bi, h):
                            def epi():
                                k.op("dve", lambda v: v.reciprocal(out=rr[64:65, 0:nt], in_=po[64:65, 0:nt]), r=[po], w=[rr])
                                pm = psm.next()
                                k.op("pe", lambda pe: pe.matmul(pm[0:64, 0:nt], lhsT=self.ones_f[64:65, 0:64], rhs=rr[64:65, 0:nt],
                                                                start=True, stop=True), r=[rr, self.ones_f], w=[pm])
                                k.op("dve", lambda v: v.tensor_tensor(out=tmp[:, 0:nt], in0=pm[0:64, 0:nt], in1=Gt[:, t0:t0 + nt],
                                                                      op=ALU.mult), r=[pm, Gt], w=[tmp])
                                og = ogr.next()
                                k.op("dve", lambda v: v.tensor_tensor(out=og[:, 0:nt], in0=po[0:64, 0:nt], in1=tmp[:, 0:nt], op=ALU.mult),
                                     r=[po, tmp], w=[og])
                                k.dma("pool", OG.t[bi, :, h, 0:nt], og[:, 0:nt], r=[og], w=[OG])
                            return epi
                        pending_epi[0] = make_epi(po, Gt, t0, nt, bi, h)
            if pending_epi[0] is not None:
                pending_epi[0]()
                pending_epi[0] = None
            k.barrier()

        if self.stop == "p2":
            return
        with ExitStack() as es:
            stg = Ring([k.sb(es, [128, 1024], F32) for _ in range(2)])
            wo = k.sb(es, [128, 8, D], BF16)
            self.load_w(es, wo, wo, w_out, 8, 128, D, stg)
            ogb = Ring([k.sb(es, [128, 8, 512], BF16) for _ in range(2)])
            xr = Ring([k.sb(es, [128, D], F32) for _ in range(3)])
            tm = k.sb(es, [128, 512], F32)
            self.outproj(es, lambda bi, nt: (OG.t[bi, :, :, 0:nt], OG), ogb, 128, 8, wo, xsrc, xr, tm)
            k.barrier()


    def layer_gla(self, L, xsrc):
        k = self.k
        w_in = self.W["gla_w_in"].t[0]
        w_out = self.W["gla_w_out"].t[0]
        REC = k.dram([NT, 128, 3584], BF16, f"GREC{L}")
        GG = k.dram([NT, 128, 1024], F32, f"GGG{L}")
        GOF = k.dram([NT, 128, 1024], F32, f"GOF{L}")
        recT = [TT(REC.t) for _ in range(NT)]
        ggT = [TT(GG.t) for _ in range(NT)]
        ofT = [TT(GOF.t) for _ in range(NT)]
        ps8 = self.ps8
        with ExitStack() as es:
            self.epsb = k.sb(es, [128, 1], F32)
            k.op("pool", lambda g: g.memset(self.epsb[:, :], EPS), w=[self.epsb])
            onec = k.sb(es, [128, 1], F32)
            k.op("pool", lambda g: g.memset(onec[:, :], 1.0), w=[onec])
            stg = Ring([k.sb(es, [128, 1024], F32) for _ in range(2)])
            win = k.sb(es, [128, 8, 3104], BF16)
            self.load_w(es, win, win, w_in, 8, 128, 3104, stg)
            wg = k.sb(es, [16, 2, 512], F32)
            k.dma("sp", wg[:, 0, :], self.W["gla_w_gf"].t[0], w=[wg])
            k.dma("sp", wg[:, 1, :], self.W["gla_w_gb"].t[0], w=[wg])
            bg = k.sb(es, [128, 2, 512], F32)
            k.dma("sp", bg[:, 0, :], self.W["gla_b_gf"].t[0, :].partition_broadcast(128), w=[bg])
            k.dma("sp", bg[:, 1, :], self.W["gla_b_gb"].t[0, :].partition_broadcast(128), w=[bg])
            xr = Ring([k.sb(es, [128, D], F32) for _ in range(4)])
            hTr = Ring([k.sb(es, [128, 8, 512], BF16) for _ in range(2)])
            scr = (k.sb(es, [128, D], BF16), k.sb(es, [128, 1], F32), k.sb(es, [128, 1], F32),
                   k.sb(es, [128, D], F32), k.sb(es, [128, D], BF16))
            recr = Ring([k.sb(es, [128, 3584], BF16) for _ in range(2)])
            ggr = Ring([k.sb(es, [128, 1024], F32) for _ in range(2)])
            rT = k.sb(es, [16, 2, 128], F32)
            zt = k.sb(es, [128, 512], F32)
            for bi, (t0, nt, cond) in enumerate(BLOCKS):
                hT = hTr.next()
                self.norm_block(es, bi, xsrc, xr, hT, scr)
                for j in range(nt // 128):
                    t = (t0 + j * 128) // 128
                    rec = recr.next()
                    gg = ggr.next()
                    hTj = lambda kk: hT[:, kk, j * 128:(j + 1) * 128]

                    def tokmm(c0, n):
                        ps = ps8.next()
                        for kk in range(8):
                            k.op("pe", lambda pe: pe.matmul(ps[:, 0:n], lhsT=hTj(kk), rhs=win[:, kk, c0:c0 + n],
                                                            start=(kk == 0), stop=(kk == 7)), r=[hT, win], w=[ps])
                        return ps

                    ps = tokmm(512, 512)
                    k.op("act", lambda a: a.copy(out=rec[:, 1024:1536], in_=ps[:, :]), r=[ps], w=[rec])
                    for cb in range(2):
                        ps = tokmm(1024 + cb * 512, 512)
                        k.op("dve", lambda v: v.tensor_copy(out=rec[:, 1536 + cb * 512:2048 + cb * 512], in_=ps[:, :]),
                             r=[ps], w=[rec])
                    for cb in range(2):
                        ps = tokmm(2048 + cb * 512, 512)
                        k.op("act", lambda a: a.activation(out=rec[:, 2560 + cb * 512:3072 + cb * 512], in_=ps[:, :],
                                                           func=AF.Silu), r=[ps], w=[rec])
                    for qk in range(2):
                        ps = ps8.next()
                        for h in range(4):
                            for kk in range(8):
                                k.op("pe", lambda pe: pe.matmul(
                                    ps[:, h * 128:(h + 1) * 128], lhsT=win[:, kk, qk * 512 + h * 128:qk * 512 + (h + 1) * 128],
                                    rhs=hTj(kk), start=(kk == 0), stop=(kk == 7)), r=[hT, win], w=[ps])
                        if qk == 0:
                            k.op("act", lambda a: a.mul(out=rec[:, 0:512], in_=ps[:, :], mul=128 ** -0.5), r=[ps], w=[rec])
                        else:
                            k.op("dve", lambda v: v.tensor_copy(out=rec[:, 512:1024], in_=ps[:, :]), r=[ps], w=[rec])
                    ps = ps8.next()
                    for d in range(2):
                        for kk in range(8):
                            k.op("pe", lambda pe: pe.matmul(
                                ps[0:16, d * 128:(d + 1) * 128], lhsT=win[:, kk, 3072 + 16 * d:3088 + 16 * d],
                                rhs=hTj(kk), start=(kk == 0), stop=(kk == 7)), r=[hT, win], w=[ps])
                    k.op("act", lambda a: a.copy(out=rT[:, :, :], in_=ps[0:16, 0:256].rearrange("p (d t) -> p d t", d=2)),
                         r=[ps], w=[rT])
                    for d in range(2):
                        ps = ps8.next()
                        k.op("pe", lambda pe: pe.matmul(ps[:, :], lhsT=rT[:, d, :], rhs=wg[:, d, :], start=True, stop=True),
                             r=[rT, wg], w=[ps])
                        k.op("dve", lambda v: v.tensor_tensor(out=zt[:, :], in0=ps[:, :], in1=bg[:, d, :], op=ALU.add),
                             r=[ps, bg], w=[zt])
                        k.op("act", lambda a: a.activation(out=zt[:, :], in_=zt[:, :], func=AF.Exp, scale=-1.0), r=[zt], w=[zt])
                        k.op("act", lambda a: a.activation(out=zt[:, :], in_=zt[:, :], func=AF.Ln, bias=onec[:, :]),
                             r=[zt, onec], w=[zt])
                        k.op("dve", lambda v: v.tensor_scalar(out=gg[:, d * 512:(d + 1) * 512], in0=zt[:, :],
                                                              scalar1=-1.0 / 16.0, scalar2=None, op0=ALU.mult),
                             r=[zt], w=[gg])
                    k.dma("pool", REC.t[t], rec[:, :], r=[rec], w=[recT[t]])
                    k.dma("pool", GG.t[t], gg[:, :], r=[gg], w=[ggT[t]])
            k.barrier()

        with ExitStack() as es:
            self.epsb = k.sb(es, [128, 1], F32)
            k.op("pool", lambda g: g.memset(self.epsb[:, :], EPS), w=[self.epsb])
            tri = k.sb(es, [128, 4, 128], F32)
            k.dma("sp", tri[:, :, :], self.tri_d.t.rearrange("m j i -> j m i"), w=[tri])
            S = [k.sb(es, [128, 256], F32) for _ in range(4)]
            Sb = [k.sb(es, [128, 256], BF16) for _ in range(4)]
            recr = Ring([k.sb(es, [128, 3584], BF16) for _ in range(3)])
            ggr = Ring([k.sb(es, [128, 1024], F32) for _ in range(3)])
            E1r = Ring([k.sb(es, [128, 512], F32) for _ in range(2)])
            E2r = Ring([k.sb(es, [128, 512], F32) for _ in range(2)])
            E3r = Ring([k.sb(es, [128, 512], F32) for _ in range(2)])
            qtr = Ring([k.sb(es, [128, 512], BF16) for _ in range(2)])
            ktr = Ring([k.sb(es, [128, 512], BF16) for _ in range(2)])
            khr = Ring([k.sb(es, [128, 512], BF16) for _ in range(2)])
            Amr = Ring([k.sb(es, [128, 512], BF16) for _ in range(2)])
            osb = Ring([k.sb(es, [128, 1024], F32) for _ in range(2)])
            stg = Ring([k.sb(es, [128, 1024], F32) for _ in range(2)])
            wo = k.sb(es, [128, 8, D], BF16)
            self.load_w(es, wo, wo, w_out, 8, 128, D, stg)
            onbc = k.sb(es, [128, 256], F32)
            k.dma("sp", onbc[:, :], self.W["gla_o_norm"].t[0, :].partition_broadcast(128), w=[onbc])
            ofr = Ring([k.sb(es, [128, 1024], F32) for _ in range(2)])
            xr = Ring([k.sb(es, [128, D], F32) for _ in range(2)])
            ogf = k.sb(es, [128, 1024], F32)
            ogb = k.sb(es, [128, 1024], BF16)
            ogT = k.sb(es, [128, 8, 128], BF16)
            junk = k.sb(es, [128, 256], BF16)
            ss4 = k.sb(es, [128, 4], F32)
            rs4 = k.sb(es, [128, 4], F32)
            tm = k.sb(es, [128, 512], F32)

            def stageA(d, rec, gg):
                cm = tri[:, 0 if d == 0 else 2, :]
                sm = tri[:, 1 if d == 0 else 3, :]
                g = lambda a_, b_: gg[:, d * 512 + a_:d * 512 + b_]
                E1, E2, E3 = E1r.next(), E2r.next(), E3r.next()
                qt, kt_, kh, Am = qtr.next(), ktr.next(), khr.next(), Amr.next()
                psA = ps8.next()
                for h in range(4):
                    k.op("pe", lambda pe: pe.matmul(psA[:, h * 128:(h + 1) * 128], lhsT=g(h * 128, (h + 1) * 128), rhs=cm,
                                                    start=True, stop=True), r=[gg, tri], w=[psA])
                psB = ps8.next()
                k.op("pe", lambda pe: pe.matmul(psB[:, :], lhsT=sm, rhs=g(0, 512), start=True, stop=True), r=[gg, tri], w=[psB])
                k.op("act", lambda a: a.activation(out=E1[:, :], in_=psA[:, :], func=AF.Exp), r=[psA], w=[E1])
                k.op("act", lambda a: a.activation(out=E2[:, :], in_=psA[:, :], func=AF.Exp, scale=-1.0), r=[psA], w=[E2])
                k.op("act", lambda a: a.activation(out=E3[:, :], in_=psB[:, :], func=AF.Exp), r=[psB], w=[E3])
                k.op("dve", lambda v: v.tensor_tensor(out=qt[:, :], in0=rec[:, 0:512], in1=E1[:, :], op=ALU.mult), r=[rec, E1], w=[qt])
                k.op("pool", lambda v: v.tensor_tensor(out=kt_[:, :], in0=rec[:, 512:1024], in1=E2[:, :], op=ALU.mult), r=[rec, E2], w=[kt_])
                k.op("pool", lambda v: v.tensor_tensor(out=kh[:, :], in0=rec[:, 1024:1536], in1=E3[:, :], op=ALU.mult), r=[rec, E3], w=[kh])
                return dict(d=d, rec=rec, E1=E1, qt=qt, kh=kh, Am=Am, kt_=kt_, cm=cm)

            def stageA2(c):
                qt, kt_, Am, cm = c["qt"], c["kt_"], c["Am"], c["cm"]
                psD = ps8.next()
                for h in range(4):
                    hs = slice(h * 128, (h + 1) * 128)
                    k.op("pe", lambda pe: pe.matmul(psD[:, hs], lhsT=kt_[:, hs], rhs=qt[:, hs], start=True, stop=True),
                         r=[kt_, qt], w=[psD])
                k.op("dve", lambda v: v.tensor_tensor(out=Am[:, :].rearrange("p (h i) -> p h i", h=4),
                                                      in0=psD[:, :].rearrange("p (h i) -> p h i", h=4),
                                                      in1=cm.unsqueeze(1).broadcast_to([128, 4, 128]), op=ALU.mult),
                     r=[psD, tri], w=[Am])

            def stageB(c):
                d, rec, E1, qt, kh, Am = (c[n] for n in ("d", "rec", "E1", "qt", "kh", "Am"))
                ecol = 127 if d == 0 else 0
                po = [ps8.next(), ps8.next()]
                for h in range(4):
                    hs = slice(h * 128, (h + 1) * 128)
                    bank = po[h // 2]
                    cs = slice((h % 2) * 256, (h % 2) * 256 + 256)
                    vs = slice(1536 + h * 256, 1536 + (h + 1) * 256)
                    k.op("pe", lambda pe: pe.matmul(bank[:, cs], lhsT=qt[:, hs], rhs=Sb[h][:, :], start=True, stop=False),
                         r=[qt, Sb[h]], w=[bank])
                    k.op("pe", lambda pe: pe.matmul(bank[:, cs], lhsT=Am[:, hs], rhs=rec[:, vs], start=False, stop=True),
                         r=[Am, rec], w=[bank])
                pss = [ps8.next(), ps8.next()]
                for h in range(4):
                    hs = slice(h * 128, (h + 1) * 128)
                    bank = pss[h // 2]
                    cs = slice((h % 2) * 256, (h % 2) * 256 + 256)
                    vs = slice(1536 + h * 256, 1536 + (h + 1) * 256)
                    k.op("pe", lambda pe: pe.matmul(bank[:, cs], lhsT=kh[:, hs], rhs=rec[:, vs], start=True, stop=True),
                         r=[kh, rec], w=[bank])
                    k.op("dve", lambda v: v.scalar_tensor_tensor(out=S[h][:, :], in0=S[h][:, :],
                                                                 scalar=E1[:, h * 128 + ecol:h * 128 + ecol + 1],
                                                                 in1=bank[:, cs], op0=ALU.mult, op1=ALU.add),
                         r=[S[h], E1, bank], w=[S[h]])
                    k.op("act", lambda gp: gp.copy(out=Sb[h][:, :], in_=S[h][:, :]), r=[S[h]], w=[Sb[h]])
                return po

            def reset_state():
                for h in range(4):
                    k.op("pool", lambda gp: gp.memset(S[h][:, :], 0.0), w=[S[h]])
                    k.op("pool", lambda gp: gp.memset(Sb[h][:, :], 0.0), w=[Sb[h]])

            def gl_loads(t):
                rec = recr.next()
                gg = ggr.next()
                k.dma("sp", rec[:, :], REC.t[t], r=[recT[t]], w=[rec])
                k.dma("sp", gg[:, :], GG.t[t], r=[ggT[t]], w=[gg])
                return rec, gg

            reset_state()
            cnext = stageA(0, *gl_loads(0))
            stageA2(cnext)
            for t in range(NT):
                c = cnext
                if t + 1 < NT:
                    cnext = stageA(0, *gl_loads(t + 1))
                po = stageB(c)
                if t + 1 < NT:
                    stageA2(cnext)
                ob = osb.next()
                for c in range(2):
                    k.op("act", lambda a: a.copy(out=ob[:, c * 512:(c + 1) * 512], in_=po[c][:, :]), r=[po[c]], w=[ob])
                k.dma("pool", GOF.t[t], ob[:, :], r=[ob], w=[ofT[t]])
            reset_state()
            order = [1, 0] + list(range(NT - 1, 1, -1))
            cnext = stageA(1, *gl_loads(order[0]))
            stageA2(cnext)
            osr = Ring([k.sb(es, [128, 1024], F32) for _ in range(2)])

            def gl_out(t, rec, osum):
                cond = 0 if t < 2 else 1
                bi = 0 if t < 2 else 1 + (t - 2) // 4
                xt = xr.next()
                sap, rr = self.xsrc_ap(xsrc, t * 128, 128)
                k.dma("sp", xt[:, :], sap, r=(rr or [self.xblk[bi]]), w=[xt])
                for h in range(4):
                    k.op("act", lambda a: a.activation(out=junk[:, :], in_=osum[:, h * 256:(h + 1) * 256], func=AF.Square,
                                                       accum_out=ss4[:, h:h + 1]), r=[osum], w=[junk, ss4])
                k.op("act", lambda a: a.activation(out=rs4[:, :], in_=ss4[:, :], func=AF.Sqrt, scale=1.0 / 256, bias=self.epsb[:, :]),
                     r=[ss4, self.epsb], w=[rs4])
                k.op("dve", lambda v: v.reciprocal(out=rs4[:, :], in_=rs4[:, :]), r=[rs4], w=[rs4])
                for h in range(4):
                    k.op("dve", lambda v: v.scalar_tensor_tensor(out=ogf[:, h * 256:(h + 1) * 256], in0=osum[:, h * 256:(h + 1) * 256],
                                                                 scalar=rs4[:, h:h + 1], in1=onbc[:, :], op0=ALU.mult, op1=ALU.mult),
                         r=[osum, rs4, onbc], w=[ogf])
                k.op("dve", lambda v: v.tensor_tensor(out=ogb[:, :], in0=ogf[:, :], in1=rec[:, 2560:3584], op=ALU.mult),
                     r=[ogf, rec], w=[ogb])
                self.tok_outproj(ogb, 8, ogT, wo, xt, tm, cond)
                k.dma("pool", self.xres.t[t * 128:(t + 1) * 128, :], xt[:, :], r=[xt], w=[self.xblk[bi]])

            pending = None
            for oi, t in enumerate(order):
                c = cnext
                rec = c["rec"]
                if oi + 1 < NT:
                    cnext = stageA(1, *gl_loads(order[oi + 1]))
                of = ofr.next()
                k.dma("sp", of[:, :], GOF.t[t], r=[ofT[t]], w=[of])
                po = stageB(c)
                if oi + 1 < NT:
                    stageA2(cnext)
                osum = osr.next()
                for cc in range(2):
                    k.op("dve", lambda v: v.tensor_tensor(out=osum[:, cc * 512:(cc + 1) * 512], in0=po[cc][:, :],
                                                          in1=of[:, cc * 512:(cc + 1) * 512], op=ALU.add),
                         r=[po[cc], of], w=[osum])
                if pending is not None:
                    gl_out(*pending)
                pending = (t, rec, osum)
            gl_out(*pending)
            k.barrier()


    def norm_rows(self, src, rtrk, n, cond, dst, scr, xt):
        k = self.k
        junk, ss, rs, hf, hb = scr
        k.dma("sp", xt[0:n, :], src, r=rtrk, w=[xt])
        k.op("act", lambda a: a.activation(out=junk[0:n, :], in_=xt[0:n, :], func=AF.Square, accum_out=ss[0:n, :]),
             r=[xt], w=[junk, ss])
        k.op("act", lambda a: a.activation(out=rs[0:n, :], in_=ss[0:n, :], func=AF.Sqrt, scale=1.0 / D, bias=self.epsb[0:n, :]),
             r=[ss, self.epsb], w=[rs])
        k.op("dve", lambda v: v.reciprocal(out=rs[0:n, :], in_=rs[0:n, :]), r=[rs], w=[rs])
        k.op("dve", lambda v: v.scalar_tensor_tensor(out=hf[0:n, :], in0=xt[0:n, :], scalar=rs[0:n, 0:1],
                                                     in1=self.gmod[cond][0:n, :], op0=ALU.mult, op1=ALU.mult),
             r=[xt, rs, self.gmod[cond]], w=[hf])
        k.op("dve", lambda v: v.tensor_tensor(out=hb[0:n, :], in0=hf[0:n, :], in1=self.bcm[cond][0:n, 0:D], op=ALU.add),
             r=[hf, self.bcm[cond]], w=[hb])
        ps = self.ps8.next()
        pv = ps[:, :].bitcast(BF16).rearrange("p (c t) -> p c t", c=8)
        for c in range(8):
            k.op("pe", lambda pe: pe.transpose(out=pv[:, c, 0:n], in_=hb[0:n, c * 128:(c + 1) * 128],
                                               identity=self.identb[0:n, 0:n]), r=[hb, self.identb], w=[ps])
        return ps, pv

    def layer_ssd(self, L, xsrc):
        k = self.k
        ps8 = self.ps8
        w_in = self.W["ssd_w_in"].t[0]
        w_out = self.W["ssd_w_out"].t[0]
        RX = k.dram([NT, 128, 2560], BF16, f"SRX{L}")
        RZ = k.dram([NT, 128, 2048], BF16, f"SRZ{L}")
        RBC = k.dram([NT, 128, 1024], BF16, f"SRBC{L}")
        DD = k.dram([NT, 128, 128], F32, f"SDD{L}")
        YF = k.dram([NT, 128, 2048], F32, f"SYF{L}")
        rxT = [TT(RX.t) for _ in range(NT)]
        rzT = [TT(RZ.t) for _ in range(NT)]
        rbcT = [TT(RBC.t) for _ in range(NB)]
        ddT = [TT(DD.t) for _ in range(NT)]
        yfT = [TT(YF.t) for _ in range(NT)]
        with ExitStack() as es:
            self.epsb = k.sb(es, [128, 1], F32)
            k.op("pool", lambda g: g.memset(self.epsb[:, :], EPS), w=[self.epsb])
            onec = k.sb(es, [128, 1], F32)
            k.op("pool", lambda g: g.memset(onec[:, :], 1.0), w=[onec])
            wz = k.sb(es, [128, 8, 2048], BF16)
            wx = k.sb(es, [128, 8, 3072], BF16)
            wdt = k.sb(es, [128, 8, 64], BF16)
            with ExitStack() as es2:
                stg = Ring([k.sb(es2, [128, 1024], F32) for _ in range(2)])
                self.load_w(es2, wz, wz, w_in[:, 0:2048], 8, 128, 2048, stg)
                self.load_w(es2, wx, wx, w_in[:, 2048:5120], 8, 128, 3072, stg)
                self.load_w(es2, wdt, wdt, w_in[:, 5120:5184], 8, 128, 64, stg)
                k.barrier()
            cw = k.sb(es, [128, 24, 5], F32)
            for kk in range(5):
                k.dma("sp", cw[:, :, kk], self.W["ssd_conv_w"].t[0, kk, :].rearrange("(c p) -> p c", p=128), w=[cw],
                      allow_slow_non_contiguous=True)
            cbias = k.sb(es, [128, 24], F32)
            k.dma("sp", cbias[:, :], self.W["ssd_conv_b"].t[0, :].rearrange("(c p) -> p c", p=128), w=[cbias],
                  allow_slow_non_contiguous=True)
            dtb = k.sb(es, [128, 64], F32)
            k.dma("sp", dtb[:, 0:32], self.W["ssd_dt_bias_f"].t[0, :].partition_broadcast(128), w=[dtb])
            k.dma("sp", dtb[:, 32:64], self.W["ssd_dt_bias_b"].t[0, :].partition_broadcast(128), w=[dtb])
            abc = k.sb(es, [128, 64], F32)
            k.dma("sp", abc[:, 0:32], self.W["ssd_a_log_f"].t[0, :].partition_broadcast(128), w=[abc])
            k.dma("sp", abc[:, 32:64], self.W["ssd_a_log_b"].t[0, :].partition_broadcast(128), w=[abc])
            k.op("act", lambda a: a.activation(out=abc[:, :], in_=abc[:, :], func=AF.Exp), r=[abc], w=[abc])
            k.op("dve", lambda v: v.tensor_scalar(out=abc[:, :], in0=abc[:, :], scalar1=-1.0, scalar2=None, op0=ALU.mult),
                 r=[abc], w=[abc])
            xr = Ring([k.sb(es, [128, D], F32) for _ in range(4)])
            hT = k.sb(es, [128, 8, 516], BF16)
            scr = (k.sb(es, [128, D], BF16), k.sb(es, [128, 1], F32), k.sb(es, [128, 1], F32),
                   k.sb(es, [128, D], F32), k.sb(es, [128, D], BF16))
            prer = Ring([k.sb(es, [128, 516], BF16) for _ in range(3)])
            accr = Ring([k.sb(es, [128, 512], F32) for _ in range(2)])
            xc = k.sb(es, [128, 24, 512], BF16)
            rxr = Ring([k.sb(es, [128, 2560], BF16) for _ in range(1)])
            rzr = Ring([k.sb(es, [128, 2048], BF16) for _ in range(1)])
            ddr = Ring([k.sb(es, [128, 128], F32) for _ in range(2)])
            dtt = k.sb(es, [128, 64], F32)
            for bi, (t0, nt, cond) in enumerate(BLOCKS):
                ntile = nt // 128
                self.norm_block(es, bi, xsrc, xr, hT, scr)
                for side in range(2):
                    col = 512 + 2 * side
                    has = (bi >= 2) if side == 0 else (1 <= bi < NB - 1)
                    if not has:
                        k.op("pool", lambda g: g.memset(hT[:, :, col:col + 2], 0.0), w=[hT])
                    else:
                        r0 = t0 - 2 if side == 0 else t0 + nt
                        nb_ = bi - 1 if side == 0 else bi + 1
                        src, rr = self.xsrc_ap(xsrc, r0, 2)
                        ps, pv = self.norm_rows(src, rr or [self.xblk[nb_]], 2, cond, None, scr, xr.next())
                        k.op("act", lambda a: a.copy(out=hT[:, :, col:col + 2], in_=pv[:, :, 0:2]), r=[ps], w=[hT])
                for c in range(24):
                    ps = ps8.next()
                    for kk in range(8):
                        k.op("pe", lambda pe: pe.matmul(ps[:, 0:nt], lhsT=wx[:, kk, c * 128:(c + 1) * 128], rhs=hT[:, kk, 0:nt],
                                                        start=(kk == 0), stop=(kk == 7)), r=[wx, hT], w=[ps])
                    ps2 = ps8.next()
                    for kk in range(8):
                        k.op("pe", lambda pe: pe.matmul(ps2[:, 0:4], lhsT=wx[:, kk, c * 128:(c + 1) * 128], rhs=hT[:, kk, 512:516],
                                                        start=(kk == 0), stop=(kk == 7)), r=[wx, hT], w=[ps2])
                    pre = prer.next()
                    k.op("act", lambda a: a.copy(out=pre[:, 2:2 + nt], in_=ps[:, 0:nt]), r=[ps], w=[pre])
                    k.op("act", lambda a: a.copy(out=pre[:, 0:2], in_=ps2[:, 0:2]), r=[ps2], w=[pre])
                    k.op("act", lambda a: a.copy(out=pre[:, 2 + nt:4 + nt], in_=ps2[:, 2:4]), r=[ps2], w=[pre])
                    acc = accr.next()
                    k.op("dve", lambda v: v.tensor_scalar(out=acc[:, 0:nt], in0=pre[:, 0:nt], scalar1=cw[:, c, 0:1], scalar2=None,
                                                          op0=ALU.mult), r=[pre, cw], w=[acc])
                    for kk in range(1, 5):
                        k.op("dve", lambda v: v.scalar_tensor_tensor(out=acc[:, 0:nt], in0=pre[:, kk:kk + nt], scalar=cw[:, c, kk:kk + 1],
                                                                     in1=acc[:, 0:nt], op0=ALU.mult, op1=ALU.add),
                             r=[pre, cw, acc], w=[acc])
                    k.op("act", lambda a: a.activation(out=xc[:, c, 0:nt], in_=acc[:, 0:nt], func=AF.Silu, bias=cbias[:, c:c + 1]),
                         r=[acc, cbias], w=[xc])
                kt0 = t0 // 128
                for j in range(ntile):
                    for s_ in range(2):
                        k.dma("sp", RBC.t[kt0 + j, :, s_ * 512:(s_ + 1) * 512].rearrange("p (g t) -> p g t", g=4),
                              xc[:, 16 + 4 * s_:20 + 4 * s_, j * 128:(j + 1) * 128], r=[xc], w=[rbcT[bi]])
                for j in range(ntile):
                    t = kt0 + j
                    rx = rxr.next()
                    for c0 in (0, 8, 16):
                        cn = 8 if c0 < 16 else 4
                        ps = ps8.next()
                        pv = ps[:, :].bitcast(BF16).rearrange("p (c t) -> p c t", c=8)
                        for c in range(cn):
                            k.op("pe", lambda pe: pe.transpose(out=pv[:, c, :], in_=xc[:, c0 + c, j * 128:(j + 1) * 128],
                                                               identity=self.identb[:, :]), r=[xc, self.identb], w=[ps])
                        k.op("act" if c0 != 8 else "dve",
                             (lambda a: a.copy(out=rx[:, c0 * 128:(c0 + cn) * 128].rearrange("p (c t) -> p c t", c=cn), in_=pv[:, 0:cn, :]))
                             if c0 != 8 else
                             (lambda v: v.tensor_copy(out=rx[:, c0 * 128:(c0 + cn) * 128].rearrange("p (c t) -> p c t", c=cn), in_=pv[:, 0:cn, :])),
                             r=[ps], w=[rx])
                    k.dma("pool", RX.t[t], rx[:, :], r=[rx], w=[rxT[t]])
                    rz = rzr.next()
                    for cb in range(4):
                        ps = ps8.next()
                        for kk in range(8):
                            k.op("pe", lambda pe: pe.matmul(ps[:, :], lhsT=hT[:, kk, j * 128:(j + 1) * 128],
                                                            rhs=wz[:, kk, cb * 512:(cb + 1) * 512],
                                                            start=(kk == 0), stop=(kk == 7)), r=[hT, wz], w=[ps])
                        k.op("act", lambda a: a.activation(out=rz[:, cb * 512:(cb + 1) * 512], in_=ps[:, :], func=AF.Silu),
                             r=[ps], w=[rz])
                    k.dma("pool", RZ.t[t], rz[:, :], r=[rz], w=[rzT[t]])
                    dd = ddr.next()
                    ps = ps8.next()
                    for kk in range(8):
                        k.op("pe", lambda pe: pe.matmul(ps[:, 0:64], lhsT=hT[:, kk, j * 128:(j + 1) * 128], rhs=wdt[:, kk, :],
                                                        start=(kk == 0), stop=(kk == 7)), r=[hT, wdt], w=[ps])
                    k.op("dve", lambda v: v.tensor_tensor(out=dtt[:, :], in0=ps[:, 0:64], in1=dtb[:, :], op=ALU.add),
                         r=[ps, dtb], w=[dtt])
                    k.op("act", lambda a: a.activation(out=dtt[:, :], in_=dtt[:, :], func=AF.Exp), r=[dtt], w=[dtt])
                    k.op("act", lambda a: a.activation(out=dd[:, 0:64], in_=dtt[:, :], func=AF.Ln, bias=onec[:, :]),
                         r=[dtt, onec], w=[dd])
                    k.op("dve", lambda v: v.tensor_tensor(out=dd[:, 64:128], in0=dd[:, 0:64], in1=abc[:, :], op=ALU.mult),
                         r=[dd, abc], w=[dd])
                    k.dma("pool", DD.t[t], dd[:, :], r=[dd], w=[ddT[t]])
            k.barrier()

        with ExitStack() as es:
            self.epsb = k.sb(es, [128, 1], F32)
            k.op("pool", lambda g: g.memset(self.epsb[:, :], EPS), w=[self.epsb])
            tri = k.sb(es, [128, 4, 128], F32)
            k.dma("sp", tri[:, :, :], self.tri_d.t.rearrange("m j i -> j m i"), w=[tri])
            negm = k.sb(es, [128, 2, 128], F32)
            k.dma("sp", negm[:, :, :], self.negm_d.t.rearrange("m j i -> j m i"), w=[negm])
            onehot = k.sb(es, [96, 32, 128], BF16)
            k.dma("sp", onehot[:, :, :], self.onehot_d.t, w=[onehot])
            negmb = k.sb(es, [128, 2, 128], BF16)
            k.dma("sp", negmb[:, :, :], self.negmb_d.t.rearrange("m j i -> j m i"), w=[negmb])
            identf = k.sb(es, [128, 128], F32)
            k.dma("sp", identf[:, :], self.identf_d.t, w=[identf])
            Hs = [k.sb(es, [128, 512], F32) for _ in range(4)]
            Hb = [k.sb(es, [128, 512], BF16) for _ in range(4)]
            rxr = Ring([k.sb(es, [128, 2560], BF16) for _ in range(3)])
            rbcr = Ring([k.sb(es, [128, 1024], BF16) for _ in range(2)])
            ddr = Ring([k.sb(es, [128, 128], F32) for _ in range(3)])
            bankY = Ring(self.psb[0:2])
            bankLb = Ring(self.psb[2:5])
            bankM = Ring(self.psb[5:8])
            def ring2(shape, dt):
                return Ring([k.sb(es, shape, dt) for _ in range(2)])
            ela_r = ring2([128, 32], F32)
            edl_r = ring2([128, 32], F32)
            etot_r = ring2([128, 32], F32)
            laT_r = ring2([96, 128], BF16)
            larep_r = ring2([128, 96], F32)
            Abf_r = ring2([96, 128], BF16)
            Bbf_r = ring2([96, 128], BF16)
            R1_r = ring2([96, 128], F32)
            nla_r = ring2([128, 32], F32)
            xdt_r = ring2([128, 2048], BF16)
            xdl_r = ring2([128, 2048], BF16)
            CBm_r = ring2([128, 4, 128], BF16)
            Eh_r = Ring([k.sb(es, [128, 4, 128], BF16) for _ in range(3)])
            Mh = Ring([k.sb(es, [128, 4, 128], BF16) for _ in range(3)])
            ysr = Ring([k.sb(es, [128, 2048], F32) for _ in range(3)])
            wo = k.sb(es, [128, 16, D], BF16)
            with ExitStack() as es2:
                stg = Ring([k.sb(es2, [128, 1024], F32) for _ in range(2)])
                self.load_w(es2, wo, wo, w_out, 16, 128, D, stg)
                k.barrier()
            gnbc = k.sb(es, [128, 2048], F32)
            k.dma("sp", gnbc[:, :], self.W["ssd_norm"].t[0, :].partition_broadcast(128), w=[gnbc])
            dsk = k.sb(es, [128, 32], F32)
            k.dma("sp", dsk[:, :], self.W["ssd_d"].t[0, :].partition_broadcast(128), w=[dsk])
            rzr = Ring([k.sb(es, [128, 2048], BF16) for _ in range(1)])
            yfr = Ring([k.sb(es, [128, 2048], F32) for _ in range(1)])
            xr = Ring([k.sb(es, [128, D], F32) for _ in range(2)])
            ynb = k.sb(es, [128, 2048], BF16)
            ogT = k.sb(es, [128, 16, 128], BF16)
            junk = k.sb(es, [128, 512], BF16)
            ss4 = k.sb(es, [128, 4], F32)
            rs4 = k.sb(es, [128, 4], F32)
            tm = k.sb(es, [128, 512], F32)

            def stageA(d, rx, rbc, dd):
                c = dict(d=d, rx=rx, rbc=rbc, dd=dd)
                cm = tri[:, 0 if d == 0 else 2, :]
                sm = tri[:, 1 if d == 0 else 3, :]
                dA = dd[:, 64 + 32 * d:96 + 32 * d]
                dt = dd[:, 32 * d:32 * d + 32]
                larep = larep_r.next()
                la_sb = larep
                ela, edl, etot, laT = ela_r.next(), edl_r.next(), etot_r.next(), laT_r.next()
                Abf, Bbf, R1 = Abf_r.next(), Bbf_r.next(), R1_r.next()
                xdt, xdl, CBm = xdt_r.next(), xdl_r.next(), CBm_r.next()
                nla = nla_r.next()
                c.update(la_sb=la_sb, ela=ela, etot=etot, laT=laT, xdt=xdt, xdl=xdl, CBm=CBm, nla=nla)
                psL = bankM.next()
                k.op("pe", lambda pe: pe.matmul(psL[:, 0:32], lhsT=cm, rhs=dA, start=True, stop=True), r=[tri, dd], w=[psL])
                k.op("pe", lambda pe: pe.matmul(psL[:, 32:64], lhsT=sm, rhs=dA, start=True, stop=True), r=[tri, dd], w=[psL])
                k.op("pe", lambda pe: pe.matmul(psL[:, 64:96], lhsT=self.ones_f[:, :], rhs=dA, start=True, stop=True),
                     r=[self.ones_f, dd], w=[psL])
                for rep in range(3):
                    k.op("act", lambda a: a.copy(out=larep[:, rep * 32:(rep + 1) * 32], in_=psL[:, 0:32]), r=[psL], w=[larep])
                k.op("act", lambda a: a.mul(out=nla[:, :], in_=psL[:, 0:32], mul=-1.0), r=[psL], w=[nla])
                k.op("act", lambda a: a.activation(out=ela[:, :], in_=psL[:, 0:32], func=AF.Exp), r=[psL], w=[ela])
                k.op("act", lambda a: a.activation(out=edl[:, :], in_=psL[:, 32:64], func=AF.Exp), r=[psL], w=[edl])
                k.op("act", lambda a: a.activation(out=etot[:, :], in_=psL[:, 64:96], func=AF.Exp), r=[psL], w=[etot])
                xs3 = rx[:, 0:2048].rearrange("p (h e) -> p h e", h=32)
                k.op("pool", lambda g: g.tensor_tensor(out=xdt[:, :].rearrange("p (h e) -> p h e", h=32), in0=xs3,
                                                       in1=dt.unsqueeze(2).broadcast_to([128, 32, 64]), op=ALU.mult),
                     r=[rx, dd], w=[xdt])
                k.op("pool", lambda g: g.tensor_tensor(out=xdl[:, :].rearrange("p (h e) -> p h e", h=32),
                                                       in0=xdt[:, :].rearrange("p (h e) -> p h e", h=32),
                                                       in1=edl[:, :].unsqueeze(2).broadcast_to([128, 32, 64]), op=ALU.mult),
                     r=[xdt, edl], w=[xdl])
                psT = bankM.next()
                k.op("pe", lambda pe: pe.transpose(out=psT[0:96, 0:128], in_=larep[:, 0:96], identity=identf[:, :]),
                     r=[larep, identf], w=[psT])
                k.op("act", lambda a: a.copy(out=Abf[:, :], in_=psT[0:96, 0:128]), r=[psT], w=[Abf])
                k.op("dve", lambda v: v.tensor_tensor(out=R1[:, :], in0=psT[0:96, 0:128], in1=Abf[:, :], op=ALU.subtract),
                     r=[psT, Abf], w=[R1])
                k.op("act", lambda a: a.copy(out=Bbf[:, :], in_=R1[:, :]), r=[R1], w=[Bbf])
                k.op("dve", lambda v: v.tensor_tensor(out=R1[:, :], in0=R1[:, :], in1=Bbf[:, :], op=ALU.subtract),
                     r=[R1, Bbf], w=[R1])
                k.op("pool", lambda g: g.tensor_copy(out=laT[0:32, :], in_=Abf[0:32, :]), r=[Abf], w=[laT])
                k.op("pool", lambda g: g.tensor_copy(out=laT[32:64, :], in_=Bbf[32:64, :]), r=[Bbf], w=[laT])
                k.op("pool", lambda g: g.tensor_copy(out=laT[64:96, :], in_=R1[64:96, :]), r=[R1], w=[laT])
                psCB = bankM.next()
                for g in range(4):
                    k.op("pe", lambda pe: pe.matmul(psCB[:, g * 128:(g + 1) * 128], lhsT=rbc[:, g * 128:(g + 1) * 128],
                                                    rhs=rbc[:, 512 + g * 128:512 + (g + 1) * 128], start=True, stop=True),
                         r=[rbc], w=[psCB])
                k.op("dve", lambda v: v.tensor_tensor(out=CBm[:, :, :], in0=psCB[:, :].rearrange("p (g i) -> p g i", g=4),
                                                      in1=cm.unsqueeze(1).broadcast_to([128, 4, 128]), op=ALU.mult),
                     r=[psCB, tri], w=[CBm])
                return c

            def stageB(c, ys):
                d, rx, rbc = c["d"], c["rx"], c["rbc"]
                la_sb, ela, etot, laT, xdt, xdl, CBm = (c[n] for n in ("la_sb", "ela", "etot", "laT", "xdt", "xdl", "CBm"))
                ng = negm[:, d, :]
                nla = c["nla"]

                def lb_mm(i):
                    h0 = i * 4
                    psLb = bankLb.next()
                    for hh in range(4):
                        k.op("pe", lambda pe: pe.matmul(psLb[:, hh * 128:(hh + 1) * 128], lhsT=onehot[:, h0 + hh, :],
                                                        rhs=laT[:, :], start=True, stop=False), r=[onehot, laT], w=[psLb])
                        k.op("pe", lambda pe: pe.matmul(psLb[:, hh * 128:(hh + 1) * 128], lhsT=self.identb[:, :],
                                                        rhs=negmb[:, d, :], start=False, stop=True), r=[self.identb, negmb], w=[psLb])
                    return psLb

                pend = [lb_mm(0), lb_mm(1)]

                def heads(g):
                    psY = bankY.next()
                    for hq in range(2):
                        i = g * 2 + hq
                        h0 = i * 4
                        psLb = pend.pop(0)
                        if i + 2 < 8:
                            pend.append(lb_mm(i + 2))
                        Eh = Eh_r.next()
                        for hh in range(4):
                            k.op("act", lambda a: a.activation(out=Eh[:, hh, :], in_=psLb[:, hh * 128:(hh + 1) * 128], func=AF.Exp,
                                                               bias=nla[:, h0 + hh:h0 + hh + 1]), r=[psLb, nla], w=[Eh])
                        mh = Mh.next()
                        k.op("dve", lambda v: v.tensor_tensor(out=mh[:, :, :], in0=Eh[:, :, :],
                                                              in1=CBm[:, g:g + 1, :].broadcast_to([128, 4, 128]), op=ALU.mult),
                             r=[Eh, CBm], w=[mh])
                        for hh in range(4):
                            h = h0 + hh
                            k.op("pe", lambda pe: pe.matmul(psY[:, (h % 8) * 64:(h % 8 + 1) * 64], lhsT=mh[:, hh, :],
                                                            rhs=xdt[:, h * 64:(h + 1) * 64], start=True, stop=True),
                                 r=[mh, xdt], w=[psY])
                    return psY

                def tail(g, psY):
                    psYi = bankM.next()
                    k.op("pe", lambda pe: pe.matmul(psYi[:, :], lhsT=rbc[:, 512 + g * 128:512 + (g + 1) * 128], rhs=Hb[g][:, :],
                                                    start=True, stop=True), r=[rbc, Hb[g]], w=[psYi])
                    yv = ys[:, g * 512:(g + 1) * 512]
                    k.op("dve", lambda v: v.tensor_tensor(out=yv.rearrange("p (h e) -> p h e", h=8),
                                                          in0=psYi[:, :].rearrange("p (h e) -> p h e", h=8),
                                                          in1=ela[:, g * 8:(g + 1) * 8].unsqueeze(2).broadcast_to([128, 8, 64]),
                                                          op=ALU.mult), r=[psYi, ela], w=[ys])
                    k.op("dve", lambda v: v.tensor_tensor(out=yv, in0=yv, in1=psY[:, :], op=ALU.add), r=[ys, psY], w=[ys])
                    psH = bankM.next()
                    k.op("pe", lambda pe: pe.matmul(psH[:, :], lhsT=rx[:, 2048 + g * 128:2048 + (g + 1) * 128],
                                                    rhs=xdl[:, g * 512:(g + 1) * 512], start=True, stop=True), r=[rx, xdl], w=[psH])
                    k.op("pool", lambda v: v.tensor_tensor(out=Hs[g][:, :].rearrange("p (h e) -> p h e", h=8),
                                                           in0=Hs[g][:, :].rearrange("p (h e) -> p h e", h=8),
                                                           in1=etot[:, g * 8:(g + 1) * 8].unsqueeze(2).broadcast_to([128, 8, 64]),
                                                           op=ALU.mult), r=[Hs[g], etot], w=[Hs[g]])
                    k.op("dve", lambda v: v.tensor_tensor(out=Hs[g][:, :], in0=Hs[g][:, :], in1=psH[:, :], op=ALU.add),
                         r=[Hs[g], psH], w=[Hs[g]])
                    k.op("pool", lambda gp: gp.tensor_copy(out=Hb[g][:, :], in_=Hs[g][:, :]), r=[Hs[g]], w=[Hb[g]])

                prev = None
                for g in range(4):
                    py = heads(g)
                    if prev is not None:
                        tail(*prev)
                    prev = (g, py)
                tail(*prev)

            def reset_state():
                for g in range(4):
                    k.op("pool", lambda gp: gp.memset(Hs[g][:, :], 0.0), w=[Hs[g]])
                    k.op("pool", lambda gp: gp.memset(Hb[g][:, :], 0.0), w=[Hb[g]])

            def loads(t):
                bi = 0 if t < 2 else 1 + (t - 2) // 4
                rx = rxr.next()
                rbc = rbcr.next()
                dd = ddr.next()
                k.dma("sp", rx[:, :], RX.t[t], r=[rxT[t]], w=[rx])
                k.dma("sp", rbc[:, :], RBC.t[t], r=[rbcT[bi]], w=[rbc])
                k.dma("sp", dd[:, :], DD.t[t], r=[ddT[t]], w=[dd])
                return rx, rbc, dd

            reset_state()
            cnext = stageA(0, *loads(0))
            for t in range(NT):
                c = cnext
                if t + 1 < NT:
                    cnext = stageA(0, *loads(t + 1))
                ys = ysr.next()
                stageB(c, ys)
                k.dma("pool", YF.t[t], ys[:, :], r=[ys], w=[yfT[t]])
            reset_state()
            order = [1, 0] + list(range(NT - 1, 1, -1))
            cnext = stageA(1, *loads(order[0]))

            def out_stage(t, rx, ys):
                cond = 0 if t < 2 else 1
                bi = 0 if t < 2 else 1 + (t - 2) // 4
                rz = rzr.next()
                yf = yfr.next()
                xt = xr.next()
                k.dma("sp", rz[:, :], RZ.t[t], r=[rzT[t]], w=[rz])
                k.dma("sp", yf[:, :], YF.t[t], r=[yfT[t]], w=[yf])
                sap, rr = self.xsrc_ap(xsrc, t * 128, 128)
                k.dma("sp", xt[:, :], sap, r=(rr or [self.xblk[bi]]), w=[xt])
                k.op("dve", lambda v: v.tensor_tensor(out=ys[:, :], in0=ys[:, :], in1=yf[:, :], op=ALU.add), r=[ys, yf], w=[ys])
                ytmp = yf
                k.op("pool", lambda g: g.tensor_tensor(out=ytmp[:, :].rearrange("p (h e) -> p h e", h=32),
                                                       in0=rx[:, 0:2048].rearrange("p (h e) -> p h e", h=32),
                                                       in1=dsk[:, :].unsqueeze(2).broadcast_to([128, 32, 64]), op=ALU.mult),
                     r=[rx, dsk, yf], w=[ytmp])
                k.op("dve", lambda v: v.tensor_tensor(out=ys[:, :], in0=ys[:, :], in1=ytmp[:, :], op=ALU.add), r=[ys, ytmp], w=[ys])
                k.op("dve", lambda v: v.tensor_tensor(out=ys[:, :], in0=ys[:, :], in1=rz[:, :], op=ALU.mult), r=[ys, rz], w=[ys])
                for g in range(4):
                    k.op("act", lambda a: a.activation(out=junk[:, :], in_=ys[:, g * 512:(g + 1) * 512], func=AF.Square,
                                                       accum_out=ss4[:, g:g + 1]), r=[ys], w=[junk, ss4])
                k.op("act", lambda a: a.activation(out=rs4[:, :], in_=ss4[:, :], func=AF.Sqrt, scale=1.0 / 512, bias=self.epsb[:, :]),
                     r=[ss4, self.epsb], w=[rs4])
                k.op("dve", lambda v: v.reciprocal(out=rs4[:, :], in_=rs4[:, :]), r=[rs4], w=[rs4])
                for g in range(4):
                    k.op("dve", lambda v: v.scalar_tensor_tensor(out=ynb[:, g * 512:(g + 1) * 512], in0=ys[:, g * 512:(g + 1) * 512],
                                                                 scalar=rs4[:, g:g + 1], in1=gnbc[:, g * 512:(g + 1) * 512],
                                                                 op0=ALU.mult, op1=ALU.mult), r=[ys, rs4, gnbc], w=[ynb])
                self.tok_outproj(ynb, 16, ogT, wo, xt, tm, cond)
                k.dma("pool", self.xres.t[t * 128:(t + 1) * 128, :], xt[:, :], r=[xt], w=[self.xblk[bi]])

            pending_out = None
            for oi, t in enumerate(order):
                c = cnext
                if oi + 1 < NT:
                    cnext = stageA(1, *loads(order[oi + 1]))
                ys = ysr.next()
                stageB(c, ys)
                if pending_out is not None:
                    out_stage(*pending_out)
                pending_out = (t, c["rx"], ys)
            out_stage(*pending_out)
            k.barrier()

    def tok_outproj(self, ogb, kc, ogT, wo, xt, tm, cond):
        k = self.k
        for c0 in range(0, kc, 8):
            ps = self.ps8.next()
            pv = ps[:, :].bitcast(BF16).rearrange("p (c t) -> p c t", c=8)
            for c in range(8):
                k.op("pe", lambda pe: pe.transpose(out=pv[:, c, :], in_=ogb[:, (c0 + c) * 128:(c0 + c + 1) * 128],
                                                   identity=self.identb[:, :]), r=[ogb, self.identb], w=[ps])
            k.op("act", lambda a: a.copy(out=ogT[:, c0:c0 + 8, :], in_=pv), r=[ps], w=[ogT])
        for cb in range(2):
            ps = self.ps8.next()
            for kk in range(kc):
                k.op("pe", lambda pe: pe.matmul(ps[:, :], lhsT=ogT[:, kk, :], rhs=wo[:, kk, cb * 512:(cb + 1) * 512],
                                                start=(kk == 0), stop=(kk == kc - 1)), r=[ogT, wo], w=[ps])
            k.op("dve", lambda v: v.tensor_tensor(out=tm[:, :], in0=ps[:, :],
                                                  in1=self.bcm[cond][:, 2 * D + cb * 512:2 * D + (cb + 1) * 512], op=ALU.mult),
                 r=[ps, self.bcm[cond]], w=[tm])
            k.op("dve", lambda v: v.tensor_tensor(out=xt[:, cb * 512:(cb + 1) * 512], in0=tm[:, :],
                                                  in1=xt[:, cb * 512:(cb + 1) * 512], op=ALU.add), r=[tm, xt], w=[xt])

    def outproj(self, es, og_src, ogb, kp, kc, wo, xsrc, xr, tm):
        k = self.k
        for bi, (t0, nt, cond) in enumerate(BLOCKS):
            ob = ogb.next()
            src, trk = og_src(bi, nt)
            if kp == 128:
                s4 = src.rearrange("d (k two) t -> d two k t", two=2)
                for hp in range(2):
                    k.dma("sp", ob[hp * 64:(hp + 1) * 64, :, 0:nt], s4[:, hp, :, :], r=[trk], w=[ob])
            else:
                k.dma("sp", ob[:, :, 0:nt], src, r=[trk], w=[ob])
            for j in range(nt // 128):
                xt = xr.next()
                sap, rr = self.xsrc_ap(xsrc, t0 + j * 128, 128)
                k.dma("sp", xt[:, :], sap, r=(rr or [self.xblk[bi]]), w=[xt])
                import os
                for cb in range(2 if not os.environ.get("DBG_SKIPMM") else 0):
                    ps = self.psg.next()
                    for kk in range(kc):
                        k.op("pe", lambda pe, kk=kk, cb=cb, ps=ps: pe.matmul(
                            ps[:, :], lhsT=ob[0:kp, kk, j * 128:(j + 1) * 128], rhs=wo[0:kp, kk, cb * 512:(cb + 1) * 512],
                            start=(kk == 0), stop=(kk == kc - 1)), r=[ob, wo], w=[ps])
                    k.op("dve", lambda v, cb=cb, ps=ps: v.tensor_tensor(
                        out=tm[:, :], in0=ps[:, :], in1=self.bcm[cond][:, 2 * D + cb * 512:2 * D + (cb + 1) * 512], op=ALU.mult),
                        r=[ps, self.bcm[cond]], w=[tm])
                    k.op("dve", lambda v, cb=cb, xt=xt: v.tensor_tensor(
                        out=xt[:, cb * 512:(cb + 1) * 512], in0=tm[:, :], in1=xt[:, cb * 512:(cb + 1) * 512], op=ALU.add),
                        r=[tm, xt], w=[xt])
                k.dma("pool", self.xres.t[t0 + j * 128:t0 + (j + 1) * 128, :], xt[:, :], r=[xt], w=[self.xblk[bi]])

    def final(self, xsrc):
        k = self.k
        with ExitStack() as es:
            xr = Ring([k.sb(es, [128, D], F32) for _ in range(3)])
            junk = k.sb(es, [128, D], BF16)
            ss = k.sb(es, [128, 1], F32)
            rs = k.sb(es, [128, 1], F32)
            epsb = k.sb(es, [128, 1], F32)
            k.op("pool", lambda g: g.memset(epsb[:, :], EPS), w=[epsb])
            fg = k.sb(es, [128, D], F32)
            k.dma("sp", fg[:, :], self.final_g.t.partition_broadcast(128), w=[fg])
            for bi, (t0, nt, cond) in enumerate(BLOCKS):
                if cond == 0 and not self.debug_x:
                    continue
                for j in range(nt // 128):
                    xt = xr.next()
                    tt0 = t0 + j * 128
                    sap, rr = self.xsrc_ap(xsrc, tt0, 128)
                    k.dma("sp", xt[:, :], sap, r=(rr or [self.xblk[bi]]), w=[xt])
                    if self.debug_x:
                        k.dma("pool", self.out.t[tt0:tt0 + 128, :], xt[:, :], r=[xt], w=[self.out])
                        continue
                    k.op("act", lambda a, xt=xt: a.activation(out=junk[:, :], in_=xt[:, :], func=AF.Square, accum_out=ss[:, :]),
                         r=[xt], w=[junk, ss])
                    k.op("act", lambda a: a.activation(out=rs[:, :], in_=ss[:, :], func=AF.Sqrt, scale=1.0 / D, bias=epsb[:, :]),
                         r=[ss, epsb], w=[rs])
                    k.op("dve", lambda v: v.reciprocal(out=rs[:, :], in_=rs[:, :]), r=[rs], w=[rs])
                    k.op("dve", lambda v, xt=xt: v.scalar_tensor_tensor(out=xt[:, :], in0=xt[:, :], scalar=rs[:, 0:1],
                                                                       in1=fg[:, :], op0=ALU.mult, op1=ALU.mult),
                         r=[xt, rs, fg], w=[xt])
                    k.dma("pool", self.out.t[tt0 - CTX:tt0 - CTX + 128, :], xt[:, :], r=[xt], w=[self.out])


WSHAPES = {
    "mla_w_in": [1, 1024, 1696], "mla_q_norm": [1, 384], "mla_w_uq": [1, 384, 1536], "mla_kv_norm": [1, 256],
    "mla_w_ukv": [1, 256, 2048], "mla_w_out": [1, 1024, 1024],
    "gla_w_in": [1, 1024, 3104], "gla_w_gf": [1, 16, 512], "gla_b_gf": [1, 512], "gla_w_gb": [1, 16, 512],
    "gla_b_gb": [1, 512], "gla_o_norm": [1, 256], "gla_w_out": [1, 1024, 1024],
    "gqa_w_in": [1, 1024, 2560], "gqa_q_norm": [1, 64], "gqa_k_norm": [1, 64], "gqa_w_out": [1, 1024, 1024],
    "ssd_w_in": [1, 1024, 5184], "ssd_conv_w": [1, 5, 3072], "ssd_conv_b": [1, 3072], "ssd_dt_bias_f": [1, 32],
    "ssd_dt_bias_b": [1, 32], "ssd_a_log_f": [1, 32], "ssd_a_log_b": [1, 32], "ssd_d": [1, 32], "ssd_norm": [1, 2048],
    "ssd_w_out": [1, 2048, 1024],
}


def tri_consts():
    j = np.arange(128)[:, None]
    i = np.arange(128)[None, :]
    return np.stack([(j <= i), (j > i), (j >= i), (j < i)]).astype(np.float32)


def rope_tables(rd):
    hf = rd // 4
    inv = 10000.0 ** (-np.arange(hf, dtype=np.float64) / hf)
    p = np.arange(SEQ)
    row = (p // 64).astype(np.float64)[:, None] * inv[None, :]
    col = (p % 64).astype(np.float64)[:, None] * inv[None, :]
    cos = np.concatenate([np.cos(row), np.cos(row), np.cos(col), np.cos(col)], axis=1)
    sin = np.concatenate([-np.sin(row), np.sin(row), -np.sin(col), np.sin(col)], axis=1)
    tab = np.zeros((T, 2, rd), np.float32)
    tab[:CTX, 0, :] = 1.0
    tab[CTX:, 0, :] = cos
    tab[CTX:, 1, :] = sin
    return tab


def run(inputs, layers=(0, 1, 2, 3), debug_x=False, cores=(0, 1), stop=None):
    nc = bass.Bass("TRN2", target_bir_lowering=False)
    Prog(nc, layers=layers, debug_x=debug_x, stop=stop).build()
    f = lambda a: np.ascontiguousarray(np.asarray(a, dtype=np.float32))
    common = {nm: f(inputs[nm]) for nm in WSHAPES}
    for nm in ("ada_w", "ada_b", "norm_g", "final_g"):
        common[nm] = f(inputs[nm])
    common["ident_bf"] = np.eye(128, dtype=np.float32).astype(ml_dtypes.bfloat16)
    common["rope_mla"] = rope_tables(32)
    common["rope_gqa"] = rope_tables(64)
    common["tri"] = tri_consts()
    common["negm"] = ((1.0 - tri_consts()[[0, 2]]) * -1e30).astype(np.float32)
    oh = np.zeros((3, 32, 32, 128), np.float32)
    oh[:, np.arange(32), np.arange(32), :] = 1.0
    common["onehot3"] = oh.reshape(96, 32, 128).astype(ml_dtypes.bfloat16)
    common["negmb"] = common["negm"].astype(ml_dtypes.bfloat16)
    common["ident_f"] = np.eye(128, dtype=np.float32)
    in_maps = []
    for b in cores:
        m = dict(common)
        m["xin"] = np.ascontiguousarray(np.concatenate([f(inputs["ctx"])[b], f(inputs["x"])[b]], axis=0))
        m["c2"] = np.ascontiguousarray(np.stack([f(inputs["c_ctx"]), f(inputs["c"])[b]], axis=0))
        in_maps.append(m)
    res = run_bass_kernel_spmd(nc, in_maps, core_ids=list(range(len(cores))))
    return [r["y"] for r in res.results]


FUSED = True


def kernel(**inputs):
    if FUSED:
        outs = run(inputs)
        return np.stack(outs, axis=0).astype(np.float32)
    cur = dict(inputs)
    for L in (0, 1, 2):
        outs = run(cur, layers=(L,), debug_x=True)
        st = np.stack(outs, axis=0)
        cur["ctx"] = np.ascontiguousarray(st[:, :CTX])
        cur["x"] = np.ascontiguousarray(st[:, CTX:])
    outs = run(cur, layers=(3,), debug_x=False)
    return np.stack(outs, axis=0).astype(np.float32)
```
